# Optimizing a Trainium2 kernel written in Bass

```python
import math
import jax, jax.numpy as jnp
from jax import lax
import numpy as np

D_MODEL = 1024
BATCH = 4
SEQ = 4096
DEPTH = 2
DEC_BATCH = 8
DEC_SEQ = 4096
PAST_LEN = 128

N_META = 16
N_MIXERS = 2
N_GDN_LAYERS = (DEPTH + 1) // 2
N_MLA_LAYERS = DEPTH // 2
NORM_EPS = 1e-6

GDN_HEADS = 8
GDN_DK = 128
GDN_DV = 256
GDN_CONV = 5
GDN_CHUNK = 64
GDN_QK_WIDTH = GDN_HEADS * GDN_DK
GDN_V_WIDTH = GDN_HEADS * GDN_DV
GDN_CONV_CH = 2 * GDN_QK_WIDTH + GDN_V_WIDTH
GDN_IN = GDN_CONV_CH + GDN_V_WIDTH + 4 * GDN_HEADS

MLA_HEADS = 16
MLA_Q_LORA = 512
MLA_KV_LORA = 256
MLA_NOPE = 128
MLA_ROPE = 64
MLA_DQK = MLA_NOPE + MLA_ROPE
MLA_DV = 128
MLA_V_WIDTH = MLA_HEADS * MLA_DV
MLA_IN = MLA_Q_LORA + MLA_KV_LORA + MLA_ROPE + MLA_V_WIDTH
ROPE_THETA = 10000.0
Q_BLOCK = 128

kernel_name = 'hybrid_gdn_mla_bidir_encoder'


def _rmsnorm(x, g):
    xf = x.astype(jnp.float32)
    y = xf * lax.rsqrt(jnp.mean(xf * xf, axis=-1, keepdims=True) + NORM_EPS)
    return (y * g.astype(jnp.float32)).astype(x.dtype)


def _l2norm(x):
    xf = x.astype(jnp.float32)
    return xf * lax.rsqrt(jnp.sum(xf * xf, axis=-1, keepdims=True) + NORM_EPS)


def _gated_delta_chunked(q, k, v, beta, g):
    B, T, H, DK = q.shape
    DV = v.shape[-1]
    C = GDN_CHUNK
    N = T // C

    def blk(t):
        t = jnp.moveaxis(t, 2, 1)
        return t.reshape((B, H, N, C) + t.shape[3:])

    q, k, v, beta, g = blk(q), blk(k), blk(v), blk(beta), blk(g)
    gc = jnp.cumsum(g, axis=-1)
    gl = gc[..., -1]
    diff = gc[..., :, None] - gc[..., None, :]
    idx = jnp.arange(C)
    strict = idx[:, None] > idx[None, :]
    incl = idx[:, None] >= idx[None, :]
    decay_strict = jnp.exp(jnp.where(strict, diff, -jnp.inf))
    decay_incl = jnp.exp(jnp.where(incl, diff, -jnp.inf))
    kb = k * beta[..., None]
    a = jnp.einsum('bhnid,bhnjd->bhnij', kb, k) * decay_strict + jnp.eye(C, dtype=q.dtype)
    rhs = jnp.concatenate([v * beta[..., None], kb * jnp.exp(gc)[..., None]], axis=-1)
    sol = lax.linalg.triangular_solve(a, rhs, left_side=True, lower=True, unit_diagonal=True)
    u, w = sol[..., :DV], sol[..., DV:]
    attn = jnp.einsum('bhnid,bhnjd->bhnij', q, k) * decay_incl
    qg = q * jnp.exp(gc)[..., None]
    kd = k * jnp.exp(gl[..., None] - gc)[..., None]

    def step(S, inp):
        qg_c, w_c, u_c, kd_c, attn_c, gl_c = inp
        v_new = u_c - jnp.einsum('bhcd,bhde->bhce', w_c, S)
        o = jnp.einsum('bhcd,bhde->bhce', qg_c, S) + jnp.einsum('bhij,bhje->bhie', attn_c, v_new)
        S = S * jnp.exp(gl_c)[..., None, None] + jnp.einsum('bhcd,bhce->bhde', kd_c, v_new)
        return S, o

    xs = (jnp.moveaxis(qg, 2, 0), jnp.moveaxis(w, 2, 0), jnp.moveaxis(u, 2, 0),
          jnp.moveaxis(kd, 2, 0), jnp.moveaxis(attn, 2, 0), jnp.moveaxis(gl, 2, 0))
    S0 = jnp.zeros((B, H, DK, DV), q.dtype)
    _, o = lax.scan(step, S0, xs)
    o = jnp.moveaxis(o, 0, 2).reshape(B, H, T, DV)
    return jnp.moveaxis(o, 1, 2)


def _pad_seq(t, front, back):
    return jnp.pad(t, ((0, 0), (front, back)) + ((0, 0),) * (t.ndim - 2))


def _bidir_gated_delta(q, k, v, beta, g):
    L = q.shape[1]
    P = (-N_META) % GDN_CHUNK
    fwd = _gated_delta_chunked(_pad_seq(q, P, 0), _pad_seq(k, P, 0), _pad_seq(v, P, 0),
                               _pad_seq(beta[:, :, 0], P, 0), _pad_seq(g[:, :, 0], P, 0))[:, P:]
    qr, kr, vr = jnp.flip(q, 1), jnp.flip(k, 1), jnp.flip(v, 1)
    br, grv = jnp.flip(beta[:, :, 1], 1), jnp.flip(g[:, :, 1], 1)
    bwd = _gated_delta_chunked(_pad_seq(qr, 0, P), _pad_seq(kr, 0, P), _pad_seq(vr, 0, P),
                               _pad_seq(br, 0, P), _pad_seq(grv, 0, P))[:, :L]
    return fwd + jnp.flip(bwd, 1)


def _gdn_mixer(h, w_in, conv_w, a_log, dt_bias, o_norm_g, w_out):
    B, L, _ = h.shape
    proj = h @ w_in
    qkv = proj[..., :GDN_CONV_CH]
    z = proj[..., GDN_CONV_CH:GDN_CONV_CH + GDN_V_WIDTH]
    ba = proj[..., GDN_CONV_CH + GDN_V_WIDTH:].reshape(B, L, 2, 2, GDN_HEADS)
    half = GDN_CONV // 2
    qkv_p = jnp.pad(qkv, ((0, 0), (half, half), (0, 0)))
    conv = qkv_p[:, 0:L] * conv_w[0]
    for j in range(1, GDN_CONV):
        conv = conv + qkv_p[:, j:j + L] * conv_w[j]
    qkv = jax.nn.silu(conv)
    q = _l2norm(qkv[..., :GDN_QK_WIDTH].reshape(B, L, GDN_HEADS, GDN_DK)) * (GDN_DK ** -0.5)
    k = _l2norm(qkv[..., GDN_QK_WIDTH:2 * GDN_QK_WIDTH].reshape(B, L, GDN_HEADS, GDN_DK))
    v = qkv[..., 2 * GDN_QK_WIDTH:].reshape(B, L, GDN_HEADS, GDN_DV).astype(jnp.float32)
    beta = jax.nn.sigmoid(ba[:, :, 0].astype(jnp.float32))
    g = -jnp.exp(a_log.astype(jnp.float32)) * jax.nn.softplus(
        ba[:, :, 1].astype(jnp.float32) + dt_bias.astype(jnp.float32))
    o = _bidir_gated_delta(q, k, v, beta, g)
    o = _rmsnorm(o, o_norm_g).astype(h.dtype)
    y = o.reshape(B, L, GDN_V_WIDTH) * jax.nn.silu(z)
    return y @ w_out


def _rope(x, cos, sin):
    r = x.shape[-1] // 2
    x1, x2 = x[..., :r], x[..., r:]
    return jnp.concatenate([x1 * cos - x2 * sin, x2 * cos + x1 * sin], axis=-1).astype(x.dtype)


def _block_attention(q, k, v):
    B, L, H, Dq = q.shape
    nb = -(-L // Q_BLOCK)
    qp = _pad_seq(q, 0, nb * Q_BLOCK - L)
    qb = jnp.moveaxis(qp.reshape(B, nb, Q_BLOCK, H, Dq), 1, 0)
    scale = Dq ** -0.5

    def one(qblk):
        s = jnp.einsum('bqhd,bkhd->bhqk', qblk, k, preferred_element_type=jnp.float32) * scale
        p = jax.nn.softmax(s, axis=-1)
        return jnp.einsum('bhqk,bkhd->bqhd', p.astype(v.dtype), v)

    o = lax.map(one, qb)
    return jnp.moveaxis(o, 0, 1).reshape(B, nb * Q_BLOCK, H, v.shape[-1])[:, :L]


def _mla_mixer(h, w_in, q_norm_g, kv_norm_g, w_uq, w_ukv, qk_q_g, qk_k_g, w_out):
    B, L, _ = h.shape
    proj = h @ w_in
    o1 = MLA_Q_LORA
    o2 = o1 + MLA_KV_LORA
    o3 = o2 + MLA_ROPE
    cq = _rmsnorm(proj[..., :o1], q_norm_g)
    ckv = _rmsnorm(proj[..., o1:o2], kv_norm_g)
    k_pe = proj[..., o2:o3]
    z = proj[..., o3:]
    q = (cq @ w_uq).reshape(B, L, MLA_HEADS, MLA_DQK)
    kv = (ckv @ w_ukv).reshape(B, L, MLA_HEADS, MLA_NOPE + MLA_DV)
    k_nope, v = kv[..., :MLA_NOPE], kv[..., MLA_NOPE:]
    k = jnp.concatenate([k_nope, jnp.broadcast_to(k_pe[:, :, None, :], (B, L, MLA_HEADS, MLA_ROPE))], axis=-1)
    q = _rmsnorm(q, qk_q_g)
    k = _rmsnorm(k, qk_k_g)
    pos = jnp.arange(L, dtype=jnp.float32)
    inv = ROPE_THETA ** (-jnp.arange(0, MLA_ROPE, 2, dtype=jnp.float32) / MLA_ROPE)
    ang = pos[:, None] * inv[None, :]
    cos = jnp.cos(ang)[:, None, :]
    sin = jnp.sin(ang)[:, None, :]
    q = jnp.concatenate([q[..., :MLA_NOPE], _rope(q[..., MLA_NOPE:], cos, sin)], axis=-1)
    k = jnp.concatenate([k[..., :MLA_NOPE], _rope(k[..., MLA_NOPE:], cos, sin)], axis=-1)
    o = _block_attention(q, k, v)
    y = o.reshape(B, L, MLA_V_WIDTH) * jax.nn.silu(z)
    return y @ w_out


def _trunk(x, meta_tokens, ln_g, gdn_w_in, gdn_conv_w, gdn_a_log, gdn_dt_bias, gdn_o_norm_g, gdn_w_out,
           mla_w_in, mla_q_norm_g, mla_kv_norm_g, mla_w_uq, mla_w_ukv, mla_qk_q_g, mla_qk_k_g, mla_w_out):
    B = x.shape[0]
    meta = jnp.broadcast_to(meta_tokens[None].astype(x.dtype), (B, N_META, D_MODEL))
    h = jnp.concatenate([meta, x], axis=1)
    for i in range(DEPTH):
        hn = _rmsnorm(h, ln_g[i])
        j = i // N_MIXERS
        if i % N_MIXERS == 0:
            out = _gdn_mixer(hn, gdn_w_in[j], gdn_conv_w[j], gdn_a_log[j], gdn_dt_bias[j],
                             gdn_o_norm_g[j], gdn_w_out[j])
        else:
            out = _mla_mixer(hn, mla_w_in[j], mla_q_norm_g[j], mla_kv_norm_g[j], mla_w_uq[j],
                             mla_w_ukv[j], mla_qk_q_g[j], mla_qk_k_g[j], mla_w_out[j])
        h = h + out
    return h[:, N_META:]


def setup_inputs(seed: int = 0) -> dict:
    key = jax.random.key(seed)
    ks = jax.random.split(key, 20)
    f32 = jnp.float32

    def nrm(k, shape, scale):
        return jax.random.normal(k, shape, f32) * scale

    def gain(k, shape):
        return 1.0 + 0.02 * jax.random.normal(k, shape, f32)

    dt = jnp.exp(jax.random.uniform(ks[6], (N_GDN_LAYERS, 2, GDN_HEADS), f32,
                                    math.log(0.001), math.log(0.1)))
    dt_bias = dt + jnp.log(-jnp.expm1(-dt))
    a_log = jnp.log(jax.random.uniform(ks[5], (N_GDN_LAYERS, 2, GDN_HEADS), f32, 1.0, 16.0))
    return {
        'x_prompt': nrm(ks[0], (BATCH, SEQ, D_MODEL), 1.0),
        'x_sample': nrm(ks[1], (DEC_BATCH, DEC_SEQ, D_MODEL), 1.0),
        'meta_tokens': nrm(ks[2], (N_META, D_MODEL), 1.0),
        'ln_g': gain(ks[3], (DEPTH, D_MODEL)),
        'gdn_w_in': nrm(ks[4], (N_GDN_LAYERS, D_MODEL, GDN_IN), D_MODEL ** -0.5),
        'gdn_conv_w': nrm(ks[7], (N_GDN_LAYERS, GDN_CONV, GDN_CONV_CH), GDN_CONV ** -0.5),
        'gdn_a_log': a_log,
        'gdn_dt_bias': dt_bias,
        'gdn_o_norm_g': gain(ks[8], (N_GDN_LAYERS, GDN_DV)),
        'gdn_w_out': nrm(ks[9], (N_GDN_LAYERS, GDN_V_WIDTH, D_MODEL), GDN_V_WIDTH ** -0.5),
        'mla_w_in': nrm(ks[10], (N_MLA_LAYERS, D_MODEL, MLA_IN), D_MODEL ** -0.5),
        'mla_q_norm_g': gain(ks[11], (N_MLA_LAYERS, MLA_Q_LORA)),
        'mla_kv_norm_g': gain(ks[12], (N_MLA_LAYERS, MLA_KV_LORA)),
        'mla_w_uq': nrm(ks[13], (N_MLA_LAYERS, MLA_Q_LORA, MLA_HEADS * MLA_DQK), MLA_Q_LORA ** -0.5),
        'mla_w_ukv': nrm(ks[14], (N_MLA_LAYERS, MLA_KV_LORA, MLA_HEADS * (MLA_NOPE + MLA_DV)), MLA_KV_LORA ** -0.5),
        'mla_qk_q_g': gain(ks[15], (N_MLA_LAYERS, MLA_DQK)),
        'mla_qk_k_g': gain(ks[16], (N_MLA_LAYERS, MLA_DQK)),
        'mla_w_out': nrm(ks[17], (N_MLA_LAYERS, MLA_V_WIDTH, D_MODEL), MLA_V_WIDTH ** -0.5),
    }


def reference(x_prompt, x_sample, meta_tokens, ln_g, gdn_w_in, gdn_conv_w, gdn_a_log, gdn_dt_bias,
              gdn_o_norm_g, gdn_w_out, mla_w_in, mla_q_norm_g, mla_kv_norm_g, mla_w_uq, mla_w_ukv,
              mla_qk_q_g, mla_qk_k_g, mla_w_out):
    y_prompt = _trunk(x_prompt, meta_tokens, ln_g, gdn_w_in, gdn_conv_w, gdn_a_log, gdn_dt_bias,
                      gdn_o_norm_g, gdn_w_out, mla_w_in, mla_q_norm_g, mla_kv_norm_g, mla_w_uq,
                      mla_w_ukv, mla_qk_q_g, mla_qk_k_g, mla_w_out)
    y_sample = _trunk(x_sample, meta_tokens, ln_g, gdn_w_in, gdn_conv_w, gdn_a_log, gdn_dt_bias,
                      gdn_o_norm_g, gdn_w_out, mla_w_in, mla_q_norm_g, mla_kv_norm_g, mla_w_uq,
                      mla_w_ukv, mla_qk_q_g, mla_qk_k_g, mla_w_out)
    return (y_prompt, y_sample)
```

```python
import numpy as np
import ml_dtypes
import concourse.bass as bass
import concourse.mybir as mybir
from concourse.bass_utils import run_bass_kernel_spmd

F32 = mybir.dt.float32
BF16 = mybir.dt.bfloat16
ALU = mybir.AluOpType
AF = mybir.ActivationFunctionType

D = 1024
LX = 4096
NMETA = 16
PAD = 48
LE = 4160
NGR = 65
NSEQ = 2
EPS = 1e-6
COLBLKS = [(0, 64)] + [(64 + 512 * i, 512) for i in range(8)]
TOKTILES = [(0, 64)] + [(64 + 128 * i, 128) for i in range(32)]
GDN_IN = 6176
import os
SCANSTOP = int(os.environ.get('SCANSTOP', '9'))
DMA_K = 6
SAME_ENG_SYNC = True


def gkeys(name, c0, n):
    return [(name, g) for g in range(c0 // 64, (c0 + n + 63) // 64)]


class Sched:
    def __init__(self, nc, es):
        self.nc = nc
        self.eng = {"pe": nc.tensor, "dve": nc.vector, "act": nc.scalar, "pool": nc.gpsimd, "sp": nc.sync}
        self.semh = {}
        for e in self.eng:
            self.semh[(e,)] = es.enter_context(nc.semaphore("s_" + e))
        for q in ("sp", "pool", "act"):
            for s in range(DMA_K):
                self.semh[(q, "d", s)] = es.enter_context(nc.semaphore(f"d_{q}{s}"))
        self.cnt = {e: 0 for e in self.eng}
        self.dman = {q: 0 for q in ("sp", "pool", "act")}
        self.seen = {e: {} for e in self.eng}
        self.lastw = {}
        self.readers = {}
        self.ninst = 0
        self.marks = []

    def _wait(self, e, semk, val):
        if val <= 0 or self.seen[e].get(semk, 0) >= val:
            return
        self.eng[e].wait_ge(self.semh[semk], val)
        self.seen[e][semk] = val

    def op(self, e, fn, reads=(), writes=(), dma=False):
        deps = {}
        for k in reads:
            t = self.lastw.get(k)
            if t is not None:
                deps[t[0]] = max(deps.get(t[0], 0), t[1])
        for k in writes:
            t = self.lastw.get(k)
            if t is not None:
                deps[t[0]] = max(deps.get(t[0], 0), t[1])
            for sk, v in self.readers.get(k, {}).items():
                deps[sk] = max(deps.get(sk, 0), v)
        for sk, v in deps.items():
            if sk == (e,) and (e == "pe" or not SAME_ENG_SYNC) and not dma:
                continue
            self._wait(e, sk, v)
        if dma:
            n = self.dman[e]
            slot = n % DMA_K
            sk = (e, "d", slot)
            self._wait(e, sk, 16 * (n // DMA_K))
            self.dman[e] = n + 1
            inst = fn(self.eng[e])
            inst.then_inc(self.semh[sk], 16)
            tok = (sk, 16 * (n // DMA_K + 1))
        else:
            fns = fn if isinstance(fn, (list, tuple)) else [fn]
            inst = None
            for f in fns:
                inst = f(self.eng[e])
                self.ninst += 1
            self.cnt[e] += 1
            inst.then_inc(self.semh[(e,)], 1)
            tok = ((e,), self.cnt[e])
        for k in reads:
            r = self.readers.setdefault(k, {})
            r[tok[0]] = max(r.get(tok[0], 0), tok[1])
        for k in writes:
            self.lastw[k] = tok
            self.readers[k] = {}
        return tok

    def pe(self, fn, r=(), w=()):
        return self.op("pe", fn, r, w)

    def dve(self, fn, r=(), w=()):
        return self.op("dve", fn, r, w)

    def act(self, fn, r=(), w=()):
        return self.op("act", fn, r, w)

    def pool(self, fn, r=(), w=()):
        return self.op("pool", fn, r, w)

    def load(self, fn, r=(), w=()):
        return self.op("sp", fn, r, w, dma=True)

    def store(self, fn, r=(), w=()):
        return self.op("pool", fn, r, w, dma=True)

    def barrier(self):
        for e in self.eng:
            for e2 in self.eng:
                if e2 != e:
                    self._wait(e, (e2,), self.cnt[e2])
            for q in self.dman:
                n = self.dman[q]
                for s in range(DMA_K):
                    if n > s:
                        last = ((n - 1 - s) // DMA_K) * DMA_K + s
                        self._wait(e, (q, "d", s), 16 * (last // DMA_K + 1))
        self.lastw = {}
        self.readers = {}

    def finish(self):
        self.barrier()


def build(debug=None):
    from contextlib import ExitStack
    nc = bass.Bass("TRN2", target_bir_lowering=False)
    es = ExitStack()

    def din(name, shape, dt=F32):
        return nc.dram_tensor(name, list(shape), dt, kind="ExternalInput").ap()

    xs = din("xs", [NSEQ, LX, D])
    meta = din("meta", [NMETA, D])
    ln_g = din("ln_g", [2, D])
    g_w_in = din("g_w_in", [D, GDN_IN])
    g_conv = din("g_conv", [5, 4096])
    g_alog = din("g_alog", [16])
    g_dtb = din("g_dtb", [16])
    g_on = din("g_on", [256])
    g_w_out = din("g_w_out", [2048, D])
    m_w_in = din("m_w_in", [D, 2880])
    m_qn = din("m_qn", [512])
    m_kvn = din("m_kvn", [256])
    m_wuq = din("m_wuq", [512, 3072])
    m_wukv = din("m_wukv", [256, 4096])
    m_qg = din("m_qg", [192])
    m_kg = din("m_kg", [192])
    m_w_out = din("m_w_out", [2048, D])
    c_ident = din("c_ident", [128, 128])
    c_masks = din("c_masks", [64, 6, 64])
    c_rope = din("c_rope", [64, 2, LE])
    c_rot = din("c_rot", [64, 64])
    out = nc.dram_tensor("out", [NSEQ, LX, D], F32, kind="ExternalOutput").ap()

    skind = "ExternalOutput" if debug else "Internal"

    def dscr(name, shape, dt):
        return nc.dram_tensor(name, list(shape), dt, kind=skind).ap()

    qk_scr = dscr("qk_scr", [16, 128, LE], BF16)
    v_scr = dscr("v_scr", [16, 128, LE], BF16)
    z_scr = dscr("z_scr", [16, 128, LE], BF16)
    y_scr = dscr("y_scr", [16, 128, LE], BF16)
    h1_scr = dscr("h1_scr", [LE, D], F32)

    S = Sched(nc, es)

    def sb(name, shape, dt=F32):
        return es.enter_context(nc.sbuf_tensor(name, list(shape), dt))

    def ps(name, shape, dt=F32):
        return es.enter_context(nc.psum_tensor(name, list(shape), dt))

    block = es.enter_context(nc.Block())

    ident = sb("ident", [128, 128])
    identb = sb("identb", [128, 128], BF16)
    onesb = sb("onesb", [128, 128], BF16)
    ones32 = sb("ones32", [64, 128])
    masks = sb("masks", [64, 6, 64])
    lng = sb("lng", [128, 2, 8])
    convw = sb("convw", [128, 5, 32])
    alog = sb("alog", [64, 16])
    dtb = sb("dtb", [64, 16])
    nea = sb("nea", [64, 16])
    gon = sb("gon", [64, 256])
    epsb = sb("epsb", [128, 1])

    pb = [ps(f"pb{i}", [128, 512]) for i in range(8) if i != 2]
    pb.insert(2, None)
    ptb = ps("ptb", [128, 1024], BF16)

    def setup():
        S.load(lambda e: e.dma_start(out=ident[:], in_=c_ident[:, :]), w=["ident"])
        S.load(lambda e: e.dma_start(out=masks[:], in_=c_masks[:, :, :]), w=["masks"])
        S.load(lambda e: e.dma_start(out=lng[:], in_=ln_g.rearrange("l (c p) -> p l c", p=128), allow_slow_non_contiguous=True), w=["lng"])
        for j in range(5):
            S.load(lambda e, j=j: e.dma_start(out=convw[:, j, :], in_=g_conv[j].rearrange("(c p) -> p c", p=128), allow_slow_non_contiguous=True), w=["convw"])
        S.load(lambda e: e.dma_start(out=alog[:], in_=g_alog.partition_broadcast(64)), w=["alog"])
        S.load(lambda e: e.dma_start(out=dtb[:], in_=g_dtb.partition_broadcast(64)), w=["dtb"])
        S.load(lambda e: e.dma_start(out=gon[:], in_=g_on.partition_broadcast(64)), w=["gon"])
        S.dve(lambda e: e.tensor_copy(out=identb[:], in_=ident[:]), r=["ident"], w=["identb"])
        S.dve(lambda e: e.memset(onesb[:], 1.0), w=["onesb"])
        S.dve(lambda e: e.memset(ones32[:], 1.0), w=["ones32"])
        S.dve(lambda e: e.memset(epsb[:], EPS), w=["epsb"])
        S.act(lambda e: e.activation(out=nea[:], in_=alog[:], func=AF.Exp), r=["alog"], w=["nea"])
        S.dve(lambda e: e.tensor_scalar(out=nea[:], in0=nea[:], scalar1=-1.0, scalar2=None, op0=ALU.mult), r=["nea"], w=["nea"])

    beta = sb("beta", [64, NGR, 2, 8])
    gg = sb("gg", [64, NGR, 2, 8])
    gc = sb("gc", [64, NGR, 2, 8])
    negc = sb("negc", [64, NGR, 2, 8])
    egc = sb("egc", [64, NGR, 2, 8])
    kds = sb("kds", [64, NGR, 2, 8])
    egl = sb("egl", [128, NGR, 2, 8])
    negm4 = sb("negm4", [64, 4, 2, 64])
    strict4 = sb("strict4", [64, 4, 2, 64])
    ARENA_BYTES = 160000
    arena = sb("arena", [128, ARENA_BYTES // 4])
    aoff = {"p": 0}

    def av(shape, dt=F32):
        n = 1
        for d_ in shape[1:]:
            n *= d_
        nb = (n * (2 if dt == BF16 else 4) + 3) // 4 * 4
        o = aoff["p"]
        aoff["p"] = o + nb
        aoff["max"] = max(aoff.get("max", 0), o + nb)
        assert aoff["p"] <= ARENA_BYTES, (aoff["p"], shape)
        v = arena[0:shape[0], o // 4:(o + nb) // 4]
        if dt == BF16:
            v = v.bitcast(BF16)
            if n % 2:
                v = v[:, 0:n]
        if len(shape) > 2:
            names = "abcd"[:len(shape) - 1]
            pat = "p (" + " ".join(names) + ") -> p " + " ".join(names)
            v = v.rearrange(pat, **{names[i]: shape[1 + i] for i in range(len(names))})
        return v

    def sbA(name, shape, dt=F32):
        return av(shape, dt)

    hnT = sbA("hnT", [128, 8, LE], BF16)
    xt = [sbA(f"xt{i}", [128, D]) for i in range(2)]
    xn = [sbA(f"xn{i}", [128, D], BF16) for i in range(2)]
    junk = sbA("junk", [128, D], BF16)
    ssb = [sbA(f"ss{i}", [128, 4]) for i in range(2)]

    aoff_after_p0 = aoff["p"]

    def norm_transpose(tt, src_ap, src_key, n, t0, layer):
        b = tt % 2
        ss = ssb[b]
        S.act(lambda e: e.activation(out=junk[0:n, :], in_=src_ap, func=AF.Square, accum_out=ss[0:n, 0:1]),
              r=[src_key], w=["junk", ("ss", b)])
        S.act(lambda e: e.activation(out=ss[0:n, 1:2], in_=ss[0:n, 0:1], func=AF.Sqrt, bias=epsb[0:n, 0:1], scale=1.0 / D),
              r=[("ss", b), "epsb"], w=[("ss", b)])
        S.dve(lambda e: e.reciprocal(out=ss[0:n, 2:3], in_=ss[0:n, 1:2]), r=[("ss", b)], w=[("ss", b)])
        S.act(lambda e: e.activation(out=xn[b][0:n, :], in_=src_ap, func=AF.Copy, scale=ss[0:n, 2:3]),
              r=[src_key, ("ss", b)], w=[("xn", b)])
        S.pe([(lambda e, c=c: e.transpose(out=ptb[:, c * 128:c * 128 + n], in_=xn[b][0:n, c * 128:(c + 1) * 128], identity=identb[0:n, 0:n]))
              for c in range(8)], r=[("xn", b), "identb"], w=["ptb"])
        pv = ptb[:, :].rearrange("p (c t) -> p c t", c=8)[:, :, 0:n]
        S.dve(lambda e: e.tensor_tensor(out=hnT[:, :, t0:t0 + n], in0=pv,
                                        in1=lng[:, layer, :].unsqueeze(2).to_broadcast([128, 8, n]), op=ALU.mult),
              r=["ptb", "lng"], w=gkeys("hnT", t0, n))

    def load_x_tile(s, tt):
        t0, n = TOKTILES[tt]
        b = tt % 2
        if tt == 0:
            S.dve(lambda e: e.memset(xt[b][0:64, :], 0.0), w=[("xt", b)])
            S.load(lambda e: e.dma_start(out=xt[b][PAD:64, :], in_=meta[:, :]), w=[("xt", b)])
        else:
            r0 = t0 - 64
            S.load(lambda e: e.dma_start(out=xt[b][0:n, :], in_=xs[s, r0:r0 + n, :]), w=[("xt", b)])

    def phase_p0(s):
        for tt, (t0, n) in enumerate(TOKTILES):
            load_x_tile(s, tt)
            norm_transpose(tt, xt[tt % 2][0:n, :], ("xt", tt % 2), n, t0, 0)

    wst = [sbA(f"wst{i}", [128, 8, 128]) for i in range(2)]
    wbf = [sbA(f"wbf{i}", [128, 8, 128], BF16) for i in range(2)]
    pre = [sbA("pre0", [128, LE + 4], BF16)] * 2
    acc = sbA("acc", [128, LE])
    sqb = sbA("sqb", [128, LE], BF16)
    obf = [sbA(f"obf{i}", [128, LE], BF16) for i in range(2)]
    rtmp = [sbA(f"rtmp{i}", [128, 512]) for i in range(2)]
    gbraw = sbA("gbraw", [64, NGR, 32])
    wba_st = sbA("wba_st", [128, 8, 32])
    wba = sbA("wba", [128, 8, 32], BF16)

    w_in_v = g_w_in.rearrange("(c p) n -> p c n", p=128)
    mmrot = [0]

    def mmbank():
        mmrot[0] ^= 1
        return mmrot[0]

    def load_w(wv_ap, idx, ncol=128):
        b = idx % 2
        S.load(lambda e: e.dma_start(out=wst[b][:, :, 0:ncol], in_=wv_ap), w=[("wst", b)])
        S.pool(lambda e: e.tensor_copy(out=wbf[b][:, :, 0:ncol], in_=wst[b][:, :, 0:ncol]), r=[("wst", b)], w=[("wbf", b)])
        return wbf[b]

    def phase_g1(s):
        for b in range(1):
            S.dve(lambda e, b=b: e.memset(pre[b][:, 0:2], 0.0), w=[("pre", b)])
            S.dve(lambda e, b=b: e.memset(pre[b][:, LE + 2:LE + 4], 0.0), w=[("pre", b)])
        flist = range(48)
        if debug == "g1a":
            flist = [0, 16, 32]
        if debug == "g1b":
            flist = []
        for f in flist:
            wt = load_w(w_in_v[:, :, f * 128:(f + 1) * 128], f)
            wk = ("wbf", f % 2)
            pb_ = 0
            for (c0, w) in COLBLKS:
                bk = mmbank()
                S.pe([(lambda e, c=c: e.matmul(pb[bk][:, 0:w], lhsT=wt[:, c, :], rhs=hnT[:, c, c0:c0 + w], start=(c == 0), stop=(c == 7)))
                      for c in range(8)], r=[wk] + gkeys("hnT", c0, w), w=[("pb", bk)])
                if f < 32:
                    S.act(lambda e: e.activation(out=pre[pb_][:, 2 + c0:2 + c0 + w], in_=pb[bk][:, 0:w], func=AF.Copy),
                          r=[("pb", bk)], w=[("pre", pb_)])
                else:
                    ob = obf[f % 2]
                    S.act(lambda e: e.activation(out=ob[:, c0:c0 + w], in_=pb[bk][:, 0:w], func=AF.Silu),
                          r=[("pb", bk)], w=[("obf", f % 2)])
            if f >= 32:
                S.store(lambda e: e.dma_start(out=z_scr[f - 32, :, :], in_=obf[f % 2][:, :]), r=[("obf", f % 2)], w=[("z_scr", f - 32)])
                continue
            pr = pre[pb_]
            S.dve(lambda e: e.tensor_scalar(out=acc[:], in0=pr[:, 0:LE], scalar1=convw[:, 0, f:f + 1], scalar2=None, op0=ALU.mult),
                  r=[("pre", pb_), "convw"], w=["acc"])
            for j in range(1, 5):
                S.dve(lambda e, j=j: e.scalar_tensor_tensor(out=acc[:], in0=pr[:, j:j + LE], scalar=convw[:, j, f:f + 1], in1=acc[:],
                                                            op0=ALU.mult, op1=ALU.add), r=[("pre", pb_), "convw", "acc"], w=["acc"])
            ob = obf[f % 2]
            ok = ("obf", f % 2)
            if f >= 16:
                S.act(lambda e: e.activation(out=ob[:], in_=acc[:], func=AF.Silu), r=["acc"], w=[ok])
                S.dve(lambda e: e.memset(ob[:, 0:PAD], 0.0), w=[ok])
                S.store(lambda e: e.dma_start(out=v_scr[f - 16, :, :], in_=ob[:, :]), r=[ok], w=[("v_scr", f - 16)])
            else:
                S.act(lambda e: e.activation(out=acc[:], in_=acc[:], func=AF.Silu), r=["acc"], w=["acc"])
                S.dve(lambda e: e.tensor_tensor(out=sqb[:], in0=acc[:], in1=acc[:], op=ALU.mult), r=["acc"], w=["sqb"])
                qscale = (128.0 ** -0.5) if f < 8 else 1.0
                for (c0, w) in COLBLKS:
                    bk = mmbank()
                    rt = rtmp[bk]
                    S.pe(lambda e: e.matmul(pb[bk][:, 0:w], lhsT=onesb[:, :], rhs=sqb[:, c0:c0 + w], start=True, stop=True),
                         r=["sqb", "onesb"], w=[("pb", bk)])
                    S.act(lambda e: e.activation(out=rt[:, 0:w], in_=pb[bk][:, 0:w], func=AF.Sqrt, bias=epsb[:, 0:1], scale=1.0),
                          r=[("pb", bk), "epsb"], w=[("rtmp", bk)])
                    S.dve(lambda e: e.reciprocal(out=rt[:, 0:w], in_=rt[:, 0:w]), r=[("rtmp", bk)], w=[("rtmp", bk)])
                    S.dve(lambda e: e.scalar_tensor_tensor(out=ob[:, c0:c0 + w], in0=acc[:, c0:c0 + w], scalar=qscale, in1=rt[:, 0:w],
                                                           op0=ALU.mult, op1=ALU.mult), r=["acc", ("rtmp", bk)], w=[ok])
                S.dve(lambda e: e.memset(ob[:, 0:PAD], 0.0), w=[ok])
                S.store(lambda e: e.dma_start(out=qk_scr[f, :, :], in_=ob[:, :]), r=[ok], w=[("qk_scr", f)])
        if debug == "g1a":
            return
        S.load(lambda e: e.dma_start(out=wba_st[:], in_=w_in_v[:, :, 6144:6176]), w=["wba_st"])
        S.pool(lambda e: e.tensor_copy(out=wba[:], in_=wba_st[:]), r=["wba_st"], w=["wba"])
        for g0 in range(0, NGR, 16):
            ng = min(16, NGR - g0)
            bk = mmbank()
            pv = pb[bk][0:64, :].rearrange("p (g n) -> p g n", n=32)
            for gi in range(ng):
                gr = g0 + gi
                S.pe([(lambda e, c=c: e.matmul(pv[:, gi, :], lhsT=hnT[:, c, gr * 64:(gr + 1) * 64], rhs=wba[:, c, :], start=(c == 0), stop=(c == 7)))
                      for c in range(8)], r=["wba", ("hnT", gr)], w=[("pb", bk)])
            S.act(lambda e: e.activation(out=gbraw[:, g0:g0 + ng, :], in_=pv[:, 0:ng, :], func=AF.Copy), r=[("pb", bk)], w=["gbraw"])
        import os
        stopat = int(os.environ.get("STOPAT", "99"))
        if stopat <= 1:
            return
        S.act(lambda e: e.activation(out=beta[:].rearrange("p g a b -> p g (a b)"), in_=gbraw[:, :, 0:16], func=AF.Sigmoid), r=["gbraw"], w=["beta"])
        ggf = gg[:].rearrange("p g a b -> p g (a b)")
        S.dve(lambda e: e.tensor_tensor(out=ggf, in0=gbraw[:, :, 16:32], in1=dtb[:].unsqueeze(1).to_broadcast([64, NGR, 16]), op=ALU.add),
              r=["gbraw", "dtb"], w=["gg"])
        S.act(lambda e: e.activation(out=ggf, in_=ggf, func=AF.Exp), r=["gg"], w=["gg"])
        S.act(lambda e: e.activation(out=ggf, in_=ggf, func=AF.Ln, bias=1.0, scale=1.0), r=["gg"], w=["gg"])
        S.dve(lambda e: e.tensor_tensor(out=ggf, in0=ggf, in1=nea[:].unsqueeze(1).to_broadcast([64, NGR, 16]), op=ALU.mult),
              r=["gg", "nea"], w=["gg"])
        if stopat <= 2:
            return
        S.dve(lambda e: e.memset(gg[0:PAD, 0, :, :], 0.0), w=["gg"])
        S.dve(lambda e: e.memset(beta[0:PAD, 0, :, :], 0.0), w=["beta"])
        if stopat <= 3:
            return
        for g0 in range(0, NGR, 32):
            ng = min(32, NGR - g0)
            bk = mmbank()
            pv = pb[bk][0:64, :].rearrange("p (g a b) -> p g a b", a=2, b=8)
            fns = []
            for gi in range(ng):
                gr = g0 + gi
                fns.append(lambda e, gi=gi, gr=gr: e.matmul(pv[:, gi, 0, :], lhsT=masks[:, 0, :], rhs=gg[:, gr, 0, :], start=True, stop=True))
                fns.append(lambda e, gi=gi, gr=gr: e.matmul(pv[:, gi, 1, :], lhsT=masks[:, 1, :], rhs=gg[:, gr, 1, :], start=True, stop=True))
            S.pe(fns, r=["gg", "masks"], w=[("pb", bk)])
            S.act(lambda e: e.activation(out=gc[:, g0:g0 + ng], in_=pv[:, 0:ng], func=AF.Copy), r=[("pb", bk)], w=["gc"])
            if stopat <= 4:
                continue
            bk2 = mmbank()
            pv2 = pb[bk2][:, :].rearrange("p (g a b) -> p g a b", a=2, b=8)
            fns = [(lambda e, gi=gi: e.matmul(pv2[:, gi].rearrange("p a b -> p (a b)"), lhsT=ones32[:, :],
                                               rhs=gg[:, g0 + gi].rearrange("p a b -> p (a b)"), start=True, stop=True)) for gi in range(ng)]
            S.pe(fns, r=["gg", "ones32"], w=[("pb", bk2)])
            if stopat <= 5:
                continue
            S.act(lambda e: e.activation(out=egl[:, g0:g0 + ng], in_=pv2[:, 0:ng], func=AF.Exp), r=[("pb", bk2)], w=["egl"])
            if stopat <= 6:
                continue
            S.act(lambda e: e.activation(out=kds[:, g0:g0 + ng], in_=pv2[0:64, 0:ng], func=AF.Copy), r=[("pb", bk2)], w=["kds"])
            S.dve(lambda e: e.tensor_tensor(out=kds[:, g0:g0 + ng].rearrange("p g a b -> p (g a b)"), in0=kds[:, g0:g0 + ng].rearrange("p g a b -> p (g a b)"),
                                            in1=gc[:, g0:g0 + ng].rearrange("p g a b -> p (g a b)"), op=ALU.subtract),
                  r=["gc", "kds"], w=["kds"])
        if stopat <= 7:
            return
        S.act(lambda e: e.activation(out=kds[:], in_=kds[:], func=AF.Exp), r=["kds"], w=["kds"])
        S.act(lambda e: e.activation(out=egc[:], in_=gc[:], func=AF.Exp), r=["gc"], w=["egc"])
        S.dve(lambda e: e.tensor_scalar(out=negc[:], in0=egc[:], scalar1=-1.0, scalar2=None, op0=ALU.mult), r=["egc"], w=["negc"])


    aoff_after_g1 = aoff["p"]
    aoff["p"] = 0
    qT = av([128, LE], BF16)
    kT = av([128, LE], BF16)
    vz = av([128, LE], BF16)
    vtok = av([64, NGR, 256], BF16)
    ob = av([64, NGR, 256], BF16)
    Rr = av([64, NGR, 2, 64], BF16)
    At = av([64, NGR, 2, 64], BF16)
    rhsD = av([64, 4, 2, 64])
    dif = av([64, 4, 2, 64])
    DTi = av([64, 4, 2, 64])
    DTs = av([64, 4, 2, 64])
    X4 = av([64, 8, 64], BF16)
    XT = av([64, 8, 64], BF16)
    Pa = [av([64, 8, 64], BF16) for _ in range(2)]
    PaT = [av([64, 8, 64], BF16) for _ in range(2)]
    R32 = av([64, 8, 64])
    Rb = av([64, 8, 64], BF16)
    S32 = [av([128, 256]) for _ in range(2)]
    Sbf = [av([128, 256], BF16) for _ in range(2)]
    xb = [av([64, 256], BF16) for _ in range(2)]
    vn = [av([64, 256], BF16) for _ in range(2)]
    kd = [av([64, 128], BF16) for _ in range(2)]
    t1 = [av([64, 256]) for _ in range(2)]
    ssq = av([64, NGR + 3])
    rstd = av([64, NGR + 3])
    junk2 = av([64, 256], BF16)
    yb = [av([128, 256], BF16) for _ in range(2)]

    def phase_g2(s, heads=range(8)):
        S.dve(lambda e: e.tensor_copy(out=negm4[:], in_=masks[:, 4:6, :].unsqueeze(1).to_broadcast([64, 4, 2, 64])), r=["masks"], w=["negm4"])
        S.dve(lambda e: e.tensor_copy(out=strict4[:], in_=masks[:, 2:4, :].unsqueeze(1).to_broadcast([64, 4, 2, 64])), r=["masks"], w=["strict4"])
        for h in heads:
            S.load(lambda e: e.dma_start(out=qT[:, :], in_=qk_scr[h, :, :]), r=[("qk_scr", h)], w=["qT"])
            S.load(lambda e: e.dma_start(out=kT[:, :], in_=qk_scr[8 + h, :, :]), r=[("qk_scr", 8 + h)], w=["kT"])
            for half in range(2):
                S.load(lambda e: e.dma_start(out=vz[:, :], in_=v_scr[2 * h + half, :, :]), r=[("v_scr", 2 * h + half)], w=["vz"])
                for g0 in range(0, NGR, 8):
                    ng = min(8, NGR - g0)
                    S.pe([(lambda e, gi=gi: e.transpose(out=ptb[0:64, gi * 128:(gi + 1) * 128], in_=vz[:, (g0 + gi) * 64:(g0 + gi + 1) * 64], identity=identb[:, :]))
                          for gi in range(ng)], r=["vz", "identb"], w=["ptb"])
                    pv = ptb[0:64, :].rearrange("p (g n) -> p g n", n=128)
                    S.act(lambda e: e.activation(out=vtok[:, g0:g0 + ng, half * 128:(half + 1) * 128], in_=pv[:, 0:ng, :], func=AF.Copy),
                          r=["ptb"], w=["vtok"])
            S.marks.append(("g2_h%d_load" % h, dict(S.cnt)))
            for c0 in range(0, NGR, 4):
                nck = min(4, NGR - c0)
                nu = nck * 2
                W = nck * 128
                for d in range(2):
                    S.dve(lambda e, d=d: e.tensor_tensor(out=rhsD[:, 0:nck, d, :], in0=gg[:, c0:c0 + nck, d, h:h + 1].to_broadcast([64, nck, 64]),
                                                         in1=masks[:, d:d + 1, :].to_broadcast([64, nck, 64]), op=ALU.mult),
                          r=["gg", "masks"], w=["rhsD"])
                S.pe([lambda e: e.matmul(pb[1][0:64, 0:W], lhsT=ones32[:, 0:64], rhs=rhsD[:, 0:nck].rearrange("p a b c -> p (a b c)"), start=True, stop=False),
                      lambda e: e.matmul(pb[1][0:64, 0:W], lhsT=ident[0:64, 0:64], rhs=negm4[:, 0:nck].rearrange("p a b c -> p (a b c)"), start=False, stop=True)],
                     r=["rhsD", "ones32", "ident", "negm4"], w=[("pb", 1)])
                S.act(lambda e: e.activation(out=dif[:, 0:nck].rearrange("p a b c -> p (a b c)"), in_=pb[1][0:64, 0:W], func=AF.Copy), r=[("pb", 1)], w=["dif"])
                S.dve(lambda e: e.tensor_tensor(out=dif[:, 0:nck].rearrange("p a b c -> p (a b) c"), in0=dif[:, 0:nck].rearrange("p a b c -> p (a b) c"),
                                                in1=gc[:, c0:c0 + nck].rearrange("p g a b -> p (g a) b")[:, :, h:h + 1].to_broadcast([64, nu, 64]), op=ALU.subtract),
                      r=["dif", "gc"], w=["dif"])
                S.act(lambda e: e.activation(out=DTi[:, 0:nck].rearrange("p a b c -> p (a b c)"), in_=dif[:, 0:nck].rearrange("p a b c -> p (a b c)"), func=AF.Exp),
                      r=["dif"], w=["DTi"])
                S.dve(lambda e: e.tensor_tensor(out=DTs[:, 0:nck].rearrange("p a b c -> p (a b c)"), in0=DTi[:, 0:nck].rearrange("p a b c -> p (a b c)"),
                                                in1=strict4[:, 0:nck].rearrange("p a b c -> p (a b c)"), op=ALU.mult), r=["DTi", "strict4"], w=["DTs"])
                S.dve(lambda e: e.tensor_tensor(out=DTs[:, 0:nck].rearrange("p a b c -> p (a b) c"), in0=DTs[:, 0:nck].rearrange("p a b c -> p (a b) c"),
                                                in1=beta[:, c0:c0 + nck].rearrange("p g a b -> p (g a) b")[:, :, h:h + 1].to_broadcast([64, nu, 64]), op=ALU.mult),
                      r=["DTs", "beta"], w=["DTs"])
                pK = pb[0][0:64, :].rearrange("p (t g n) -> p t g n", t=2, g=4)
                fns = []
                for ci in range(nck):
                    cs = slice((c0 + ci) * 64, (c0 + ci + 1) * 64)
                    fns.append(lambda e, ci=ci, cs=cs: e.matmul(pK[:, 0, ci, :], lhsT=kT[:, cs], rhs=kT[:, cs], start=True, stop=True))
                    fns.append(lambda e, ci=ci, cs=cs: e.matmul(pK[:, 1, ci, :], lhsT=kT[:, cs], rhs=qT[:, cs], start=True, stop=True))
                S.pe(fns, r=["kT", "qT"], w=[("pb", 0)])
                X44 = X4[:, :, :].rearrange("p (g d) n -> p g d n", d=2)
                for d in range(2):
                    S.dve(lambda e, d=d: e.tensor_tensor(out=X44[:, 0:nck, d, :], in0=pK[:, 0, 0:nck, :], in1=DTs[:, 0:nck, d, :], op=ALU.mult),
                          r=[("pb", 0), "DTs"], w=["X4"])
                    S.dve(lambda e, d=d: e.tensor_tensor(out=At[:, c0:c0 + nck, d, :], in0=pK[:, 1, 0:nck, :], in1=DTi[:, 0:nck, d, :], op=ALU.mult),
                          r=[("pb", 0), "DTi"], w=["At"])
                S.pe([(lambda e, u=u: e.transpose(out=ptb[0:64, 512 + u * 64:512 + (u + 1) * 64], in_=X4[:, u, :], identity=identb[0:64, 0:64])) for u in range(nu)],
                     r=["X4", "identb"], w=["ptx"])
                S.act(lambda e: e.activation(out=XT[:, 0:nu, :].rearrange("p a b -> p (a b)"), in_=ptb[0:64, 512:512 + nu * 64], func=AF.Copy), r=["ptx"], w=["XT"])
                S.dve(lambda e: e.tensor_tensor(out=R32[:, 0:nu, :], in0=ident[0:64, 0:64].unsqueeze(1).to_broadcast([64, nu, 64]), in1=X4[:, 0:nu, :], op=ALU.subtract),
                      r=["ident", "X4"], w=["R32"])
                S.pool(lambda e: e.tensor_copy(out=Rb[:, 0:nu, :], in_=R32[:, 0:nu, :]), r=["R32"], w=["Rb"])
                P, PT, Pk, PTk = X4, XT, "X4", "XT"
                for lvl in range(5):
                    nb = lvl % 2
                    last = (lvl == 4)
                    if not last:
                        S.pe([(lambda e, u=u, P=P, PT=PT: e.matmul(pb[0][0:64, u * 64:(u + 1) * 64], lhsT=PT[:, u, :], rhs=P[:, u, :], start=True, stop=True)) for u in range(nu)],
                             r=[Pk, PTk], w=[("pb", 0)])
                    S.pe([(lambda e, u=u, P=P, PT=PT: e.matmul(pb[1][0:64, u * 64:(u + 1) * 64], lhsT=P[:, u, :], rhs=PT[:, u, :], start=True, stop=True)) for u in range(nu)],
                         r=[Pk, PTk], w=[("pb", 1)])
                    if not last:
                        S.act(lambda e, nb=nb: e.activation(out=Pa[nb][:, 0:nu, :].rearrange("p a b -> p (a b)"), in_=pb[0][0:64, 0:nu * 64], func=AF.Copy),
                              r=[("pb", 0)], w=[("Pa", nb)])
                    S.dve(lambda e, nb=nb: e.tensor_copy(out=PaT[nb][:, 0:nu, :].rearrange("p a b -> p (a b)"), in_=pb[1][0:64, 0:nu * 64]),
                          r=[("pb", 1)], w=[("PaT", nb)])
                    S.pe([(lambda e, u=u, nb=nb: e.matmul(pb[3][0:64, u * 64:(u + 1) * 64], lhsT=PaT[nb][:, u, :], rhs=Rb[:, u, :], start=True, stop=True)) for u in range(nu)],
                         r=[("PaT", nb), "Rb"], w=[("pb", 3)])
                    if not last:
                        S.dve(lambda e: e.tensor_tensor(out=R32[:, 0:nu, :].rearrange("p a b -> p (a b)"), in0=pb[3][0:64, 0:nu * 64],
                                                        in1=R32[:, 0:nu, :].rearrange("p a b -> p (a b)"), op=ALU.add), r=[("pb", 3), "R32"], w=["R32"])
                        S.pool(lambda e: e.tensor_copy(out=Rb[:, 0:nu, :], in_=R32[:, 0:nu, :]), r=["R32"], w=["Rb"])
                    else:
                        S.dve(lambda e: e.tensor_tensor(out=Rr[:, c0:c0 + nck].rearrange("p a b c -> p (a b c)"), in0=pb[3][0:64, 0:nu * 64],
                                                        in1=R32[:, 0:nu, :].rearrange("p a b -> p (a b)"), op=ALU.add), r=[("pb", 3), "R32"], w=["Rr"])
                    P, PT, Pk, PTk = Pa[nb], PaT[nb], ("Pa", nb), ("PaT", nb)
            S.marks.append(("g2_h%d_prep" % h, dict(S.cnt)))
            for d in range(2):
                S.dve(lambda e, d=d: e.memset(S32[d][:, :], 0.0), w=[("S32", d)])
                S.dve(lambda e, d=d: e.memset(Sbf[d][:, :], 0.0), w=[("Sbf", d)])
            for t in range(NGR):
                for d in range(2):
                    c = t if d == 0 else NGR - 1 - t
                    cs = slice(c * 64, (c + 1) * 64)
                    pS, pO = pb[4 + 2 * d], pb[5 + 2 * d]
                    kS_, kO_ = ("pb", 4 + 2 * d), ("pb", 5 + 2 * d)
                    S.pe(lambda e: e.matmul(pS[0:64, 0:256], lhsT=kT[:, cs], rhs=Sbf[d][:, :], start=True, stop=True), r=["kT", ("Sbf", d)], w=[(kS_, 0)])
                    S.pe(lambda e: e.matmul(pO[0:64, 0:256], lhsT=qT[:, cs], rhs=Sbf[d][:, :], start=True, stop=True), r=["qT", ("Sbf", d)], w=[(kO_, 0)])
                    S.pe(lambda e: e.transpose(out=ptb[0:64, d * 128:(d + 1) * 128], in_=kT[:, cs], identity=identb[:, :]), r=["kT", "identb"], w=[("ptk", d)])
                    S.dve(lambda e: e.scalar_tensor_tensor(out=xb[d][:, :], in0=pS[0:64, 0:256], scalar=negc[:, c, d, h:h + 1], in1=vtok[:, c, :],
                                                           op0=ALU.mult, op1=ALU.add), r=[(kS_, 0), "negc", "vtok"], w=[("xb", d)])
                    S.act(lambda e: e.activation(out=kd[d][:, :], in_=ptb[0:64, d * 128:(d + 1) * 128], func=AF.Copy, scale=kds[:, c, d, h:h + 1]),
                          r=[("ptk", d), "kds"], w=[("kd", d)])
                    S.act(lambda e: e.activation(out=t1[d][:, :], in_=pO[0:64, 0:256], func=AF.Copy, scale=egc[:, c, d, h:h + 1]),
                          r=[(kO_, 0), "egc"], w=[("t1", d)])
                    S.pe(lambda e: e.matmul(pS[0:64, 256:512], lhsT=Rr[:, c, d, :], rhs=xb[d][:, :], start=True, stop=True), r=["Rr", ("xb", d)], w=[(kS_, 1)])
                    S.act(lambda e: e.activation(out=vn[d][:, :], in_=pS[0:64, 256:512], func=AF.Copy, scale=beta[:, c, d, h:h + 1]),
                          r=[(kS_, 1), "beta"], w=[("vn", d)])
                    S.pe(lambda e: e.matmul(pO[0:64, 256:512], lhsT=At[:, c, d, :], rhs=vn[d][:, :], start=True, stop=True), r=["At", ("vn", d)], w=[(kO_, 1)])
                    S.pe(lambda e: e.matmul(pS[:, 0:256], lhsT=kd[d][:, :], rhs=vn[d][:, :], start=True, stop=True), r=[("kd", d), ("vn", d)], w=[(kS_, 0)])
                    if t < 32 or (t == 32 and d == 0):
                        S.dve(lambda e: e.tensor_tensor(out=ob[:, c, :], in0=pO[0:64, 256:512], in1=t1[d][:, :], op=ALU.add), r=[(kO_, 1), ("t1", d)], w=[("ob", c)])
                    else:
                        S.dve(lambda e: e.tensor_tensor(out=t1[d][:, :], in0=pO[0:64, 256:512], in1=t1[d][:, :], op=ALU.add), r=[(kO_, 1), ("t1", d)], w=[("t1", d)])
                        S.dve(lambda e: e.tensor_tensor(out=ob[:, c, :], in0=ob[:, c, :], in1=t1[d][:, :], op=ALU.add), r=[("ob", c), ("t1", d)], w=[("ob", c)])
                    S.dve(lambda e: e.scalar_tensor_tensor(out=Sbf[d][:, :], in0=S32[d][:, :], scalar=egl[:, c, d, h:h + 1], in1=pS[:, 0:256],
                                                           op0=ALU.mult, op1=ALU.add), r=[("S32", d), "egl", (kS_, 0)], w=[("Sbf", d)])
                    S.dve(lambda e: e.scalar_tensor_tensor(out=S32[d][:, :], in0=S32[d][:, :], scalar=egl[:, c, d, h:h + 1], in1=pS[:, 0:256],
                                                           op0=ALU.mult, op1=ALU.add), r=[("S32", d), "egl", (kS_, 0)], w=[("S32", d)])
            S.marks.append(("g2_h%d_scan" % h, dict(S.cnt)))
            S.marks.append(("g2_h%d_scan" % h, dict(S.cnt)))
            for c in range(NGR):
                S.act(lambda e, c=c: e.activation(out=junk2[:, :], in_=ob[:, c, :], func=AF.Square, accum_out=ssq[:, c:c + 1]), r=[("ob", c)], w=["junk2", "ssq"])
            S.act(lambda e: e.activation(out=rstd[:, 0:NGR], in_=ssq[:, 0:NGR], func=AF.Sqrt, bias=epsb[0:64, 0:1], scale=1.0 / 256), r=["ssq", "epsb"], w=["rstd"])
            S.dve(lambda e: e.reciprocal(out=rstd[:, 0:NGR], in_=rstd[:, 0:NGR]), r=["rstd"], w=["rstd"])
            for c in range(NGR):
                S.dve(lambda e, c=c: e.scalar_tensor_tensor(out=ob[:, c, :], in0=ob[:, c, :], scalar=rstd[:, c:c + 1], in1=gon[:, :], op0=ALU.mult, op1=ALU.mult),
                      r=[("ob", c), "rstd", "gon"], w=[("ob", c)])
            for half in range(2):
                S.load(lambda e: e.dma_start(out=vz[:, :], in_=z_scr[2 * h + half, :, :]), r=[("z_scr", 2 * h + half)], w=["vz"])
                for gi_, c0 in enumerate(range(0, NGR, 4)):
                    nck = min(4, NGR - c0)
                    ybb = yb[gi_ % 2]
                    yk = ("yb", gi_ % 2)
                    S.pe([(lambda e, ci=ci: e.transpose(out=ptb[:, 256 + ci * 64:256 + (ci + 1) * 64], in_=ob[:, c0 + ci, half * 128:(half + 1) * 128], identity=identb[0:64, 0:64]))
                          for ci in range(nck)], r=[("ob", c0 + ci) for ci in range(nck)] + ["identb"], w=["ptb2"])
                    S.dve(lambda e: e.tensor_tensor(out=ybb[:, 0:nck * 64], in0=ptb[:, 256:256 + nck * 64], in1=vz[:, c0 * 64:(c0 + nck) * 64], op=ALU.mult),
                          r=["ptb2", "vz"], w=[yk])
                    S.store(lambda e: e.dma_start(out=y_scr[2 * h + half, :, c0 * 64:(c0 + nck) * 64], in_=ybb[:, 0:nck * 64]), r=[yk], w=[("y_scr", 2 * h + half)])
            S.barrier()

    aoff_g2 = aoff["p"]
    aoff["p"] = aoff_after_p0
    wo = av([128, 16, D], BF16)
    wo_st = [av([128, D]) for _ in range(2)]
    ytile = [av([128, 16, 128], BF16) for _ in range(2)]
    h1t = [av([128, D]) for _ in range(2)]

    def load_wout(w_ap):
        wv = w_ap.rearrange("(c p) n -> p c n", p=128)
        for c in range(16):
            b = c % 2
            S.load(lambda e, c=c, b=b: e.dma_start(out=wo_st[b][:, :], in_=wv[:, c, :]), w=[("wo_st", b)])
            S.pool(lambda e, c=c, b=b: e.tensor_copy(out=wo[:, c, :], in_=wo_st[b][:, :]), r=[("wo_st", b)], w=["wo"])

    def phase_outproj(s, layer):
        load_wout(g_w_out if layer == 0 else m_w_out)
        yv = y_scr.rearrange("c p t -> p c t")
        for tt, (t0, n) in enumerate(TOKTILES):
            if layer == 1 and tt == 0:
                continue
            b = tt % 2
            S.load(lambda e: e.dma_start(out=ytile[b][:, :, 0:n], in_=yv[:, :, t0:t0 + n]), r=[("y_scr", c) for c in range(16)], w=[("ytile", b)])
            if layer == 0:
                load_x_tile(s, tt)
            else:
                S.load(lambda e: e.dma_start(out=xt[b][0:n, :], in_=h1_scr[t0:t0 + n, :]), r=["h1_scr"], w=[("xt", b)])
            for hf in range(2):
                bk = mmbank()
                S.pe([(lambda e, c=c: e.matmul(pb[bk][0:n, 0:512], lhsT=ytile[b][:, c, 0:n], rhs=wo[:, c, hf * 512:(hf + 1) * 512], start=(c == 0), stop=(c == 15)))
                      for c in range(16)], r=[("ytile", b), "wo"], w=[("pb", bk)])
                S.dve(lambda e: e.tensor_tensor(out=h1t[b][0:n, hf * 512:(hf + 1) * 512], in0=pb[bk][0:n, 0:512], in1=xt[b][0:n, hf * 512:(hf + 1) * 512], op=ALU.add),
                      r=[("pb", bk), ("xt", b)], w=[("h1t", b)])
            if layer == 0:
                S.store(lambda e: e.dma_start(out=h1_scr[t0:t0 + n, :], in_=h1t[b][0:n, :]), r=[("h1t", b)], w=["h1_scr"])
                norm_transpose(tt, h1t[b][0:n, :], ("h1t", b), n, t0, 1)
            else:
                r0 = t0 - 64
                S.store(lambda e: e.dma_start(out=out[s, r0:r0 + n, :], in_=h1t[b][0:n, :]), r=[("h1t", b)], w=["out"])
        S.barrier()


    aoff["p"] = aoff_after_p0
    wM = av([128, 8, 832], BF16)
    wMst = [av([128, 8, 128]) for _ in range(2)]
    wMz = [av([128, 8, 128], BF16) for _ in range(2)]
    raw = av([128, 7, 512])
    sqt = av([128, 7, 512], BF16)
    rs1 = [av([128, 512]) for _ in range(2)]
    o1 = [av([128, 7, 512], BF16) for _ in range(2)]
    kpg = av([64, 512], BF16)
    tmpa = av([64, 512])
    tmpb = av([64, 512])
    zo = [av([128, 512], BF16) for _ in range(2)]
    ropeb1 = av([64, 2, 512])
    gq = sb("gq", [128, 4])
    gkv = sb("gkv", [128, 2])
    gqa = sb("gqa", [128, 1])
    gqb = sb("gqb", [64, 1])
    gka = sb("gka", [128, 1])
    gkb = sb("gkb", [64, 1])
    rotb = sb("rotb", [64, 64], BF16)
    rot_st = sb("rot_st", [64, 64])
    nshift = sb("nshift", [128, 1])
    m_w_in_v = m_w_in.rearrange("(c p) n -> p c n", p=128)

    def setup_mla():
        S.load(lambda e: e.dma_start(out=gq[:], in_=m_qn.rearrange("(c p) -> p c", p=128), allow_slow_non_contiguous=True), w=["gq"])
        S.load(lambda e: e.dma_start(out=gkv[:], in_=m_kvn.rearrange("(c p) -> p c", p=128), allow_slow_non_contiguous=True), w=["gkv"])
        S.load(lambda e: e.dma_start(out=gqa[:], in_=m_qg[0:128].rearrange("(p c) -> p c", c=1)), w=["gqa"])
        S.load(lambda e: e.dma_start(out=gqb[:], in_=m_qg[128:192].rearrange("(p c) -> p c", c=1)), w=["gqb"])
        S.load(lambda e: e.dma_start(out=gka[:], in_=m_kg[0:128].rearrange("(p c) -> p c", c=1)), w=["gka"])
        S.load(lambda e: e.dma_start(out=gkb[:], in_=m_kg[128:192].rearrange("(p c) -> p c", c=1)), w=["gkb"])
        S.load(lambda e: e.dma_start(out=rot_st[:], in_=c_rot[:, :]), w=["rot_st"])
        S.dve(lambda e: e.tensor_copy(out=rotb[:], in_=rot_st[:]), r=["rot_st"], w=["rotb"])
        S.dve(lambda e: e.memset(nshift[:], -8.0), w=["nshift"])

    def rstd_from_psum(pbank, npart, w, div, dst):
        S.act(lambda e: e.activation(out=dst[0:npart, 0:w], in_=pbank[0:npart, 0:w], func=AF.Sqrt, bias=epsb[0:npart, 0:1], scale=1.0 / div),
              r=[("pb", 3), "epsb"], w=[("rs", id(dst))])
        S.dve(lambda e: e.reciprocal(out=dst[0:npart, 0:w], in_=dst[0:npart, 0:w]), r=[("rs", id(dst))], w=[("rs", id(dst))])

    def phase_m1(s):
        for f in range(7):
            b = f % 2
            nc_ = 128 if f < 6 else 64
            S.load(lambda e, f=f, b=b, nc_=nc_: e.dma_start(out=wMst[b][:, :, 0:nc_], in_=m_w_in_v[:, :, f * 128:f * 128 + nc_]), w=[("wMst", b)])
            S.pool(lambda e, f=f, b=b, nc_=nc_: e.tensor_copy(out=wM[:, :, f * 128:f * 128 + nc_], in_=wMst[b][:, :, 0:nc_]), r=[("wMst", b)], w=["wM"])
        for bi, (c0, w) in enumerate(COLBLKS):
            ob_ = o1[bi % 2]
            ok = ("o1", bi % 2)
            for f in range(7):
                nr = 128 if f < 6 else 64
                bk = mmbank()
                S.pe([(lambda e, c=c: e.matmul(pb[bk][0:nr, 0:w], lhsT=wM[:, c, f * 128:f * 128 + nr], rhs=hnT[:, c, c0:c0 + w], start=(c == 0), stop=(c == 7)))
                      for c in range(8)], r=["wM"] + gkeys("hnT", c0, w), w=[("pb", bk)])
                S.act(lambda e: e.activation(out=raw[0:nr, f, 0:w], in_=pb[bk][0:nr, 0:w], func=AF.Copy), r=[("pb", bk)], w=[("raw", f)])
                S.dve(lambda e: e.tensor_tensor(out=sqt[0:nr, f, 0:w], in0=raw[0:nr, f, 0:w], in1=raw[0:nr, f, 0:w], op=ALU.mult), r=[("raw", f)], w=[("sqt", f)])
            for (fl, div, gt, ri) in (([0, 1, 2, 3], 512.0, gq, 0), ([4, 5], 256.0, gkv, 1)):
                S.pe([(lambda e, i=i, f=f: e.matmul(pb[3][:, 0:w], lhsT=onesb[:, :], rhs=sqt[:, f, 0:w], start=(i == 0), stop=(i == len(fl) - 1)))
                      for i, f in enumerate(fl)], r=[("sqt", f) for f in fl] + ["onesb"], w=[("pb", 3)])
                rstd_from_psum(pb[3], 128, w, div, rs1[ri])
                for i, f in enumerate(fl):
                    S.dve(lambda e, i=i, f=f: e.scalar_tensor_tensor(out=ob_[:, f, 0:w], in0=raw[:, f, 0:w], scalar=gt[:, i:i + 1], in1=rs1[ri][:, 0:w],
                                                                     op0=ALU.mult, op1=ALU.mult), r=[("raw", f), ("rs", id(rs1[ri]))], w=[ok])
            S.dve(lambda e: e.tensor_scalar(out=kpg[:, 0:w], in0=raw[0:64, 6, 0:w], scalar1=gkb[:, 0:1], scalar2=None, op0=ALU.mult), r=[("raw", 6), "gkb"], w=["kpg"])
            S.pe(lambda e: e.matmul(pb[3][0:64, 0:w], lhsT=rotb[:, :], rhs=kpg[:, 0:w], start=True, stop=True), r=["kpg", "rotb"], w=[("pb", 3)])
            S.load(lambda e: e.dma_start(out=ropeb1[:, :, 0:w], in_=c_rope[:, :, c0:c0 + w]), w=["ropeb1"])
            S.dve(lambda e: e.tensor_tensor(out=tmpa[:, 0:w], in0=pb[3][0:64, 0:w], in1=ropeb1[:, 1, 0:w], op=ALU.mult), r=[("pb", 3), "ropeb1"], w=["tmpa"])
            S.dve(lambda e: e.tensor_tensor(out=tmpb[:, 0:w], in0=kpg[:, 0:w], in1=ropeb1[:, 0, 0:w], op=ALU.mult), r=["kpg", "ropeb1"], w=["tmpb"])
            S.dve(lambda e: e.tensor_tensor(out=ob_[0:64, 6, 0:w], in0=tmpa[:, 0:w], in1=tmpb[:, 0:w], op=ALU.add), r=["tmpa", "tmpb"], w=[ok])
            for f in range(6):
                S.store(lambda e, f=f: e.dma_start(out=qk_scr[f, :, c0:c0 + w], in_=ob_[:, f, 0:w]), r=[ok], w=[("qk_scr", f)])
            S.store(lambda e: e.dma_start(out=qk_scr[6, 0:64, c0:c0 + w], in_=ob_[0:64, 6, 0:w]), r=[ok], w=[("qk_scr", 6)])
            S.store(lambda e: e.dma_start(out=qk_scr[7, 0:64, c0:c0 + w], in_=sqt[0:64, 6, 0:w]), r=[("sqt", 6)], w=[("qk_scr", 7)])
        for hh in range(16):
            b = hh % 2
            S.load(lambda e, hh=hh, b=b: e.dma_start(out=wMst[b][:, :, :], in_=m_w_in_v[:, :, 832 + hh * 128:832 + (hh + 1) * 128]), w=[("wMst", b)])
            S.pool(lambda e, b=b: e.tensor_copy(out=wMz[b][:, :, :], in_=wMst[b][:, :, :]), r=[("wMst", b)], w=[("wMz", b)])
            for bi, (c0, w) in enumerate(COLBLKS):
                bk = mmbank()
                zb = zo[bi % 2]
                S.pe([(lambda e, c=c: e.matmul(pb[bk][:, 0:w], lhsT=wMz[b][:, c, :], rhs=hnT[:, c, c0:c0 + w], start=(c == 0), stop=(c == 7)))
                      for c in range(8)], r=[("wMz", b)] + gkeys("hnT", c0, w), w=[("pb", bk)])
                S.act(lambda e: e.activation(out=zb[:, 0:w], in_=pb[bk][:, 0:w], func=AF.Silu), r=[("pb", bk)], w=[("zo", bi % 2)])
                S.store(lambda e: e.dma_start(out=z_scr[hh, :, c0:c0 + w], in_=zb[:, 0:w]), r=[("zo", bi % 2)], w=[("z_scr", hh)])
        S.barrier()

    aoff["p"] = 0
    cqT = av([128, 4, LE], BF16)
    ckvT = av([128, 2, LE], BF16)
    krT = av([64, LE], BF16)
    sqk = av([64, LE], BF16)
    qTa = av([128, LE], BF16)
    qTb = av([64, LE], BF16)
    kTa = av([128, LE], BF16)
    kTb = av([64, LE], BF16)
    zT = av([128, LE], BF16)
    vaug = av([128, 33, 130], BF16)
    wq_st = av([128, 4, 192])
    wq = av([128, 4, 192], BF16)
    wkv_st = av([128, 2, 256])
    wkv = av([128, 2, 256], BF16)
    ra = av([128, 512])
    rb = av([64, 512])
    sqa = av([128, 512], BF16)
    sqb2 = av([64, 512], BF16)
    rsq = av([128, 512])
    rsk = av([128, 512])
    qbg = av([64, 512], BF16)
    t2a = av([64, 512])
    t2b = av([64, 512])
    ropeb2 = av([64, 2, 512])
    pT = [av([128, 512], BF16) for _ in range(2)]
    rdn = av([1, 512])
    dacc = [av([128, 512]) for _ in range(2)]
    ones128 = av([128, 1])
    rbc = av([128, 512])
    ytb = [av([128, 512], BF16) for _ in range(2)]
    KT = [(64 + 128 * i, 128) for i in range(32)] + [(PAD, 16)]
    SCALE = 192.0 ** -0.5
    wuq_v = m_wuq.rearrange("(c p) n -> p c n", p=128)
    wukv_v = m_wukv.rearrange("(c p) n -> p c n", p=128)

    def phase_m2(s, heads=range(16)):
        for f in range(4):
            S.load(lambda e, f=f: e.dma_start(out=cqT[:, f, :], in_=qk_scr[f, :, :]), r=[("qk_scr", f)], w=["cqT"])
        for f in range(2):
            S.load(lambda e, f=f: e.dma_start(out=ckvT[:, f, :], in_=qk_scr[4 + f, :, :]), r=[("qk_scr", 4 + f)], w=["ckvT"])
        S.load(lambda e: e.dma_start(out=krT[:, :], in_=qk_scr[6, 0:64, :]), r=[("qk_scr", 6)], w=["krT"])
        S.load(lambda e: e.dma_start(out=sqk[:, :], in_=qk_scr[7, 0:64, :]), r=[("qk_scr", 7)], w=["sqk"])
        S.dve(lambda e: e.memset(ones128[:, :], 1.0), w=["ones128"])
        for h in heads:
            S.marks.append(("m2_h%d_start" % h, dict(S.cnt)))
            S.load(lambda e: e.dma_start(out=wq_st[:], in_=wuq_v[:, :, h * 192:(h + 1) * 192]), w=["wq_st"])
            S.pool(lambda e: e.tensor_copy(out=wq[:], in_=wq_st[:]), r=["wq_st"], w=["wq"])
            S.load(lambda e: e.dma_start(out=wkv_st[:], in_=wukv_v[:, :, h * 256:(h + 1) * 256]), w=["wkv_st"])
            S.pool(lambda e: e.tensor_copy(out=wkv[:], in_=wkv_st[:]), r=["wkv_st"], w=["wkv"])
            S.load(lambda e: e.dma_start(out=zT[:, :], in_=z_scr[h, :, :]), r=[("z_scr", h)], w=["zT"])
            for (c0, w) in COLBLKS:
                cs = slice(c0, c0 + w)
                bk = mmbank()
                S.pe([(lambda e, c=c: e.matmul(pb[bk][:, 0:w], lhsT=wq[:, c, 0:128], rhs=cqT[:, c, cs], start=(c == 0), stop=(c == 3))) for c in range(4)],
                     r=["wq", "cqT"], w=[("pb", bk)])
                S.act(lambda e: e.activation(out=ra[:, 0:w], in_=pb[bk][:, 0:w], func=AF.Copy), r=[("pb", bk)], w=["ra"])
                bk2 = mmbank()
                S.pe([(lambda e, c=c: e.matmul(pb[bk2][0:64, 0:w], lhsT=wq[:, c, 128:192], rhs=cqT[:, c, cs], start=(c == 0), stop=(c == 3))) for c in range(4)],
                     r=["wq", "cqT"], w=[("pb", bk2)])
                S.act(lambda e: e.activation(out=rb[:, 0:w], in_=pb[bk2][0:64, 0:w], func=AF.Copy), r=[("pb", bk2)], w=["rb"])
                S.dve(lambda e: e.tensor_tensor(out=sqa[:, 0:w], in0=ra[:, 0:w], in1=ra[:, 0:w], op=ALU.mult), r=["ra"], w=["sqa"])
                S.dve(lambda e: e.tensor_tensor(out=sqb2[:, 0:w], in0=rb[:, 0:w], in1=rb[:, 0:w], op=ALU.mult), r=["rb"], w=["sqb2"])
                S.pe([lambda e: e.matmul(pb[3][:, 0:w], lhsT=onesb[:, :], rhs=sqa[:, 0:w], start=True, stop=False),
                      lambda e: e.matmul(pb[3][:, 0:w], lhsT=onesb[0:64, :], rhs=sqb2[:, 0:w], start=False, stop=True)], r=["sqa", "sqb2", "onesb"], w=[("pb", 3)])
                rstd_from_psum(pb[3], 128, w, 192.0, rsq)
                S.dve(lambda e: e.scalar_tensor_tensor(out=qTa[:, cs], in0=ra[:, 0:w], scalar=gqa[:, 0:1], in1=rsq[:, 0:w], op0=ALU.mult, op1=ALU.mult),
                      r=["ra", ("rs", id(rsq)), "gqa"], w=["qTa"])
                S.dve(lambda e: e.scalar_tensor_tensor(out=qbg[:, 0:w], in0=rb[:, 0:w], scalar=gqb[:, 0:1], in1=rsq[0:64, 0:w], op0=ALU.mult, op1=ALU.mult),
                      r=["rb", ("rs", id(rsq)), "gqb"], w=["qbg"])
                S.pe(lambda e: e.matmul(pb[3][0:64, 0:w], lhsT=rotb[:, :], rhs=qbg[:, 0:w], start=True, stop=True), r=["qbg", "rotb"], w=[("pb", 3)])
                S.load(lambda e: e.dma_start(out=ropeb2[:, :, 0:w], in_=c_rope[:, :, cs]), w=["ropeb2"])
                S.dve(lambda e: e.tensor_tensor(out=t2a[:, 0:w], in0=pb[3][0:64, 0:w], in1=ropeb2[:, 1, 0:w], op=ALU.mult), r=[("pb", 3), "ropeb2"], w=["t2a"])
                S.dve(lambda e: e.tensor_tensor(out=t2b[:, 0:w], in0=qbg[:, 0:w], in1=ropeb2[:, 0, 0:w], op=ALU.mult), r=["qbg", "ropeb2"], w=["t2b"])
                S.dve(lambda e: e.tensor_tensor(out=qTb[:, cs], in0=t2a[:, 0:w], in1=t2b[:, 0:w], op=ALU.add), r=["t2a", "t2b"], w=["qTb"])
                bk = mmbank()
                S.pe([(lambda e, c=c: e.matmul(pb[bk][:, 0:w], lhsT=wkv[:, c, 0:128], rhs=ckvT[:, c, cs], start=(c == 0), stop=(c == 1))) for c in range(2)],
                     r=["wkv", "ckvT"], w=[("pb", bk)])
                S.act(lambda e: e.activation(out=ra[:, 0:w], in_=pb[bk][:, 0:w], func=AF.Copy), r=[("pb", bk)], w=["ra"])
                S.dve(lambda e: e.tensor_tensor(out=sqa[:, 0:w], in0=ra[:, 0:w], in1=ra[:, 0:w], op=ALU.mult), r=["ra"], w=["sqa"])
                S.pe([lambda e: e.matmul(pb[3][:, 0:w], lhsT=onesb[:, :], rhs=sqa[:, 0:w], start=True, stop=False),
                      lambda e: e.matmul(pb[3][:, 0:w], lhsT=onesb[0:64, :], rhs=sqk[:, cs], start=False, stop=True)], r=["sqa", "sqk", "onesb"], w=[("pb", 3)])
                rstd_from_psum(pb[3], 128, w, 192.0, rsk)
                S.dve(lambda e: e.scalar_tensor_tensor(out=kTa[:, cs], in0=ra[:, 0:w], scalar=gka[:, 0:1], in1=rsk[:, 0:w], op0=ALU.mult, op1=ALU.mult),
                      r=["ra", ("rs", id(rsk)), "gka"], w=["kTa"])
                S.dve(lambda e: e.tensor_tensor(out=kTb[:, cs], in0=krT[:, cs], in1=rsk[0:64, 0:w], op=ALU.mult), r=["krT", ("rs", id(rsk))], w=["kTb"])
            S.marks.append(("m2_h%d_qk" % h, dict(S.cnt)))
            for kt, (k0, nk) in enumerate(KT):
                bk = mmbank()
                S.pe([(lambda e, c=c: e.matmul(pb[bk][0:nk, 0:128], lhsT=ckvT[:, c, k0:k0 + nk], rhs=wkv[:, c, 128:256], start=(c == 0), stop=(c == 1))) for c in range(2)],
                     r=["wkv", "ckvT"], w=[("pb", bk)])
                S.act(lambda e: e.activation(out=vaug[0:nk, kt, 0:128], in_=pb[bk][0:nk, 0:128], func=AF.Copy), r=[("pb", bk)], w=["vaug"])
            S.marks.append(("m2_h%d_v" % h, dict(S.cnt)))
            steps = [(qi, kt) for qi in range(8) for kt in range(33)]

            def emit_scores(st):
                qi, kt = steps[st]
                k0, nk = KT[kt]
                q0 = 64 + 512 * qi
                bk = st % 2
                S.pe([lambda e: e.matmul(pb[bk][0:nk, 0:512], lhsT=kTa[:, k0:k0 + nk], rhs=qTa[:, q0:q0 + 512], start=True, stop=False),
                      lambda e: e.matmul(pb[bk][0:nk, 0:512], lhsT=kTb[:, k0:k0 + nk], rhs=qTb[:, q0:q0 + 512], start=False, stop=True)],
                     r=["kTa", "kTb", "qTa", "qTb"], w=[("pb", bk)])

            emit_scores(0)
            for st, (qi, kt) in enumerate(steps):
                k0, nk = KT[kt]
                q0 = 64 + 512 * qi
                bk = st % 2
                ab = qi % 2
                acc_o = pb[4 + 2 * ab]
                acc_d = pb[5 + 2 * ab]
                if st + 1 < len(steps):
                    emit_scores(st + 1)
                S.act(lambda e: e.activation(out=pT[bk][0:nk, :], in_=pb[bk][0:nk, 0:512], func=AF.Exp, bias=nshift[0:nk, 0:1], scale=SCALE),
                      r=[("pb", bk), "nshift"], w=[("pT", bk)])
                S.pe(lambda e: e.matmul(acc_o[:, 0:512], lhsT=vaug[0:nk, kt, 0:128], rhs=pT[bk][0:nk, :], start=(kt == 0), stop=(kt == 32)),
                     r=[("pT", bk), "vaug"], w=[("acc", ab)])
                if kt == 0:
                    S.dve(lambda e: e.tensor_copy(out=dacc[ab][:, :], in_=pT[bk][:, :]), r=[("pT", bk)], w=[("dacc", ab)])
                else:
                    S.dve(lambda e: e.tensor_tensor(out=dacc[ab][0:nk, :], in0=dacc[ab][0:nk, :], in1=pT[bk][0:nk, :], op=ALU.add),
                          r=[("pT", bk), ("dacc", ab)], w=[("dacc", ab)])
                if kt == 32:
                    yb_ = ytb[qi % 2]
                    S.pe(lambda e: e.matmul(acc_d[0:1, 0:512], lhsT=ones128[:, 0:1], rhs=dacc[ab][:, :], start=True, stop=True),
                         r=[("dacc", ab), "ones128"], w=[("accd", ab)])
                    S.dve(lambda e: e.reciprocal(out=rdn[0:1, :], in_=acc_d[0:1, 0:512]), r=[("accd", ab)], w=["rdn"])
                    S.pe(lambda e: e.matmul(pb[3][:, 0:512], lhsT=ones32[0:1, :], rhs=rdn[0:1, :], start=True, stop=True), r=["rdn", "ones32"], w=[("pb", 3)])
                    S.act(lambda e: e.activation(out=rbc[:, :], in_=pb[3][:, 0:512], func=AF.Copy), r=[("pb", 3)], w=["rbc"])
                    S.dve(lambda e: e.tensor_tensor(out=rbc[:, :], in0=acc_o[:, 0:512], in1=rbc[:, :], op=ALU.mult), r=[("acc", ab), "rbc"], w=["rbc"])
                    S.dve(lambda e: e.tensor_tensor(out=yb_[:, :], in0=rbc[:, :], in1=zT[:, q0:q0 + 512], op=ALU.mult), r=["rbc", "zT"], w=[("ytb", qi % 2)])
                    S.store(lambda e: e.dma_start(out=y_scr[h, :, q0:q0 + 512], in_=yb_[:, :]), r=[("ytb", qi % 2)], w=[("y_scr", h)])
            S.barrier()

    def dump(name, t, shape, dt, keys):
        d = nc.dram_tensor("dbg_" + name, list(shape), dt, kind="ExternalOutput").ap()
        S.store(lambda e: e.dma_start(out=d, in_=t), r=keys)

    setup()
    setup_mla()
    for s in range(NSEQ):
        S.marks.append(("start%d" % s, dict(S.cnt)))
        phase_p0(s)
        S.marks.append(("p0", dict(S.cnt)))
        phase_g1(s)
        S.marks.append(("g1", dict(S.cnt)))
        S.barrier()
        if debug == "g2":
            phase_g2(s, heads=[0])
            break
        phase_g2(s)
        S.marks.append(("g2", dict(S.cnt)))
        phase_outproj(s, 0)
        S.marks.append(("op0", dict(S.cnt)))
        if debug == "l0":
            break
        phase_m1(s)
        S.marks.append(("m1", dict(S.cnt)))
        phase_m2(s)
        S.marks.append(("m2", dict(S.cnt)))
        phase_outproj(s, 1)
        S.marks.append(("op1", dict(S.cnt)))
    print("arena max", aoff.get("max"), "ninst", S.ninst, S.cnt, "sbuf left", nc.sbuf_bytes_remaining)
    nc._marks = S.marks
    S.finish()
    es.close()
    return nc


def _consts():
    ident = np.eye(128, dtype=np.float32)
    i = np.arange(64)
    U = (i[:, None] <= i[None, :]).astype(np.float32)
    Lo = (i[:, None] >= i[None, :]).astype(np.float32)
    Us = (i[:, None] < i[None, :]).astype(np.float32)
    Ls = (i[:, None] > i[None, :]).astype(np.float32)
    NEG = -30000.0
    masks = np.stack([U, Lo, Us, Ls, (1 - U) * NEG, (1 - Lo) * NEG], axis=1).astype(np.float32)
    pos = np.arange(LE, dtype=np.float64) - PAD
    inv = 10000.0 ** (-np.arange(0, 64, 2, dtype=np.float64) / 64)
    ang = pos[None, :] * inv[:, None]
    cos = np.concatenate([np.cos(ang), np.cos(ang)], 0)
    sin = np.concatenate([np.sin(ang), np.sin(ang)], 0)
    rope = np.stack([cos, sin], 1).astype(np.float32)
    rot = np.zeros((64, 64), np.float32)
    for m in range(32):
        rot[m + 32, m] = -1.0
        rot[m, m + 32] = 1.0
    return dict(c_ident=ident, c_masks=masks, c_rope=rope, c_rot=rot)


_NC_CACHE = {}


def _in_maps(inputs):
    allx = np.concatenate([np.asarray(inputs["x_prompt"]), np.asarray(inputs["x_sample"])], 0)
    seqs = [[0, 1], [2, 3], [4, 5], [6, 7], [8, 8], [9, 9], [10, 10], [11, 11]]
    common = dict(
        meta=np.asarray(inputs["meta_tokens"]), ln_g=np.asarray(inputs["ln_g"]),
        g_w_in=np.asarray(inputs["gdn_w_in"])[0], g_conv=np.asarray(inputs["gdn_conv_w"])[0],
        g_alog=np.asarray(inputs["gdn_a_log"])[0].reshape(16), g_dtb=np.asarray(inputs["gdn_dt_bias"])[0].reshape(16),
        g_on=np.asarray(inputs["gdn_o_norm_g"])[0], g_w_out=np.asarray(inputs["gdn_w_out"])[0],
        m_w_in=np.asarray(inputs["mla_w_in"])[0], m_qn=np.asarray(inputs["mla_q_norm_g"])[0],
        m_kvn=np.asarray(inputs["mla_kv_norm_g"])[0], m_wuq=np.asarray(inputs["mla_w_uq"])[0],
        m_wukv=np.asarray(inputs["mla_w_ukv"])[0], m_qg=np.asarray(inputs["mla_qk_q_g"])[0],
        m_kg=np.asarray(inputs["mla_qk_k_g"])[0], m_w_out=np.asarray(inputs["mla_w_out"])[0],
    )
    common = {k: np.ascontiguousarray(v, dtype=np.float32) for k, v in common.items()}
    common.update(_consts())
    maps = []
    for c in range(8):
        m = dict(common)
        m["xs"] = np.ascontiguousarray(allx[seqs[c]])
        maps.append(m)
    return maps, seqs


def kernel(**inputs):
    if "nc" not in _NC_CACHE:
        _NC_CACHE["nc"] = build()
    nc = _NC_CACHE["nc"]
    maps, seqs = _in_maps(inputs)
    res = run_bass_kernel_spmd(nc, maps, core_ids=list(range(8)))
    full = np.zeros((12, LX, D), np.float32)
    for c in range(8):
        o = res.results[c]["out"]
        full[seqs[c][0]] = o[0]
        if seqs[c][1] != seqs[c][0]:
            full[seqs[c][1]] = o[1]
    return full[:4], full[4:]
```

```python
import numpy as np
import ml_dtypes
import concourse.bass as bass
import concourse.mybir as mybir
from concourse.bass_utils import run_bass_kernel_spmd

F32 = mybir.dt.float32
BF16 = mybir.dt.bfloat16
ALU = mybir.AluOpType
AF = mybir.ActivationFunctionType

D = 1024
LX = 4096
NMETA = 16
PAD = 48
LE = 4160
NGR = 65
NSEQ = 2
EPS = 1e-6
COLBLKS = [(0, 64)] + [(64 + 512 * i, 512) for i in range(8)]
TOKTILES = [(0, 64)] + [(64 + 128 * i, 128) for i in range(32)]
GDN_IN = 6176
import os
SCANSTOP = int(os.environ.get('SCANSTOP', '9'))
DMA_K = 6
SAME_ENG_SYNC = True


def gkeys(name, c0, n):
    return [(name, g) for g in range(c0 // 64, (c0 + n + 63) // 64)]


class Sched:
    def __init__(self, nc, es):
        self.nc = nc
        self.eng = {"pe": nc.tensor, "dve": nc.vector, "act": nc.scalar, "pool": nc.gpsimd, "sp": nc.sync}
        self.semh = {}
        for e in self.eng:
            self.semh[(e,)] = es.enter_context(nc.semaphore("s_" + e))
        for q in ("sp", "pool", "act"):
            for s in range(DMA_K):
                self.semh[(q, "d", s)] = es.enter_context(nc.semaphore(f"d_{q}{s}"))
        self.cnt = {e: 0 for e in self.eng}
        self.dman = {q: 0 for q in ("sp", "pool", "act")}
        self.seen = {e: {} for e in self.eng}
        self.lastw = {}
        self.readers = {}
        self.ninst = 0
        self.marks = []

    def _wait(self, e, semk, val):
        if val <= 0 or self.seen[e].get(semk, 0) >= val:
            return
        self.eng[e].wait_ge(self.semh[semk], val)
        self.seen[e][semk] = val

    def op(self, e, fn, reads=(), writes=(), dma=False):
        deps = {}
        for k in reads:
            t = self.lastw.get(k)
            if t is not None:
                deps[t[0]] = max(deps.get(t[0], 0), t[1])
        for k in writes:
            t = self.lastw.get(k)
            if t is not None:
                deps[t[0]] = max(deps.get(t[0], 0), t[1])
            for sk, v in self.readers.get(k, {}).items():
                deps[sk] = max(deps.get(sk, 0), v)
        for sk, v in deps.items():
            if sk == (e,) and (e == "pe" or not SAME_ENG_SYNC) and not dma:
                continue
            self._wait(e, sk, v)
        if dma:
            n = self.dman[e]
            slot = n % DMA_K
            sk = (e, "d", slot)
            self._wait(e, sk, 16 * (n // DMA_K))
            self.dman[e] = n + 1
            inst = fn(self.eng[e])
            inst.then_inc(self.semh[sk], 16)
            tok = (sk, 16 * (n // DMA_K + 1))
        else:
            fns = fn if isinstance(fn, (list, tuple)) else [fn]
            inst = None
            for f in fns:
                inst = f(self.eng[e])
                self.ninst += 1
            self.cnt[e] += 1
            inst.then_inc(self.semh[(e,)], 1)
            tok = ((e,), self.cnt[e])
        for k in reads:
            r = self.readers.setdefault(k, {})
            r[tok[0]] = max(r.get(tok[0], 0), tok[1])
        for k in writes:
            self.lastw[k] = tok
            self.readers[k] = {}
        return tok

    def pe(self, fn, r=(), w=()):
        return self.op("pe", fn, r, w)

    def dve(self, fn, r=(), w=()):
        return self.op("dve", fn, r, w)

    def act(self, fn, r=(), w=()):
        return self.op("act", fn, r, w)

    def pool(self, fn, r=(), w=()):
        return self.op("pool", fn, r, w)

    def load(self, fn, r=(), w=()):
        return self.op("sp", fn, r, w, dma=True)

    def store(self, fn, r=(), w=()):
        return self.op("pool", fn, r, w, dma=True)

    def barrier(self):
        for e in self.eng:
            for e2 in self.eng:
                if e2 != e:
                    self._wait(e, (e2,), self.cnt[e2])
            for q in self.dman:
                n = self.dman[q]
                for s in range(DMA_K):
                    if n > s:
                        last = ((n - 1 - s) // DMA_K) * DMA_K + s
                        self._wait(e, (q, "d", s), 16 * (last // DMA_K + 1))
        self.lastw = {}
        self.readers = {}

    def finish(self):
        self.barrier()


def build(debug=None):
    from contextlib import ExitStack
    nc = bass.Bass("TRN2", target_bir_lowering=False)
    es = ExitStack()

    def din(name, shape, dt=F32):
        return nc.dram_tensor(name, list(shape), dt, kind="ExternalInput").ap()

    xs = din("xs", [NSEQ, LX, D])
    meta = din("meta", [NMETA, D])
    ln_g = din("ln_g", [2, D])
    g_w_in = din("g_w_in", [D, GDN_IN])
    g_conv = din("g_conv", [5, 4096])
    g_alog = din("g_alog", [16])
    g_dtb = din("g_dtb", [16])
    g_on = din("g_on", [256])
    g_w_out = din("g_w_out", [2048, D])
    m_w_in = din("m_w_in", [D, 2880])
    m_qn = din("m_qn", [512])
    m_kvn = din("m_kvn", [256])
    m_wuq = din("m_wuq", [512, 3072])
    m_wukv = din("m_wukv", [256, 4096])
    m_qg = din("m_qg", [192])
    m_kg = din("m_kg", [192])
    m_w_out = din("m_w_out", [2048, D])
    c_ident = din("c_ident", [128, 128])
    c_masks = din("c_masks", [64, 6, 64])
    c_rope = din("c_rope", [64, 2, LE])
    c_rot = din("c_rot", [64, 64])
    out = nc.dram_tensor("out", [NSEQ, LX, D], F32, kind="ExternalOutput").ap()

    skind = "ExternalOutput" if debug else "Internal"

    def dscr(name, shape, dt):
        return nc.dram_tensor(name, list(shape), dt, kind=skind).ap()

    qk_scr = dscr("qk_scr", [16, 128, LE], BF16)
    v_scr = dscr("v_scr", [16, 128, LE], BF16)
    z_scr = dscr("z_scr", [16, 128, LE], BF16)
    y_scr = dscr("y_scr", [16, 128, LE], BF16)
    h1_scr = dscr("h1_scr", [LE, D], F32)

    S = Sched(nc, es)

    def sb(name, shape, dt=F32):
        return es.enter_context(nc.sbuf_tensor(name, list(shape), dt))

    def ps(name, shape, dt=F32):
        return es.enter_context(nc.psum_tensor(name, list(shape), dt))

    block = es.enter_context(nc.Block())

    ident = sb("ident", [128, 128])
    identb = sb("identb", [128, 128], BF16)
    onesb = sb("onesb", [128, 128], BF16)
    ones32 = sb("ones32", [64, 128])
    masks = sb("masks", [64, 6, 64])
    lng = sb("lng", [128, 2, 8])
    convw = sb("convw", [128, 5, 32])
    alog = sb("alog", [64, 16])
    dtb = sb("dtb", [64, 16])
    nea = sb("nea", [64, 16])
    gon = sb("gon", [64, 256])
    epsb = sb("epsb", [128, 1])

    pb = [ps(f"pb{i}", [128, 512]) for i in range(8) if i != 2]
    pb.insert(2, None)
    ptb = ps("ptb", [128, 1024], BF16)

    def setup():
        S.load(lambda e: e.dma_start(out=ident[:], in_=c_ident[:, :]), w=["ident"])
        S.load(lambda e: e.dma_start(out=masks[:], in_=c_masks[:, :, :]), w=["masks"])
        S.load(lambda e: e.dma_start(out=lng[:], in_=ln_g.rearrange("l (c p) -> p l c", p=128), allow_slow_non_contiguous=True), w=["lng"])
        for j in range(5):
            S.load(lambda e, j=j: e.dma_start(out=convw[:, j, :], in_=g_conv[j].rearrange("(c p) -> p c", p=128), allow_slow_non_contiguous=True), w=["convw"])
        S.load(lambda e: e.dma_start(out=alog[:], in_=g_alog.partition_broadcast(64)), w=["alog"])
        S.load(lambda e: e.dma_start(out=dtb[:], in_=g_dtb.partition_broadcast(64)), w=["dtb"])
        S.load(lambda e: e.dma_start(out=gon[:], in_=g_on.partition_broadcast(64)), w=["gon"])
        S.dve(lambda e: e.tensor_copy(out=identb[:], in_=ident[:]), r=["ident"], w=["identb"])
        S.dve(lambda e: e.memset(onesb[:], 1.0), w=["onesb"])
        S.dve(lambda e: e.memset(ones32[:], 1.0), w=["ones32"])
        S.dve(lambda e: e.memset(epsb[:], EPS), w=["epsb"])
        S.act(lambda e: e.activation(out=nea[:], in_=alog[:], func=AF.Exp), r=["alog"], w=["nea"])
        S.dve(lambda e: e.tensor_scalar(out=nea[:], in0=nea[:], scalar1=-1.0, scalar2=None, op0=ALU.mult), r=["nea"], w=["nea"])

    beta = sb("beta", [64, NGR, 2, 8])
    gg = sb("gg", [64, NGR, 2, 8])
    gc = sb("gc", [64, NGR, 2, 8])
    negc = sb("negc", [64, NGR, 2, 8])
    egc = sb("egc", [64, NGR, 2, 8])
    kds = sb("kds", [64, NGR, 2, 8])
    egl = sb("egl", [128, NGR, 2, 8])
    negm4 = sb("negm4", [64, 4, 2, 64])
    strict4 = sb("strict4", [64, 4, 2, 64])
    ARENA_BYTES = 160000
    arena = sb("arena", [128, ARENA_BYTES // 4])
    aoff = {"p": 0}

    def av(shape, dt=F32):
        n = 1
        for d_ in shape[1:]:
            n *= d_
        nb = (n * (2 if dt == BF16 else 4) + 3) // 4 * 4
        o = aoff["p"]
        aoff["p"] = o + nb
        aoff["max"] = max(aoff.get("max", 0), o + nb)
        assert aoff["p"] <= ARENA_BYTES, (aoff["p"], shape)
        v = arena[0:shape[0], o // 4:(o + nb) // 4]
        if dt == BF16:
            v = v.bitcast(BF16)
            if n % 2:
                v = v[:, 0:n]
        if len(shape) > 2:
            names = "abcd"[:len(shape) - 1]
            pat = "p (" + " ".join(names) + ") -> p " + " ".join(names)
            v = v.rearrange(pat, **{names[i]: shape[1 + i] for i in range(len(names))})
        return v

    def sbA(name, shape, dt=F32):
        return av(shape, dt)

    hnT = sbA("hnT", [128, 8, LE], BF16)
    xt = [sbA(f"xt{i}", [128, D]) for i in range(2)]
    xn = [sbA(f"xn{i}", [128, D], BF16) for i in range(2)]
    junk = sbA("junk", [128, D], BF16)
    ssb = [sbA(f"ss{i}", [128, 4]) for i in range(2)]

    aoff_after_p0 = aoff["p"]

    def norm_transpose(tt, src_ap, src_key, n, t0, layer):
        b = tt % 2
        ss = ssb[b]
        S.act(lambda e: e.activation(out=junk[0:n, :], in_=src_ap, func=AF.Square, accum_out=ss[0:n, 0:1]),
              r=[src_key], w=["junk", ("ss", b)])
        S.act(lambda e: e.activation(out=ss[0:n, 1:2], in_=ss[0:n, 0:1], func=AF.Sqrt, bias=epsb[0:n, 0:1], scale=1.0 / D),
              r=[("ss", b), "epsb"], w=[("ss", b)])
        S.dve(lambda e: e.reciprocal(out=ss[0:n, 2:3], in_=ss[0:n, 1:2]), r=[("ss", b)], w=[("ss", b)])
        S.act(lambda e: e.activation(out=xn[b][0:n, :], in_=src_ap, func=AF.Copy, scale=ss[0:n, 2:3]),
              r=[src_key, ("ss", b)], w=[("xn", b)])
        S.pe([(lambda e, c=c: e.transpose(out=ptb[:, c * 128:c * 128 + n], in_=xn[b][0:n, c * 128:(c + 1) * 128], identity=identb[0:n, 0:n]))
              for c in range(8)], r=[("xn", b), "identb"], w=["ptb"])
        pv = ptb[:, :].rearrange("p (c t) -> p c t", c=8)[:, :, 0:n]
        S.dve(lambda e: e.tensor_tensor(out=hnT[:, :, t0:t0 + n], in0=pv,
                                        in1=lng[:, layer, :].unsqueeze(2).to_broadcast([128, 8, n]), op=ALU.mult),
              r=["ptb", "lng"], w=gkeys("hnT", t0, n))

    def load_x_tile(s, tt):
        t0, n = TOKTILES[tt]
        b = tt % 2
        if tt == 0:
            S.dve(lambda e: e.memset(xt[b][0:64, :], 0.0), w=[("xt", b)])
            S.load(lambda e: e.dma_start(out=xt[b][PAD:64, :], in_=meta[:, :]), w=[("xt", b)])
        else:
            r0 = t0 - 64
            S.load(lambda e: e.dma_start(out=xt[b][0:n, :], in_=xs[s, r0:r0 + n, :]), w=[("xt", b)])

    def phase_p0(s):
        for tt, (t0, n) in enumerate(TOKTILES):
            load_x_tile(s, tt)
            norm_transpose(tt, xt[tt % 2][0:n, :], ("xt", tt % 2), n, t0, 0)

    wst = [sbA(f"wst{i}", [128, 8, 128]) for i in range(2)]
    wbf = [sbA(f"wbf{i}", [128, 8, 128], BF16) for i in range(2)]
    pre = [sbA("pre0", [128, LE + 4], BF16)] * 2
    acc = sbA("acc", [128, LE])
    sqb = sbA("sqb", [128, LE], BF16)
    obf = [sbA(f"obf{i}", [128, LE], BF16) for i in range(2)]
    rtmp = [sbA(f"rtmp{i}", [128, 512]) for i in range(2)]
    gbraw = sbA("gbraw", [64, NGR, 32])
    wba_st = sbA("wba_st", [128, 8, 32])
    wba = sbA("wba", [128, 8, 32], BF16)

    w_in_v = g_w_in.rearrange("(c p) n -> p c n", p=128)
    mmrot = [0]

    def mmbank():
        mmrot[0] ^= 1
        return mmrot[0]

    def load_w(wv_ap, idx, ncol=128):
        b = idx % 2
        S.load(lambda e: e.dma_start(out=wst[b][:, :, 0:ncol], in_=wv_ap), w=[("wst", b)])
        S.pool(lambda e: e.tensor_copy(out=wbf[b][:, :, 0:ncol], in_=wst[b][:, :, 0:ncol]), r=[("wst", b)], w=[("wbf", b)])
        return wbf[b]

    def phase_g1(s):
        for b in range(1):
            S.dve(lambda e, b=b: e.memset(pre[b][:, 0:2], 0.0), w=[("pre", b)])
            S.dve(lambda e, b=b: e.memset(pre[b][:, LE + 2:LE + 4], 0.0), w=[("pre", b)])
        flist = range(48)
        if debug == "g1a":
            flist = [0, 16, 32]
        if debug == "g1b":
            flist = []
        for f in flist:
            wt = load_w(w_in_v[:, :, f * 128:(f + 1) * 128], f)
            wk = ("wbf", f % 2)
            pb_ = 0
            for (c0, w) in COLBLKS:
                bk = mmbank()
                S.pe([(lambda e, c=c: e.matmul(pb[bk][:, 0:w], lhsT=wt[:, c, :], rhs=hnT[:, c, c0:c0 + w], start=(c == 0), stop=(c == 7)))
                      for c in range(8)], r=[wk] + gkeys("hnT", c0, w), w=[("pb", bk)])
                if f < 32:
                    S.act(lambda e: e.activation(out=pre[pb_][:, 2 + c0:2 + c0 + w], in_=pb[bk][:, 0:w], func=AF.Copy),
                          r=[("pb", bk)], w=[("pre", pb_)])
                else:
                    ob = obf[f % 2]
                    S.act(lambda e: e.activation(out=ob[:, c0:c0 + w], in_=pb[bk][:, 0:w], func=AF.Silu),
                          r=[("pb", bk)], w=[("obf", f % 2)])
            if f >= 32:
                S.store(lambda e: e.dma_start(out=z_scr[f - 32, :, :], in_=obf[f % 2][:, :]), r=[("obf", f % 2)], w=[("z_scr", f - 32)])
                continue
            pr = pre[pb_]
            S.dve(lambda e: e.tensor_scalar(out=acc[:], in0=pr[:, 0:LE], scalar1=convw[:, 0, f:f + 1], scalar2=None, op0=ALU.mult),
                  r=[("pre", pb_), "convw"], w=["acc"])
            for j in range(1, 5):
                S.dve(lambda e, j=j: e.scalar_tensor_tensor(out=acc[:], in0=pr[:, j:j + LE], scalar=convw[:, j, f:f + 1], in1=acc[:],
                                                            op0=ALU.mult, op1=ALU.add), r=[("pre", pb_), "convw", "acc"], w=["acc"])
            ob = obf[f % 2]
            ok = ("obf", f % 2)
            if f >= 16:
                S.act(lambda e: e.activation(out=ob[:], in_=acc[:], func=AF.Silu), r=["acc"], w=[ok])
                S.dve(lambda e: e.memset(ob[:, 0:PAD], 0.0), w=[ok])
                S.store(lambda e: e.dma_start(out=v_scr[f - 16, :, :], in_=ob[:, :]), r=[ok], w=[("v_scr", f - 16)])
            else:
                S.act(lambda e: e.activation(out=acc[:], in_=acc[:], func=AF.Silu), r=["acc"], w=["acc"])
                S.dve(lambda e: e.tensor_tensor(out=sqb[:], in0=acc[:], in1=acc[:], op=ALU.mult), r=["acc"], w=["sqb"])
                qscale = (128.0 ** -0.5) if f < 8 else 1.0
                for (c0, w) in COLBLKS:
                    bk = mmbank()
                    rt = rtmp[bk]
                    S.pe(lambda e: e.matmul(pb[bk][:, 0:w], lhsT=onesb[:, :], rhs=sqb[:, c0:c0 + w], start=True, stop=True),
                         r=["sqb", "onesb"], w=[("pb", bk)])
                    S.act(lambda e: e.activation(out=rt[:, 0:w], in_=pb[bk][:, 0:w], func=AF.Sqrt, bias=epsb[:, 0:1], scale=1.0),
                          r=[("pb", bk), "epsb"], w=[("rtmp", bk)])
                    S.dve(lambda e: e.reciprocal(out=rt[:, 0:w], in_=rt[:, 0:w]), r=[("rtmp", bk)], w=[("rtmp", bk)])
                    S.dve(lambda e: e.scalar_tensor_tensor(out=ob[:, c0:c0 + w], in0=acc[:, c0:c0 + w], scalar=qscale, in1=rt[:, 0:w],
                                                           op0=ALU.mult, op1=ALU.mult), r=["acc", ("rtmp", bk)], w=[ok])
                S.dve(lambda e: e.memset(ob[:, 0:PAD], 0.0), w=[ok])
                S.store(lambda e: e.dma_start(out=qk_scr[f, :, :], in_=ob[:, :]), r=[ok], w=[("qk_scr", f)])
        if debug == "g1a":
            return
        S.load(lambda e: e.dma_start(out=wba_st[:], in_=w_in_v[:, :, 6144:6176]), w=["wba_st"])
        S.pool(lambda e: e.tensor_copy(out=wba[:], in_=wba_st[:]), r=["wba_st"], w=["wba"])
        for g0 in range(0, NGR, 16):
            ng = min(16, NGR - g0)
            bk = mmbank()
            pv = pb[bk][0:64, :].rearrange("p (g n) -> p g n", n=32)
            for gi in range(ng):
                gr = g0 + gi
                S.pe([(lambda e, c=c: e.matmul(pv[:, gi, :], lhsT=hnT[:, c, gr * 64:(gr + 1) * 64], rhs=wba[:, c, :], start=(c == 0), stop=(c == 7)))
                      for c in range(8)], r=["wba", ("hnT", gr)], w=[("pb", bk)])
            S.act(lambda e: e.activation(out=gbraw[:, g0:g0 + ng, :], in_=pv[:, 0:ng, :], func=AF.Copy), r=[("pb", bk)], w=["gbraw"])
        import os
        stopat = int(os.environ.get("STOPAT", "99"))
        if stopat <= 1:
            return
        S.act(lambda e: e.activation(out=beta[:].rearrange("p g a b -> p g (a b)"), in_=gbraw[:, :, 0:16], func=AF.Sigmoid), r=["gbraw"], w=["beta"])
        ggf = gg[:].rearrange("p g a b -> p g (a b)")
        S.dve(lambda e: e.tensor_tensor(out=ggf, in0=gbraw[:, :, 16:32], in1=dtb[:].unsqueeze(1).to_broadcast([64, NGR, 16]), op=ALU.add),
              r=["gbraw", "dtb"], w=["gg"])
        S.act(lambda e: e.activation(out=ggf, in_=ggf, func=AF.Exp), r=["gg"], w=["gg"])
        S.act(lambda e: e.activation(out=ggf, in_=ggf, func=AF.Ln, bias=1.0, scale=1.0), r=["gg"], w=["gg"])
        S.dve(lambda e: e.tensor_tensor(out=ggf, in0=ggf, in1=nea[:].unsqueeze(1).to_broadcast([64, NGR, 16]), op=ALU.mult),
              r=["gg", "nea"], w=["gg"])
        if stopat <= 2:
            return
        S.dve(lambda e: e.memset(gg[0:PAD, 0, :, :], 0.0), w=["gg"])
        S.dve(lambda e: e.memset(beta[0:PAD, 0, :, :], 0.0), w=["beta"])
        if stopat <= 3:
            return
        for g0 in range(0, NGR, 32):
            ng = min(32, NGR - g0)
            bk = mmbank()
            pv = pb[bk][0:64, :].rearrange("p (g a b) -> p g a b", a=2, b=8)
            fns = []
            for gi in range(ng):
                gr = g0 + gi
                fns.append(lambda e, gi=gi, gr=gr: e.matmul(pv[:, gi, 0, :], lhsT=masks[:, 0, :], rhs=gg[:, gr, 0, :], start=True, stop=True))
                fns.append(lambda e, gi=gi, gr=gr: e.matmul(pv[:, gi, 1, :], lhsT=masks[:, 1, :], rhs=gg[:, gr, 1, :], start=True, stop=True))
            S.pe(fns, r=["gg", "masks"], w=[("pb", bk)])
            S.act(lambda e: e.activation(out=gc[:, g0:g0 + ng], in_=pv[:, 0:ng], func=AF.Copy), r=[("pb", bk)], w=["gc"])
            if stopat <= 4:
                continue
            bk2 = mmbank()
            pv2 = pb[bk2][:, :].rearrange("p (g a b) -> p g a b", a=2, b=8)
            fns = [(lambda e, gi=gi: e.matmul(pv2[:, gi].rearrange("p a b -> p (a b)"), lhsT=ones32[:, :],
                                               rhs=gg[:, g0 + gi].rearrange("p a b -> p (a b)"), start=True, stop=True)) for gi in range(ng)]
            S.pe(fns, r=["gg", "ones32"], w=[("pb", bk2)])
            if stopat <= 5:
                continue
            S.act(lambda e: e.activation(out=egl[:, g0:g0 + ng], in_=pv2[:, 0:ng], func=AF.Exp), r=[("pb", bk2)], w=["egl"])
            if stopat <= 6:
                continue
            S.act(lambda e: e.activation(out=kds[:, g0:g0 + ng], in_=pv2[0:64, 0:ng], func=AF.Copy), r=[("pb", bk2)], w=["kds"])
            S.dve(lambda e: e.tensor_tensor(out=kds[:, g0:g0 + ng].rearrange("p g a b -> p (g a b)"), in0=kds[:, g0:g0 + ng].rearrange("p g a b -> p (g a b)"),
                                            in1=gc[:, g0:g0 + ng].rearrange("p g a b -> p (g a b)"), op=ALU.subtract),
                  r=["gc", "kds"], w=["kds"])
        if stopat <= 7:
            return
        S.act(lambda e: e.activation(out=kds[:], in_=kds[:], func=AF.Exp), r=["kds"], w=["kds"])
        S.act(lambda e: e.activation(out=egc[:], in_=gc[:], func=AF.Exp), r=["gc"], w=["egc"])
        S.dve(lambda e: e.tensor_scalar(out=negc[:], in0=egc[:], scalar1=-1.0, scalar2=None, op0=ALU.mult), r=["egc"], w=["negc"])


    aoff_after_g1 = aoff["p"]
    aoff["p"] = 0
    qT = av([128, LE], BF16)
    kT = av([128, LE], BF16)
    vz = av([128, LE], BF16)
    vtok = av([64, NGR, 256], BF16)
    ob = av([64, NGR, 256], BF16)
    Rr = av([128, NGR, 2, 64], BF16)
    At = av([128, NGR, 2, 64], BF16)
    rhsD = av([64, 4, 2, 64])
    dif = av([64, 4, 2, 64])
    DTi = av([64, 4, 2, 64])
    DTs = av([64, 4, 2, 64])
    X4 = av([64, 8, 64], BF16)
    XT = av([64, 8, 64], BF16)
    Pa = [av([64, 8, 64], BF16) for _ in range(2)]
    PaT = [av([64, 8, 64], BF16) for _ in range(2)]
    R32 = av([64, 8, 64])
    Rb = av([64, 8, 64], BF16)
    S32 = [av([128, 256]) for _ in range(2)]
    Sbf = [av([128, 256], BF16) for _ in range(2)]
    xb = [av([128, 256], BF16) for _ in range(2)]
    vn = [av([128, 256], BF16) for _ in range(2)]
    kd = [av([128, 128], BF16) for _ in range(2)]
    t1 = [av([64, 256]) for _ in range(2)]
    ssq = av([64, NGR + 3])
    rstd = av([64, NGR + 3])
    junk2 = av([64, 256], BF16)
    yb = [av([128, 256], BF16) for _ in range(2)]

    def phase_g2(s, heads=range(8)):
        S.dve(lambda e: e.tensor_copy(out=negm4[:], in_=masks[:, 4:6, :].unsqueeze(1).to_broadcast([64, 4, 2, 64])), r=["masks"], w=["negm4"])
        S.dve(lambda e: e.tensor_copy(out=strict4[:], in_=masks[:, 2:4, :].unsqueeze(1).to_broadcast([64, 4, 2, 64])), r=["masks"], w=["strict4"])
        S.dve(lambda e: e.memset(Rr[64:128].rearrange("p a b c -> p (a b c)"), 0.0), w=["Rr"])
        S.dve(lambda e: e.memset(At[64:128].rearrange("p a b c -> p (a b c)"), 0.0), w=["At"])
        for d_ in range(2):
            S.dve(lambda e, d_=d_: e.memset(xb[d_][64:128, :], 0.0), w=[("xb", d_)])
            S.dve(lambda e, d_=d_: e.memset(vn[d_][64:128, :], 0.0), w=[("vn", d_)])
            S.dve(lambda e, d_=d_: e.memset(kd[d_][64:128, :], 0.0), w=[("kd", d_)])
        for h in heads:
            S.load(lambda e: e.dma_start(out=qT[:, :], in_=qk_scr[h, :, :]), r=[("qk_scr", h)], w=["qT"])
            S.load(lambda e: e.dma_start(out=kT[:, :], in_=qk_scr[8 + h, :, :]), r=[("qk_scr", 8 + h)], w=["kT"])
            for half in range(2):
                S.load(lambda e: e.dma_start(out=vz[:, :], in_=v_scr[2 * h + half, :, :]), r=[("v_scr", 2 * h + half)], w=["vz"])
                for g0 in range(0, NGR, 8):
                    ng = min(8, NGR - g0)
                    S.pe([(lambda e, gi=gi: e.transpose(out=ptb[0:64, gi * 128:(gi + 1) * 128], in_=vz[:, (g0 + gi) * 64:(g0 + gi + 1) * 64], identity=identb[:, :]))
                          for gi in range(ng)], r=["vz", "identb"], w=["ptb"])
                    pv = ptb[0:64, :].rearrange("p (g n) -> p g n", n=128)
                    S.act(lambda e: e.activation(out=vtok[:, g0:g0 + ng, half * 128:(half + 1) * 128], in_=pv[:, 0:ng, :], func=AF.Copy),
                          r=["ptb"], w=["vtok"])
            S.marks.append(("g2_h%d_load" % h, dict(S.cnt)))
            for c0 in range(0, NGR, 4):
                nck = min(4, NGR - c0)
                nu = nck * 2
                W = nck * 128
                for d in range(2):
                    S.dve(lambda e, d=d: e.tensor_tensor(out=rhsD[:, 0:nck, d, :], in0=gg[:, c0:c0 + nck, d, h:h + 1].to_broadcast([64, nck, 64]),
                                                         in1=masks[:, d:d + 1, :].to_broadcast([64, nck, 64]), op=ALU.mult),
                          r=["gg", "masks"], w=["rhsD"])
                S.pe([lambda e: e.matmul(pb[1][0:64, 0:W], lhsT=ones32[:, 0:64], rhs=rhsD[:, 0:nck].rearrange("p a b c -> p (a b c)"), start=True, stop=False),
                      lambda e: e.matmul(pb[1][0:64, 0:W], lhsT=ident[0:64, 0:64], rhs=negm4[:, 0:nck].rearrange("p a b c -> p (a b c)"), start=False, stop=True)],
                     r=["rhsD", "ones32", "ident", "negm4"], w=[("pb", 1)])
                S.act(lambda e: e.activation(out=dif[:, 0:nck].rearrange("p a b c -> p (a b c)"), in_=pb[1][0:64, 0:W], func=AF.Copy), r=[("pb", 1)], w=["dif"])
                S.dve(lambda e: e.tensor_tensor(out=dif[:, 0:nck].rearrange("p a b c -> p (a b) c"), in0=dif[:, 0:nck].rearrange("p a b c -> p (a b) c"),
                                                in1=gc[:, c0:c0 + nck].rearrange("p g a b -> p (g a) b")[:, :, h:h + 1].to_broadcast([64, nu, 64]), op=ALU.subtract),
                      r=["dif", "gc"], w=["dif"])
                S.act(lambda e: e.activation(out=DTi[:, 0:nck].rearrange("p a b c -> p (a b c)"), in_=dif[:, 0:nck].rearrange("p a b c -> p (a b c)"), func=AF.Exp),
                      r=["dif"], w=["DTi"])
                S.dve(lambda e: e.tensor_tensor(out=DTs[:, 0:nck].rearrange("p a b c -> p (a b c)"), in0=DTi[:, 0:nck].rearrange("p a b c -> p (a b c)"),
                                                in1=strict4[:, 0:nck].rearrange("p a b c -> p (a b c)"), op=ALU.mult), r=["DTi", "strict4"], w=["DTs"])
                S.dve(lambda e: e.tensor_tensor(out=DTs[:, 0:nck].rearrange("p a b c -> p (a b) c"), in0=DTs[:, 0:nck].rearrange("p a b c -> p (a b) c"),
                                                in1=beta[:, c0:c0 + nck].rearrange("p g a b -> p (g a) b")[:, :, h:h + 1].to_broadcast([64, nu, 64]), op=ALU.mult),
                      r=["DTs", "beta"], w=["DTs"])
                pK = pb[0][0:64, :].rearrange("p (t g n) -> p t g n", t=2, g=4)
                fns = []
                for ci in range(nck):
                    cs = slice((c0 + ci) * 64, (c0 + ci + 1) * 64)
                    fns.append(lambda e, ci=ci, cs=cs: e.matmul(pK[:, 0, ci, :], lhsT=kT[:, cs], rhs=kT[:, cs], start=True, stop=True))
                    fns.append(lambda e, ci=ci, cs=cs: e.matmul(pK[:, 1, ci, :], lhsT=kT[:, cs], rhs=qT[:, cs], start=True, stop=True))
                S.pe(fns, r=["kT", "qT"], w=[("pb", 0)])
                X44 = X4[:, :, :].rearrange("p (g d) n -> p g d n", d=2)
                for d in range(2):
                    S.dve(lambda e, d=d: e.tensor_tensor(out=X44[:, 0:nck, d, :], in0=pK[:, 0, 0:nck, :], in1=DTs[:, 0:nck, d, :], op=ALU.mult),
                          r=[("pb", 0), "DTs"], w=["X4"])
                    S.dve(lambda e, d=d: e.tensor_tensor(out=At[0:64, c0:c0 + nck, d, :], in0=pK[:, 1, 0:nck, :], in1=DTi[:, 0:nck, d, :], op=ALU.mult),
                          r=[("pb", 0), "DTi"], w=["At"])
                S.pe([(lambda e, u=u: e.transpose(out=ptb[0:64, 512 + u * 64:512 + (u + 1) * 64], in_=X4[:, u, :], identity=identb[0:64, 0:64])) for u in range(nu)],
                     r=["X4", "identb"], w=["ptx"])
                S.act(lambda e: e.activation(out=XT[:, 0:nu, :].rearrange("p a b -> p (a b)"), in_=ptb[0:64, 512:512 + nu * 64], func=AF.Copy), r=["ptx"], w=["XT"])
                S.dve(lambda e: e.tensor_tensor(out=R32[:, 0:nu, :], in0=ident[0:64, 0:64].unsqueeze(1).to_broadcast([64, nu, 64]), in1=X4[:, 0:nu, :], op=ALU.subtract),
                      r=["ident", "X4"], w=["R32"])
                S.pool(lambda e: e.tensor_copy(out=Rb[:, 0:nu, :], in_=R32[:, 0:nu, :]), r=["R32"], w=["Rb"])
                P, PT, Pk, PTk = X4, XT, "X4", "XT"
                for lvl in range(5):
                    nb = lvl % 2
                    last = (lvl == 4)
                    if not last:
                        S.pe([(lambda e, u=u, P=P, PT=PT: e.matmul(pb[0][0:64, u * 64:(u + 1) * 64], lhsT=PT[:, u, :], rhs=P[:, u, :], start=True, stop=True)) for u in range(nu)],
                             r=[Pk, PTk], w=[("pb", 0)])
                    S.pe([(lambda e, u=u, P=P, PT=PT: e.matmul(pb[1][0:64, u * 64:(u + 1) * 64], lhsT=P[:, u, :], rhs=PT[:, u, :], start=True, stop=True)) for u in range(nu)],
                         r=[Pk, PTk], w=[("pb", 1)])
                    if not last:
                        S.act(lambda e, nb=nb: e.activation(out=Pa[nb][:, 0:nu, :].rearrange("p a b -> p (a b)"), in_=pb[0][0:64, 0:nu * 64], func=AF.Copy),
                              r=[("pb", 0)], w=[("Pa", nb)])
                    S.dve(lambda e, nb=nb: e.tensor_copy(out=PaT[nb][:, 0:nu, :].rearrange("p a b -> p (a b)"), in_=pb[1][0:64, 0:nu * 64]),
                          r=[("pb", 1)], w=[("PaT", nb)])
                    S.pe([(lambda e, u=u, nb=nb: e.matmul(pb[3][0:64, u * 64:(u + 1) * 64], lhsT=PaT[nb][:, u, :], rhs=Rb[:, u, :], start=True, stop=True)) for u in range(nu)],
                         r=[("PaT", nb), "Rb"], w=[("pb", 3)])
                    if not last:
                        S.dve(lambda e: e.tensor_tensor(out=R32[:, 0:nu, :].rearrange("p a b -> p (a b)"), in0=pb[3][0:64, 0:nu * 64],
                                                        in1=R32[:, 0:nu, :].rearrange("p a b -> p (a b)"), op=ALU.add), r=[("pb", 3), "R32"], w=["R32"])
                        S.pool(lambda e: e.tensor_copy(out=Rb[:, 0:nu, :], in_=R32[:, 0:nu, :]), r=["R32"], w=["Rb"])
                    else:
                        S.dve(lambda e: e.tensor_tensor(out=Rr[0:64, c0:c0 + nck].rearrange("p a b c -> p (a b c)"), in0=pb[3][0:64, 0:nu * 64],
                                                        in1=R32[:, 0:nu, :].rearrange("p a b -> p (a b)"), op=ALU.add), r=[("pb", 3), "R32"], w=["Rr"])
                    P, PT, Pk, PTk = Pa[nb], PaT[nb], ("Pa", nb), ("PaT", nb)
            S.marks.append(("g2_h%d_prep" % h, dict(S.cnt)))
            for d in range(2):
                S.dve(lambda e, d=d: e.memset(S32[d][:, :], 0.0), w=[("S32", d)])
                S.dve(lambda e, d=d: e.memset(Sbf[d][:, :], 0.0), w=[("Sbf", d)])
            for t in range(NGR):
                for d in range(2):
                    c = t if d == 0 else NGR - 1 - t
                    cs = slice(c * 64, (c + 1) * 64)
                    pS, pO = pb[4 + 2 * d], pb[5 + 2 * d]
                    kS_, kO_ = ("pb", 4 + 2 * d), ("pb", 5 + 2 * d)
                    S.pe(lambda e: e.matmul(pS[0:64, 0:256], lhsT=kT[:, cs], rhs=Sbf[d][:, :], start=True, stop=True), r=["kT", ("Sbf", d)], w=[(kS_, 0)])
                    S.pe(lambda e: e.matmul(pO[0:64, 0:256], lhsT=qT[:, cs], rhs=Sbf[d][:, :], start=True, stop=True), r=["qT", ("Sbf", d)], w=[(kO_, 0)])
                    S.pe(lambda e: e.transpose(out=ptb[0:64, d * 128:(d + 1) * 128], in_=kT[:, cs], identity=identb[:, :]), r=["kT", "identb"], w=[("ptk", d)])
                    S.dve(lambda e: e.scalar_tensor_tensor(out=xb[d][0:64, :], in0=pS[0:64, 0:256], scalar=negc[:, c, d, h:h + 1], in1=vtok[:, c, :],
                                                           op0=ALU.mult, op1=ALU.add), r=[(kS_, 0), "negc", "vtok"], w=[("xb", d)])
                    S.act(lambda e: e.activation(out=kd[d][0:64, :], in_=ptb[0:64, d * 128:(d + 1) * 128], func=AF.Copy, scale=kds[:, c, d, h:h + 1]),
                          r=[("ptk", d), "kds"], w=[("kd", d)])
                    S.act(lambda e: e.activation(out=t1[d][:, :], in_=pO[0:64, 0:256], func=AF.Copy, scale=egc[:, c, d, h:h + 1]),
                          r=[(kO_, 0), "egc"], w=[("t1", d)])
                    S.pe(lambda e: e.matmul(pS[0:64, 256:512], lhsT=Rr[:, c, d, :], rhs=xb[d][:, :], start=True, stop=True), r=["Rr", ("xb", d)], w=[(kS_, 1)])
                    S.act(lambda e: e.activation(out=vn[d][0:64, :], in_=pS[0:64, 256:512], func=AF.Copy, scale=beta[:, c, d, h:h + 1]),
                          r=[(kS_, 1), "beta"], w=[("vn", d)])
                    S.pe(lambda e: e.matmul(pO[0:64, 256:512], lhsT=At[:, c, d, :], rhs=vn[d][:, :], start=True, stop=True), r=["At", ("vn", d)], w=[(kO_, 1)])
                    S.pe(lambda e: e.matmul(pS[:, 0:256], lhsT=kd[d][:, :], rhs=vn[d][:, :], start=True, stop=True), r=[("kd", d), ("vn", d)], w=[(kS_, 0)])
                    if t < 32 or (t == 32 and d == 0):
                        S.dve(lambda e: e.tensor_tensor(out=ob[:, c, :], in0=pO[0:64, 256:512], in1=t1[d][:, :], op=ALU.add), r=[(kO_, 1), ("t1", d)], w=[("ob", c)])
                    else:
                        S.dve(lambda e: e.tensor_tensor(out=t1[d][:, :], in0=pO[0:64, 256:512], in1=t1[d][:, :], op=ALU.add), r=[(kO_, 1), ("t1", d)], w=[("t1", d)])
                        S.dve(lambda e: e.tensor_tensor(out=ob[:, c, :], in0=ob[:, c, :], in1=t1[d][:, :], op=ALU.add), r=[("ob", c), ("t1", d)], w=[("ob", c)])
                    S.dve(lambda e: e.scalar_tensor_tensor(out=Sbf[d][:, :], in0=S32[d][:, :], scalar=egl[:, c, d, h:h + 1], in1=pS[:, 0:256],
                                                           op0=ALU.mult, op1=ALU.add), r=[("S32", d), "egl", (kS_, 0)], w=[("Sbf", d)])
                    S.dve(lambda e: e.scalar_tensor_tensor(out=S32[d][:, :], in0=S32[d][:, :], scalar=egl[:, c, d, h:h + 1], in1=pS[:, 0:256],
                                                           op0=ALU.mult, op1=ALU.add), r=[("S32", d), "egl", (kS_, 0)], w=[("S32", d)])
            S.marks.append(("g2_h%d_scan" % h, dict(S.cnt)))
            S.marks.append(("g2_h%d_scan" % h, dict(S.cnt)))
            for c in range(NGR):
                S.act(lambda e, c=c: e.activation(out=junk2[:, :], in_=ob[:, c, :], func=AF.Square, accum_out=ssq[:, c:c + 1]), r=[("ob", c)], w=["junk2", "ssq"])
            S.act(lambda e: e.activation(out=rstd[:, 0:NGR], in_=ssq[:, 0:NGR], func=AF.Sqrt, bias=epsb[0:64, 0:1], scale=1.0 / 256), r=["ssq", "epsb"], w=["rstd"])
            S.dve(lambda e: e.reciprocal(out=rstd[:, 0:NGR], in_=rstd[:, 0:NGR]), r=["rstd"], w=["rstd"])
            for c in range(NGR):
                S.dve(lambda e, c=c: e.scalar_tensor_tensor(out=ob[:, c, :], in0=ob[:, c, :], scalar=rstd[:, c:c + 1], in1=gon[:, :], op0=ALU.mult, op1=ALU.mult),
                      r=[("ob", c), "rstd", "gon"], w=[("ob", c)])
            for half in range(2):
                S.load(lambda e: e.dma_start(out=vz[:, :], in_=z_scr[2 * h + half, :, :]), r=[("z_scr", 2 * h + half)], w=["vz"])
                for gi_, c0 in enumerate(range(0, NGR, 4)):
                    nck = min(4, NGR - c0)
                    ybb = yb[gi_ % 2]
                    yk = ("yb", gi_ % 2)
                    S.pe([(lambda e, ci=ci: e.transpose(out=ptb[:, 256 + ci * 64:256 + (ci + 1) * 64], in_=ob[:, c0 + ci, half * 128:(half + 1) * 128], identity=identb[0:64, 0:64]))
                          for ci in range(nck)], r=[("ob", c0 + ci) for ci in range(nck)] + ["identb"], w=["ptb2"])
                    S.dve(lambda e: e.tensor_tensor(out=ybb[:, 0:nck * 64], in0=ptb[:, 256:256 + nck * 64], in1=vz[:, c0 * 64:(c0 + nck) * 64], op=ALU.mult),
                          r=["ptb2", "vz"], w=[yk])
                    S.store(lambda e: e.dma_start(out=y_scr[2 * h + half, :, c0 * 64:(c0 + nck) * 64], in_=ybb[:, 0:nck * 64]), r=[yk], w=[("y_scr", 2 * h + half)])
            S.barrier()

    aoff_g2 = aoff["p"]
    aoff["p"] = aoff_after_p0
    wo = av([128, 16, D], BF16)
    wo_st = [av([128, D]) for _ in range(2)]
    ytile = [av([128, 16, 128], BF16) for _ in range(2)]
    h1t = [av([128, D]) for _ in range(2)]

    def load_wout(w_ap):
        wv = w_ap.rearrange("(c p) n -> p c n", p=128)
        for c in range(16):
            b = c % 2
            S.load(lambda e, c=c, b=b: e.dma_start(out=wo_st[b][:, :], in_=wv[:, c, :]), w=[("wo_st", b)])
            S.pool(lambda e, c=c, b=b: e.tensor_copy(out=wo[:, c, :], in_=wo_st[b][:, :]), r=[("wo_st", b)], w=["wo"])

    def phase_outproj(s, layer):
        load_wout(g_w_out if layer == 0 else m_w_out)
        yv = y_scr.rearrange("c p t -> p c t")
        for tt, (t0, n) in enumerate(TOKTILES):
            if layer == 1 and tt == 0:
                continue
            b = tt % 2
            S.load(lambda e: e.dma_start(out=ytile[b][:, :, 0:n], in_=yv[:, :, t0:t0 + n]), r=[("y_scr", c) for c in range(16)], w=[("ytile", b)])
            if layer == 0:
                load_x_tile(s, tt)
            else:
                S.load(lambda e: e.dma_start(out=xt[b][0:n, :], in_=h1_scr[t0:t0 + n, :]), r=["h1_scr"], w=[("xt", b)])
            for hf in range(2):
                bk = mmbank()
                S.pe([(lambda e, c=c: e.matmul(pb[bk][0:n, 0:512], lhsT=ytile[b][:, c, 0:n], rhs=wo[:, c, hf * 512:(hf + 1) * 512], start=(c == 0), stop=(c == 15)))
                      for c in range(16)], r=[("ytile", b), "wo"], w=[("pb", bk)])
                S.dve(lambda e: e.tensor_tensor(out=h1t[b][0:n, hf * 512:(hf + 1) * 512], in0=pb[bk][0:n, 0:512], in1=xt[b][0:n, hf * 512:(hf + 1) * 512], op=ALU.add),
                      r=[("pb", bk), ("xt", b)], w=[("h1t", b)])
            if layer == 0:
                S.store(lambda e: e.dma_start(out=h1_scr[t0:t0 + n, :], in_=h1t[b][0:n, :]), r=[("h1t", b)], w=["h1_scr"])
                norm_transpose(tt, h1t[b][0:n, :], ("h1t", b), n, t0, 1)
            else:
                r0 = t0 - 64
                S.store(lambda e: e.dma_start(out=out[s, r0:r0 + n, :], in_=h1t[b][0:n, :]), r=[("h1t", b)], w=["out"])
        S.barrier()


    aoff["p"] = aoff_after_p0
    wM = av([128, 8, 832], BF16)
    wMst = [av([128, 8, 128]) for _ in range(2)]
    wMz = [av([128, 8, 128], BF16) for _ in range(2)]
    raw = av([128, 7, 512])
    sqt = av([128, 7, 512], BF16)
    rs1 = [av([128, 512]) for _ in range(2)]
    o1 = [av([128, 7, 512], BF16) for _ in range(2)]
    kpg = av([64, 512], BF16)
    tmpa = av([64, 512])
    tmpb = av([64, 512])
    zo = [av([128, 512], BF16) for _ in range(2)]
    ropeb1 = av([64, 2, 512])
    gq = sb("gq", [128, 4])
    gkv = sb("gkv", [128, 2])
    gqa = sb("gqa", [128, 1])
    gqb = sb("gqb", [64, 1])
    gka = sb("gka", [128, 1])
    gkb = sb("gkb", [64, 1])
    rotb = sb("rotb", [64, 64], BF16)
    rot_st = sb("rot_st", [64, 64])
    nshift = sb("nshift", [128, 1])
    m_w_in_v = m_w_in.rearrange("(c p) n -> p c n", p=128)

    def setup_mla():
        S.load(lambda e: e.dma_start(out=gq[:], in_=m_qn.rearrange("(c p) -> p c", p=128), allow_slow_non_contiguous=True), w=["gq"])
        S.load(lambda e: e.dma_start(out=gkv[:], in_=m_kvn.rearrange("(c p) -> p c", p=128), allow_slow_non_contiguous=True), w=["gkv"])
        S.load(lambda e: e.dma_start(out=gqa[:], in_=m_qg[0:128].rearrange("(p c) -> p c", c=1)), w=["gqa"])
        S.load(lambda e: e.dma_start(out=gqb[:], in_=m_qg[128:192].rearrange("(p c) -> p c", c=1)), w=["gqb"])
        S.load(lambda e: e.dma_start(out=gka[:], in_=m_kg[0:128].rearrange("(p c) -> p c", c=1)), w=["gka"])
        S.load(lambda e: e.dma_start(out=gkb[:], in_=m_kg[128:192].rearrange("(p c) -> p c", c=1)), w=["gkb"])
        S.load(lambda e: e.dma_start(out=rot_st[:], in_=c_rot[:, :]), w=["rot_st"])
        S.dve(lambda e: e.tensor_copy(out=rotb[:], in_=rot_st[:]), r=["rot_st"], w=["rotb"])
        S.dve(lambda e: e.memset(nshift[:], -8.0), w=["nshift"])

    def rstd_from_psum(pbank, npart, w, div, dst):
        S.act(lambda e: e.activation(out=dst[0:npart, 0:w], in_=pbank[0:npart, 0:w], func=AF.Sqrt, bias=epsb[0:npart, 0:1], scale=1.0 / div),
              r=[("pb", 3), "epsb"], w=[("rs", id(dst))])
        S.dve(lambda e: e.reciprocal(out=dst[0:npart, 0:w], in_=dst[0:npart, 0:w]), r=[("rs", id(dst))], w=[("rs", id(dst))])

    def phase_m1(s):
        for f in range(7):
            b = f % 2
            nc_ = 128 if f < 6 else 64
            S.load(lambda e, f=f, b=b, nc_=nc_: e.dma_start(out=wMst[b][:, :, 0:nc_], in_=m_w_in_v[:, :, f * 128:f * 128 + nc_]), w=[("wMst", b)])
            S.pool(lambda e, f=f, b=b, nc_=nc_: e.tensor_copy(out=wM[:, :, f * 128:f * 128 + nc_], in_=wMst[b][:, :, 0:nc_]), r=[("wMst", b)], w=["wM"])
        for bi, (c0, w) in enumerate(COLBLKS):
            ob_ = o1[bi % 2]
            ok = ("o1", bi % 2)
            for f in range(7):
                nr = 128 if f < 6 else 64
                bk = mmbank()
                S.pe([(lambda e, c=c: e.matmul(pb[bk][0:nr, 0:w], lhsT=wM[:, c, f * 128:f * 128 + nr], rhs=hnT[:, c, c0:c0 + w], start=(c == 0), stop=(c == 7)))
                      for c in range(8)], r=["wM"] + gkeys("hnT", c0, w), w=[("pb", bk)])
                S.act(lambda e: e.activation(out=raw[0:nr, f, 0:w], in_=pb[bk][0:nr, 0:w], func=AF.Copy), r=[("pb", bk)], w=[("raw", f)])
                S.dve(lambda e: e.tensor_tensor(out=sqt[0:nr, f, 0:w], in0=raw[0:nr, f, 0:w], in1=raw[0:nr, f, 0:w], op=ALU.mult), r=[("raw", f)], w=[("sqt", f)])
            for (fl, div, gt, ri) in (([0, 1, 2, 3], 512.0, gq, 0), ([4, 5], 256.0, gkv, 1)):
                S.pe([(lambda e, i=i, f=f: e.matmul(pb[3][:, 0:w], lhsT=onesb[:, :], rhs=sqt[:, f, 0:w], start=(i == 0), stop=(i == len(fl) - 1)))
                      for i, f in enumerate(fl)], r=[("sqt", f) for f in fl] + ["onesb"], w=[("pb", 3)])
                rstd_from_psum(pb[3], 128, w, div, rs1[ri])
                for i, f in enumerate(fl):
                    S.dve(lambda e, i=i, f=f: e.scalar_tensor_tensor(out=ob_[:, f, 0:w], in0=raw[:, f, 0:w], scalar=gt[:, i:i + 1], in1=rs1[ri][:, 0:w],
                                                                     op0=ALU.mult, op1=ALU.mult), r=[("raw", f), ("rs", id(rs1[ri]))], w=[ok])
            S.dve(lambda e: e.tensor_scalar(out=kpg[:, 0:w], in0=raw[0:64, 6, 0:w], scalar1=gkb[:, 0:1], scalar2=None, op0=ALU.mult), r=[("raw", 6), "gkb"], w=["kpg"])
            S.pe(lambda e: e.matmul(pb[3][0:64, 0:w], lhsT=rotb[:, :], rhs=kpg[:, 0:w], start=True, stop=True), r=["kpg", "rotb"], w=[("pb", 3)])
            S.load(lambda e: e.dma_start(out=ropeb1[:, :, 0:w], in_=c_rope[:, :, c0:c0 + w]), w=["ropeb1"])
            S.dve(lambda e: e.tensor_tensor(out=tmpa[:, 0:w], in0=pb[3][0:64, 0:w], in1=ropeb1[:, 1, 0:w], op=ALU.mult), r=[("pb", 3), "ropeb1"], w=["tmpa"])
            S.dve(lambda e: e.tensor_tensor(out=tmpb[:, 0:w], in0=kpg[:, 0:w], in1=ropeb1[:, 0, 0:w], op=ALU.mult), r=["kpg", "ropeb1"], w=["tmpb"])
            S.dve(lambda e: e.tensor_tensor(out=ob_[0:64, 6, 0:w], in0=tmpa[:, 0:w], in1=tmpb[:, 0:w], op=ALU.add), r=["tmpa", "tmpb"], w=[ok])
            for f in range(6):
                S.store(lambda e, f=f: e.dma_start(out=qk_scr[f, :, c0:c0 + w], in_=ob_[:, f, 0:w]), r=[ok], w=[("qk_scr", f)])
            S.store(lambda e: e.dma_start(out=qk_scr[6, 0:64, c0:c0 + w], in_=ob_[0:64, 6, 0:w]), r=[ok], w=[("qk_scr", 6)])
            S.store(lambda e: e.dma_start(out=qk_scr[7, 0:64, c0:c0 + w], in_=sqt[0:64, 6, 0:w]), r=[("sqt", 6)], w=[("qk_scr", 7)])
        for hh in range(16):
            b = hh % 2
            S.load(lambda e, hh=hh, b=b: e.dma_start(out=wMst[b][:, :, :], in_=m_w_in_v[:, :, 832 + hh * 128:832 + (hh + 1) * 128]), w=[("wMst", b)])
            S.pool(lambda e, b=b: e.tensor_copy(out=wMz[b][:, :, :], in_=wMst[b][:, :, :]), r=[("wMst", b)], w=[("wMz", b)])
            for bi, (c0, w) in enumerate(COLBLKS):
                bk = mmbank()
                zb = zo[bi % 2]
                S.pe([(lambda e, c=c: e.matmul(pb[bk][:, 0:w], lhsT=wMz[b][:, c, :], rhs=hnT[:, c, c0:c0 + w], start=(c == 0), stop=(c == 7)))
                      for c in range(8)], r=[("wMz", b)] + gkeys("hnT", c0, w), w=[("pb", bk)])
                S.act(lambda e: e.activation(out=zb[:, 0:w], in_=pb[bk][:, 0:w], func=AF.Silu), r=[("pb", bk)], w=[("zo", bi % 2)])
                S.store(lambda e: e.dma_start(out=z_scr[hh, :, c0:c0 + w], in_=zb[:, 0:w]), r=[("zo", bi % 2)], w=[("z_scr", hh)])
        S.barrier()

    aoff["p"] = 0
    cqT = av([128, 4, LE], BF16)
    ckvT = av([128, 2, LE], BF16)
    krT = av([64, LE], BF16)
    sqk = av([128, LE], BF16)
    qTa = av([128, LE], BF16)
    qTb = av([128, LE], BF16)
    kTa = av([128, LE], BF16)
    kTb = av([128, LE], BF16)
    zT = av([128, LE], BF16)
    vaug = av([128, 33, 130], BF16)
    wq_st = av([128, 4, 192])
    wq = av([128, 4, 192], BF16)
    wkv_st = av([128, 2, 256])
    wkv = av([128, 2, 256], BF16)
    ra = av([128, 512])
    rb = av([64, 512])
    sqa = av([128, 512], BF16)
    sqb2 = av([128, 512], BF16)
    rsq = av([128, 512])
    rsk = av([128, 512])
    qbg = av([64, 512], BF16)
    t2a = av([64, 512])
    t2b = av([64, 512])
    ropeb2 = av([64, 2, 512])
    pT = [av([128, 512], BF16) for _ in range(2)]
    rdn = av([1, 512])
    dacc = [av([128, 512]) for _ in range(2)]
    ones128 = av([128, 1])
    rbc = av([128, 512])
    ytb = [av([128, 512], BF16) for _ in range(2)]
    KT = [(64 + 128 * i, 128) for i in range(32)] + [(PAD, 16)]
    SCALE = 192.0 ** -0.5
    wuq_v = m_wuq.rearrange("(c p) n -> p c n", p=128)
    wukv_v = m_wukv.rearrange("(c p) n -> p c n", p=128)

    def phase_m2(s, heads=range(16)):
        for f in range(4):
            S.load(lambda e, f=f: e.dma_start(out=cqT[:, f, :], in_=qk_scr[f, :, :]), r=[("qk_scr", f)], w=["cqT"])
        for f in range(2):
            S.load(lambda e, f=f: e.dma_start(out=ckvT[:, f, :], in_=qk_scr[4 + f, :, :]), r=[("qk_scr", 4 + f)], w=["ckvT"])
        S.load(lambda e: e.dma_start(out=krT[:, :], in_=qk_scr[6, 0:64, :]), r=[("qk_scr", 6)], w=["krT"])
        S.load(lambda e: e.dma_start(out=sqk[0:64, :], in_=qk_scr[7, 0:64, :]), r=[("qk_scr", 7)], w=["sqk"])
        for t_, k_ in ((sqk, "sqk"), (qTb, "qTb"), (kTb, "kTb"), (sqb2, "sqb2")):
            S.dve(lambda e, t_=t_: e.memset(t_[64:128, :], 0.0), w=[k_])
        S.dve(lambda e: e.memset(ones128[:, :], 1.0), w=["ones128"])
        for h in heads:
            S.marks.append(("m2_h%d_start" % h, dict(S.cnt)))
            S.load(lambda e: e.dma_start(out=wq_st[:], in_=wuq_v[:, :, h * 192:(h + 1) * 192]), w=["wq_st"])
            S.pool(lambda e: e.tensor_copy(out=wq[:], in_=wq_st[:]), r=["wq_st"], w=["wq"])
            S.load(lambda e: e.dma_start(out=wkv_st[:], in_=wukv_v[:, :, h * 256:(h + 1) * 256]), w=["wkv_st"])
            S.pool(lambda e: e.tensor_copy(out=wkv[:], in_=wkv_st[:]), r=["wkv_st"], w=["wkv"])
            S.load(lambda e: e.dma_start(out=zT[:, :], in_=z_scr[h, :, :]), r=[("z_scr", h)], w=["zT"])
            for (c0, w) in COLBLKS:
                cs = slice(c0, c0 + w)
                bk = mmbank()
                S.pe([(lambda e, c=c: e.matmul(pb[bk][:, 0:w], lhsT=wq[:, c, 0:128], rhs=cqT[:, c, cs], start=(c == 0), stop=(c == 3))) for c in range(4)],
                     r=["wq", "cqT"], w=[("pb", bk)])
                S.act(lambda e: e.activation(out=ra[:, 0:w], in_=pb[bk][:, 0:w], func=AF.Copy), r=[("pb", bk)], w=["ra"])
                bk2 = mmbank()
                S.pe([(lambda e, c=c: e.matmul(pb[bk2][0:64, 0:w], lhsT=wq[:, c, 128:192], rhs=cqT[:, c, cs], start=(c == 0), stop=(c == 3))) for c in range(4)],
                     r=["wq", "cqT"], w=[("pb", bk2)])
                S.act(lambda e: e.activation(out=rb[:, 0:w], in_=pb[bk2][0:64, 0:w], func=AF.Copy), r=[("pb", bk2)], w=["rb"])
                S.dve(lambda e: e.tensor_tensor(out=sqa[:, 0:w], in0=ra[:, 0:w], in1=ra[:, 0:w], op=ALU.mult), r=["ra"], w=["sqa"])
                S.dve(lambda e: e.tensor_tensor(out=sqb2[0:64, 0:w], in0=rb[:, 0:w], in1=rb[:, 0:w], op=ALU.mult), r=["rb"], w=["sqb2"])
                S.pe([lambda e: e.matmul(pb[3][:, 0:w], lhsT=onesb[:, :], rhs=sqa[:, 0:w], start=True, stop=False),
                      lambda e: e.matmul(pb[3][:, 0:w], lhsT=onesb[:, :], rhs=sqb2[:, 0:w], start=False, stop=True)], r=["sqa", "sqb2", "onesb"], w=[("pb", 3)])
                rstd_from_psum(pb[3], 128, w, 192.0, rsq)
                S.dve(lambda e: e.scalar_tensor_tensor(out=qTa[:, cs], in0=ra[:, 0:w], scalar=gqa[:, 0:1], in1=rsq[:, 0:w], op0=ALU.mult, op1=ALU.mult),
                      r=["ra", ("rs", id(rsq)), "gqa"], w=["qTa"])
                S.dve(lambda e: e.scalar_tensor_tensor(out=qbg[:, 0:w], in0=rb[:, 0:w], scalar=gqb[:, 0:1], in1=rsq[0:64, 0:w], op0=ALU.mult, op1=ALU.mult),
                      r=["rb", ("rs", id(rsq)), "gqb"], w=["qbg"])
                S.pe(lambda e: e.matmul(pb[3][0:64, 0:w], lhsT=rotb[:, :], rhs=qbg[:, 0:w], start=True, stop=True), r=["qbg", "rotb"], w=[("pb", 3)])
                S.load(lambda e: e.dma_start(out=ropeb2[:, :, 0:w], in_=c_rope[:, :, cs]), w=["ropeb2"])
                S.dve(lambda e: e.tensor_tensor(out=t2a[:, 0:w], in0=pb[3][0:64, 0:w], in1=ropeb2[:, 1, 0:w], op=ALU.mult), r=[("pb", 3), "ropeb2"], w=["t2a"])
                S.dve(lambda e: e.tensor_tensor(out=t2b[:, 0:w], in0=qbg[:, 0:w], in1=ropeb2[:, 0, 0:w], op=ALU.mult), r=["qbg", "ropeb2"], w=["t2b"])
                S.dve(lambda e: e.tensor_tensor(out=qTb[0:64, cs], in0=t2a[:, 0:w], in1=t2b[:, 0:w], op=ALU.add), r=["t2a", "t2b"], w=["qTb"])
                bk = mmbank()
                S.pe([(lambda e, c=c: e.matmul(pb[bk][:, 0:w], lhsT=wkv[:, c, 0:128], rhs=ckvT[:, c, cs], start=(c == 0), stop=(c == 1))) for c in range(2)],
                     r=["wkv", "ckvT"], w=[("pb", bk)])
                S.act(lambda e: e.activation(out=ra[:, 0:w], in_=pb[bk][:, 0:w], func=AF.Copy), r=[("pb", bk)], w=["ra"])
                S.dve(lambda e: e.tensor_tensor(out=sqa[:, 0:w], in0=ra[:, 0:w], in1=ra[:, 0:w], op=ALU.mult), r=["ra"], w=["sqa"])
                S.pe([lambda e: e.matmul(pb[3][:, 0:w], lhsT=onesb[:, :], rhs=sqa[:, 0:w], start=True, stop=False),
                      lambda e: e.matmul(pb[3][:, 0:w], lhsT=onesb[:, :], rhs=sqk[:, cs], start=False, stop=True)], r=["sqa", "sqk", "onesb"], w=[("pb", 3)])
                rstd_from_psum(pb[3], 128, w, 192.0, rsk)
                S.dve(lambda e: e.scalar_tensor_tensor(out=kTa[:, cs], in0=ra[:, 0:w], scalar=gka[:, 0:1], in1=rsk[:, 0:w], op0=ALU.mult, op1=ALU.mult),
                      r=["ra", ("rs", id(rsk)), "gka"], w=["kTa"])
                S.dve(lambda e: e.tensor_tensor(out=kTb[0:64, cs], in0=krT[:, cs], in1=rsk[0:64, 0:w], op=ALU.mult), r=["krT", ("rs", id(rsk))], w=["kTb"])
            S.marks.append(("m2_h%d_qk" % h, dict(S.cnt)))
            for kt, (k0, nk) in enumerate(KT):
                bk = mmbank()
                S.pe([(lambda e, c=c: e.matmul(pb[bk][0:nk, 0:128], lhsT=ckvT[:, c, k0:k0 + nk], rhs=wkv[:, c, 128:256], start=(c == 0), stop=(c == 1))) for c in range(2)],
                     r=["wkv", "ckvT"], w=[("pb", bk)])
                S.act(lambda e: e.activation(out=vaug[0:nk, kt, 0:128], in_=pb[bk][0:nk, 0:128], func=AF.Copy), r=[("pb", bk)], w=["vaug"])
            S.marks.append(("m2_h%d_v" % h, dict(S.cnt)))
            steps = [(qi, kt) for qi in range(8) for kt in range(33)]

            def emit_scores(st):
                qi, kt = steps[st]
                k0, nk = KT[kt]
                q0 = 64 + 512 * qi
                bk = st % 2
                S.pe([lambda e: e.matmul(pb[bk][0:nk, 0:512], lhsT=kTa[:, k0:k0 + nk], rhs=qTa[:, q0:q0 + 512], start=True, stop=False),
                      lambda e: e.matmul(pb[bk][0:nk, 0:512], lhsT=kTb[:, k0:k0 + nk], rhs=qTb[:, q0:q0 + 512], start=False, stop=True)],
                     r=["kTa", "kTb", "qTa", "qTb"], w=[("pb", bk)])

            emit_scores(0)
            for st, (qi, kt) in enumerate(steps):
                k0, nk = KT[kt]
                q0 = 64 + 512 * qi
                bk = st % 2
                ab = qi % 2
                acc_o = pb[4 + 2 * ab]
                acc_d = pb[5 + 2 * ab]
                if st + 1 < len(steps):
                    emit_scores(st + 1)
                S.act(lambda e: e.activation(out=pT[bk][0:nk, :], in_=pb[bk][0:nk, 0:512], func=AF.Exp, bias=nshift[0:nk, 0:1], scale=SCALE),
                      r=[("pb", bk), "nshift"], w=[("pT", bk)])
                S.pe(lambda e: e.matmul(acc_o[:, 0:512], lhsT=vaug[0:nk, kt, 0:128], rhs=pT[bk][0:nk, :], start=(kt == 0), stop=(kt == 32)),
                     r=[("pT", bk), "vaug"], w=[("acc", ab)])
                if kt == 0:
                    S.dve(lambda e: e.tensor_copy(out=dacc[ab][:, :], in_=pT[bk][:, :]), r=[("pT", bk)], w=[("dacc", ab)])
                else:
                    S.dve(lambda e: e.tensor_tensor(out=dacc[ab][0:nk, :], in0=dacc[ab][0:nk, :], in1=pT[bk][0:nk, :], op=ALU.add),
                          r=[("pT", bk), ("dacc", ab)], w=[("dacc", ab)])
                if kt == 32:
                    yb_ = ytb[qi % 2]
                    S.pe(lambda e: e.matmul(acc_d[0:1, 0:512], lhsT=ones128[:, 0:1], rhs=dacc[ab][:, :], start=True, stop=True),
                         r=[("dacc", ab), "ones128"], w=[("accd", ab)])
                    S.dve(lambda e: e.reciprocal(out=rdn[0:1, :], in_=acc_d[0:1, 0:512]), r=[("accd", ab)], w=["rdn"])
                    S.pe(lambda e: e.matmul(pb[3][:, 0:512], lhsT=ones32[0:1, :], rhs=rdn[0:1, :], start=True, stop=True), r=["rdn", "ones32"], w=[("pb", 3)])
                    S.act(lambda e: e.activation(out=rbc[:, :], in_=pb[3][:, 0:512], func=AF.Copy), r=[("pb", 3)], w=["rbc"])
                    S.dve(lambda e: e.tensor_tensor(out=rbc[:, :], in0=acc_o[:, 0:512], in1=rbc[:, :], op=ALU.mult), r=[("acc", ab), "rbc"], w=["rbc"])
                    S.dve(lambda e: e.tensor_tensor(out=yb_[:, :], in0=rbc[:, :], in1=zT[:, q0:q0 + 512], op=ALU.mult), r=["rbc", "zT"], w=[("ytb", qi % 2)])
                    S.store(lambda e: e.dma_start(out=y_scr[h, :, q0:q0 + 512], in_=yb_[:, :]), r=[("ytb", qi % 2)], w=[("y_scr", h)])
            S.barrier()

    def dump(name, t, shape, dt, keys):
        d = nc.dram_tensor("dbg_" + name, list(shape), dt, kind="ExternalOutput").ap()
        S.store(lambda e: e.dma_start(out=d, in_=t), r=keys)

    setup()
    setup_mla()
    for s in range(NSEQ):
        S.marks.append(("start%d" % s, dict(S.cnt)))
        phase_p0(s)
        S.marks.append(("p0", dict(S.cnt)))
        phase_g1(s)
        S.marks.append(("g1", dict(S.cnt)))
        S.barrier()
        if debug == "g2":
            phase_g2(s, heads=[0])
            break
        if debug not in ("m2", "m1"):
            phase_g2(s)
        S.marks.append(("g2", dict(S.cnt)))
        phase_outproj(s, 0)
        S.marks.append(("op0", dict(S.cnt)))
        if debug == "l0":
            break
        phase_m1(s)
        S.marks.append(("m1", dict(S.cnt)))
        if debug == "m1":
            break
        if debug == "m2":
            phase_m2(s, heads=[0])
            break
        phase_m2(s)
        S.marks.append(("m2", dict(S.cnt)))
        phase_outproj(s, 1)
        S.marks.append(("op1", dict(S.cnt)))
    print("arena max", aoff.get("max"), "ninst", S.ninst, S.cnt, "sbuf left", nc.sbuf_bytes_remaining)
    nc._marks = S.marks
    S.finish()
    es.close()
    return nc


def _consts():
    ident = np.eye(128, dtype=np.float32)
    i = np.arange(64)
    U = (i[:, None] <= i[None, :]).astype(np.float32)
    Lo = (i[:, None] >= i[None, :]).astype(np.float32)
    Us = (i[:, None] < i[None, :]).astype(np.float32)
    Ls = (i[:, None] > i[None, :]).astype(np.float32)
    NEG = -30000.0
    masks = np.stack([U, Lo, Us, Ls, (1 - U) * NEG, (1 - Lo) * NEG], axis=1).astype(np.float32)
    pos = np.arange(LE, dtype=np.float64) - PAD
    inv = 10000.0 ** (-np.arange(0, 64, 2, dtype=np.float64) / 64)
    ang = pos[None, :] * inv[:, None]
    cos = np.concatenate([np.cos(ang), np.cos(ang)], 0)
    sin = np.concatenate([np.sin(ang), np.sin(ang)], 0)
    rope = np.stack([cos, sin], 1).astype(np.float32)
    rot = np.zeros((64, 64), np.float32)
    for m in range(32):
        rot[m + 32, m] = -1.0
        rot[m, m + 32] = 1.0
    return dict(c_ident=ident, c_masks=masks, c_rope=rope, c_rot=rot)


_NC_CACHE = {}


def _in_maps(inputs):
    allx = np.concatenate([np.asarray(inputs["x_prompt"]), np.asarray(inputs["x_sample"])], 0)
    seqs = [[0, 1], [2, 3], [4, 5], [6, 7], [8, 8], [9, 9], [10, 10], [11, 11]]
    common = dict(
        meta=np.asarray(inputs["meta_tokens"]), ln_g=np.asarray(inputs["ln_g"]),
        g_w_in=np.asarray(inputs["gdn_w_in"])[0], g_conv=np.asarray(inputs["gdn_conv_w"])[0],
        g_alog=np.asarray(inputs["gdn_a_log"])[0].reshape(16), g_dtb=np.asarray(inputs["gdn_dt_bias"])[0].reshape(16),
        g_on=np.asarray(inputs["gdn_o_norm_g"])[0], g_w_out=np.asarray(inputs["gdn_w_out"])[0],
        m_w_in=np.asarray(inputs["mla_w_in"])[0], m_qn=np.asarray(inputs["mla_q_norm_g"])[0],
        m_kvn=np.asarray(inputs["mla_kv_norm_g"])[0], m_wuq=np.asarray(inputs["mla_w_uq"])[0],
        m_wukv=np.asarray(inputs["mla_w_ukv"])[0], m_qg=np.asarray(inputs["mla_qk_q_g"])[0],
        m_kg=np.asarray(inputs["mla_qk_k_g"])[0], m_w_out=np.asarray(inputs["mla_w_out"])[0],
    )
    common = {k: np.ascontiguousarray(v, dtype=np.float32) for k, v in common.items()}
    common.update(_consts())
    maps = []
    for c in range(8):
        m = dict(common)
        m["xs"] = np.ascontiguousarray(allx[seqs[c]])
        maps.append(m)
    return maps, seqs


def kernel(**inputs):
    if "nc" not in _NC_CACHE:
        _NC_CACHE["nc"] = build()
    nc = _NC_CACHE["nc"]
    maps, seqs = _in_maps(inputs)
    res = run_bass_kernel_spmd(nc, maps, core_ids=list(range(8)))
    full = np.zeros((12, LX, D), np.float32)
    for c in range(8):
        o = res.results[c]["out"]
        full[seqs[c][0]] = o[0]
        if seqs[c][1] != seqs[c][0]:
            full[seqs[c][1]] = o[1]
    return full[:4], full[4:]
```

```python
import numpy as np
import ml_dtypes
import concourse.bass as bass
import concourse.mybir as mybir
from concourse.bass_utils import run_bass_kernel_spmd

F32 = mybir.dt.float32
BF16 = mybir.dt.bfloat16
ALU = mybir.AluOpType
AF = mybir.ActivationFunctionType

D = 1024
LX = 4096
NMETA = 16
PAD = 48
LE = 4160
NGR = 65
NSEQ = 2
EPS = 1e-6
COLBLKS = [(0, 64)] + [(64 + 512 * i, 512) for i in range(8)]
TOKTILES = [(0, 64)] + [(64 + 128 * i, 128) for i in range(32)]
GDN_IN = 6176
import os
SCANSTOP = int(os.environ.get('SCANSTOP', '9'))
DMA_K = 6
SAME_ENG_SYNC = True


def gkeys(name, c0, n):
    return [(name, g) for g in range(c0 // 64, (c0 + n + 63) // 64)]


class Sched:
    def __init__(self, nc, es):
        self.nc = nc
        self.eng = {"pe": nc.tensor, "dve": nc.vector, "act": nc.scalar, "pool": nc.gpsimd, "sp": nc.sync}
        self.semh = {}
        for e in self.eng:
            self.semh[(e,)] = es.enter_context(nc.semaphore("s_" + e))
        for q in ("sp", "pool", "act"):
            for s in range(DMA_K):
                self.semh[(q, "d", s)] = es.enter_context(nc.semaphore(f"d_{q}{s}"))
        self.cnt = {e: 0 for e in self.eng}
        self.dman = {q: 0 for q in ("sp", "pool", "act")}
        self.seen = {e: {} for e in self.eng}
        self.lastw = {}
        self.readers = {}
        self.ninst = 0
        self.marks = []

    def _wait(self, e, semk, val):
        if val <= 0 or self.seen[e].get(semk, 0) >= val:
            return
        self.eng[e].wait_ge(self.semh[semk], val)
        self.seen[e][semk] = val

    def op(self, e, fn, reads=(), writes=(), dma=False):
        deps = {}
        for k in reads:
            t = self.lastw.get(k)
            if t is not None:
                deps[t[0]] = max(deps.get(t[0], 0), t[1])
        for k in writes:
            t = self.lastw.get(k)
            if t is not None:
                deps[t[0]] = max(deps.get(t[0], 0), t[1])
            for sk, v in self.readers.get(k, {}).items():
                deps[sk] = max(deps.get(sk, 0), v)
        for sk, v in deps.items():
            if sk == (e,) and (e == "pe" or not SAME_ENG_SYNC) and not dma:
                continue
            self._wait(e, sk, v)
        if dma:
            n = self.dman[e]
            slot = n % DMA_K
            sk = (e, "d", slot)
            self._wait(e, sk, 16 * (n // DMA_K))
            self.dman[e] = n + 1
            inst = fn(self.eng[e])
            inst.then_inc(self.semh[sk], 16)
            tok = (sk, 16 * (n // DMA_K + 1))
        else:
            fns = fn if isinstance(fn, (list, tuple)) else [fn]
            inst = None
            for f in fns:
                inst = f(self.eng[e])
                self.ninst += 1
            self.cnt[e] += 1
            inst.then_inc(self.semh[(e,)], 1)
            tok = ((e,), self.cnt[e])
        for k in reads:
            r = self.readers.setdefault(k, {})
            r[tok[0]] = max(r.get(tok[0], 0), tok[1])
        for k in writes:
            self.lastw[k] = tok
            self.readers[k] = {}
        return tok

    def pe(self, fn, r=(), w=()):
        return self.op("pe", fn, r, w)

    def dve(self, fn, r=(), w=()):
        return self.op("dve", fn, r, w)

    def act(self, fn, r=(), w=()):
        return self.op("act", fn, r, w)

    def pool(self, fn, r=(), w=()):
        return self.op("pool", fn, r, w)

    def load(self, fn, r=(), w=()):
        return self.op("sp", fn, r, w, dma=True)

    def store(self, fn, r=(), w=()):
        return self.op("pool", fn, r, w, dma=True)

    def barrier(self):
        for e in self.eng:
            for e2 in self.eng:
                if e2 != e:
                    self._wait(e, (e2,), self.cnt[e2])
            for q in self.dman:
                n = self.dman[q]
                for s in range(DMA_K):
                    if n > s:
                        last = ((n - 1 - s) // DMA_K) * DMA_K + s
                        self._wait(e, (q, "d", s), 16 * (last // DMA_K + 1))
        self.lastw = {}
        self.readers = {}

    def finish(self):
        self.barrier()


def build(debug=None):
    from contextlib import ExitStack
    nc = bass.Bass("TRN2", target_bir_lowering=False)
    es = ExitStack()

    def din(name, shape, dt=F32):
        return nc.dram_tensor(name, list(shape), dt, kind="ExternalInput").ap()

    xs = din("xs", [NSEQ, LX, D])
    meta = din("meta", [NMETA, D])
    ln_g = din("ln_g", [2, D])
    g_w_in = din("g_w_in", [D, GDN_IN])
    g_conv = din("g_conv", [5, 4096])
    g_alog = din("g_alog", [16])
    g_dtb = din("g_dtb", [16])
    g_on = din("g_on", [256])
    g_w_out = din("g_w_out", [2048, D])
    m_w_in = din("m_w_in", [D, 2880])
    m_qn = din("m_qn", [512])
    m_kvn = din("m_kvn", [256])
    m_wuq = din("m_wuq", [512, 3072])
    m_wukv = din("m_wukv", [256, 4096])
    m_qg = din("m_qg", [192])
    m_kg = din("m_kg", [192])
    m_w_out = din("m_w_out", [2048, D])
    c_ident = din("c_ident", [128, 128])
    c_masks = din("c_masks", [64, 6, 64])
    c_rope = din("c_rope", [64, 2, LE])
    c_rot = din("c_rot", [64, 64])
    out = nc.dram_tensor("out", [NSEQ, LX, D], F32, kind="ExternalOutput").ap()

    skind = "ExternalOutput" if debug else "Internal"

    def dscr(name, shape, dt):
        return nc.dram_tensor(name, list(shape), dt, kind=skind).ap()

    qk_scr = dscr("qk_scr", [16, 128, LE], BF16)
    v_scr = dscr("v_scr", [16, 128, LE], BF16)
    z_scr = dscr("z_scr", [16, 128, LE], BF16)
    y_scr = dscr("y_scr", [16, 128, LE], BF16)
    h1_scr = dscr("h1_scr", [LE, D], F32)

    S = Sched(nc, es)

    def sb(name, shape, dt=F32):
        return es.enter_context(nc.sbuf_tensor(name, list(shape), dt))

    def ps(name, shape, dt=F32):
        return es.enter_context(nc.psum_tensor(name, list(shape), dt))

    block = es.enter_context(nc.Block())

    ident = sb("ident", [128, 128])
    identb = sb("identb", [128, 128], BF16)
    onesb = sb("onesb", [128, 128], BF16)
    ones32 = sb("ones32", [64, 128])
    masks = sb("masks", [64, 6, 64])
    lng = sb("lng", [128, 2, 8])
    convw = sb("convw", [128, 5, 32])
    alog = sb("alog", [64, 16])
    dtb = sb("dtb", [64, 16])
    nea = sb("nea", [64, 16])
    gon = sb("gon", [64, 256])
    epsb = sb("epsb", [128, 1])

    pb = [ps(f"pb{i}", [128, 512]) for i in range(8) if i != 2]
    pb.insert(2, None)
    ptb = ps("ptb", [128, 1024], BF16)

    def setup():
        S.load(lambda e: e.dma_start(out=ident[:], in_=c_ident[:, :]), w=["ident"])
        S.load(lambda e: e.dma_start(out=masks[:], in_=c_masks[:, :, :]), w=["masks"])
        S.load(lambda e: e.dma_start(out=lng[:], in_=ln_g.rearrange("l (c p) -> p l c", p=128), allow_slow_non_contiguous=True), w=["lng"])
        for j in range(5):
            S.load(lambda e, j=j: e.dma_start(out=convw[:, j, :], in_=g_conv[j].rearrange("(c p) -> p c", p=128), allow_slow_non_contiguous=True), w=["convw"])
        S.load(lambda e: e.dma_start(out=alog[:], in_=g_alog.partition_broadcast(64)), w=["alog"])
        S.load(lambda e: e.dma_start(out=dtb[:], in_=g_dtb.partition_broadcast(64)), w=["dtb"])
        S.load(lambda e: e.dma_start(out=gon[:], in_=g_on.partition_broadcast(64)), w=["gon"])
        S.dve(lambda e: e.tensor_copy(out=identb[:], in_=ident[:]), r=["ident"], w=["identb"])
        S.dve(lambda e: e.memset(onesb[:], 1.0), w=["onesb"])
        S.dve(lambda e: e.memset(ones32[:], 1.0), w=["ones32"])
        S.dve(lambda e: e.memset(epsb[:], EPS), w=["epsb"])
        S.act(lambda e: e.activation(out=nea[:], in_=alog[:], func=AF.Exp), r=["alog"], w=["nea"])
        S.dve(lambda e: e.tensor_scalar(out=nea[:], in0=nea[:], scalar1=-1.0, scalar2=None, op0=ALU.mult), r=["nea"], w=["nea"])

    beta = sb("beta", [64, NGR, 2, 8])
    gg = sb("gg", [64, NGR, 2, 8])
    gc = sb("gc", [64, NGR, 2, 8])
    negc = sb("negc", [64, NGR, 2, 8])
    egc = sb("egc", [64, NGR, 2, 8])
    kds = sb("kds", [64, NGR, 2, 8])
    egl = sb("egl", [128, NGR, 2, 8])
    negm4 = sb("negm4", [64, 4, 2, 64])
    strict4 = sb("strict4", [64, 4, 2, 64])
    ARENA_BYTES = 171500
    arena = sb("arena", [128, ARENA_BYTES // 4])
    aoff = {"p": 0}

    def av(shape, dt=F32):
        n = 1
        for d_ in shape[1:]:
            n *= d_
        nb = (n * (2 if dt == BF16 else 4) + 3) // 4 * 4
        o = aoff["p"]
        aoff["p"] = o + nb
        aoff["max"] = max(aoff.get("max", 0), o + nb)
        assert aoff["p"] <= ARENA_BYTES, (aoff["p"], shape)
        v = arena[0:shape[0], o // 4:(o + nb) // 4]
        if dt == BF16:
            v = v.bitcast(BF16)
            if n % 2:
                v = v[:, 0:n]
        if len(shape) > 2:
            names = "abcd"[:len(shape) - 1]
            pat = "p (" + " ".join(names) + ") -> p " + " ".join(names)
            v = v.rearrange(pat, **{names[i]: shape[1 + i] for i in range(len(names))})
        return v

    def sbA(name, shape, dt=F32):
        return av(shape, dt)

    hnT = sbA("hnT", [128, 8, LE], BF16)
    xt = [sbA(f"xt{i}", [128, D]) for i in range(2)]
    xn = [sbA(f"xn{i}", [128, D], BF16) for i in range(2)]
    junk = sbA("junk", [128, D], BF16)
    ssb = [sbA(f"ss{i}", [128, 4]) for i in range(2)]

    aoff_after_p0 = aoff["p"]

    def norm_transpose(tt, src_ap, src_key, n, t0, layer):
        b = tt % 2
        ss = ssb[b]
        S.act(lambda e: e.activation(out=junk[0:n, :], in_=src_ap, func=AF.Square, accum_out=ss[0:n, 0:1]),
              r=[src_key], w=["junk", ("ss", b)])
        S.act(lambda e: e.activation(out=ss[0:n, 1:2], in_=ss[0:n, 0:1], func=AF.Sqrt, bias=epsb[0:n, 0:1], scale=1.0 / D),
              r=[("ss", b), "epsb"], w=[("ss", b)])
        S.dve(lambda e: e.reciprocal(out=ss[0:n, 2:3], in_=ss[0:n, 1:2]), r=[("ss", b)], w=[("ss", b)])
        S.act(lambda e: e.activation(out=xn[b][0:n, :], in_=src_ap, func=AF.Copy, scale=ss[0:n, 2:3]),
              r=[src_key, ("ss", b)], w=[("xn", b)])
        S.pe([(lambda e, c=c: e.transpose(out=ptb[:, c * 128:c * 128 + n], in_=xn[b][0:n, c * 128:(c + 1) * 128], identity=identb[0:n, 0:n]))
              for c in range(8)], r=[("xn", b), "identb"], w=["ptb"])
        pv = ptb[:, :].rearrange("p (c t) -> p c t", c=8)[:, :, 0:n]
        S.dve(lambda e: e.tensor_tensor(out=hnT[:, :, t0:t0 + n], in0=pv,
                                        in1=lng[:, layer, :].unsqueeze(2).to_broadcast([128, 8, n]), op=ALU.mult),
              r=["ptb", "lng"], w=gkeys("hnT", t0, n))

    def load_x_tile(s, tt):
        t0, n = TOKTILES[tt]
        b = tt % 2
        if tt == 0:
            S.dve(lambda e: e.memset(xt[b][0:64, :], 0.0), w=[("xt", b)])
            S.load(lambda e: e.dma_start(out=xt[b][PAD:64, :], in_=meta[:, :]), w=[("xt", b)])
        else:
            r0 = t0 - 64
            S.load(lambda e: e.dma_start(out=xt[b][0:n, :], in_=xs[s, r0:r0 + n, :]), w=[("xt", b)])

    def phase_p0(s):
        for tt, (t0, n) in enumerate(TOKTILES):
            load_x_tile(s, tt)
            norm_transpose(tt, xt[tt % 2][0:n, :], ("xt", tt % 2), n, t0, 0)

    wst = [sbA(f"wst{i}", [128, 8, 128]) for i in range(2)]
    wbf = [sbA(f"wbf{i}", [128, 8, 128], BF16) for i in range(2)]
    pre = [sbA("pre0", [128, LE + 4], BF16)] * 2
    acc = sbA("acc", [128, LE])
    sqb = sbA("sqb", [128, LE], BF16)
    obf = [sbA(f"obf{i}", [128, LE], BF16) for i in range(2)]
    rtmp = [sbA(f"rtmp{i}", [128, 512]) for i in range(2)]
    gbraw = sbA("gbraw", [64, NGR, 32])
    wba_st = sbA("wba_st", [128, 8, 32])
    wba = sbA("wba", [128, 8, 32], BF16)

    w_in_v = g_w_in.rearrange("(c p) n -> p c n", p=128)
    mmrot = [0]

    def mmbank():
        mmrot[0] ^= 1
        return mmrot[0]

    def load_w(wv_ap, idx, ncol=128):
        b = idx % 2
        S.load(lambda e: e.dma_start(out=wst[b][:, :, 0:ncol], in_=wv_ap), w=[("wst", b)])
        S.pool(lambda e: e.tensor_copy(out=wbf[b][:, :, 0:ncol], in_=wst[b][:, :, 0:ncol]), r=[("wst", b)], w=[("wbf", b)])
        return wbf[b]

    def phase_g1(s):
        for b in range(1):
            S.dve(lambda e, b=b: e.memset(pre[b][:, 0:2], 0.0), w=[("pre", b)])
            S.dve(lambda e, b=b: e.memset(pre[b][:, LE + 2:LE + 4], 0.0), w=[("pre", b)])
        flist = range(48)
        if debug == "g1a":
            flist = [0, 16, 32]
        if debug == "g1b":
            flist = []
        for f in flist:
            wt = load_w(w_in_v[:, :, f * 128:(f + 1) * 128], f)
            wk = ("wbf", f % 2)
            pb_ = 0
            for (c0, w) in COLBLKS:
                bk = mmbank()
                S.pe([(lambda e, c=c: e.matmul(pb[bk][:, 0:w], lhsT=wt[:, c, :], rhs=hnT[:, c, c0:c0 + w], start=(c == 0), stop=(c == 7)))
                      for c in range(8)], r=[wk] + gkeys("hnT", c0, w), w=[("pb", bk)])
                if f < 32:
                    S.act(lambda e: e.activation(out=pre[pb_][:, 2 + c0:2 + c0 + w], in_=pb[bk][:, 0:w], func=AF.Copy),
                          r=[("pb", bk)], w=[("pre", pb_)])
                else:
                    ob = obf[f % 2]
                    S.act(lambda e: e.activation(out=ob[:, c0:c0 + w], in_=pb[bk][:, 0:w], func=AF.Silu),
                          r=[("pb", bk)], w=[("obf", f % 2)])
            if f >= 32:
                S.store(lambda e: e.dma_start(out=z_scr[f - 32, :, :], in_=obf[f % 2][:, :]), r=[("obf", f % 2)], w=[("z_scr", f - 32)])
                continue
            pr = pre[pb_]
            S.dve(lambda e: e.tensor_scalar(out=acc[:], in0=pr[:, 0:LE], scalar1=convw[:, 0, f:f + 1], scalar2=None, op0=ALU.mult),
                  r=[("pre", pb_), "convw"], w=["acc"])
            for j in range(1, 5):
                S.dve(lambda e, j=j: e.scalar_tensor_tensor(out=acc[:], in0=pr[:, j:j + LE], scalar=convw[:, j, f:f + 1], in1=acc[:],
                                                            op0=ALU.mult, op1=ALU.add), r=[("pre", pb_), "convw", "acc"], w=["acc"])
            ob = obf[f % 2]
            ok = ("obf", f % 2)
            if f >= 16:
                S.act(lambda e: e.activation(out=ob[:], in_=acc[:], func=AF.Silu), r=["acc"], w=[ok])
                S.dve(lambda e: e.memset(ob[:, 0:PAD], 0.0), w=[ok])
                S.store(lambda e: e.dma_start(out=v_scr[f - 16, :, :], in_=ob[:, :]), r=[ok], w=[("v_scr", f - 16)])
            else:
                S.act(lambda e: e.activation(out=acc[:], in_=acc[:], func=AF.Silu), r=["acc"], w=["acc"])
                S.dve(lambda e: e.tensor_tensor(out=sqb[:], in0=acc[:], in1=acc[:], op=ALU.mult), r=["acc"], w=["sqb"])
                qscale = (128.0 ** -0.5) if f < 8 else 1.0
                for (c0, w) in COLBLKS:
                    bk = mmbank()
                    rt = rtmp[bk]
                    S.pe(lambda e: e.matmul(pb[bk][:, 0:w], lhsT=onesb[:, :], rhs=sqb[:, c0:c0 + w], start=True, stop=True),
                         r=["sqb", "onesb"], w=[("pb", bk)])
                    S.act(lambda e: e.activation(out=rt[:, 0:w], in_=pb[bk][:, 0:w], func=AF.Sqrt, bias=epsb[:, 0:1], scale=1.0),
                          r=[("pb", bk), "epsb"], w=[("rtmp", bk)])
                    S.dve(lambda e: e.reciprocal(out=rt[:, 0:w], in_=rt[:, 0:w]), r=[("rtmp", bk)], w=[("rtmp", bk)])
                    S.dve(lambda e: e.scalar_tensor_tensor(out=ob[:, c0:c0 + w], in0=acc[:, c0:c0 + w], scalar=qscale, in1=rt[:, 0:w],
                                                           op0=ALU.mult, op1=ALU.mult), r=["acc", ("rtmp", bk)], w=[ok])
                S.dve(lambda e: e.memset(ob[:, 0:PAD], 0.0), w=[ok])
                S.store(lambda e: e.dma_start(out=qk_scr[f, :, :], in_=ob[:, :]), r=[ok], w=[("qk_scr", f)])
        if debug == "g1a":
            return
        S.load(lambda e: e.dma_start(out=wba_st[:], in_=w_in_v[:, :, 6144:6176]), w=["wba_st"])
        S.pool(lambda e: e.tensor_copy(out=wba[:], in_=wba_st[:]), r=["wba_st"], w=["wba"])
        for g0 in range(0, NGR, 16):
            ng = min(16, NGR - g0)
            bk = mmbank()
            pv = pb[bk][0:64, :].rearrange("p (g n) -> p g n", n=32)
            for gi in range(ng):
                gr = g0 + gi
                S.pe([(lambda e, c=c: e.matmul(pv[:, gi, :], lhsT=hnT[:, c, gr * 64:(gr + 1) * 64], rhs=wba[:, c, :], start=(c == 0), stop=(c == 7)))
                      for c in range(8)], r=["wba", ("hnT", gr)], w=[("pb", bk)])
            S.act(lambda e: e.activation(out=gbraw[:, g0:g0 + ng, :], in_=pv[:, 0:ng, :], func=AF.Copy), r=[("pb", bk)], w=["gbraw"])
        import os
        stopat = int(os.environ.get("STOPAT", "99"))
        if stopat <= 1:
            return
        S.act(lambda e: e.activation(out=beta[:].rearrange("p g a b -> p g (a b)"), in_=gbraw[:, :, 0:16], func=AF.Sigmoid), r=["gbraw"], w=["beta"])
        ggf = gg[:].rearrange("p g a b -> p g (a b)")
        S.dve(lambda e: e.tensor_tensor(out=ggf, in0=gbraw[:, :, 16:32], in1=dtb[:].unsqueeze(1).to_broadcast([64, NGR, 16]), op=ALU.add),
              r=["gbraw", "dtb"], w=["gg"])
        S.act(lambda e: e.activation(out=ggf, in_=ggf, func=AF.Exp), r=["gg"], w=["gg"])
        S.act(lambda e: e.activation(out=ggf, in_=ggf, func=AF.Ln, bias=1.0, scale=1.0), r=["gg"], w=["gg"])
        S.dve(lambda e: e.tensor_tensor(out=ggf, in0=ggf, in1=nea[:].unsqueeze(1).to_broadcast([64, NGR, 16]), op=ALU.mult),
              r=["gg", "nea"], w=["gg"])
        if stopat <= 2:
            return
        S.dve(lambda e: e.memset(gg[0:PAD, 0, :, :], 0.0), w=["gg"])
        S.dve(lambda e: e.memset(beta[0:PAD, 0, :, :], 0.0), w=["beta"])
        if stopat <= 3:
            return
        for g0 in range(0, NGR, 32):
            ng = min(32, NGR - g0)
            bk = mmbank()
            pv = pb[bk][0:64, :].rearrange("p (g a b) -> p g a b", a=2, b=8)
            fns = []
            for gi in range(ng):
                gr = g0 + gi
                fns.append(lambda e, gi=gi, gr=gr: e.matmul(pv[:, gi, 0, :], lhsT=masks[:, 0, :], rhs=gg[:, gr, 0, :], start=True, stop=True))
                fns.append(lambda e, gi=gi, gr=gr: e.matmul(pv[:, gi, 1, :], lhsT=masks[:, 1, :], rhs=gg[:, gr, 1, :], start=True, stop=True))
            S.pe(fns, r=["gg", "masks"], w=[("pb", bk)])
            S.act(lambda e: e.activation(out=gc[:, g0:g0 + ng], in_=pv[:, 0:ng], func=AF.Copy), r=[("pb", bk)], w=["gc"])
            if stopat <= 4:
                continue
            bk2 = mmbank()
            pv2 = pb[bk2][:, :].rearrange("p (g a b) -> p g a b", a=2, b=8)
            fns = [(lambda e, gi=gi: e.matmul(pv2[:, gi].rearrange("p a b -> p (a b)"), lhsT=ones32[:, :],
                                               rhs=gg[:, g0 + gi].rearrange("p a b -> p (a b)"), start=True, stop=True)) for gi in range(ng)]
            S.pe(fns, r=["gg", "ones32"], w=[("pb", bk2)])
            if stopat <= 5:
                continue
            S.act(lambda e: e.activation(out=egl[:, g0:g0 + ng], in_=pv2[:, 0:ng], func=AF.Exp), r=[("pb", bk2)], w=["egl"])
            if stopat <= 6:
                continue
            S.act(lambda e: e.activation(out=kds[:, g0:g0 + ng], in_=pv2[0:64, 0:ng], func=AF.Copy), r=[("pb", bk2)], w=["kds"])
            S.dve(lambda e: e.tensor_tensor(out=kds[:, g0:g0 + ng].rearrange("p g a b -> p (g a b)"), in0=kds[:, g0:g0 + ng].rearrange("p g a b -> p (g a b)"),
                                            in1=gc[:, g0:g0 + ng].rearrange("p g a b -> p (g a b)"), op=ALU.subtract),
                  r=["gc", "kds"], w=["kds"])
        if stopat <= 7:
            return
        S.act(lambda e: e.activation(out=kds[:], in_=kds[:], func=AF.Exp), r=["kds"], w=["kds"])
        S.act(lambda e: e.activation(out=egc[:], in_=gc[:], func=AF.Exp), r=["gc"], w=["egc"])
        S.dve(lambda e: e.tensor_scalar(out=negc[:], in0=egc[:], scalar1=-1.0, scalar2=None, op0=ALU.mult), r=["egc"], w=["negc"])


    aoff_after_g1 = aoff["p"]
    aoff["p"] = 0
    qT = av([128, LE], BF16)
    kT = av([128, LE], BF16)
    vz = av([128, LE], BF16)
    vtok = av([64, NGR, 256], BF16)
    ob = av([64, NGR, 256], BF16)
    Rr = av([128, NGR, 2, 64], BF16)
    At = av([128, NGR, 2, 64], BF16)
    PSET = []
    for si_ in range(2):
        PSET.append(dict(rhsD=av([64, 4, 2, 64]), DTi=av([64, 4, 2, 64]), DTs=av([64, 4, 2, 64]), X4=av([64, 8, 64], BF16), XT=av([64, 8, 64], BF16),
                         Pa=[av([64, 8, 64], BF16) for _ in range(2)], PaT=[av([64, 8, 64], BF16) for _ in range(2)], R32=av([64, 8, 64]), Rb=av([64, 8, 64], BF16)))
    PSET[0].update(banks=(pb[0], pb[1], pb[3]), bkeys=(("pb", 0), ("pb", 1), ("pb", 3)), ptx=ptb[:, 512:1024], ptxk="ptx0")
    PSET[1].update(banks=(pb[4], pb[5], pb[6]), bkeys=(("pb", 4), ("pb", 5), ("pb", 6)), ptx=ptb[:, 0:512], ptxk="ptx1")
    S32 = [av([128, 256]) for _ in range(2)]
    Sbf = [av([128, 256], BF16) for _ in range(2)]
    xb = [av([128, 256], BF16) for _ in range(2)]
    vn = [av([128, 256], BF16) for _ in range(2)]
    kd = [av([128, 128], BF16) for _ in range(2)]
    t1 = [av([64, 256]) for _ in range(2)]
    ssq = av([64, NGR + 3])
    rstd = av([64, NGR + 3])
    junk2 = av([64, 256], BF16)
    yb = [av([128, 256], BF16) for _ in range(2)]

    def phase_g2(s, heads=range(8)):
        S.dve(lambda e: e.tensor_copy(out=negm4[:], in_=masks[:, 4:6, :].unsqueeze(1).to_broadcast([64, 4, 2, 64])), r=["masks"], w=["negm4"])
        S.dve(lambda e: e.tensor_copy(out=strict4[:], in_=masks[:, 2:4, :].unsqueeze(1).to_broadcast([64, 4, 2, 64])), r=["masks"], w=["strict4"])
        S.dve(lambda e: e.memset(Rr[64:128].rearrange("p a b c -> p (a b c)"), 0.0), w=["Rr"])
        S.dve(lambda e: e.memset(At[64:128].rearrange("p a b c -> p (a b c)"), 0.0), w=["At"])
        for d_ in range(2):
            S.dve(lambda e, d_=d_: e.memset(xb[d_][64:128, :], 0.0), w=[("xb", d_)])
            S.dve(lambda e, d_=d_: e.memset(vn[d_][64:128, :], 0.0), w=[("vn", d_)])
            S.dve(lambda e, d_=d_: e.memset(kd[d_][64:128, :], 0.0), w=[("kd", d_)])
        for h in heads:
            S.load(lambda e: e.dma_start(out=qT[:, :], in_=qk_scr[h, :, :]), r=[("qk_scr", h)], w=["qT"])
            S.load(lambda e: e.dma_start(out=kT[:, :], in_=qk_scr[8 + h, :, :]), r=[("qk_scr", 8 + h)], w=["kT"])
            for half in range(2):
                S.load(lambda e: e.dma_start(out=vz[:, :], in_=v_scr[2 * h + half, :, :]), r=[("v_scr", 2 * h + half)], w=["vz"])
                for g0 in range(0, NGR, 8):
                    ng = min(8, NGR - g0)
                    S.pe([(lambda e, gi=gi: e.transpose(out=ptb[0:64, gi * 128:(gi + 1) * 128], in_=vz[:, (g0 + gi) * 64:(g0 + gi + 1) * 64], identity=identb[:, :]))
                          for gi in range(ng)], r=["vz", "identb"], w=["ptb"])
                    pv = ptb[0:64, :].rearrange("p (g n) -> p g n", n=128)
                    S.act(lambda e: e.activation(out=vtok[:, g0:g0 + ng, half * 128:(half + 1) * 128], in_=pv[:, 0:ng, :], func=AF.Copy),
                          r=["ptb"], w=["vtok"])
            S.marks.append(("g2_h%d_load" % h, dict(S.cnt)))
            def prep_gen(c0, si):
                B = PSET[si]
                rhsD_, DTi_, DTs_, X4_, XT_, Pa_, PaT_, R32_, Rb_ = B["rhsD"], B["DTi"], B["DTs"], B["X4"], B["XT"], B["Pa"], B["PaT"], B["R32"], B["Rb"]
                dif_ = rhsD_
                pA, pB_, pC = B["banks"]
                kA, kB, kC = B["bkeys"]
                px = B["ptx"]
                kx = B["ptxk"]
                sfx = "_%d" % si
                nck = min(4, NGR - c0)
                nu = nck * 2
                W = nck * 128
                for d in range(2):
                    S.dve(lambda e, d=d: e.tensor_tensor(out=rhsD_[:, 0:nck, d, :], in0=gg[:, c0:c0 + nck, d, h:h + 1].to_broadcast([64, nck, 64]),
                                                         in1=masks[:, d:d + 1, :].to_broadcast([64, nck, 64]), op=ALU.mult),
                          r=["gg", "masks"], w=["rhsD" + sfx])
                yield
                S.pe([lambda e: e.matmul(pB_[0:64, 0:W], lhsT=ones32[:, 0:64], rhs=rhsD_[:, 0:nck].rearrange("p a b c -> p (a b c)"), start=True, stop=False),
                      lambda e: e.matmul(pB_[0:64, 0:W], lhsT=ident[0:64, 0:64], rhs=negm4[:, 0:nck].rearrange("p a b c -> p (a b c)"), start=False, stop=True)],
                     r=["rhsD" + sfx, "ones32", "ident", "negm4"], w=[kB])
                pK = pA[0:64, :].rearrange("p (t g n) -> p t g n", t=2, g=4)
                fns = []
                for ci in range(nck):
                    cs = slice((c0 + ci) * 64, (c0 + ci + 1) * 64)
                    fns.append(lambda e, ci=ci, cs=cs: e.matmul(pK[:, 0, ci, :], lhsT=kT[:, cs], rhs=kT[:, cs], start=True, stop=True))
                    fns.append(lambda e, ci=ci, cs=cs: e.matmul(pK[:, 1, ci, :], lhsT=kT[:, cs], rhs=qT[:, cs], start=True, stop=True))
                S.pe(fns, r=["kT", "qT"], w=[kA])
                yield
                S.act(lambda e: e.activation(out=dif_[:, 0:nck].rearrange("p a b c -> p (a b c)"), in_=pB_[0:64, 0:W], func=AF.Copy), r=[kB], w=["rhsD" + sfx])
                yield
                S.dve(lambda e: e.tensor_tensor(out=dif_[:, 0:nck].rearrange("p a b c -> p (a b) c"), in0=dif_[:, 0:nck].rearrange("p a b c -> p (a b) c"),
                                                in1=gc[:, c0:c0 + nck].rearrange("p g a b -> p (g a) b")[:, :, h:h + 1].to_broadcast([64, nu, 64]), op=ALU.subtract),
                      r=["rhsD" + sfx, "gc"], w=["rhsD" + sfx])
                yield
                S.act(lambda e: e.activation(out=DTi_[:, 0:nck].rearrange("p a b c -> p (a b c)"), in_=dif_[:, 0:nck].rearrange("p a b c -> p (a b c)"), func=AF.Exp),
                      r=["rhsD" + sfx], w=["DTi" + sfx])
                yield
                S.dve(lambda e: e.tensor_tensor(out=DTs_[:, 0:nck].rearrange("p a b c -> p (a b c)"), in0=DTi_[:, 0:nck].rearrange("p a b c -> p (a b c)"),
                                                in1=strict4[:, 0:nck].rearrange("p a b c -> p (a b c)"), op=ALU.mult), r=["DTi" + sfx, "strict4"], w=["DTs" + sfx])
                S.dve(lambda e: e.tensor_tensor(out=DTs_[:, 0:nck].rearrange("p a b c -> p (a b) c"), in0=DTs_[:, 0:nck].rearrange("p a b c -> p (a b) c"),
                                                in1=beta[:, c0:c0 + nck].rearrange("p g a b -> p (g a) b")[:, :, h:h + 1].to_broadcast([64, nu, 64]), op=ALU.mult),
                      r=["DTs" + sfx, "beta"], w=["DTs" + sfx])
                X44 = X4_[:, :, :].rearrange("p (g d) n -> p g d n", d=2)
                for d in range(2):
                    S.dve(lambda e, d=d: e.tensor_tensor(out=X44[:, 0:nck, d, :], in0=pK[:, 0, 0:nck, :], in1=DTs_[:, 0:nck, d, :], op=ALU.mult),
                          r=[kA, "DTs" + sfx], w=["X4" + sfx])
                yield
                S.pe([(lambda e, u=u: e.transpose(out=px[0:64, u * 64:(u + 1) * 64], in_=X4_[:, u, :], identity=identb[0:64, 0:64])) for u in range(nu)],
                     r=["X4" + sfx, "identb"], w=[kx, "ptb"])
                for d in range(2):
                    S.dve(lambda e, d=d: e.tensor_tensor(out=At[0:64, c0:c0 + nck, d, :], in0=pK[:, 1, 0:nck, :], in1=DTi_[:, 0:nck, d, :], op=ALU.mult),
                          r=[kA, "DTi" + sfx], w=["At"])
                S.dve(lambda e: e.tensor_tensor(out=R32_[:, 0:nu, :], in0=ident[0:64, 0:64].unsqueeze(1).to_broadcast([64, nu, 64]), in1=X4_[:, 0:nu, :], op=ALU.subtract),
                      r=["ident", "X4" + sfx], w=["R32" + sfx])
                S.pool(lambda e: e.tensor_copy(out=Rb_[:, 0:nu, :], in_=R32_[:, 0:nu, :]), r=["R32" + sfx], w=["Rb" + sfx])
                yield
                S.act(lambda e: e.activation(out=XT_[:, 0:nu, :].rearrange("p a b -> p (a b)"), in_=px[0:64, 0:nu * 64], func=AF.Copy), r=[kx], w=["XT" + sfx])
                yield
                P, PT, Pk, PTk = X4_, XT_, "X4" + sfx, "XT" + sfx
                for lvl in range(5):
                    nb = lvl % 2
                    last = (lvl == 4)
                    if not last:
                        S.pe([(lambda e, u=u, P=P, PT=PT: e.matmul(pA[0:64, u * 64:(u + 1) * 64], lhsT=PT[:, u, :], rhs=P[:, u, :], start=True, stop=True)) for u in range(nu)],
                             r=[Pk, PTk], w=[kA])
                    S.pe([(lambda e, u=u, P=P, PT=PT: e.matmul(pB_[0:64, u * 64:(u + 1) * 64], lhsT=P[:, u, :], rhs=PT[:, u, :], start=True, stop=True)) for u in range(nu)],
                         r=[Pk, PTk], w=[kB])
                    yield
                    if not last:
                        S.act(lambda e, nb=nb: e.activation(out=Pa_[nb][:, 0:nu, :].rearrange("p a b -> p (a b)"), in_=pA[0:64, 0:nu * 64], func=AF.Copy),
                              r=[kA], w=[("Pa" + sfx, nb)])
                    S.dve(lambda e, nb=nb: e.tensor_copy(out=PaT_[nb][:, 0:nu, :].rearrange("p a b -> p (a b)"), in_=pB_[0:64, 0:nu * 64]),
                          r=[kB], w=[("PaT" + sfx, nb)])
                    yield
                    S.pe([(lambda e, u=u, nb=nb: e.matmul(pC[0:64, u * 64:(u + 1) * 64], lhsT=PaT_[nb][:, u, :], rhs=Rb_[:, u, :], start=True, stop=True)) for u in range(nu)],
                         r=[("PaT" + sfx, nb), "Rb" + sfx], w=[kC])
                    yield
                    if not last:
                        S.dve(lambda e: e.tensor_tensor(out=R32_[:, 0:nu, :].rearrange("p a b -> p (a b)"), in0=pC[0:64, 0:nu * 64],
                                                        in1=R32_[:, 0:nu, :].rearrange("p a b -> p (a b)"), op=ALU.add), r=[kC, "R32" + sfx], w=["R32" + sfx])
                        S.pool(lambda e: e.tensor_copy(out=Rb_[:, 0:nu, :], in_=R32_[:, 0:nu, :]), r=["R32" + sfx], w=["Rb" + sfx])
                    else:
                        S.dve(lambda e: e.tensor_tensor(out=Rr[0:64, c0:c0 + nck].rearrange("p a b c -> p (a b c)"), in0=pC[0:64, 0:nu * 64],
                                                        in1=R32_[:, 0:nu, :].rearrange("p a b -> p (a b)"), op=ALU.add), r=[kC, "R32" + sfx], w=["Rr"])
                    yield
                    P, PT, Pk, PTk = Pa_[nb], PaT_[nb], ("Pa" + sfx, nb), ("PaT" + sfx, nb)

            glist = list(range(0, NGR, 4))
            for gi0 in range(0, len(glist), 2):
                active = [prep_gen(glist[gi0], 0)]
                if gi0 + 1 < len(glist):
                    active.append(prep_gen(glist[gi0 + 1], 1))
                while active:
                    for g_ in list(active):
                        try:
                            next(g_)
                        except StopIteration:
                            active.remove(g_)
            S.barrier()
            S.marks.append(("g2_h%d_prep" % h, dict(S.cnt)))
            for d in range(2):
                S.dve(lambda e, d=d: e.memset(S32[d][:, :], 0.0), w=[("S32", d)])
                S.dve(lambda e, d=d: e.memset(Sbf[d][:, :], 0.0), w=[("Sbf", d)])
            for t in range(NGR):
                for d in range(2):
                    c = t if d == 0 else NGR - 1 - t
                    cs = slice(c * 64, (c + 1) * 64)
                    pS, pO = pb[4 + 2 * d], pb[5 + 2 * d]
                    kS_, kO_ = ("pb", 4 + 2 * d), ("pb", 5 + 2 * d)
                    S.pe(lambda e: e.matmul(pS[0:64, 0:256], lhsT=kT[:, cs], rhs=Sbf[d][:, :], start=True, stop=True), r=["kT", ("Sbf", d)], w=[(kS_, 0)])
                    S.pe(lambda e: e.matmul(pO[0:64, 0:256], lhsT=qT[:, cs], rhs=Sbf[d][:, :], start=True, stop=True), r=["qT", ("Sbf", d)], w=[(kO_, 0)])
                    S.pe(lambda e: e.transpose(out=ptb[0:64, d * 128:(d + 1) * 128], in_=kT[:, cs], identity=identb[:, :]), r=["kT", "identb"], w=[("ptk", d)])
                    S.dve(lambda e: e.scalar_tensor_tensor(out=xb[d][0:64, :], in0=pS[0:64, 0:256], scalar=negc[:, c, d, h:h + 1], in1=vtok[:, c, :],
                                                           op0=ALU.mult, op1=ALU.add), r=[(kS_, 0), "negc", "vtok"], w=[("xb", d)])
                    S.act(lambda e: e.activation(out=kd[d][0:64, :], in_=ptb[0:64, d * 128:(d + 1) * 128], func=AF.Copy, scale=kds[:, c, d, h:h + 1]),
                          r=[("ptk", d), "kds"], w=[("kd", d)])
                    S.act(lambda e: e.activation(out=t1[d][:, :], in_=pO[0:64, 0:256], func=AF.Copy, scale=egc[:, c, d, h:h + 1]),
                          r=[(kO_, 0), "egc"], w=[("t1", d)])
                    S.pe(lambda e: e.matmul(pS[0:64, 256:512], lhsT=Rr[:, c, d, :], rhs=xb[d][:, :], start=True, stop=True), r=["Rr", ("xb", d)], w=[(kS_, 1)])
                    S.act(lambda e: e.activation(out=vn[d][0:64, :], in_=pS[0:64, 256:512], func=AF.Copy, scale=beta[:, c, d, h:h + 1]),
                          r=[(kS_, 1), "beta"], w=[("vn", d)])
                    S.pe(lambda e: e.matmul(pO[0:64, 256:512], lhsT=At[:, c, d, :], rhs=vn[d][:, :], start=True, stop=True), r=["At", ("vn", d)], w=[(kO_, 1)])
                    S.pe(lambda e: e.matmul(pS[:, 0:256], lhsT=kd[d][:, :], rhs=vn[d][:, :], start=True, stop=True), r=[("kd", d), ("vn", d)], w=[(kS_, 0)])
                    if t < 32 or (t == 32 and d == 0):
                        S.dve(lambda e: e.tensor_tensor(out=ob[:, c, :], in0=pO[0:64, 256:512], in1=t1[d][:, :], op=ALU.add), r=[(kO_, 1), ("t1", d)], w=[("ob", c)])
                    else:
                        S.dve(lambda e: e.tensor_tensor(out=t1[d][:, :], in0=pO[0:64, 256:512], in1=t1[d][:, :], op=ALU.add), r=[(kO_, 1), ("t1", d)], w=[("t1", d)])
                        S.dve(lambda e: e.tensor_tensor(out=ob[:, c, :], in0=ob[:, c, :], in1=t1[d][:, :], op=ALU.add), r=[("ob", c), ("t1", d)], w=[("ob", c)])
                    S.dve(lambda e: e.scalar_tensor_tensor(out=Sbf[d][:, :], in0=S32[d][:, :], scalar=egl[:, c, d, h:h + 1], in1=pS[:, 0:256],
                                                           op0=ALU.mult, op1=ALU.add), r=[("S32", d), "egl", (kS_, 0)], w=[("Sbf", d)])
                    S.dve(lambda e: e.scalar_tensor_tensor(out=S32[d][:, :], in0=S32[d][:, :], scalar=egl[:, c, d, h:h + 1], in1=pS[:, 0:256],
                                                           op0=ALU.mult, op1=ALU.add), r=[("S32", d), "egl", (kS_, 0)], w=[("S32", d)])
            S.marks.append(("g2_h%d_scan" % h, dict(S.cnt)))
            S.marks.append(("g2_h%d_scan" % h, dict(S.cnt)))
            for c in range(NGR):
                S.act(lambda e, c=c: e.activation(out=junk2[:, :], in_=ob[:, c, :], func=AF.Square, accum_out=ssq[:, c:c + 1]), r=[("ob", c)], w=["junk2", "ssq"])
            S.act(lambda e: e.activation(out=rstd[:, 0:NGR], in_=ssq[:, 0:NGR], func=AF.Sqrt, bias=epsb[0:64, 0:1], scale=1.0 / 256), r=["ssq", "epsb"], w=["rstd"])
            S.dve(lambda e: e.reciprocal(out=rstd[:, 0:NGR], in_=rstd[:, 0:NGR]), r=["rstd"], w=["rstd"])
            for c in range(NGR):
                S.dve(lambda e, c=c: e.scalar_tensor_tensor(out=ob[:, c, :], in0=ob[:, c, :], scalar=rstd[:, c:c + 1], in1=gon[:, :], op0=ALU.mult, op1=ALU.mult),
                      r=[("ob", c), "rstd", "gon"], w=[("ob", c)])
            for half in range(2):
                S.load(lambda e: e.dma_start(out=vz[:, :], in_=z_scr[2 * h + half, :, :]), r=[("z_scr", 2 * h + half)], w=["vz"])
                for gi_, c0 in enumerate(range(0, NGR, 4)):
                    nck = min(4, NGR - c0)
                    ybb = yb[gi_ % 2]
                    yk = ("yb", gi_ % 2)
                    S.pe([(lambda e, ci=ci: e.transpose(out=ptb[:, 256 + ci * 64:256 + (ci + 1) * 64], in_=ob[:, c0 + ci, half * 128:(half + 1) * 128], identity=identb[0:64, 0:64]))
                          for ci in range(nck)], r=[("ob", c0 + ci) for ci in range(nck)] + ["identb"], w=["ptb2"])
                    S.dve(lambda e: e.tensor_tensor(out=ybb[:, 0:nck * 64], in0=ptb[:, 256:256 + nck * 64], in1=vz[:, c0 * 64:(c0 + nck) * 64], op=ALU.mult),
                          r=["ptb2", "vz"], w=[yk])
                    S.store(lambda e: e.dma_start(out=y_scr[2 * h + half, :, c0 * 64:(c0 + nck) * 64], in_=ybb[:, 0:nck * 64]), r=[yk], w=[("y_scr", 2 * h + half)])
            S.barrier()

    aoff_g2 = aoff["p"]
    aoff["p"] = aoff_after_p0
    wo = av([128, 16, D], BF16)
    wo_st = [av([128, D]) for _ in range(2)]
    ytile = [av([128, 16, 128], BF16) for _ in range(2)]
    h1t = [av([128, D]) for _ in range(2)]

    def load_wout(w_ap):
        wv = w_ap.rearrange("(c p) n -> p c n", p=128)
        for c in range(16):
            b = c % 2
            S.load(lambda e, c=c, b=b: e.dma_start(out=wo_st[b][:, :], in_=wv[:, c, :]), w=[("wo_st", b)])
            S.pool(lambda e, c=c, b=b: e.tensor_copy(out=wo[:, c, :], in_=wo_st[b][:, :]), r=[("wo_st", b)], w=["wo"])

    def phase_outproj(s, layer):
        load_wout(g_w_out if layer == 0 else m_w_out)
        yv = y_scr.rearrange("c p t -> p c t")
        for tt, (t0, n) in enumerate(TOKTILES):
            if layer == 1 and tt == 0:
                continue
            b = tt % 2
            S.load(lambda e: e.dma_start(out=ytile[b][:, :, 0:n], in_=yv[:, :, t0:t0 + n]), r=[("y_scr", c) for c in range(16)], w=[("ytile", b)])
            if layer == 0:
                load_x_tile(s, tt)
            else:
                S.load(lambda e: e.dma_start(out=xt[b][0:n, :], in_=h1_scr[t0:t0 + n, :]), r=["h1_scr"], w=[("xt", b)])
            for hf in range(2):
                bk = mmbank()
                S.pe([(lambda e, c=c: e.matmul(pb[bk][0:n, 0:512], lhsT=ytile[b][:, c, 0:n], rhs=wo[:, c, hf * 512:(hf + 1) * 512], start=(c == 0), stop=(c == 15)))
                      for c in range(16)], r=[("ytile", b), "wo"], w=[("pb", bk)])
                S.dve(lambda e: e.tensor_tensor(out=h1t[b][0:n, hf * 512:(hf + 1) * 512], in0=pb[bk][0:n, 0:512], in1=xt[b][0:n, hf * 512:(hf + 1) * 512], op=ALU.add),
                      r=[("pb", bk), ("xt", b)], w=[("h1t", b)])
            if layer == 0:
                S.store(lambda e: e.dma_start(out=h1_scr[t0:t0 + n, :], in_=h1t[b][0:n, :]), r=[("h1t", b)], w=["h1_scr"])
                norm_transpose(tt, h1t[b][0:n, :], ("h1t", b), n, t0, 1)
            else:
                r0 = t0 - 64
                S.store(lambda e: e.dma_start(out=out[s, r0:r0 + n, :], in_=h1t[b][0:n, :]), r=[("h1t", b)], w=["out"])
        S.barrier()


    aoff["p"] = aoff_after_p0
    wM = av([128, 8, 832], BF16)
    wMst = [av([128, 8, 128]) for _ in range(2)]
    wMz = [av([128, 8, 128], BF16) for _ in range(2)]
    raw = av([128, 7, 512])
    sqt = av([128, 7, 512], BF16)
    rs1 = [av([128, 512]) for _ in range(2)]
    o1 = [av([128, 7, 512], BF16) for _ in range(2)]
    kpg = av([64, 512], BF16)
    tmpa = av([64, 512])
    tmpb = av([64, 512])
    zo = [av([128, 512], BF16) for _ in range(2)]
    ropeb1 = av([64, 2, 512])
    gq = sb("gq", [128, 4])
    gkv = sb("gkv", [128, 2])
    gqa = sb("gqa", [128, 1])
    gqb = sb("gqb", [64, 1])
    gka = sb("gka", [128, 1])
    gkb = sb("gkb", [64, 1])
    rotb = sb("rotb", [64, 64], BF16)
    rot_st = sb("rot_st", [64, 64])
    nshift = sb("nshift", [128, 1])
    m_w_in_v = m_w_in.rearrange("(c p) n -> p c n", p=128)

    def setup_mla():
        S.load(lambda e: e.dma_start(out=gq[:], in_=m_qn.rearrange("(c p) -> p c", p=128), allow_slow_non_contiguous=True), w=["gq"])
        S.load(lambda e: e.dma_start(out=gkv[:], in_=m_kvn.rearrange("(c p) -> p c", p=128), allow_slow_non_contiguous=True), w=["gkv"])
        S.load(lambda e: e.dma_start(out=gqa[:], in_=m_qg[0:128].rearrange("(p c) -> p c", c=1)), w=["gqa"])
        S.load(lambda e: e.dma_start(out=gqb[:], in_=m_qg[128:192].rearrange("(p c) -> p c", c=1)), w=["gqb"])
        S.load(lambda e: e.dma_start(out=gka[:], in_=m_kg[0:128].rearrange("(p c) -> p c", c=1)), w=["gka"])
        S.load(lambda e: e.dma_start(out=gkb[:], in_=m_kg[128:192].rearrange("(p c) -> p c", c=1)), w=["gkb"])
        S.load(lambda e: e.dma_start(out=rot_st[:], in_=c_rot[:, :]), w=["rot_st"])
        S.dve(lambda e: e.tensor_copy(out=rotb[:], in_=rot_st[:]), r=["rot_st"], w=["rotb"])
        S.dve(lambda e: e.memset(nshift[:], -8.0), w=["nshift"])

    def rstd_from_psum(pbank, npart, w, div, dst):
        S.act(lambda e: e.activation(out=dst[0:npart, 0:w], in_=pbank[0:npart, 0:w], func=AF.Sqrt, bias=epsb[0:npart, 0:1], scale=1.0 / div),
              r=[("pb", 3), "epsb"], w=[("rs", id(dst))])
        S.dve(lambda e: e.reciprocal(out=dst[0:npart, 0:w], in_=dst[0:npart, 0:w]), r=[("rs", id(dst))], w=[("rs", id(dst))])

    def phase_m1(s):
        for f in range(7):
            b = f % 2
            nc_ = 128 if f < 6 else 64
            S.load(lambda e, f=f, b=b, nc_=nc_: e.dma_start(out=wMst[b][:, :, 0:nc_], in_=m_w_in_v[:, :, f * 128:f * 128 + nc_]), w=[("wMst", b)])
            S.pool(lambda e, f=f, b=b, nc_=nc_: e.tensor_copy(out=wM[:, :, f * 128:f * 128 + nc_], in_=wMst[b][:, :, 0:nc_]), r=[("wMst", b)], w=["wM"])
        for bi, (c0, w) in enumerate(COLBLKS):
            ob_ = o1[bi % 2]
            ok = ("o1", bi % 2)
            for f in range(7):
                nr = 128 if f < 6 else 64
                bk = mmbank()
                S.pe([(lambda e, c=c: e.matmul(pb[bk][0:nr, 0:w], lhsT=wM[:, c, f * 128:f * 128 + nr], rhs=hnT[:, c, c0:c0 + w], start=(c == 0), stop=(c == 7)))
                      for c in range(8)], r=["wM"] + gkeys("hnT", c0, w), w=[("pb", bk)])
                S.act(lambda e: e.activation(out=raw[0:nr, f, 0:w], in_=pb[bk][0:nr, 0:w], func=AF.Copy), r=[("pb", bk)], w=[("raw", f)])
                S.dve(lambda e: e.tensor_tensor(out=sqt[0:nr, f, 0:w], in0=raw[0:nr, f, 0:w], in1=raw[0:nr, f, 0:w], op=ALU.mult), r=[("raw", f)], w=[("sqt", f)])
            for (fl, div, gt, ri) in (([0, 1, 2, 3], 512.0, gq, 0), ([4, 5], 256.0, gkv, 1)):
                S.pe([(lambda e, i=i, f=f: e.matmul(pb[3][:, 0:w], lhsT=onesb[:, :], rhs=sqt[:, f, 0:w], start=(i == 0), stop=(i == len(fl) - 1)))
                      for i, f in enumerate(fl)], r=[("sqt", f) for f in fl] + ["onesb"], w=[("pb", 3)])
                rstd_from_psum(pb[3], 128, w, div, rs1[ri])
                for i, f in enumerate(fl):
                    S.dve(lambda e, i=i, f=f: e.scalar_tensor_tensor(out=ob_[:, f, 0:w], in0=raw[:, f, 0:w], scalar=gt[:, i:i + 1], in1=rs1[ri][:, 0:w],
                                                                     op0=ALU.mult, op1=ALU.mult), r=[("raw", f), ("rs", id(rs1[ri]))], w=[ok])
            S.dve(lambda e: e.tensor_scalar(out=kpg[:, 0:w], in0=raw[0:64, 6, 0:w], scalar1=gkb[:, 0:1], scalar2=None, op0=ALU.mult), r=[("raw", 6), "gkb"], w=["kpg"])
            S.pe(lambda e: e.matmul(pb[3][0:64, 0:w], lhsT=rotb[:, :], rhs=kpg[:, 0:w], start=True, stop=True), r=["kpg", "rotb"], w=[("pb", 3)])
            S.load(lambda e: e.dma_start(out=ropeb1[:, :, 0:w], in_=c_rope[:, :, c0:c0 + w]), w=["ropeb1"])
            S.dve(lambda e: e.tensor_tensor(out=tmpa[:, 0:w], in0=pb[3][0:64, 0:w], in1=ropeb1[:, 1, 0:w], op=ALU.mult), r=[("pb", 3), "ropeb1"], w=["tmpa"])
            S.dve(lambda e: e.tensor_tensor(out=tmpb[:, 0:w], in0=kpg[:, 0:w], in1=ropeb1[:, 0, 0:w], op=ALU.mult), r=["kpg", "ropeb1"], w=["tmpb"])
            S.dve(lambda e: e.tensor_tensor(out=ob_[0:64, 6, 0:w], in0=tmpa[:, 0:w], in1=tmpb[:, 0:w], op=ALU.add), r=["tmpa", "tmpb"], w=[ok])
            for f in range(6):
                S.store(lambda e, f=f: e.dma_start(out=qk_scr[f, :, c0:c0 + w], in_=ob_[:, f, 0:w]), r=[ok], w=[("qk_scr", f)])
            S.store(lambda e: e.dma_start(out=qk_scr[6, 0:64, c0:c0 + w], in_=ob_[0:64, 6, 0:w]), r=[ok], w=[("qk_scr", 6)])
            S.store(lambda e: e.dma_start(out=qk_scr[7, 0:64, c0:c0 + w], in_=sqt[0:64, 6, 0:w]), r=[("sqt", 6)], w=[("qk_scr", 7)])
        for hh in range(16):
            b = hh % 2
            S.load(lambda e, hh=hh, b=b: e.dma_start(out=wMst[b][:, :, :], in_=m_w_in_v[:, :, 832 + hh * 128:832 + (hh + 1) * 128]), w=[("wMst", b)])
            S.pool(lambda e, b=b: e.tensor_copy(out=wMz[b][:, :, :], in_=wMst[b][:, :, :]), r=[("wMst", b)], w=[("wMz", b)])
            for bi, (c0, w) in enumerate(COLBLKS):
                bk = mmbank()
                zb = zo[bi % 2]
                S.pe([(lambda e, c=c: e.matmul(pb[bk][:, 0:w], lhsT=wMz[b][:, c, :], rhs=hnT[:, c, c0:c0 + w], start=(c == 0), stop=(c == 7)))
                      for c in range(8)], r=[("wMz", b)] + gkeys("hnT", c0, w), w=[("pb", bk)])
                S.act(lambda e: e.activation(out=zb[:, 0:w], in_=pb[bk][:, 0:w], func=AF.Silu), r=[("pb", bk)], w=[("zo", bi % 2)])
                S.store(lambda e: e.dma_start(out=z_scr[hh, :, c0:c0 + w], in_=zb[:, 0:w]), r=[("zo", bi % 2)], w=[("z_scr", hh)])
        S.barrier()

    aoff["p"] = 0
    cqT = av([128, 4, LE], BF16)
    ckvT = av([128, 2, LE], BF16)
    krT = av([64, LE], BF16)
    sqk = av([128, LE], BF16)
    qTa = av([128, LE], BF16)
    qTb = av([128, LE], BF16)
    kTa = av([128, LE], BF16)
    kTb = av([128, LE], BF16)
    zT = av([128, LE], BF16)
    vaug = av([128, 33, 130], BF16)
    wq_st = av([128, 4, 192])
    wq = av([128, 4, 192], BF16)
    wkv_st = av([128, 2, 256])
    wkv = av([128, 2, 256], BF16)
    ra = av([128, 512])
    rb = av([64, 512])
    sqa = av([128, 512], BF16)
    sqb2 = av([128, 512], BF16)
    rsq = av([128, 512])
    rsk = av([128, 512])
    qbg = av([64, 512], BF16)
    t2a = av([64, 512])
    t2b = av([64, 512])
    ropeb2 = av([64, 2, 512])
    pT = [av([128, 512], BF16) for _ in range(3)]
    rdn = av([1, 512])
    dacc = [av([128, 512]) for _ in range(2)]
    ones128 = av([128, 1])
    rbc = av([128, 512])
    ytb = [av([128, 512], BF16) for _ in range(2)]
    KT = [(64 + 128 * i, 128) for i in range(32)] + [(PAD, 16)]
    SCALE = 192.0 ** -0.5
    wuq_v = m_wuq.rearrange("(c p) n -> p c n", p=128)
    wukv_v = m_wukv.rearrange("(c p) n -> p c n", p=128)

    def phase_m2(s, heads=range(16)):
        for f in range(4):
            S.load(lambda e, f=f: e.dma_start(out=cqT[:, f, :], in_=qk_scr[f, :, :]), r=[("qk_scr", f)], w=["cqT"])
        for f in range(2):
            S.load(lambda e, f=f: e.dma_start(out=ckvT[:, f, :], in_=qk_scr[4 + f, :, :]), r=[("qk_scr", 4 + f)], w=["ckvT"])
        S.load(lambda e: e.dma_start(out=krT[:, :], in_=qk_scr[6, 0:64, :]), r=[("qk_scr", 6)], w=["krT"])
        S.load(lambda e: e.dma_start(out=sqk[0:64, :], in_=qk_scr[7, 0:64, :]), r=[("qk_scr", 7)], w=["sqk"])
        for t_, k_ in ((sqk, "sqk"), (qTb, "qTb"), (kTb, "kTb"), (sqb2, "sqb2")):
            S.dve(lambda e, t_=t_: e.memset(t_[64:128, :], 0.0), w=[k_])
        S.dve(lambda e: e.memset(ones128[:, :], 1.0), w=["ones128"])
        for h in heads:
            S.marks.append(("m2_h%d_start" % h, dict(S.cnt)))
            S.load(lambda e: e.dma_start(out=wq_st[:], in_=wuq_v[:, :, h * 192:(h + 1) * 192]), w=["wq_st"])
            S.pool(lambda e: e.tensor_copy(out=wq[:], in_=wq_st[:]), r=["wq_st"], w=["wq"])
            S.load(lambda e: e.dma_start(out=wkv_st[:], in_=wukv_v[:, :, h * 256:(h + 1) * 256]), w=["wkv_st"])
            S.pool(lambda e: e.tensor_copy(out=wkv[:], in_=wkv_st[:]), r=["wkv_st"], w=["wkv"])
            S.load(lambda e: e.dma_start(out=zT[:, :], in_=z_scr[h, :, :]), r=[("z_scr", h)], w=["zT"])
            for (c0, w) in COLBLKS:
                cs = slice(c0, c0 + w)
                bk = mmbank()
                S.pe([(lambda e, c=c: e.matmul(pb[bk][:, 0:w], lhsT=wq[:, c, 0:128], rhs=cqT[:, c, cs], start=(c == 0), stop=(c == 3))) for c in range(4)],
                     r=["wq", "cqT"], w=[("pb", bk)])
                S.act(lambda e: e.activation(out=ra[:, 0:w], in_=pb[bk][:, 0:w], func=AF.Copy), r=[("pb", bk)], w=["ra"])
                bk2 = mmbank()
                S.pe([(lambda e, c=c: e.matmul(pb[bk2][0:64, 0:w], lhsT=wq[:, c, 128:192], rhs=cqT[:, c, cs], start=(c == 0), stop=(c == 3))) for c in range(4)],
                     r=["wq", "cqT"], w=[("pb", bk2)])
                S.act(lambda e: e.activation(out=rb[:, 0:w], in_=pb[bk2][0:64, 0:w], func=AF.Copy), r=[("pb", bk2)], w=["rb"])
                S.dve(lambda e: e.tensor_tensor(out=sqa[:, 0:w], in0=ra[:, 0:w], in1=ra[:, 0:w], op=ALU.mult), r=["ra"], w=["sqa"])
                S.dve(lambda e: e.tensor_tensor(out=sqb2[0:64, 0:w], in0=rb[:, 0:w], in1=rb[:, 0:w], op=ALU.mult), r=["rb"], w=["sqb2"])
                S.pe([lambda e: e.matmul(pb[3][:, 0:w], lhsT=onesb[:, :], rhs=sqa[:, 0:w], start=True, stop=False),
                      lambda e: e.matmul(pb[3][:, 0:w], lhsT=onesb[:, :], rhs=sqb2[:, 0:w], start=False, stop=True)], r=["sqa", "sqb2", "onesb"], w=[("pb", 3)])
                rstd_from_psum(pb[3], 128, w, 192.0, rsq)
                S.dve(lambda e: e.scalar_tensor_tensor(out=qTa[:, cs], in0=ra[:, 0:w], scalar=gqa[:, 0:1], in1=rsq[:, 0:w], op0=ALU.mult, op1=ALU.mult),
                      r=["ra", ("rs", id(rsq)), "gqa"], w=["qTa"])
                S.dve(lambda e: e.scalar_tensor_tensor(out=qbg[:, 0:w], in0=rb[:, 0:w], scalar=gqb[:, 0:1], in1=rsq[0:64, 0:w], op0=ALU.mult, op1=ALU.mult),
                      r=["rb", ("rs", id(rsq)), "gqb"], w=["qbg"])
                S.pe(lambda e: e.matmul(pb[3][0:64, 0:w], lhsT=rotb[:, :], rhs=qbg[:, 0:w], start=True, stop=True), r=["qbg", "rotb"], w=[("pb", 3)])
                S.load(lambda e: e.dma_start(out=ropeb2[:, :, 0:w], in_=c_rope[:, :, cs]), w=["ropeb2"])
                S.dve(lambda e: e.tensor_tensor(out=t2a[:, 0:w], in0=pb[3][0:64, 0:w], in1=ropeb2[:, 1, 0:w], op=ALU.mult), r=[("pb", 3), "ropeb2"], w=["t2a"])
                S.dve(lambda e: e.tensor_tensor(out=t2b[:, 0:w], in0=qbg[:, 0:w], in1=ropeb2[:, 0, 0:w], op=ALU.mult), r=["qbg", "ropeb2"], w=["t2b"])
                S.dve(lambda e: e.tensor_tensor(out=qTb[0:64, cs], in0=t2a[:, 0:w], in1=t2b[:, 0:w], op=ALU.add), r=["t2a", "t2b"], w=["qTb"])
                bk = mmbank()
                S.pe([(lambda e, c=c: e.matmul(pb[bk][:, 0:w], lhsT=wkv[:, c, 0:128], rhs=ckvT[:, c, cs], start=(c == 0), stop=(c == 1))) for c in range(2)],
                     r=["wkv", "ckvT"], w=[("pb", bk)])
                S.act(lambda e: e.activation(out=ra[:, 0:w], in_=pb[bk][:, 0:w], func=AF.Copy), r=[("pb", bk)], w=["ra"])
                S.dve(lambda e: e.tensor_tensor(out=sqa[:, 0:w], in0=ra[:, 0:w], in1=ra[:, 0:w], op=ALU.mult), r=["ra"], w=["sqa"])
                S.pe([lambda e: e.matmul(pb[3][:, 0:w], lhsT=onesb[:, :], rhs=sqa[:, 0:w], start=True, stop=False),
                      lambda e: e.matmul(pb[3][:, 0:w], lhsT=onesb[:, :], rhs=sqk[:, cs], start=False, stop=True)], r=["sqa", "sqk", "onesb"], w=[("pb", 3)])
                rstd_from_psum(pb[3], 128, w, 192.0, rsk)
                S.dve(lambda e: e.scalar_tensor_tensor(out=kTa[:, cs], in0=ra[:, 0:w], scalar=gka[:, 0:1], in1=rsk[:, 0:w], op0=ALU.mult, op1=ALU.mult),
                      r=["ra", ("rs", id(rsk)), "gka"], w=["kTa"])
                S.dve(lambda e: e.tensor_tensor(out=kTb[0:64, cs], in0=krT[:, cs], in1=rsk[0:64, 0:w], op=ALU.mult), r=["krT", ("rs", id(rsk))], w=["kTb"])
            S.marks.append(("m2_h%d_qk" % h, dict(S.cnt)))
            for kt, (k0, nk) in enumerate(KT):
                bk = mmbank()
                S.pe([(lambda e, c=c: e.matmul(pb[bk][0:nk, 0:128], lhsT=ckvT[:, c, k0:k0 + nk], rhs=wkv[:, c, 128:256], start=(c == 0), stop=(c == 1))) for c in range(2)],
                     r=["wkv", "ckvT"], w=[("pb", bk)])
                S.act(lambda e: e.activation(out=vaug[0:nk, kt, 0:128], in_=pb[bk][0:nk, 0:128], func=AF.Copy), r=[("pb", bk)], w=["vaug"])
            S.marks.append(("m2_h%d_v" % h, dict(S.cnt)))
            steps = [(qi, kt) for qi in range(8) for kt in range(33)]
            SB = [0, 1, 3]

            def emit_scores(st):
                qi, kt = steps[st]
                k0, nk = KT[kt]
                q0 = 64 + 512 * qi
                bk = SB[st % 3]
                S.pe([lambda e: e.matmul(pb[bk][0:nk, 0:512], lhsT=kTa[:, k0:k0 + nk], rhs=qTa[:, q0:q0 + 512], start=True, stop=False),
                      lambda e: e.matmul(pb[bk][0:nk, 0:512], lhsT=kTb[:, k0:k0 + nk], rhs=qTb[:, q0:q0 + 512], start=False, stop=True)],
                     r=["kTa", "kTb", "qTa", "qTb"], w=[("pb", bk)])

            emit_scores(0)
            emit_scores(1)
            for st, (qi, kt) in enumerate(steps):
                k0, nk = KT[kt]
                q0 = 64 + 512 * qi
                bk = SB[st % 3]
                ab = qi % 2
                acc_o = pb[4 + 2 * ab]
                acc_d = pb[5]
                if st + 2 < len(steps):
                    emit_scores(st + 2)
                S.act(lambda e: e.activation(out=pT[st % 3][0:nk, :], in_=pb[bk][0:nk, 0:512], func=AF.Exp, bias=nshift[0:nk, 0:1], scale=SCALE),
                      r=[("pb", bk), "nshift"], w=[("pT", st % 3)])
                S.pe(lambda e: e.matmul(acc_o[:, 0:512], lhsT=vaug[0:nk, kt, 0:128], rhs=pT[st % 3][0:nk, :], start=(kt == 0), stop=(kt == 32)),
                     r=[("pT", st % 3), "vaug"], w=[("acc", ab)])
                if kt == 0:
                    S.dve(lambda e: e.tensor_copy(out=dacc[ab][:, :], in_=pT[st % 3][:, :]), r=[("pT", st % 3)], w=[("dacc", ab)])
                else:
                    S.dve(lambda e: e.tensor_tensor(out=dacc[ab][0:nk, :], in0=dacc[ab][0:nk, :], in1=pT[st % 3][0:nk, :], op=ALU.add),
                          r=[("pT", st % 3), ("dacc", ab)], w=[("dacc", ab)])
                if kt == 32:
                    yb_ = ytb[qi % 2]
                    S.pe(lambda e: e.matmul(acc_d[0:1, 0:512], lhsT=ones128[:, 0:1], rhs=dacc[ab][:, :], start=True, stop=True),
                         r=[("dacc", ab), "ones128"], w=["accd"])
                    S.dve(lambda e: e.reciprocal(out=rdn[0:1, :], in_=acc_d[0:1, 0:512]), r=["accd"], w=["rdn"])
                    S.pe(lambda e: e.matmul(pb[7][:, 0:512], lhsT=ones32[0:1, :], rhs=rdn[0:1, :], start=True, stop=True), r=["rdn", "ones32"], w=[("pb", 7)])
                    S.act(lambda e: e.activation(out=rbc[:, :], in_=pb[7][:, 0:512], func=AF.Copy), r=[("pb", 7)], w=["rbc"])
                    S.dve(lambda e: e.tensor_tensor(out=rbc[:, :], in0=acc_o[:, 0:512], in1=rbc[:, :], op=ALU.mult), r=[("acc", ab), "rbc"], w=["rbc"])
                    S.dve(lambda e: e.tensor_tensor(out=yb_[:, :], in0=rbc[:, :], in1=zT[:, q0:q0 + 512], op=ALU.mult), r=["rbc", "zT"], w=[("ytb", qi % 2)])
                    S.store(lambda e: e.dma_start(out=y_scr[h, :, q0:q0 + 512], in_=yb_[:, :]), r=[("ytb", qi % 2)], w=[("y_scr", h)])
            S.barrier()

    def dump(name, t, shape, dt, keys):
        d = nc.dram_tensor("dbg_" + name, list(shape), dt, kind="ExternalOutput").ap()
        S.store(lambda e: e.dma_start(out=d, in_=t), r=keys)

    setup()
    setup_mla()
    for s in range(NSEQ):
        S.marks.append(("start%d" % s, dict(S.cnt)))
        phase_p0(s)
        S.marks.append(("p0", dict(S.cnt)))
        phase_g1(s)
        S.marks.append(("g1", dict(S.cnt)))
        S.barrier()
        if debug == "g2":
            phase_g2(s, heads=[0])
            break
        if debug not in ("m2", "m1"):
            phase_g2(s)
        S.marks.append(("g2", dict(S.cnt)))
        phase_outproj(s, 0)
        S.marks.append(("op0", dict(S.cnt)))
        if debug == "l0":
            break
        phase_m1(s)
        S.marks.append(("m1", dict(S.cnt)))
        if debug == "m1":
            break
        if debug == "m2":
            phase_m2(s, heads=[0])
            break
        phase_m2(s)
        S.marks.append(("m2", dict(S.cnt)))
        phase_outproj(s, 1)
        S.marks.append(("op1", dict(S.cnt)))
    print("arena max", aoff.get("max"), "ninst", S.ninst, S.cnt, "sbuf left", nc.sbuf_bytes_remaining)
    nc._marks = S.marks
    S.finish()
    es.close()
    return nc


def _consts():
    ident = np.eye(128, dtype=np.float32)
    i = np.arange(64)
    U = (i[:, None] <= i[None, :]).astype(np.float32)
    Lo = (i[:, None] >= i[None, :]).astype(np.float32)
    Us = (i[:, None] < i[None, :]).astype(np.float32)
    Ls = (i[:, None] > i[None, :]).astype(np.float32)
    NEG = -30000.0
    masks = np.stack([U, Lo, Us, Ls, (1 - U) * NEG, (1 - Lo) * NEG], axis=1).astype(np.float32)
    pos = np.arange(LE, dtype=np.float64) - PAD
    inv = 10000.0 ** (-np.arange(0, 64, 2, dtype=np.float64) / 64)
    ang = pos[None, :] * inv[:, None]
    cos = np.concatenate([np.cos(ang), np.cos(ang)], 0)
    sin = np.concatenate([np.sin(ang), np.sin(ang)], 0)
    rope = np.stack([cos, sin], 1).astype(np.float32)
    rot = np.zeros((64, 64), np.float32)
    for m in range(32):
        rot[m + 32, m] = -1.0
        rot[m, m + 32] = 1.0
    return dict(c_ident=ident, c_masks=masks, c_rope=rope, c_rot=rot)


_NC_CACHE = {}


def _in_maps(inputs):
    allx = np.concatenate([np.asarray(inputs["x_prompt"]), np.asarray(inputs["x_sample"])], 0)
    seqs = [[0, 1], [2, 3], [4, 5], [6, 7], [8, 8], [9, 9], [10, 10], [11, 11]]
    common = dict(
        meta=np.asarray(inputs["meta_tokens"]), ln_g=np.asarray(inputs["ln_g"]),
        g_w_in=np.asarray(inputs["gdn_w_in"])[0], g_conv=np.asarray(inputs["gdn_conv_w"])[0],
        g_alog=np.asarray(inputs["gdn_a_log"])[0].reshape(16), g_dtb=np.asarray(inputs["gdn_dt_bias"])[0].reshape(16),
        g_on=np.asarray(inputs["gdn_o_norm_g"])[0], g_w_out=np.asarray(inputs["gdn_w_out"])[0],
        m_w_in=np.asarray(inputs["mla_w_in"])[0], m_qn=np.asarray(inputs["mla_q_norm_g"])[0],
        m_kvn=np.asarray(inputs["mla_kv_norm_g"])[0], m_wuq=np.asarray(inputs["mla_w_uq"])[0],
        m_wukv=np.asarray(inputs["mla_w_ukv"])[0], m_qg=np.asarray(inputs["mla_qk_q_g"])[0],
        m_kg=np.asarray(inputs["mla_qk_k_g"])[0], m_w_out=np.asarray(inputs["mla_w_out"])[0],
    )
    common = {k: np.ascontiguousarray(v, dtype=np.float32) for k, v in common.items()}
    common.update(_consts())
    maps = []
    for c in range(8):
        m = dict(common)
        m["xs"] = np.ascontiguousarray(allx[seqs[c]])
        maps.append(m)
    return maps, seqs


def kernel(**inputs):
    if "nc" not in _NC_CACHE:
        _NC_CACHE["nc"] = build()
    nc = _NC_CACHE["nc"]
    maps, seqs = _in_maps(inputs)
    res = run_bass_kernel_spmd(nc, maps, core_ids=list(range(8)))
    full = np.zeros((12, LX, D), np.float32)
    for c in range(8):
        o = res.results[c]["out"]
        full[seqs[c][0]] = o[0]
        if seqs[c][1] != seqs[c][0]:
            full[seqs[c][1]] = o[1]
    return full[:4], full[4:]
```

```python
import numpy as np
import ml_dtypes
import concourse.bass as bass
import concourse.mybir as mybir
from concourse.bass_utils import run_bass_kernel_spmd

F32 = mybir.dt.float32
BF16 = mybir.dt.bfloat16
ALU = mybir.AluOpType
AF = mybir.ActivationFunctionType

D = 1024
LX = 4096
NMETA = 16
PAD = 48
LE = 4160
NGR = 65
NSEQ = 2
EPS = 1e-6
COLBLKS = [(0, 64)] + [(64 + 512 * i, 512) for i in range(8)]
TOKTILES = [(0, 64)] + [(64 + 128 * i, 128) for i in range(32)]
GDN_IN = 6176
import os
SCANSTOP = int(os.environ.get('SCANSTOP', '9'))
DMA_K = 6
SAME_ENG_SYNC = True


def gkeys(name, c0, n):
    return [(name, g) for g in range(c0 // 64, (c0 + n + 63) // 64)]


class Sched:
    def __init__(self, nc, es):
        self.nc = nc
        self.eng = {"pe": nc.tensor, "dve": nc.vector, "act": nc.scalar, "pool": nc.gpsimd, "sp": nc.sync}
        self.semh = {}
        for e in self.eng:
            self.semh[(e,)] = es.enter_context(nc.semaphore("s_" + e))
        for q in ("sp", "pool", "act"):
            for s in range(DMA_K):
                self.semh[(q, "d", s)] = es.enter_context(nc.semaphore(f"d_{q}{s}"))
        self.cnt = {e: 0 for e in self.eng}
        self.dman = {q: 0 for q in ("sp", "pool", "act")}
        self.seen = {e: {} for e in self.eng}
        self.lastw = {}
        self.readers = {}
        self.ninst = 0
        self.marks = []

    def _wait(self, e, semk, val):
        if val <= 0 or self.seen[e].get(semk, 0) >= val:
            return
        self.eng[e].wait_ge(self.semh[semk], val)
        self.seen[e][semk] = val

    def op(self, e, fn, reads=(), writes=(), dma=False):
        deps = {}
        for k in reads:
            t = self.lastw.get(k)
            if t is not None:
                deps[t[0]] = max(deps.get(t[0], 0), t[1])
        for k in writes:
            t = self.lastw.get(k)
            if t is not None:
                deps[t[0]] = max(deps.get(t[0], 0), t[1])
            for sk, v in self.readers.get(k, {}).items():
                deps[sk] = max(deps.get(sk, 0), v)
        for sk, v in deps.items():
            if sk == (e,) and (e == "pe" or not SAME_ENG_SYNC) and not dma:
                continue
            self._wait(e, sk, v)
        if dma:
            n = self.dman[e]
            slot = n % DMA_K
            sk = (e, "d", slot)
            self._wait(e, sk, 16 * (n // DMA_K))
            self.dman[e] = n + 1
            inst = fn(self.eng[e])
            inst.then_inc(self.semh[sk], 16)
            tok = (sk, 16 * (n // DMA_K + 1))
        else:
            fns = fn if isinstance(fn, (list, tuple)) else [fn]
            inst = None
            for f in fns:
                inst = f(self.eng[e])
                self.ninst += 1
            self.cnt[e] += 1
            inst.then_inc(self.semh[(e,)], 1)
            tok = ((e,), self.cnt[e])
        for k in reads:
            r = self.readers.setdefault(k, {})
            r[tok[0]] = max(r.get(tok[0], 0), tok[1])
        for k in writes:
            self.lastw[k] = tok
            self.readers[k] = {}
        return tok

    def pe(self, fn, r=(), w=()):
        return self.op("pe", fn, r, w)

    def dve(self, fn, r=(), w=()):
        return self.op("dve", fn, r, w)

    def act(self, fn, r=(), w=()):
        return self.op("act", fn, r, w)

    def pool(self, fn, r=(), w=()):
        return self.op("pool", fn, r, w)

    def load(self, fn, r=(), w=()):
        return self.op("sp", fn, r, w, dma=True)

    def store(self, fn, r=(), w=()):
        return self.op("pool", fn, r, w, dma=True)

    def barrier(self):
        for e in self.eng:
            for e2 in self.eng:
                if e2 != e:
                    self._wait(e, (e2,), self.cnt[e2])
            for q in self.dman:
                n = self.dman[q]
                for s in range(DMA_K):
                    if n > s:
                        last = ((n - 1 - s) // DMA_K) * DMA_K + s
                        self._wait(e, (q, "d", s), 16 * (last // DMA_K + 1))
        self.lastw = {}
        self.readers = {}

    def finish(self):
        self.barrier()


def build(debug=None):
    from contextlib import ExitStack
    nc = bass.Bass("TRN2", target_bir_lowering=False)
    es = ExitStack()

    def din(name, shape, dt=F32):
        return nc.dram_tensor(name, list(shape), dt, kind="ExternalInput").ap()

    xs = din("xs", [NSEQ, LX, D])
    meta = din("meta", [NMETA, D])
    ln_g = din("ln_g", [2, D])
    g_w_in = din("g_w_in", [D, GDN_IN])
    g_conv = din("g_conv", [5, 4096])
    g_alog = din("g_alog", [16])
    g_dtb = din("g_dtb", [16])
    g_on = din("g_on", [256])
    g_w_out = din("g_w_out", [2048, D])
    m_w_in = din("m_w_in", [D, 2880])
    m_qn = din("m_qn", [512])
    m_kvn = din("m_kvn", [256])
    m_wuq = din("m_wuq", [512, 3072])
    m_wukv = din("m_wukv", [256, 4096])
    m_qg = din("m_qg", [192])
    m_kg = din("m_kg", [192])
    m_w_out = din("m_w_out", [2048, D])
    c_ident = din("c_ident", [128, 128])
    c_masks = din("c_masks", [64, 6, 64])
    c_rope = din("c_rope", [64, 2, LE])
    c_rot = din("c_rot", [64, 64])
    out = nc.dram_tensor("out", [NSEQ, LX, D], F32, kind="ExternalOutput").ap()

    skind = "ExternalOutput" if debug else "Internal"

    def dscr(name, shape, dt):
        return nc.dram_tensor(name, list(shape), dt, kind=skind).ap()

    qk_scr = dscr("qk_scr", [16, 128, LE], BF16)
    v_scr = dscr("v_scr", [16, 128, LE], BF16)
    z_scr = dscr("z_scr", [16, 128, LE], BF16)
    y_scr = dscr("y_scr", [16, 128, LE], BF16)
    h1_scr = dscr("h1_scr", [LE, D], F32)

    S = Sched(nc, es)

    def sb(name, shape, dt=F32):
        return es.enter_context(nc.sbuf_tensor(name, list(shape), dt))

    def ps(name, shape, dt=F32):
        return es.enter_context(nc.psum_tensor(name, list(shape), dt))

    block = es.enter_context(nc.Block())

    ident = sb("ident", [128, 128])
    identb = sb("identb", [128, 128], BF16)
    onesb = sb("onesb", [128, 128], BF16)
    ones32 = sb("ones32", [64, 128])
    masks = sb("masks", [64, 6, 64])
    lng = sb("lng", [128, 2, 8])
    convw = sb("convw", [128, 5, 32])
    alog = sb("alog", [64, 16])
    dtb = sb("dtb", [64, 16])
    nea = sb("nea", [64, 16])
    gon = sb("gon", [64, 256])
    epsb = sb("epsb", [128, 1])

    pb = [ps(f"pb{i}", [128, 512]) for i in range(8) if i != 2]
    pb.insert(2, None)
    ptb = ps("ptb", [128, 1024], BF16)

    def setup():
        S.load(lambda e: e.dma_start(out=ident[:], in_=c_ident[:, :]), w=["ident"])
        S.load(lambda e: e.dma_start(out=masks[:], in_=c_masks[:, :, :]), w=["masks"])
        S.load(lambda e: e.dma_start(out=lng[:], in_=ln_g.rearrange("l (c p) -> p l c", p=128), allow_slow_non_contiguous=True), w=["lng"])
        for j in range(5):
            S.load(lambda e, j=j: e.dma_start(out=convw[:, j, :], in_=g_conv[j].rearrange("(c p) -> p c", p=128), allow_slow_non_contiguous=True), w=["convw"])
        S.load(lambda e: e.dma_start(out=alog[:], in_=g_alog.partition_broadcast(64)), w=["alog"])
        S.load(lambda e: e.dma_start(out=dtb[:], in_=g_dtb.partition_broadcast(64)), w=["dtb"])
        S.load(lambda e: e.dma_start(out=gon[:], in_=g_on.partition_broadcast(64)), w=["gon"])
        S.dve(lambda e: e.tensor_copy(out=identb[:], in_=ident[:]), r=["ident"], w=["identb"])
        S.dve(lambda e: e.memset(onesb[:], 1.0), w=["onesb"])
        S.dve(lambda e: e.memset(ones32[:], 1.0), w=["ones32"])
        S.dve(lambda e: e.memset(epsb[:], EPS), w=["epsb"])
        S.act(lambda e: e.activation(out=nea[:], in_=alog[:], func=AF.Exp), r=["alog"], w=["nea"])
        S.dve(lambda e: e.tensor_scalar(out=nea[:], in0=nea[:], scalar1=-1.0, scalar2=None, op0=ALU.mult), r=["nea"], w=["nea"])

    beta = sb("beta", [64, NGR, 2, 8])
    gg = sb("gg", [64, NGR, 2, 8])
    gc = sb("gc", [64, NGR, 2, 8])
    negc = sb("negc", [64, NGR, 2, 8])
    egc = sb("egc", [64, NGR, 2, 8])
    kds = sb("kds", [64, NGR, 2, 8])
    egl = sb("egl", [128, NGR, 2, 8])
    negm4 = sb("negm4", [64, 4, 2, 64])
    strict4 = sb("strict4", [64, 4, 2, 64])
    ARENA_BYTES = 171500
    arena = sb("arena", [128, ARENA_BYTES // 4])
    aoff = {"p": 0}

    def av(shape, dt=F32):
        n = 1
        for d_ in shape[1:]:
            n *= d_
        nb = (n * (2 if dt == BF16 else 4) + 3) // 4 * 4
        o = aoff["p"]
        aoff["p"] = o + nb
        aoff["max"] = max(aoff.get("max", 0), o + nb)
        assert aoff["p"] <= ARENA_BYTES, (aoff["p"], shape)
        v = arena[0:shape[0], o // 4:(o + nb) // 4]
        if dt == BF16:
            v = v.bitcast(BF16)
            if n % 2:
                v = v[:, 0:n]
        if len(shape) > 2:
            names = "abcd"[:len(shape) - 1]
            pat = "p (" + " ".join(names) + ") -> p " + " ".join(names)
            v = v.rearrange(pat, **{names[i]: shape[1 + i] for i in range(len(names))})
        return v

    def sbA(name, shape, dt=F32):
        return av(shape, dt)

    hnT = sbA("hnT", [128, 8, LE], BF16)
    xt = [sbA(f"xt{i}", [128, D]) for i in range(2)]
    xn = [sbA(f"xn{i}", [128, D], BF16) for i in range(2)]
    junk = sbA("junk", [128, D], BF16)
    ssb = [sbA(f"ss{i}", [128, 4]) for i in range(2)]

    aoff_after_p0 = aoff["p"]

    def norm_transpose(tt, src_ap, src_key, n, t0, layer):
        b = tt % 2
        ss = ssb[b]
        S.act(lambda e: e.activation(out=junk[0:n, :], in_=src_ap, func=AF.Square, accum_out=ss[0:n, 0:1]),
              r=[src_key], w=["junk", ("ss", b)])
        S.act(lambda e: e.activation(out=ss[0:n, 1:2], in_=ss[0:n, 0:1], func=AF.Sqrt, bias=epsb[0:n, 0:1], scale=1.0 / D),
              r=[("ss", b), "epsb"], w=[("ss", b)])
        S.dve(lambda e: e.reciprocal(out=ss[0:n, 2:3], in_=ss[0:n, 1:2]), r=[("ss", b)], w=[("ss", b)])
        S.act(lambda e: e.activation(out=xn[b][0:n, :], in_=src_ap, func=AF.Copy, scale=ss[0:n, 2:3]),
              r=[src_key, ("ss", b)], w=[("xn", b)])
        S.pe([(lambda e, c=c: e.transpose(out=ptb[:, c * 128:c * 128 + n], in_=xn[b][0:n, c * 128:(c + 1) * 128], identity=identb[0:n, 0:n]))
              for c in range(8)], r=[("xn", b), "identb"], w=["ptb"])
        pv = ptb[:, :].rearrange("p (c t) -> p c t", c=8)[:, :, 0:n]
        S.dve(lambda e: e.tensor_tensor(out=hnT[:, :, t0:t0 + n], in0=pv,
                                        in1=lng[:, layer, :].unsqueeze(2).to_broadcast([128, 8, n]), op=ALU.mult),
              r=["ptb", "lng"], w=gkeys("hnT", t0, n))

    def load_x_tile(s, tt):
        t0, n = TOKTILES[tt]
        b = tt % 2
        if tt == 0:
            S.dve(lambda e: e.memset(xt[b][0:64, :], 0.0), w=[("xt", b)])
            S.load(lambda e: e.dma_start(out=xt[b][PAD:64, :], in_=meta[:, :]), w=[("xt", b)])
        else:
            r0 = t0 - 64
            S.load(lambda e: e.dma_start(out=xt[b][0:n, :], in_=xs[s, r0:r0 + n, :]), w=[("xt", b)])

    def phase_p0(s):
        for tt, (t0, n) in enumerate(TOKTILES):
            load_x_tile(s, tt)
            norm_transpose(tt, xt[tt % 2][0:n, :], ("xt", tt % 2), n, t0, 0)

    wst = [sbA(f"wst{i}", [128, 8, 128]) for i in range(2)]
    wbf = [sbA(f"wbf{i}", [128, 8, 128], BF16) for i in range(2)]
    pre = [sbA("pre0", [128, LE + 4], BF16)] * 2
    acc = sbA("acc", [128, LE])
    sqb = sbA("sqb", [128, LE], BF16)
    obf = [sbA(f"obf{i}", [128, LE], BF16) for i in range(2)]
    rtmp = [sbA(f"rtmp{i}", [128, 512]) for i in range(2)]
    gbraw = sbA("gbraw", [64, NGR, 32])
    wba_st = sbA("wba_st", [128, 8, 32])
    wba = sbA("wba", [128, 8, 32], BF16)

    w_in_v = g_w_in.rearrange("(c p) n -> p c n", p=128)
    mmrot = [0]

    def mmbank():
        mmrot[0] ^= 1
        return mmrot[0]

    def load_w(wv_ap, idx, ncol=128):
        b = idx % 2
        S.load(lambda e: e.dma_start(out=wst[b][:, :, 0:ncol], in_=wv_ap), w=[("wst", b)])
        S.pool(lambda e: e.tensor_copy(out=wbf[b][:, :, 0:ncol], in_=wst[b][:, :, 0:ncol]), r=[("wst", b)], w=[("wbf", b)])
        return wbf[b]

    def phase_g1(s):
        for b in range(1):
            S.dve(lambda e, b=b: e.memset(pre[b][:, 0:2], 0.0), w=[("pre", b)])
            S.dve(lambda e, b=b: e.memset(pre[b][:, LE + 2:LE + 4], 0.0), w=[("pre", b)])
        flist = range(48)
        if debug == "g1a":
            flist = [0, 16, 32]
        if debug == "g1b":
            flist = []
        for f in flist:
            wt = load_w(w_in_v[:, :, f * 128:(f + 1) * 128], f)
            wk = ("wbf", f % 2)
            pb_ = 0
            for (c0, w) in COLBLKS:
                bk = mmbank()
                S.pe([(lambda e, c=c: e.matmul(pb[bk][:, 0:w], lhsT=wt[:, c, :], rhs=hnT[:, c, c0:c0 + w], start=(c == 0), stop=(c == 7)))
                      for c in range(8)], r=[wk] + gkeys("hnT", c0, w), w=[("pb", bk)])
                if f < 32:
                    S.act(lambda e: e.activation(out=pre[pb_][:, 2 + c0:2 + c0 + w], in_=pb[bk][:, 0:w], func=AF.Copy),
                          r=[("pb", bk)], w=[("pre", pb_)])
                else:
                    ob = obf[f % 2]
                    S.act(lambda e: e.activation(out=ob[:, c0:c0 + w], in_=pb[bk][:, 0:w], func=AF.Silu),
                          r=[("pb", bk)], w=[("obf", f % 2)])
            if f >= 32:
                S.store(lambda e: e.dma_start(out=z_scr[f - 32, :, :], in_=obf[f % 2][:, :]), r=[("obf", f % 2)], w=[("z_scr", f - 32)])
                continue
            pr = pre[pb_]
            S.dve(lambda e: e.tensor_scalar(out=acc[:], in0=pr[:, 0:LE], scalar1=convw[:, 0, f:f + 1], scalar2=None, op0=ALU.mult),
                  r=[("pre", pb_), "convw"], w=["acc"])
            for j in range(1, 5):
                S.dve(lambda e, j=j: e.scalar_tensor_tensor(out=acc[:], in0=pr[:, j:j + LE], scalar=convw[:, j, f:f + 1], in1=acc[:],
                                                            op0=ALU.mult, op1=ALU.add), r=[("pre", pb_), "convw", "acc"], w=["acc"])
            ob = obf[f % 2]
            ok = ("obf", f % 2)
            if f >= 16:
                S.act(lambda e: e.activation(out=ob[:], in_=acc[:], func=AF.Silu), r=["acc"], w=[ok])
                S.dve(lambda e: e.memset(ob[:, 0:PAD], 0.0), w=[ok])
                S.store(lambda e: e.dma_start(out=v_scr[f - 16, :, :], in_=ob[:, :]), r=[ok], w=[("v_scr", f - 16)])
            else:
                S.act(lambda e: e.activation(out=acc[:], in_=acc[:], func=AF.Silu), r=["acc"], w=["acc"])
                S.dve(lambda e: e.tensor_tensor(out=sqb[:], in0=acc[:], in1=acc[:], op=ALU.mult), r=["acc"], w=["sqb"])
                qscale = (128.0 ** -0.5) if f < 8 else 1.0
                for (c0, w) in COLBLKS:
                    bk = mmbank()
                    rt = rtmp[bk]
                    S.pe(lambda e: e.matmul(pb[bk][:, 0:w], lhsT=onesb[:, :], rhs=sqb[:, c0:c0 + w], start=True, stop=True),
                         r=["sqb", "onesb"], w=[("pb", bk)])
                    S.act(lambda e: e.activation(out=rt[:, 0:w], in_=pb[bk][:, 0:w], func=AF.Sqrt, bias=epsb[:, 0:1], scale=1.0),
                          r=[("pb", bk), "epsb"], w=[("rtmp", bk)])
                    S.dve(lambda e: e.reciprocal(out=rt[:, 0:w], in_=rt[:, 0:w]), r=[("rtmp", bk)], w=[("rtmp", bk)])
                    S.dve(lambda e: e.scalar_tensor_tensor(out=ob[:, c0:c0 + w], in0=acc[:, c0:c0 + w], scalar=qscale, in1=rt[:, 0:w],
                                                           op0=ALU.mult, op1=ALU.mult), r=["acc", ("rtmp", bk)], w=[ok])
                S.dve(lambda e: e.memset(ob[:, 0:PAD], 0.0), w=[ok])
                S.store(lambda e: e.dma_start(out=qk_scr[f, :, :], in_=ob[:, :]), r=[ok], w=[("qk_scr", f)])
        if debug == "g1a":
            return
        S.load(lambda e: e.dma_start(out=wba_st[:], in_=w_in_v[:, :, 6144:6176]), w=["wba_st"])
        S.pool(lambda e: e.tensor_copy(out=wba[:], in_=wba_st[:]), r=["wba_st"], w=["wba"])
        for g0 in range(0, NGR, 16):
            ng = min(16, NGR - g0)
            bk = mmbank()
            pv = pb[bk][0:64, :].rearrange("p (g n) -> p g n", n=32)
            for gi in range(ng):
                gr = g0 + gi
                S.pe([(lambda e, c=c: e.matmul(pv[:, gi, :], lhsT=hnT[:, c, gr * 64:(gr + 1) * 64], rhs=wba[:, c, :], start=(c == 0), stop=(c == 7)))
                      for c in range(8)], r=["wba", ("hnT", gr)], w=[("pb", bk)])
            S.act(lambda e: e.activation(out=gbraw[:, g0:g0 + ng, :], in_=pv[:, 0:ng, :], func=AF.Copy), r=[("pb", bk)], w=["gbraw"])
        import os
        stopat = int(os.environ.get("STOPAT", "99"))
        if stopat <= 1:
            return
        S.act(lambda e: e.activation(out=beta[:].rearrange("p g a b -> p g (a b)"), in_=gbraw[:, :, 0:16], func=AF.Sigmoid), r=["gbraw"], w=["beta"])
        ggf = gg[:].rearrange("p g a b -> p g (a b)")
        S.dve(lambda e: e.tensor_tensor(out=ggf, in0=gbraw[:, :, 16:32], in1=dtb[:].unsqueeze(1).to_broadcast([64, NGR, 16]), op=ALU.add),
              r=["gbraw", "dtb"], w=["gg"])
        S.act(lambda e: e.activation(out=ggf, in_=ggf, func=AF.Exp), r=["gg"], w=["gg"])
        S.act(lambda e: e.activation(out=ggf, in_=ggf, func=AF.Ln, bias=1.0, scale=1.0), r=["gg"], w=["gg"])
        S.dve(lambda e: e.tensor_tensor(out=ggf, in0=ggf, in1=nea[:].unsqueeze(1).to_broadcast([64, NGR, 16]), op=ALU.mult),
              r=["gg", "nea"], w=["gg"])
        if stopat <= 2:
            return
        S.dve(lambda e: e.memset(gg[0:PAD, 0, :, :], 0.0), w=["gg"])
        S.dve(lambda e: e.memset(beta[0:PAD, 0, :, :], 0.0), w=["beta"])
        if stopat <= 3:
            return
        for g0 in range(0, NGR, 32):
            ng = min(32, NGR - g0)
            bk = mmbank()
            pv = pb[bk][0:64, :].rearrange("p (g a b) -> p g a b", a=2, b=8)
            fns = []
            for gi in range(ng):
                gr = g0 + gi
                fns.append(lambda e, gi=gi, gr=gr: e.matmul(pv[:, gi, 0, :], lhsT=masks[:, 0, :], rhs=gg[:, gr, 0, :], start=True, stop=True))
                fns.append(lambda e, gi=gi, gr=gr: e.matmul(pv[:, gi, 1, :], lhsT=masks[:, 1, :], rhs=gg[:, gr, 1, :], start=True, stop=True))
            S.pe(fns, r=["gg", "masks"], w=[("pb", bk)])
            S.act(lambda e: e.activation(out=gc[:, g0:g0 + ng], in_=pv[:, 0:ng], func=AF.Copy), r=[("pb", bk)], w=["gc"])
            if stopat <= 4:
                continue
            bk2 = mmbank()
            pv2 = pb[bk2][:, :].rearrange("p (g a b) -> p g a b", a=2, b=8)
            fns = [(lambda e, gi=gi: e.matmul(pv2[:, gi].rearrange("p a b -> p (a b)"), lhsT=ones32[:, :],
                                               rhs=gg[:, g0 + gi].rearrange("p a b -> p (a b)"), start=True, stop=True)) for gi in range(ng)]
            S.pe(fns, r=["gg", "ones32"], w=[("pb", bk2)])
            if stopat <= 5:
                continue
            S.act(lambda e: e.activation(out=egl[:, g0:g0 + ng], in_=pv2[:, 0:ng], func=AF.Exp), r=[("pb", bk2)], w=["egl"])
            if stopat <= 6:
                continue
            S.act(lambda e: e.activation(out=kds[:, g0:g0 + ng], in_=pv2[0:64, 0:ng], func=AF.Copy), r=[("pb", bk2)], w=["kds"])
            S.dve(lambda e: e.tensor_tensor(out=kds[:, g0:g0 + ng].rearrange("p g a b -> p (g a b)"), in0=kds[:, g0:g0 + ng].rearrange("p g a b -> p (g a b)"),
                                            in1=gc[:, g0:g0 + ng].rearrange("p g a b -> p (g a b)"), op=ALU.subtract),
                  r=["gc", "kds"], w=["kds"])
        if stopat <= 7:
            return
        S.act(lambda e: e.activation(out=kds[:], in_=kds[:], func=AF.Exp), r=["kds"], w=["kds"])
        S.act(lambda e: e.activation(out=egc[:], in_=gc[:], func=AF.Exp), r=["gc"], w=["egc"])
        S.dve(lambda e: e.tensor_scalar(out=negc[:], in0=egc[:], scalar1=-1.0, scalar2=None, op0=ALU.mult), r=["egc"], w=["negc"])


    aoff_after_g1 = aoff["p"]
    aoff["p"] = 0
    qT = av([128, LE], BF16)
    kT = av([128, LE], BF16)
    vz = av([128, LE], BF16)
    vtok = av([64, NGR, 256], BF16)
    ob = av([64, NGR, 256], BF16)
    Rr = av([128, NGR, 2, 64], BF16)
    At = av([128, NGR, 2, 64], BF16)
    PSET = []
    for si_ in range(2):
        PSET.append(dict(rhsD=av([64, 4, 2, 64]), DTi=av([64, 4, 2, 64]), DTs=av([64, 4, 2, 64]), X4=av([64, 8, 64], BF16), XT=av([64, 8, 64], BF16),
                         Pa=[av([64, 8, 64], BF16) for _ in range(2)], PaT=[av([64, 8, 64], BF16) for _ in range(2)], R32=av([64, 8, 64]), Rb=av([64, 8, 64], BF16)))
    PSET[0].update(banks=(pb[0], pb[1], pb[3]), bkeys=(("pb", 0), ("pb", 1), ("pb", 3)), ptx=ptb[:, 512:1024], ptxk="ptx0")
    PSET[1].update(banks=(pb[4], pb[5], pb[6]), bkeys=(("pb", 4), ("pb", 5), ("pb", 6)), ptx=ptb[:, 0:512], ptxk="ptx1")
    S32 = [av([128, 256]) for _ in range(2)]
    Sbf = [av([128, 256], BF16) for _ in range(2)]
    xb = [av([128, 256], BF16) for _ in range(2)]
    vn = [av([128, 256], BF16) for _ in range(2)]
    kd = [av([128, 128], BF16) for _ in range(2)]
    t1 = [av([64, 256]) for _ in range(2)]
    ssq = av([64, NGR + 3])
    rstd = av([64, NGR + 3])
    junk2 = av([64, 256], BF16)
    yb = [av([128, 256], BF16) for _ in range(2)]

    def phase_g2(s, heads=range(8)):
        S.dve(lambda e: e.tensor_copy(out=negm4[:], in_=masks[:, 4:6, :].unsqueeze(1).to_broadcast([64, 4, 2, 64])), r=["masks"], w=["negm4"])
        S.dve(lambda e: e.tensor_copy(out=strict4[:], in_=masks[:, 2:4, :].unsqueeze(1).to_broadcast([64, 4, 2, 64])), r=["masks"], w=["strict4"])
        S.dve(lambda e: e.memset(Rr[64:128].rearrange("p a b c -> p (a b c)"), 0.0), w=["Rr"])
        S.dve(lambda e: e.memset(At[64:128].rearrange("p a b c -> p (a b c)"), 0.0), w=["At"])
        for d_ in range(2):
            S.dve(lambda e, d_=d_: e.memset(xb[d_][64:128, :], 0.0), w=[("xb", d_)])
            S.dve(lambda e, d_=d_: e.memset(vn[d_][64:128, :], 0.0), w=[("vn", d_)])
            S.dve(lambda e, d_=d_: e.memset(kd[d_][64:128, :], 0.0), w=[("kd", d_)])
        for h in heads:
            S.load(lambda e: e.dma_start(out=qT[:, :], in_=qk_scr[h, :, :]), r=[("qk_scr", h)], w=["qT"])
            S.load(lambda e: e.dma_start(out=kT[:, :], in_=qk_scr[8 + h, :, :]), r=[("qk_scr", 8 + h)], w=["kT"])
            for half in range(2):
                S.load(lambda e: e.dma_start(out=vz[:, :], in_=v_scr[2 * h + half, :, :]), r=[("v_scr", 2 * h + half)], w=["vz"])
                for g0 in range(0, NGR, 8):
                    ng = min(8, NGR - g0)
                    S.pe([(lambda e, gi=gi: e.transpose(out=ptb[0:64, gi * 128:(gi + 1) * 128], in_=vz[:, (g0 + gi) * 64:(g0 + gi + 1) * 64], identity=identb[:, :]))
                          for gi in range(ng)], r=["vz", "identb"], w=["ptb"])
                    pv = ptb[0:64, :].rearrange("p (g n) -> p g n", n=128)
                    S.act(lambda e: e.activation(out=vtok[:, g0:g0 + ng, half * 128:(half + 1) * 128], in_=pv[:, 0:ng, :], func=AF.Copy),
                          r=["ptb"], w=["vtok"])
            S.marks.append(("g2_h%d_load" % h, dict(S.cnt)))
            def prep_gen(c0, si):
                B = PSET[si]
                rhsD_, DTi_, DTs_, X4_, XT_, Pa_, PaT_, R32_, Rb_ = B["rhsD"], B["DTi"], B["DTs"], B["X4"], B["XT"], B["Pa"], B["PaT"], B["R32"], B["Rb"]
                dif_ = rhsD_
                pA, pB_, pC = B["banks"]
                kA, kB, kC = B["bkeys"]
                px = B["ptx"]
                kx = B["ptxk"]
                sfx = "_%d" % si
                nck = min(4, NGR - c0)
                nu = nck * 2
                W = nck * 128
                for d in range(2):
                    S.dve(lambda e, d=d: e.tensor_tensor(out=rhsD_[:, 0:nck, d, :], in0=gg[:, c0:c0 + nck, d, h:h + 1].to_broadcast([64, nck, 64]),
                                                         in1=masks[:, d:d + 1, :].to_broadcast([64, nck, 64]), op=ALU.mult),
                          r=["gg", "masks"], w=["rhsD" + sfx])
                yield
                S.pe([lambda e: e.matmul(pB_[0:64, 0:W], lhsT=ones32[:, 0:64], rhs=rhsD_[:, 0:nck].rearrange("p a b c -> p (a b c)"), start=True, stop=False),
                      lambda e: e.matmul(pB_[0:64, 0:W], lhsT=ident[0:64, 0:64], rhs=negm4[:, 0:nck].rearrange("p a b c -> p (a b c)"), start=False, stop=True)],
                     r=["rhsD" + sfx, "ones32", "ident", "negm4"], w=[kB])
                pK = pA[0:64, :].rearrange("p (t g n) -> p t g n", t=2, g=4)
                fns = []
                for ci in range(nck):
                    cs = slice((c0 + ci) * 64, (c0 + ci + 1) * 64)
                    fns.append(lambda e, ci=ci, cs=cs: e.matmul(pK[:, 0, ci, :], lhsT=kT[:, cs], rhs=kT[:, cs], start=True, stop=True))
                    fns.append(lambda e, ci=ci, cs=cs: e.matmul(pK[:, 1, ci, :], lhsT=kT[:, cs], rhs=qT[:, cs], start=True, stop=True))
                S.pe(fns, r=["kT", "qT"], w=[kA])
                yield
                S.act(lambda e: e.activation(out=dif_[:, 0:nck].rearrange("p a b c -> p (a b c)"), in_=pB_[0:64, 0:W], func=AF.Copy), r=[kB], w=["rhsD" + sfx])
                yield
                S.dve(lambda e: e.tensor_tensor(out=dif_[:, 0:nck].rearrange("p a b c -> p (a b) c"), in0=dif_[:, 0:nck].rearrange("p a b c -> p (a b) c"),
                                                in1=gc[:, c0:c0 + nck].rearrange("p g a b -> p (g a) b")[:, :, h:h + 1].to_broadcast([64, nu, 64]), op=ALU.subtract),
                      r=["rhsD" + sfx, "gc"], w=["rhsD" + sfx])
                yield
                S.act(lambda e: e.activation(out=DTi_[:, 0:nck].rearrange("p a b c -> p (a b c)"), in_=dif_[:, 0:nck].rearrange("p a b c -> p (a b c)"), func=AF.Exp),
                      r=["rhsD" + sfx], w=["DTi" + sfx])
                yield
                S.dve(lambda e: e.tensor_tensor(out=DTs_[:, 0:nck].rearrange("p a b c -> p (a b c)"), in0=DTi_[:, 0:nck].rearrange("p a b c -> p (a b c)"),
                                                in1=strict4[:, 0:nck].rearrange("p a b c -> p (a b c)"), op=ALU.mult), r=["DTi" + sfx, "strict4"], w=["DTs" + sfx])
                S.dve(lambda e: e.tensor_tensor(out=DTs_[:, 0:nck].rearrange("p a b c -> p (a b) c"), in0=DTs_[:, 0:nck].rearrange("p a b c -> p (a b) c"),
                                                in1=beta[:, c0:c0 + nck].rearrange("p g a b -> p (g a) b")[:, :, h:h + 1].to_broadcast([64, nu, 64]), op=ALU.mult),
                      r=["DTs" + sfx, "beta"], w=["DTs" + sfx])
                X44 = X4_[:, :, :].rearrange("p (g d) n -> p g d n", d=2)
                for d in range(2):
                    S.dve(lambda e, d=d: e.tensor_tensor(out=X44[:, 0:nck, d, :], in0=pK[:, 0, 0:nck, :], in1=DTs_[:, 0:nck, d, :], op=ALU.mult),
                          r=[kA, "DTs" + sfx], w=["X4" + sfx])
                yield
                S.pe([(lambda e, u=u: e.transpose(out=px[0:64, u * 64:(u + 1) * 64], in_=X4_[:, u, :], identity=identb[0:64, 0:64])) for u in range(nu)],
                     r=["X4" + sfx, "identb"], w=[kx, "ptb"])
                for d in range(2):
                    S.dve(lambda e, d=d: e.tensor_tensor(out=At[0:64, c0:c0 + nck, d, :], in0=pK[:, 1, 0:nck, :], in1=DTi_[:, 0:nck, d, :], op=ALU.mult),
                          r=[kA, "DTi" + sfx], w=["At"])
                S.dve(lambda e: e.tensor_tensor(out=R32_[:, 0:nu, :], in0=ident[0:64, 0:64].unsqueeze(1).to_broadcast([64, nu, 64]), in1=X4_[:, 0:nu, :], op=ALU.subtract),
                      r=["ident", "X4" + sfx], w=["R32" + sfx])
                S.pool(lambda e: e.tensor_copy(out=Rb_[:, 0:nu, :], in_=R32_[:, 0:nu, :]), r=["R32" + sfx], w=["Rb" + sfx])
                yield
                S.act(lambda e: e.activation(out=XT_[:, 0:nu, :].rearrange("p a b -> p (a b)"), in_=px[0:64, 0:nu * 64], func=AF.Copy), r=[kx], w=["XT" + sfx])
                yield
                P, PT, Pk, PTk = X4_, XT_, "X4" + sfx, "XT" + sfx
                for lvl in range(5):
                    nb = lvl % 2
                    last = (lvl == 4)
                    if not last:
                        S.pe([(lambda e, u=u, P=P, PT=PT: e.matmul(pA[0:64, u * 64:(u + 1) * 64], lhsT=PT[:, u, :], rhs=P[:, u, :], start=True, stop=True)) for u in range(nu)],
                             r=[Pk, PTk], w=[kA])
                    S.pe([(lambda e, u=u, P=P, PT=PT: e.matmul(pB_[0:64, u * 64:(u + 1) * 64], lhsT=P[:, u, :], rhs=PT[:, u, :], start=True, stop=True)) for u in range(nu)],
                         r=[Pk, PTk], w=[kB])
                    yield
                    if not last:
                        S.act(lambda e, nb=nb: e.activation(out=Pa_[nb][:, 0:nu, :].rearrange("p a b -> p (a b)"), in_=pA[0:64, 0:nu * 64], func=AF.Copy),
                              r=[kA], w=[("Pa" + sfx, nb)])
                    S.dve(lambda e, nb=nb: e.tensor_copy(out=PaT_[nb][:, 0:nu, :].rearrange("p a b -> p (a b)"), in_=pB_[0:64, 0:nu * 64]),
                          r=[kB], w=[("PaT" + sfx, nb)])
                    yield
                    S.pe([(lambda e, u=u, nb=nb: e.matmul(pC[0:64, u * 64:(u + 1) * 64], lhsT=PaT_[nb][:, u, :], rhs=Rb_[:, u, :], start=True, stop=True)) for u in range(nu)],
                         r=[("PaT" + sfx, nb), "Rb" + sfx], w=[kC])
                    yield
                    if not last:
                        S.dve(lambda e: e.tensor_tensor(out=R32_[:, 0:nu, :].rearrange("p a b -> p (a b)"), in0=pC[0:64, 0:nu * 64],
                                                        in1=R32_[:, 0:nu, :].rearrange("p a b -> p (a b)"), op=ALU.add), r=[kC, "R32" + sfx], w=["R32" + sfx])
                        S.pool(lambda e: e.tensor_copy(out=Rb_[:, 0:nu, :], in_=R32_[:, 0:nu, :]), r=["R32" + sfx], w=["Rb" + sfx])
                    else:
                        S.dve(lambda e: e.tensor_tensor(out=Rr[0:64, c0:c0 + nck].rearrange("p a b c -> p (a b c)"), in0=pC[0:64, 0:nu * 64],
                                                        in1=R32_[:, 0:nu, :].rearrange("p a b -> p (a b)"), op=ALU.add), r=[kC, "R32" + sfx], w=["Rr"])
                    yield
                    P, PT, Pk, PTk = Pa_[nb], PaT_[nb], ("Pa" + sfx, nb), ("PaT" + sfx, nb)

            glist = list(range(0, NGR, 4))
            for gi0 in range(0, len(glist), 2):
                active = [prep_gen(glist[gi0], 0)]
                if gi0 + 1 < len(glist):
                    active.append(prep_gen(glist[gi0 + 1], 1))
                while active:
                    for g_ in list(active):
                        try:
                            next(g_)
                        except StopIteration:
                            active.remove(g_)
            S.barrier()
            S.marks.append(("g2_h%d_prep" % h, dict(S.cnt)))
            for d in range(2):
                S.dve(lambda e, d=d: e.memset(S32[d][:, :], 0.0), w=[("S32", d)])
                S.dve(lambda e, d=d: e.memset(Sbf[d][:, :], 0.0), w=[("Sbf", d)])
            for t in range(NGR):
                for d in range(2):
                    c = t if d == 0 else NGR - 1 - t
                    cs = slice(c * 64, (c + 1) * 64)
                    pS, pO = pb[4 + 2 * d], pb[5 + 2 * d]
                    kS_, kO_ = ("pb", 4 + 2 * d), ("pb", 5 + 2 * d)
                    S.pe(lambda e: e.matmul(pS[0:64, 0:256], lhsT=kT[:, cs], rhs=Sbf[d][:, :], start=True, stop=True), r=["kT", ("Sbf", d)], w=[(kS_, 0)])
                    S.pe(lambda e: e.matmul(pO[0:64, 0:256], lhsT=qT[:, cs], rhs=Sbf[d][:, :], start=True, stop=True), r=["qT", ("Sbf", d)], w=[(kO_, 0)])
                    S.pe(lambda e: e.transpose(out=ptb[0:64, d * 128:(d + 1) * 128], in_=kT[:, cs], identity=identb[:, :]), r=["kT", "identb"], w=[("ptk", d)])
                    S.dve(lambda e: e.scalar_tensor_tensor(out=xb[d][0:64, :], in0=pS[0:64, 0:256], scalar=negc[:, c, d, h:h + 1], in1=vtok[:, c, :],
                                                           op0=ALU.mult, op1=ALU.add), r=[(kS_, 0), "negc", "vtok"], w=[("xb", d)])
                    S.act(lambda e: e.activation(out=kd[d][0:64, :], in_=ptb[0:64, d * 128:(d + 1) * 128], func=AF.Copy, scale=kds[:, c, d, h:h + 1]),
                          r=[("ptk", d), "kds"], w=[("kd", d)])
                    S.act(lambda e: e.activation(out=t1[d][:, :], in_=pO[0:64, 0:256], func=AF.Copy, scale=egc[:, c, d, h:h + 1]),
                          r=[(kO_, 0), "egc"], w=[("t1", d)])
                    S.pe(lambda e: e.matmul(pS[0:64, 256:512], lhsT=Rr[:, c, d, :], rhs=xb[d][:, :], start=True, stop=True), r=["Rr", ("xb", d)], w=[(kS_, 1)])
                    S.act(lambda e: e.activation(out=vn[d][0:64, :], in_=pS[0:64, 256:512], func=AF.Copy, scale=beta[:, c, d, h:h + 1]),
                          r=[(kS_, 1), "beta"], w=[("vn", d)])
                    S.pe(lambda e: e.matmul(pO[0:64, 256:512], lhsT=At[:, c, d, :], rhs=vn[d][:, :], start=True, stop=True), r=["At", ("vn", d)], w=[(kO_, 1)])
                    S.pe(lambda e: e.matmul(pS[:, 0:256], lhsT=kd[d][:, :], rhs=vn[d][:, :], start=True, stop=True), r=[("kd", d), ("vn", d)], w=[(kS_, 0)])
                    if t < 32 or (t == 32 and d == 0):
                        S.dve(lambda e: e.tensor_tensor(out=ob[:, c, :], in0=pO[0:64, 256:512], in1=t1[d][:, :], op=ALU.add), r=[(kO_, 1), ("t1", d)], w=[("ob", c)])
                    else:
                        S.dve(lambda e: e.tensor_tensor(out=t1[d][:, :], in0=pO[0:64, 256:512], in1=t1[d][:, :], op=ALU.add), r=[(kO_, 1), ("t1", d)], w=[("t1", d)])
                        S.dve(lambda e: e.tensor_tensor(out=ob[:, c, :], in0=ob[:, c, :], in1=t1[d][:, :], op=ALU.add), r=[("ob", c), ("t1", d)], w=[("ob", c)])
                    S.dve(lambda e: e.scalar_tensor_tensor(out=Sbf[d][:, :], in0=S32[d][:, :], scalar=egl[:, c, d, h:h + 1], in1=pS[:, 0:256],
                                                           op0=ALU.mult, op1=ALU.add), r=[("S32", d), "egl", (kS_, 0)], w=[("Sbf", d)])
                    S.dve(lambda e: e.scalar_tensor_tensor(out=S32[d][:, :], in0=S32[d][:, :], scalar=egl[:, c, d, h:h + 1], in1=pS[:, 0:256],
                                                           op0=ALU.mult, op1=ALU.add), r=[("S32", d), "egl", (kS_, 0)], w=[("S32", d)])
            S.marks.append(("g2_h%d_scan" % h, dict(S.cnt)))
            S.marks.append(("g2_h%d_scan" % h, dict(S.cnt)))
            for c in range(NGR):
                S.act(lambda e, c=c: e.activation(out=junk2[:, :], in_=ob[:, c, :], func=AF.Square, accum_out=ssq[:, c:c + 1]), r=[("ob", c)], w=["junk2", "ssq"])
            S.act(lambda e: e.activation(out=rstd[:, 0:NGR], in_=ssq[:, 0:NGR], func=AF.Sqrt, bias=epsb[0:64, 0:1], scale=1.0 / 256), r=["ssq", "epsb"], w=["rstd"])
            S.dve(lambda e: e.reciprocal(out=rstd[:, 0:NGR], in_=rstd[:, 0:NGR]), r=["rstd"], w=["rstd"])
            for c in range(NGR):
                S.dve(lambda e, c=c: e.scalar_tensor_tensor(out=ob[:, c, :], in0=ob[:, c, :], scalar=rstd[:, c:c + 1], in1=gon[:, :], op0=ALU.mult, op1=ALU.mult),
                      r=[("ob", c), "rstd", "gon"], w=[("ob", c)])
            for half in range(2):
                S.load(lambda e: e.dma_start(out=vz[:, :], in_=z_scr[2 * h + half, :, :]), r=[("z_scr", 2 * h + half)], w=["vz"])
                for gi_, c0 in enumerate(range(0, NGR, 4)):
                    nck = min(4, NGR - c0)
                    ybb = yb[gi_ % 2]
                    yk = ("yb", gi_ % 2)
                    S.pe([(lambda e, ci=ci: e.transpose(out=ptb[:, 256 + ci * 64:256 + (ci + 1) * 64], in_=ob[:, c0 + ci, half * 128:(half + 1) * 128], identity=identb[0:64, 0:64]))
                          for ci in range(nck)], r=[("ob", c0 + ci) for ci in range(nck)] + ["identb"], w=["ptb2"])
                    S.dve(lambda e: e.tensor_tensor(out=ybb[:, 0:nck * 64], in0=ptb[:, 256:256 + nck * 64], in1=vz[:, c0 * 64:(c0 + nck) * 64], op=ALU.mult),
                          r=["ptb2", "vz"], w=[yk])
                    S.store(lambda e: e.dma_start(out=y_scr[2 * h + half, :, c0 * 64:(c0 + nck) * 64], in_=ybb[:, 0:nck * 64]), r=[yk], w=[("y_scr", 2 * h + half)])
            S.barrier()

    aoff_g2 = aoff["p"]
    aoff["p"] = aoff_after_p0
    wo = av([128, 16, D], BF16)
    wo_st = [av([128, D]) for _ in range(2)]
    ytile = [av([128, 16, 128], BF16) for _ in range(2)]
    h1t = [av([128, D]) for _ in range(2)]

    def load_wout(w_ap):
        wv = w_ap.rearrange("(c p) n -> p c n", p=128)
        for c in range(16):
            b = c % 2
            S.load(lambda e, c=c, b=b: e.dma_start(out=wo_st[b][:, :], in_=wv[:, c, :]), w=[("wo_st", b)])
            S.pool(lambda e, c=c, b=b: e.tensor_copy(out=wo[:, c, :], in_=wo_st[b][:, :]), r=[("wo_st", b)], w=["wo"])

    def phase_outproj(s, layer):
        load_wout(g_w_out if layer == 0 else m_w_out)
        yv = y_scr.rearrange("c p t -> p c t")
        for tt, (t0, n) in enumerate(TOKTILES):
            if layer == 1 and tt == 0:
                continue
            b = tt % 2
            S.load(lambda e: e.dma_start(out=ytile[b][:, :, 0:n], in_=yv[:, :, t0:t0 + n]), r=[("y_scr", c) for c in range(16)], w=[("ytile", b)])
            if layer == 0:
                load_x_tile(s, tt)
            else:
                S.load(lambda e: e.dma_start(out=xt[b][0:n, :], in_=h1_scr[t0:t0 + n, :]), r=["h1_scr"], w=[("xt", b)])
            for hf in range(2):
                bk = mmbank()
                S.pe([(lambda e, c=c: e.matmul(pb[bk][0:n, 0:512], lhsT=ytile[b][:, c, 0:n], rhs=wo[:, c, hf * 512:(hf + 1) * 512], start=(c == 0), stop=(c == 15)))
                      for c in range(16)], r=[("ytile", b), "wo"], w=[("pb", bk)])
                S.dve(lambda e: e.tensor_tensor(out=h1t[b][0:n, hf * 512:(hf + 1) * 512], in0=pb[bk][0:n, 0:512], in1=xt[b][0:n, hf * 512:(hf + 1) * 512], op=ALU.add),
                      r=[("pb", bk), ("xt", b)], w=[("h1t", b)])
            if layer == 0:
                S.store(lambda e: e.dma_start(out=h1_scr[t0:t0 + n, :], in_=h1t[b][0:n, :]), r=[("h1t", b)], w=["h1_scr"])
                norm_transpose(tt, h1t[b][0:n, :], ("h1t", b), n, t0, 1)
            else:
                r0 = t0 - 64
                S.store(lambda e: e.dma_start(out=out[s, r0:r0 + n, :], in_=h1t[b][0:n, :]), r=[("h1t", b)], w=["out"])
        S.barrier()


    aoff["p"] = aoff_after_p0
    wM = av([128, 8, 832], BF16)
    wMst = [av([128, 8, 128]) for _ in range(2)]
    wMz = [av([128, 8, 128], BF16) for _ in range(2)]
    raw = av([128, 7, 512])
    sqt = av([128, 7, 512], BF16)
    rs1 = [av([128, 512]) for _ in range(2)]
    o1 = [av([128, 7, 512], BF16) for _ in range(2)]
    kpg = av([64, 512], BF16)
    tmpa = av([64, 512])
    tmpb = av([64, 512])
    zo = [av([128, 512], BF16) for _ in range(2)]
    ropeb1 = av([64, 2, 512])
    gq = sb("gq", [128, 4])
    gkv = sb("gkv", [128, 2])
    gqa = sb("gqa", [128, 1])
    gqb = sb("gqb", [64, 1])
    gka = sb("gka", [128, 1])
    gkb = sb("gkb", [64, 1])
    rotb = sb("rotb", [64, 64], BF16)
    rot_st = sb("rot_st", [64, 64])
    nshift = sb("nshift", [128, 1])
    m_w_in_v = m_w_in.rearrange("(c p) n -> p c n", p=128)

    def setup_mla():
        S.load(lambda e: e.dma_start(out=gq[:], in_=m_qn.rearrange("(c p) -> p c", p=128), allow_slow_non_contiguous=True), w=["gq"])
        S.load(lambda e: e.dma_start(out=gkv[:], in_=m_kvn.rearrange("(c p) -> p c", p=128), allow_slow_non_contiguous=True), w=["gkv"])
        S.load(lambda e: e.dma_start(out=gqa[:], in_=m_qg[0:128].rearrange("(p c) -> p c", c=1)), w=["gqa"])
        S.load(lambda e: e.dma_start(out=gqb[:], in_=m_qg[128:192].rearrange("(p c) -> p c", c=1)), w=["gqb"])
        S.load(lambda e: e.dma_start(out=gka[:], in_=m_kg[0:128].rearrange("(p c) -> p c", c=1)), w=["gka"])
        S.load(lambda e: e.dma_start(out=gkb[:], in_=m_kg[128:192].rearrange("(p c) -> p c", c=1)), w=["gkb"])
        S.load(lambda e: e.dma_start(out=rot_st[:], in_=c_rot[:, :]), w=["rot_st"])
        S.dve(lambda e: e.tensor_copy(out=rotb[:], in_=rot_st[:]), r=["rot_st"], w=["rotb"])
        S.dve(lambda e: e.memset(nshift[:], -8.0), w=["nshift"])

    def rstd_from_psum(pbank, npart, w, div, dst):
        S.act(lambda e: e.activation(out=dst[0:npart, 0:w], in_=pbank[0:npart, 0:w], func=AF.Sqrt, bias=epsb[0:npart, 0:1], scale=1.0 / div),
              r=[("pb", 3), "epsb"], w=[("rs", id(dst))])
        S.dve(lambda e: e.reciprocal(out=dst[0:npart, 0:w], in_=dst[0:npart, 0:w]), r=[("rs", id(dst))], w=[("rs", id(dst))])

    def phase_m1(s):
        for f in range(7):
            b = f % 2
            nc_ = 128 if f < 6 else 64
            S.load(lambda e, f=f, b=b, nc_=nc_: e.dma_start(out=wMst[b][:, :, 0:nc_], in_=m_w_in_v[:, :, f * 128:f * 128 + nc_]), w=[("wMst", b)])
            S.pool(lambda e, f=f, b=b, nc_=nc_: e.tensor_copy(out=wM[:, :, f * 128:f * 128 + nc_], in_=wMst[b][:, :, 0:nc_]), r=[("wMst", b)], w=["wM"])
        for bi, (c0, w) in enumerate(COLBLKS):
            ob_ = o1[bi % 2]
            ok = ("o1", bi % 2)
            for f in range(7):
                nr = 128 if f < 6 else 64
                bk = mmbank()
                S.pe([(lambda e, c=c: e.matmul(pb[bk][0:nr, 0:w], lhsT=wM[:, c, f * 128:f * 128 + nr], rhs=hnT[:, c, c0:c0 + w], start=(c == 0), stop=(c == 7)))
                      for c in range(8)], r=["wM"] + gkeys("hnT", c0, w), w=[("pb", bk)])
                S.act(lambda e: e.activation(out=raw[0:nr, f, 0:w], in_=pb[bk][0:nr, 0:w], func=AF.Copy), r=[("pb", bk)], w=[("raw", f)])
                S.dve(lambda e: e.tensor_tensor(out=sqt[0:nr, f, 0:w], in0=raw[0:nr, f, 0:w], in1=raw[0:nr, f, 0:w], op=ALU.mult), r=[("raw", f)], w=[("sqt", f)])
            for (fl, div, gt, ri) in (([0, 1, 2, 3], 512.0, gq, 0), ([4, 5], 256.0, gkv, 1)):
                S.pe([(lambda e, i=i, f=f: e.matmul(pb[3][:, 0:w], lhsT=onesb[:, :], rhs=sqt[:, f, 0:w], start=(i == 0), stop=(i == len(fl) - 1)))
                      for i, f in enumerate(fl)], r=[("sqt", f) for f in fl] + ["onesb"], w=[("pb", 3)])
                rstd_from_psum(pb[3], 128, w, div, rs1[ri])
                for i, f in enumerate(fl):
                    S.dve(lambda e, i=i, f=f: e.scalar_tensor_tensor(out=ob_[:, f, 0:w], in0=raw[:, f, 0:w], scalar=gt[:, i:i + 1], in1=rs1[ri][:, 0:w],
                                                                     op0=ALU.mult, op1=ALU.mult), r=[("raw", f), ("rs", id(rs1[ri]))], w=[ok])
            S.dve(lambda e: e.tensor_scalar(out=kpg[:, 0:w], in0=raw[0:64, 6, 0:w], scalar1=gkb[:, 0:1], scalar2=None, op0=ALU.mult), r=[("raw", 6), "gkb"], w=["kpg"])
            S.pe(lambda e: e.matmul(pb[3][0:64, 0:w], lhsT=rotb[:, :], rhs=kpg[:, 0:w], start=True, stop=True), r=["kpg", "rotb"], w=[("pb", 3)])
            S.load(lambda e: e.dma_start(out=ropeb1[:, :, 0:w], in_=c_rope[:, :, c0:c0 + w]), w=["ropeb1"])
            S.dve(lambda e: e.tensor_tensor(out=tmpa[:, 0:w], in0=pb[3][0:64, 0:w], in1=ropeb1[:, 1, 0:w], op=ALU.mult), r=[("pb", 3), "ropeb1"], w=["tmpa"])
            S.dve(lambda e: e.tensor_tensor(out=tmpb[:, 0:w], in0=kpg[:, 0:w], in1=ropeb1[:, 0, 0:w], op=ALU.mult), r=["kpg", "ropeb1"], w=["tmpb"])
            S.dve(lambda e: e.tensor_tensor(out=ob_[0:64, 6, 0:w], in0=tmpa[:, 0:w], in1=tmpb[:, 0:w], op=ALU.add), r=["tmpa", "tmpb"], w=[ok])
            for f in range(6):
                S.store(lambda e, f=f: e.dma_start(out=qk_scr[f, :, c0:c0 + w], in_=ob_[:, f, 0:w]), r=[ok], w=[("qk_scr", f)])
            S.store(lambda e: e.dma_start(out=qk_scr[6, 0:64, c0:c0 + w], in_=ob_[0:64, 6, 0:w]), r=[ok], w=[("qk_scr", 6)])
            S.store(lambda e: e.dma_start(out=qk_scr[7, 0:64, c0:c0 + w], in_=sqt[0:64, 6, 0:w]), r=[("sqt", 6)], w=[("qk_scr", 7)])
        for hh in range(16):
            b = hh % 2
            S.load(lambda e, hh=hh, b=b: e.dma_start(out=wMst[b][:, :, :], in_=m_w_in_v[:, :, 832 + hh * 128:832 + (hh + 1) * 128]), w=[("wMst", b)])
            S.pool(lambda e, b=b: e.tensor_copy(out=wMz[b][:, :, :], in_=wMst[b][:, :, :]), r=[("wMst", b)], w=[("wMz", b)])
            for bi, (c0, w) in enumerate(COLBLKS):
                bk = mmbank()
                zb = zo[bi % 2]
                S.pe([(lambda e, c=c: e.matmul(pb[bk][:, 0:w], lhsT=wMz[b][:, c, :], rhs=hnT[:, c, c0:c0 + w], start=(c == 0), stop=(c == 7)))
                      for c in range(8)], r=[("wMz", b)] + gkeys("hnT", c0, w), w=[("pb", bk)])
                S.act(lambda e: e.activation(out=zb[:, 0:w], in_=pb[bk][:, 0:w], func=AF.Silu), r=[("pb", bk)], w=[("zo", bi % 2)])
                S.store(lambda e: e.dma_start(out=z_scr[hh, :, c0:c0 + w], in_=zb[:, 0:w]), r=[("zo", bi % 2)], w=[("z_scr", hh)])
        S.barrier()

    aoff["p"] = 0
    cqT = av([128, 4, LE], BF16)
    ckvT = av([128, 2, LE], BF16)
    krT = av([64, LE], BF16)
    sqk = av([128, LE], BF16)
    qTa = av([128, LE], BF16)
    qTb = av([128, LE], BF16)
    kTa = av([128, LE], BF16)
    kTb = av([128, LE], BF16)
    zq = [av([128, 512], BF16) for _ in range(2)]
    vaug = av([128, 33, 130], BF16)
    wq_st = av([128, 4, 192])
    wq = av([128, 4, 192], BF16)
    wkv_st = av([128, 2, 256])
    wkv = av([128, 2, 256], BF16)
    MSET = []
    for si_ in range(2):
        MSET.append(dict(ra=av([128, 512]), rb=av([64, 512]), sqa=av([128, 512], BF16), sqb2=av([128, 512], BF16), rsq=av([128, 512]),
                         qbg=av([64, 512], BF16), t2a=av([64, 512], BF16), t2b=av([64, 512], BF16), rope=av([64, 2, 512])))
        MSET[-1]["rsk"] = MSET[-1]["rsq"]
    MSET[0].update(banks=(pb[0], pb[3]), bkeys=(("pb", 0), ("pb", 3)))
    MSET[1].update(banks=(pb[1], pb[7]), bkeys=(("pb", 1), ("pb", 7)))
    pT = [av([128, 512], BF16) for _ in range(3)]
    rdn = av([1, 512])
    dacc = [[av([128, 512]) for _ in range(2)] for _ in range(2)]
    ones128 = av([128, 1])
    rbc = av([128, 512])
    ytb = [av([128, 512], BF16) for _ in range(2)]
    KT = [(64 + 128 * i, 128) for i in range(32)] + [(PAD, 16)]
    SCALE = 192.0 ** -0.5
    wuq_v = m_wuq.rearrange("(c p) n -> p c n", p=128)
    wukv_v = m_wukv.rearrange("(c p) n -> p c n", p=128)

    def phase_m2(s, heads=range(16)):
        for f in range(4):
            S.load(lambda e, f=f: e.dma_start(out=cqT[:, f, :], in_=qk_scr[f, :, :]), r=[("qk_scr", f)], w=["cqT"])
        for f in range(2):
            S.load(lambda e, f=f: e.dma_start(out=ckvT[:, f, :], in_=qk_scr[4 + f, :, :]), r=[("qk_scr", 4 + f)], w=["ckvT"])
        S.load(lambda e: e.dma_start(out=krT[:, :], in_=qk_scr[6, 0:64, :]), r=[("qk_scr", 6)], w=["krT"])
        S.load(lambda e: e.dma_start(out=sqk[0:64, :], in_=qk_scr[7, 0:64, :]), r=[("qk_scr", 7)], w=["sqk"])
        for t_, k_ in ((sqk, "sqk"), (qTb, "qTb"), (kTb, "kTb"), (MSET[0]["sqb2"], "sqb2_m0"), (MSET[1]["sqb2"], "sqb2_m1")):
            S.dve(lambda e, t_=t_: e.memset(t_[64:128, :], 0.0), w=[k_])
        S.dve(lambda e: e.memset(ones128[:, :], 1.0), w=["ones128"])
        for h in heads:
            S.marks.append(("m2_h%d_start" % h, dict(S.cnt)))
            S.load(lambda e: e.dma_start(out=wq_st[:], in_=wuq_v[:, :, h * 192:(h + 1) * 192]), w=["wq_st"])
            S.pool(lambda e: e.tensor_copy(out=wq[:], in_=wq_st[:]), r=["wq_st"], w=["wq"])
            S.load(lambda e: e.dma_start(out=wkv_st[:], in_=wukv_v[:, :, h * 256:(h + 1) * 256]), w=["wkv_st"])
            S.pool(lambda e: e.tensor_copy(out=wkv[:], in_=wkv_st[:]), r=["wkv_st"], w=["wkv"])
            def prol_gen(c0, w, si):
                Bf = MSET[si]
                ra_, rb_, sqa_, sqb2_, rsq_, rsk_, qbg_, t2a_, t2b_, rope_ = (Bf[k_] for k_ in ("ra", "rb", "sqa", "sqb2", "rsq", "rsk", "qbg", "t2a", "t2b", "rope"))
                pP, pQ = Bf["banks"]
                kP, kQ = Bf["bkeys"]
                x_ = "_m%d" % si
                cs = slice(c0, c0 + w)
                S.load(lambda e: e.dma_start(out=rope_[:, :, 0:w], in_=c_rope[:, :, cs]), w=["rope" + x_])
                S.pe([(lambda e, c=c: e.matmul(pP[:, 0:w], lhsT=wq[:, c, 0:128], rhs=cqT[:, c, cs], start=(c == 0), stop=(c == 3))) for c in range(4)],
                     r=["wq", "cqT"], w=[kP])
                yield
                S.act(lambda e: e.activation(out=ra_[:, 0:w], in_=pP[:, 0:w], func=AF.Copy), r=[kP], w=["ra" + x_])
                yield
                S.pe([(lambda e, c=c: e.matmul(pP[0:64, 0:w], lhsT=wq[:, c, 128:192], rhs=cqT[:, c, cs], start=(c == 0), stop=(c == 3))) for c in range(4)],
                     r=["wq", "cqT"], w=[kP])
                S.dve(lambda e: e.tensor_tensor(out=sqa_[:, 0:w], in0=ra_[:, 0:w], in1=ra_[:, 0:w], op=ALU.mult), r=["ra" + x_], w=["sqa" + x_])
                yield
                S.act(lambda e: e.activation(out=rb_[:, 0:w], in_=pP[0:64, 0:w], func=AF.Copy), r=[kP], w=["rb" + x_])
                yield
                S.dve(lambda e: e.tensor_tensor(out=sqb2_[0:64, 0:w], in0=rb_[:, 0:w], in1=rb_[:, 0:w], op=ALU.mult), r=["rb" + x_], w=["sqb2" + x_])
                yield
                S.pe([lambda e: e.matmul(pQ[:, 0:w], lhsT=onesb[:, :], rhs=sqa_[:, 0:w], start=True, stop=False),
                      lambda e: e.matmul(pQ[:, 0:w], lhsT=onesb[:, :], rhs=sqb2_[:, 0:w], start=False, stop=True)], r=["sqa" + x_, "sqb2" + x_, "onesb"], w=[kQ])
                S.pe([(lambda e, c=c: e.matmul(pP[:, 0:w], lhsT=wkv[:, c, 0:128], rhs=ckvT[:, c, cs], start=(c == 0), stop=(c == 1))) for c in range(2)],
                     r=["wkv", "ckvT"], w=[kP])
                yield
                S.act(lambda e: e.activation(out=rsq_[:, 0:w], in_=pQ[:, 0:w], func=AF.Sqrt, bias=epsb[:, 0:1], scale=1.0 / 192.0), r=[kQ, "epsb"], w=["rsq" + x_])
                yield
                S.dve(lambda e: e.reciprocal(out=rsq_[:, 0:w], in_=rsq_[:, 0:w]), r=["rsq" + x_], w=["rsq" + x_])
                yield
                S.dve(lambda e: e.scalar_tensor_tensor(out=qbg_[:, 0:w], in0=rb_[:, 0:w], scalar=gqb[:, 0:1], in1=rsq_[0:64, 0:w], op0=ALU.mult, op1=ALU.mult),
                      r=["rb" + x_, "rsq" + x_, "gqb"], w=["qbg" + x_])
                S.dve(lambda e: e.scalar_tensor_tensor(out=qTa[:, cs], in0=ra_[:, 0:w], scalar=gqa[:, 0:1], in1=rsq_[:, 0:w], op0=ALU.mult, op1=ALU.mult),
                      r=["ra" + x_, "rsq" + x_, "gqa"], w=["qTa"])
                yield
                S.pe(lambda e: e.matmul(pQ[0:64, 0:w], lhsT=rotb[:, :], rhs=qbg_[:, 0:w], start=True, stop=True), r=["qbg" + x_, "rotb"], w=[kQ])
                S.act(lambda e: e.activation(out=ra_[:, 0:w], in_=pP[:, 0:w], func=AF.Copy), r=[kP], w=["ra" + x_])
                S.dve(lambda e: e.tensor_tensor(out=t2b_[:, 0:w], in0=qbg_[:, 0:w], in1=rope_[:, 0, 0:w], op=ALU.mult), r=["qbg" + x_, "rope" + x_], w=["t2b" + x_])
                yield
                S.dve(lambda e: e.tensor_tensor(out=t2a_[:, 0:w], in0=pQ[0:64, 0:w], in1=rope_[:, 1, 0:w], op=ALU.mult), r=[kQ, "rope" + x_], w=["t2a" + x_])
                S.dve(lambda e: e.tensor_tensor(out=sqa_[:, 0:w], in0=ra_[:, 0:w], in1=ra_[:, 0:w], op=ALU.mult), r=["ra" + x_], w=["sqa" + x_])
                yield
                S.dve(lambda e: e.tensor_tensor(out=qTb[0:64, cs], in0=t2a_[:, 0:w], in1=t2b_[:, 0:w], op=ALU.add), r=["t2a" + x_, "t2b" + x_], w=["qTb"])
                S.pe([lambda e: e.matmul(pQ[:, 0:w], lhsT=onesb[:, :], rhs=sqa_[:, 0:w], start=True, stop=False),
                      lambda e: e.matmul(pQ[:, 0:w], lhsT=onesb[:, :], rhs=sqk[:, cs], start=False, stop=True)], r=["sqa" + x_, "sqk", "onesb"], w=[kQ])
                yield
                S.act(lambda e: e.activation(out=rsk_[:, 0:w], in_=pQ[:, 0:w], func=AF.Sqrt, bias=epsb[:, 0:1], scale=1.0 / 192.0), r=[kQ, "epsb"], w=["rsq" + x_])
                yield
                S.dve(lambda e: e.reciprocal(out=rsk_[:, 0:w], in_=rsk_[:, 0:w]), r=["rsq" + x_], w=["rsq" + x_])
                yield
                S.dve(lambda e: e.scalar_tensor_tensor(out=kTa[:, cs], in0=ra_[:, 0:w], scalar=gka[:, 0:1], in1=rsk_[:, 0:w], op0=ALU.mult, op1=ALU.mult),
                      r=["ra" + x_, "rsq" + x_, "gka"], w=["kTa"])
                S.dve(lambda e: e.tensor_tensor(out=kTb[0:64, cs], in0=krT[:, cs], in1=rsk_[0:64, 0:w], op=ALU.mult), r=["krT", "rsq" + x_], w=["kTb"])
                yield

            for bi0 in range(0, len(COLBLKS), 2):
                active = [prol_gen(COLBLKS[bi0][0], COLBLKS[bi0][1], 0)]
                if bi0 + 1 < len(COLBLKS):
                    active.append(prol_gen(COLBLKS[bi0 + 1][0], COLBLKS[bi0 + 1][1], 1))
                while active:
                    for g_ in list(active):
                        try:
                            next(g_)
                        except StopIteration:
                            active.remove(g_)
            S.marks.append(("m2_h%d_qk" % h, dict(S.cnt)))
            for kt, (k0, nk) in enumerate(KT):
                bk = mmbank()
                S.pe([(lambda e, c=c: e.matmul(pb[bk][0:nk, 0:128], lhsT=ckvT[:, c, k0:k0 + nk], rhs=wkv[:, c, 128:256], start=(c == 0), stop=(c == 1))) for c in range(2)],
                     r=["wkv", "ckvT"], w=[("pb", bk)])
                S.act(lambda e: e.activation(out=vaug[0:nk, kt, 0:128], in_=pb[bk][0:nk, 0:128], func=AF.Copy), r=[("pb", bk)], w=["vaug"])
            S.marks.append(("m2_h%d_v" % h, dict(S.cnt)))
            steps = [(qi, kt) for qi in range(8) for kt in range(33)]
            SB = [0, 1, 3]

            def emit_scores(st):
                qi, kt = steps[st]
                k0, nk = KT[kt]
                q0 = 64 + 512 * qi
                bk = SB[st % 3]
                S.pe([lambda e: e.matmul(pb[bk][0:nk, 0:512], lhsT=kTa[:, k0:k0 + nk], rhs=qTa[:, q0:q0 + 512], start=True, stop=False),
                      lambda e: e.matmul(pb[bk][0:nk, 0:512], lhsT=kTb[:, k0:k0 + nk], rhs=qTb[:, q0:q0 + 512], start=False, stop=True)],
                     r=["kTa", "kTb", "qTa", "qTb"], w=[("pb", bk)])

            emit_scores(0)
            emit_scores(1)
            for st, (qi, kt) in enumerate(steps):
                k0, nk = KT[kt]
                q0 = 64 + 512 * qi
                bk = SB[st % 3]
                ab = qi % 2
                acc_o = pb[4 + 2 * ab]
                acc_d = pb[5]
                if st + 2 < len(steps):
                    emit_scores(st + 2)
                S.act(lambda e: e.activation(out=pT[st % 3][0:nk, :], in_=pb[bk][0:nk, 0:512], func=AF.Exp, bias=nshift[0:nk, 0:1], scale=SCALE),
                      r=[("pb", bk), "nshift"], w=[("pT", st % 3)])
                S.pe(lambda e: e.matmul(acc_o[:, 0:512], lhsT=vaug[0:nk, kt, 0:128], rhs=pT[st % 3][0:nk, :], start=(kt == 0), stop=(kt == 32)),
                     r=[("pT", st % 3), "vaug"], w=[("acc", ab)])
                par = kt % 2
                if kt < 2:
                    S.dve(lambda e: e.tensor_copy(out=dacc[ab][par][:, :], in_=pT[st % 3][:, :]), r=[("pT", st % 3)], w=[("dacc", ab, par)])
                else:
                    S.dve(lambda e: e.tensor_tensor(out=dacc[ab][par][0:nk, :], in0=dacc[ab][par][0:nk, :], in1=pT[st % 3][0:nk, :], op=ALU.add),
                          r=[("pT", st % 3), ("dacc", ab, par)], w=[("dacc", ab, par)])
                if kt == 0:
                    S.load(lambda e: e.dma_start(out=zq[ab][:, :], in_=z_scr[h, :, q0:q0 + 512]), r=[("z_scr", h)], w=[("zq", ab)])
                if kt == 32:
                    yb_ = ytb[qi % 2]
                    S.dve(lambda e: e.tensor_tensor(out=dacc[ab][0][:, :], in0=dacc[ab][0][:, :], in1=dacc[ab][1][:, :], op=ALU.add),
                          r=[("dacc", ab, 0), ("dacc", ab, 1)], w=[("dacc", ab, 0)])
                    S.pe(lambda e: e.matmul(acc_d[0:1, 0:512], lhsT=ones128[:, 0:1], rhs=dacc[ab][0][:, :], start=True, stop=True),
                         r=[("dacc", ab, 0), "ones128"], w=["accd"])
                    S.dve(lambda e: e.reciprocal(out=rdn[0:1, :], in_=acc_d[0:1, 0:512]), r=["accd"], w=["rdn", "rbc"])
                    S.pe(lambda e: e.matmul(pb[7][:, 0:512], lhsT=ones32[0:1, :], rhs=rdn[0:1, :], start=True, stop=True), r=["rdn", "ones32"], w=[("pb", 7)])
                    S.act(lambda e: e.activation(out=rbc[:, :], in_=pb[7][:, 0:512], func=AF.Copy), r=[("pb", 7)], w=["rbc", "rdn"])
                    S.dve(lambda e: e.tensor_tensor(out=rbc[:, :], in0=acc_o[:, 0:512], in1=rbc[:, :], op=ALU.mult), r=[("acc", ab), "rbc"], w=["rbc"])
                    S.dve(lambda e: e.tensor_tensor(out=yb_[:, :], in0=rbc[:, :], in1=zq[ab][:, :], op=ALU.mult), r=["rbc", ("zq", ab)], w=[("ytb", qi % 2)])
                    S.store(lambda e: e.dma_start(out=y_scr[h, :, q0:q0 + 512], in_=yb_[:, :]), r=[("ytb", qi % 2)], w=[("y_scr", h)])
            S.barrier()

    def dump(name, t, shape, dt, keys):
        d = nc.dram_tensor("dbg_" + name, list(shape), dt, kind="ExternalOutput").ap()
        S.store(lambda e: e.dma_start(out=d, in_=t), r=keys)

    setup()
    setup_mla()
    for s in range(NSEQ):
        S.marks.append(("start%d" % s, dict(S.cnt)))
        phase_p0(s)
        S.marks.append(("p0", dict(S.cnt)))
        phase_g1(s)
        S.marks.append(("g1", dict(S.cnt)))
        S.barrier()
        if debug == "g2":
            phase_g2(s, heads=[0])
            break
        if debug not in ("m2", "m1"):
            phase_g2(s)
        S.marks.append(("g2", dict(S.cnt)))
        phase_outproj(s, 0)
        S.marks.append(("op0", dict(S.cnt)))
        if debug == "l0":
            break
        phase_m1(s)
        S.marks.append(("m1", dict(S.cnt)))
        if debug == "m1":
            break
        if debug == "m2":
            phase_m2(s, heads=[0])
            break
        phase_m2(s)
        S.marks.append(("m2", dict(S.cnt)))
        phase_outproj(s, 1)
        S.marks.append(("op1", dict(S.cnt)))
    print("arena max", aoff.get("max"), "ninst", S.ninst, S.cnt, "sbuf left", nc.sbuf_bytes_remaining)
    nc._marks = S.marks
    S.finish()
    es.close()
    return nc


def _consts():
    ident = np.eye(128, dtype=np.float32)
    i = np.arange(64)
    U = (i[:, None] <= i[None, :]).astype(np.float32)
    Lo = (i[:, None] >= i[None, :]).astype(np.float32)
    Us = (i[:, None] < i[None, :]).astype(np.float32)
    Ls = (i[:, None] > i[None, :]).astype(np.float32)
    NEG = -30000.0
    masks = np.stack([U, Lo, Us, Ls, (1 - U) * NEG, (1 - Lo) * NEG], axis=1).astype(np.float32)
    pos = np.arange(LE, dtype=np.float64) - PAD
    inv = 10000.0 ** (-np.arange(0, 64, 2, dtype=np.float64) / 64)
    ang = pos[None, :] * inv[:, None]
    cos = np.concatenate([np.cos(ang), np.cos(ang)], 0)
    sin = np.concatenate([np.sin(ang), np.sin(ang)], 0)
    rope = np.stack([cos, sin], 1).astype(np.float32)
    rot = np.zeros((64, 64), np.float32)
    for m in range(32):
        rot[m + 32, m] = -1.0
        rot[m, m + 32] = 1.0
    return dict(c_ident=ident, c_masks=masks, c_rope=rope, c_rot=rot)


_NC_CACHE = {}


def _in_maps(inputs):
    allx = np.concatenate([np.asarray(inputs["x_prompt"]), np.asarray(inputs["x_sample"])], 0)
    seqs = [[0, 1], [2, 3], [4, 5], [6, 7], [8, 8], [9, 9], [10, 10], [11, 11]]
    common = dict(
        meta=np.asarray(inputs["meta_tokens"]), ln_g=np.asarray(inputs["ln_g"]),
        g_w_in=np.asarray(inputs["gdn_w_in"])[0], g_conv=np.asarray(inputs["gdn_conv_w"])[0],
        g_alog=np.asarray(inputs["gdn_a_log"])[0].reshape(16), g_dtb=np.asarray(inputs["gdn_dt_bias"])[0].reshape(16),
        g_on=np.asarray(inputs["gdn_o_norm_g"])[0], g_w_out=np.asarray(inputs["gdn_w_out"])[0],
        m_w_in=np.asarray(inputs["mla_w_in"])[0], m_qn=np.asarray(inputs["mla_q_norm_g"])[0],
        m_kvn=np.asarray(inputs["mla_kv_norm_g"])[0], m_wuq=np.asarray(inputs["mla_w_uq"])[0],
        m_wukv=np.asarray(inputs["mla_w_ukv"])[0], m_qg=np.asarray(inputs["mla_qk_q_g"])[0],
        m_kg=np.asarray(inputs["mla_qk_k_g"])[0], m_w_out=np.asarray(inputs["mla_w_out"])[0],
    )
    common = {k: np.ascontiguousarray(v, dtype=np.float32) for k, v in common.items()}
    common.update(_consts())
    maps = []
    for c in range(8):
        m = dict(common)
        m["xs"] = np.ascontiguousarray(allx[seqs[c]])
        maps.append(m)
    return maps, seqs


def kernel(**inputs):
    if "nc" not in _NC_CACHE:
        _NC_CACHE["nc"] = build()
    nc = _NC_CACHE["nc"]
    maps, seqs = _in_maps(inputs)
    res = run_bass_kernel_spmd(nc, maps, core_ids=list(range(8)))
    full = np.zeros((12, LX, D), np.float32)
    for c in range(8):
        o = res.results[c]["out"]
        full[seqs[c][0]] = o[0]
        if seqs[c][1] != seqs[c][0]:
            full[seqs[c][1]] = o[1]
    return full[:4], full[4:]
```

```python
import numpy as np
import ml_dtypes
import concourse.bass as bass
import concourse.mybir as mybir
from concourse.bass_utils import run_bass_kernel_spmd

F32 = mybir.dt.float32
BF16 = mybir.dt.bfloat16
ALU = mybir.AluOpType
AF = mybir.ActivationFunctionType

D = 1024
LX = 4096
NMETA = 16
PAD = 48
LE = 4160
NGR = 65
NSEQ = 2
EPS = 1e-6
COLBLKS = [(0, 64)] + [(64 + 512 * i, 512) for i in range(8)]
TOKTILES = [(0, 64)] + [(64 + 128 * i, 128) for i in range(32)]
GDN_IN = 6176
import os
SCANSTOP = int(os.environ.get('SCANSTOP', '9'))
DMA_K = 6
SAME_ENG_SYNC = True


def gkeys(name, c0, n):
    return [(name, g) for g in range(c0 // 64, (c0 + n + 63) // 64)]


class Sched:
    def __init__(self, nc, es):
        self.nc = nc
        self.eng = {"pe": nc.tensor, "dve": nc.vector, "act": nc.scalar, "pool": nc.gpsimd, "sp": nc.sync}
        self.semh = {}
        for e in self.eng:
            self.semh[(e,)] = es.enter_context(nc.semaphore("s_" + e))
        for q in ("sp", "pool", "act"):
            for s in range(DMA_K):
                self.semh[(q, "d", s)] = es.enter_context(nc.semaphore(f"d_{q}{s}"))
        self.cnt = {e: 0 for e in self.eng}
        self.dman = {q: 0 for q in ("sp", "pool", "act")}
        self.seen = {e: {} for e in self.eng}
        self.lastw = {}
        self.readers = {}
        self.ninst = 0
        self.marks = []

    def _wait(self, e, semk, val):
        if val <= 0 or self.seen[e].get(semk, 0) >= val:
            return
        self.eng[e].wait_ge(self.semh[semk], val)
        self.seen[e][semk] = val

    def op(self, e, fn, reads=(), writes=(), dma=False):
        deps = {}
        for k in reads:
            t = self.lastw.get(k)
            if t is not None:
                deps[t[0]] = max(deps.get(t[0], 0), t[1])
        for k in writes:
            t = self.lastw.get(k)
            if t is not None:
                deps[t[0]] = max(deps.get(t[0], 0), t[1])
            for sk, v in self.readers.get(k, {}).items():
                deps[sk] = max(deps.get(sk, 0), v)
        for sk, v in deps.items():
            if sk == (e,) and (e == "pe" or not SAME_ENG_SYNC) and not dma:
                continue
            self._wait(e, sk, v)
        if dma:
            n = self.dman[e]
            slot = n % DMA_K
            sk = (e, "d", slot)
            self._wait(e, sk, 16 * (n // DMA_K))
            self.dman[e] = n + 1
            inst = fn(self.eng[e])
            inst.then_inc(self.semh[sk], 16)
            tok = (sk, 16 * (n // DMA_K + 1))
        else:
            fns = fn if isinstance(fn, (list, tuple)) else [fn]
            inst = None
            for f in fns:
                inst = f(self.eng[e])
                self.ninst += 1
            self.cnt[e] += 1
            inst.then_inc(self.semh[(e,)], 1)
            tok = ((e,), self.cnt[e])
        for k in reads:
            r = self.readers.setdefault(k, {})
            r[tok[0]] = max(r.get(tok[0], 0), tok[1])
        for k in writes:
            self.lastw[k] = tok
            self.readers[k] = {}
        return tok

    def pe(self, fn, r=(), w=()):
        return self.op("pe", fn, r, w)

    def dve(self, fn, r=(), w=()):
        return self.op("dve", fn, r, w)

    def act(self, fn, r=(), w=()):
        return self.op("act", fn, r, w)

    def pool(self, fn, r=(), w=()):
        return self.op("pool", fn, r, w)

    def load(self, fn, r=(), w=()):
        return self.op("sp", fn, r, w, dma=True)

    def store(self, fn, r=(), w=()):
        return self.op("pool", fn, r, w, dma=True)

    def barrier(self):
        for e in self.eng:
            for e2 in self.eng:
                if e2 != e:
                    self._wait(e, (e2,), self.cnt[e2])
            for q in self.dman:
                n = self.dman[q]
                for s in range(DMA_K):
                    if n > s:
                        last = ((n - 1 - s) // DMA_K) * DMA_K + s
                        self._wait(e, (q, "d", s), 16 * (last // DMA_K + 1))
        self.lastw = {}
        self.readers = {}

    def finish(self):
        self.barrier()


def build(debug=None):
    from contextlib import ExitStack
    nc = bass.Bass("TRN2", target_bir_lowering=False)
    es = ExitStack()

    def din(name, shape, dt=F32):
        return nc.dram_tensor(name, list(shape), dt, kind="ExternalInput").ap()

    xs = din("xs", [NSEQ, LX, D])
    meta = din("meta", [NMETA, D])
    ln_g = din("ln_g", [2, D])
    g_w_in = din("g_w_in", [D, GDN_IN])
    g_conv = din("g_conv", [5, 4096])
    g_alog = din("g_alog", [16])
    g_dtb = din("g_dtb", [16])
    g_on = din("g_on", [256])
    g_w_out = din("g_w_out", [2048, D])
    m_w_in = din("m_w_in", [D, 2880])
    m_qn = din("m_qn", [512])
    m_kvn = din("m_kvn", [256])
    m_wuq = din("m_wuq", [512, 3072])
    m_wukv = din("m_wukv", [256, 4096])
    m_qg = din("m_qg", [192])
    m_kg = din("m_kg", [192])
    m_w_out = din("m_w_out", [2048, D])
    c_ident = din("c_ident", [128, 128])
    c_masks = din("c_masks", [64, 6, 64])
    c_rope = din("c_rope", [64, 2, LE])
    c_rot = din("c_rot", [64, 64])
    out = nc.dram_tensor("out", [NSEQ, LX, D], F32, kind="ExternalOutput").ap()

    skind = "ExternalOutput" if debug else "Internal"

    def dscr(name, shape, dt):
        return nc.dram_tensor(name, list(shape), dt, kind=skind).ap()

    qk_scr = dscr("qk_scr", [16, 128, LE], BF16)
    v_scr = dscr("v_scr", [16, 128, LE], BF16)
    z_scr = dscr("z_scr", [16, 128, LE], BF16)
    y_scr = dscr("y_scr", [16, 128, LE], BF16)
    h1_scr = dscr("h1_scr", [LE, D], F32)

    S = Sched(nc, es)

    def sb(name, shape, dt=F32):
        return es.enter_context(nc.sbuf_tensor(name, list(shape), dt))

    def ps(name, shape, dt=F32):
        return es.enter_context(nc.psum_tensor(name, list(shape), dt))

    block = es.enter_context(nc.Block())

    ident = sb("ident", [128, 128])
    identb = sb("identb", [128, 128], BF16)
    onesb = sb("onesb", [128, 128], BF16)
    ones32 = sb("ones32", [64, 128])
    masks = sb("masks", [64, 6, 64])
    lng = sb("lng", [128, 2, 8])
    convw = sb("convw", [128, 5, 32])
    alog = sb("alog", [64, 16])
    dtb = sb("dtb", [64, 16])
    nea = sb("nea", [64, 16])
    gon = sb("gon", [64, 256])
    epsb = sb("epsb", [128, 1])

    pb = [ps(f"pb{i}", [128, 512]) for i in range(8) if i != 2]
    pb.insert(2, None)
    ptb = ps("ptb", [128, 1024], BF16)

    def setup():
        S.load(lambda e: e.dma_start(out=ident[:], in_=c_ident[:, :]), w=["ident"])
        S.load(lambda e: e.dma_start(out=masks[:], in_=c_masks[:, :, :]), w=["masks"])
        S.load(lambda e: e.dma_start(out=lng[:], in_=ln_g.rearrange("l (c p) -> p l c", p=128), allow_slow_non_contiguous=True), w=["lng"])
        for j in range(5):
            S.load(lambda e, j=j: e.dma_start(out=convw[:, j, :], in_=g_conv[j].rearrange("(c p) -> p c", p=128), allow_slow_non_contiguous=True), w=["convw"])
        S.load(lambda e: e.dma_start(out=alog[:], in_=g_alog.partition_broadcast(64)), w=["alog"])
        S.load(lambda e: e.dma_start(out=dtb[:], in_=g_dtb.partition_broadcast(64)), w=["dtb"])
        S.load(lambda e: e.dma_start(out=gon[:], in_=g_on.partition_broadcast(64)), w=["gon"])
        S.dve(lambda e: e.tensor_copy(out=identb[:], in_=ident[:]), r=["ident"], w=["identb"])
        S.dve(lambda e: e.memset(onesb[:], 1.0), w=["onesb"])
        S.dve(lambda e: e.memset(ones32[:], 1.0), w=["ones32"])
        S.dve(lambda e: e.memset(epsb[:], EPS), w=["epsb"])
        S.act(lambda e: e.activation(out=nea[:], in_=alog[:], func=AF.Exp), r=["alog"], w=["nea"])
        S.dve(lambda e: e.tensor_scalar(out=nea[:], in0=nea[:], scalar1=-1.0, scalar2=None, op0=ALU.mult), r=["nea"], w=["nea"])

    beta = sb("beta", [64, NGR, 2, 8])
    gg = sb("gg", [64, NGR, 2, 8])
    gc = sb("gc", [64, NGR, 2, 8])
    negc = sb("negc", [64, NGR, 2, 8])
    egc = sb("egc", [64, NGR, 2, 8])
    kds = sb("kds", [64, NGR, 2, 8])
    egl = sb("egl", [128, NGR, 2, 8])
    negm4 = sb("negm4", [64, 4, 2, 64])
    strict4 = sb("strict4", [64, 4, 2, 64])
    ARENA_BYTES = 171500
    arena = sb("arena", [128, ARENA_BYTES // 4])
    aoff = {"p": 0}

    def av(shape, dt=F32):
        n = 1
        for d_ in shape[1:]:
            n *= d_
        nb = (n * (2 if dt == BF16 else 4) + 3) // 4 * 4
        o = aoff["p"]
        aoff["p"] = o + nb
        aoff["max"] = max(aoff.get("max", 0), o + nb)
        assert aoff["p"] <= ARENA_BYTES, (aoff["p"], shape)
        v = arena[0:shape[0], o // 4:(o + nb) // 4]
        if dt == BF16:
            v = v.bitcast(BF16)
            if n % 2:
                v = v[:, 0:n]
        if len(shape) > 2:
            names = "abcd"[:len(shape) - 1]
            pat = "p (" + " ".join(names) + ") -> p " + " ".join(names)
            v = v.rearrange(pat, **{names[i]: shape[1 + i] for i in range(len(names))})
        return v

    def sbA(name, shape, dt=F32):
        return av(shape, dt)

    hnT = sbA("hnT", [128, 8, LE], BF16)
    xt = [sbA(f"xt{i}", [128, D]) for i in range(2)]
    xn = [sbA(f"xn{i}", [128, D], BF16) for i in range(2)]
    junk = sbA("junk", [128, D], BF16)
    ssb = [sbA(f"ss{i}", [128, 4]) for i in range(2)]

    aoff_after_p0 = aoff["p"]

    def norm_transpose(tt, src_ap, src_key, n, t0, layer):
        b = tt % 2
        ss = ssb[b]
        S.act(lambda e: e.activation(out=junk[0:n, :], in_=src_ap, func=AF.Square, accum_out=ss[0:n, 0:1]),
              r=[src_key], w=["junk", ("ss", b)])
        S.act(lambda e: e.activation(out=ss[0:n, 1:2], in_=ss[0:n, 0:1], func=AF.Sqrt, bias=epsb[0:n, 0:1], scale=1.0 / D),
              r=[("ss", b), "epsb"], w=[("ss", b)])
        S.dve(lambda e: e.reciprocal(out=ss[0:n, 2:3], in_=ss[0:n, 1:2]), r=[("ss", b)], w=[("ss", b)])
        S.act(lambda e: e.activation(out=xn[b][0:n, :], in_=src_ap, func=AF.Copy, scale=ss[0:n, 2:3]),
              r=[src_key, ("ss", b)], w=[("xn", b)])
        S.pe([(lambda e, c=c: e.transpose(out=ptb[:, c * 128:c * 128 + n], in_=xn[b][0:n, c * 128:(c + 1) * 128], identity=identb[0:n, 0:n]))
              for c in range(8)], r=[("xn", b), "identb"], w=["ptb"])
        pv = ptb[:, :].rearrange("p (c t) -> p c t", c=8)[:, :, 0:n]
        S.dve(lambda e: e.tensor_tensor(out=hnT[:, :, t0:t0 + n], in0=pv,
                                        in1=lng[:, layer, :].unsqueeze(2).to_broadcast([128, 8, n]), op=ALU.mult),
              r=["ptb", "lng"], w=gkeys("hnT", t0, n))

    def load_x_tile(s, tt):
        t0, n = TOKTILES[tt]
        b = tt % 2
        if tt == 0:
            S.dve(lambda e: e.memset(xt[b][0:64, :], 0.0), w=[("xt", b)])
            S.load(lambda e: e.dma_start(out=xt[b][PAD:64, :], in_=meta[:, :]), w=[("xt", b)])
        else:
            r0 = t0 - 64
            S.load(lambda e: e.dma_start(out=xt[b][0:n, :], in_=xs[s, r0:r0 + n, :]), w=[("xt", b)])

    def phase_p0(s):
        for tt, (t0, n) in enumerate(TOKTILES):
            load_x_tile(s, tt)
            norm_transpose(tt, xt[tt % 2][0:n, :], ("xt", tt % 2), n, t0, 0)

    wst = [sbA(f"wst{i}", [128, 8, 128]) for i in range(2)]
    wbf = [sbA(f"wbf{i}", [128, 8, 128], BF16) for i in range(2)]
    pre = [sbA(f"pre{i}", [128, LE + 4], BF16) for i in range(2)]
    acc = sbA("acc", [128, LE])
    sqb = sbA("sqb", [128, LE], BF16)
    obf = [sbA(f"obf{i}", [128, LE], BF16) for i in range(2)]
    rtmp = [sbA(f"rtmp{i}", [128, 512]) for i in range(2)]
    gbraw = sbA("gbraw", [64, NGR, 32])
    wba_st = sbA("wba_st", [128, 8, 32])
    wba = sbA("wba", [128, 8, 32], BF16)

    w_in_v = g_w_in.rearrange("(c p) n -> p c n", p=128)
    mmrot = [0]

    def mmbank():
        mmrot[0] ^= 1
        return mmrot[0]

    def load_w(wv_ap, idx, ncol=128):
        b = idx % 2
        S.load(lambda e: e.dma_start(out=wst[b][:, :, 0:ncol], in_=wv_ap), w=[("wst", b)])
        S.pool(lambda e: e.tensor_copy(out=wbf[b][:, :, 0:ncol], in_=wst[b][:, :, 0:ncol]), r=[("wst", b)], w=[("wbf", b)])
        return wbf[b]

    def phase_g1(s):
        for b in range(2):
            S.dve(lambda e, b=b: e.memset(pre[b][:, 0:2], 0.0), w=[("pre", b)])
            S.dve(lambda e, b=b: e.memset(pre[b][:, LE + 2:LE + 4], 0.0), w=[("pre", b)])
        flist = range(48)
        if debug == "g1a":
            flist = [0, 16, 32]
        if debug == "g1b":
            flist = []
        for f in flist:
            wt = load_w(w_in_v[:, :, f * 128:(f + 1) * 128], f)
            wk = ("wbf", f % 2)
            pb_ = f % 2
            for (c0, w) in COLBLKS:
                bk = mmbank()
                S.pe([(lambda e, c=c: e.matmul(pb[bk][:, 0:w], lhsT=wt[:, c, :], rhs=hnT[:, c, c0:c0 + w], start=(c == 0), stop=(c == 7)))
                      for c in range(8)], r=[wk] + gkeys("hnT", c0, w), w=[("pb", bk)])
                if f < 32:
                    S.act(lambda e: e.activation(out=pre[pb_][:, 2 + c0:2 + c0 + w], in_=pb[bk][:, 0:w], func=AF.Copy),
                          r=[("pb", bk)], w=[("pre", pb_)])
                else:
                    ob = obf[f % 2]
                    S.act(lambda e: e.activation(out=ob[:, c0:c0 + w], in_=pb[bk][:, 0:w], func=AF.Silu),
                          r=[("pb", bk)], w=[("obf", f % 2)])
            if f >= 32:
                S.store(lambda e: e.dma_start(out=z_scr[f - 32, :, :], in_=obf[f % 2][:, :]), r=[("obf", f % 2)], w=[("z_scr", f - 32)])
                continue
            pr = pre[pb_]
            S.dve(lambda e: e.tensor_scalar(out=acc[:], in0=pr[:, 0:LE], scalar1=convw[:, 0, f:f + 1], scalar2=None, op0=ALU.mult),
                  r=[("pre", pb_), "convw"], w=["acc"])
            for j in range(1, 5):
                S.dve(lambda e, j=j: e.scalar_tensor_tensor(out=acc[:], in0=pr[:, j:j + LE], scalar=convw[:, j, f:f + 1], in1=acc[:],
                                                            op0=ALU.mult, op1=ALU.add), r=[("pre", pb_), "convw", "acc"], w=["acc"])
            ob = obf[f % 2]
            ok = ("obf", f % 2)
            if f >= 16:
                S.act(lambda e: e.activation(out=ob[:], in_=acc[:], func=AF.Silu), r=["acc"], w=[ok])
                S.dve(lambda e: e.memset(ob[:, 0:PAD], 0.0), w=[ok])
                S.store(lambda e: e.dma_start(out=v_scr[f - 16, :, :], in_=ob[:, :]), r=[ok], w=[("v_scr", f - 16)])
            else:
                S.act(lambda e: e.activation(out=acc[:], in_=acc[:], func=AF.Silu), r=["acc"], w=["acc"])
                S.dve(lambda e: e.tensor_tensor(out=sqb[:], in0=acc[:], in1=acc[:], op=ALU.mult), r=["acc"], w=["sqb"])
                qscale = (128.0 ** -0.5) if f < 8 else 1.0
                for (c0, w) in COLBLKS:
                    bk = mmbank()
                    rt = rtmp[bk]
                    S.pe(lambda e: e.matmul(pb[bk][:, 0:w], lhsT=onesb[:, :], rhs=sqb[:, c0:c0 + w], start=True, stop=True),
                         r=["sqb", "onesb"], w=[("pb", bk)])
                    S.act(lambda e: e.activation(out=rt[:, 0:w], in_=pb[bk][:, 0:w], func=AF.Sqrt, bias=epsb[:, 0:1], scale=1.0),
                          r=[("pb", bk), "epsb"], w=[("rtmp", bk)])
                    S.dve(lambda e: e.reciprocal(out=rt[:, 0:w], in_=rt[:, 0:w]), r=[("rtmp", bk)], w=[("rtmp", bk)])
                    S.dve(lambda e: e.scalar_tensor_tensor(out=ob[:, c0:c0 + w], in0=acc[:, c0:c0 + w], scalar=qscale, in1=rt[:, 0:w],
                                                           op0=ALU.mult, op1=ALU.mult), r=["acc", ("rtmp", bk)], w=[ok])
                S.dve(lambda e: e.memset(ob[:, 0:PAD], 0.0), w=[ok])
                S.store(lambda e: e.dma_start(out=qk_scr[f, :, :], in_=ob[:, :]), r=[ok], w=[("qk_scr", f)])
        if debug == "g1a":
            return
        S.load(lambda e: e.dma_start(out=wba_st[:], in_=w_in_v[:, :, 6144:6176]), w=["wba_st"])
        S.pool(lambda e: e.tensor_copy(out=wba[:], in_=wba_st[:]), r=["wba_st"], w=["wba"])
        for g0 in range(0, NGR, 16):
            ng = min(16, NGR - g0)
            bk = mmbank()
            pv = pb[bk][0:64, :].rearrange("p (g n) -> p g n", n=32)
            for gi in range(ng):
                gr = g0 + gi
                S.pe([(lambda e, c=c: e.matmul(pv[:, gi, :], lhsT=hnT[:, c, gr * 64:(gr + 1) * 64], rhs=wba[:, c, :], start=(c == 0), stop=(c == 7)))
                      for c in range(8)], r=["wba", ("hnT", gr)], w=[("pb", bk)])
            S.act(lambda e: e.activation(out=gbraw[:, g0:g0 + ng, :], in_=pv[:, 0:ng, :], func=AF.Copy), r=[("pb", bk)], w=["gbraw"])
        import os
        stopat = int(os.environ.get("STOPAT", "99"))
        if stopat <= 1:
            return
        S.act(lambda e: e.activation(out=beta[:].rearrange("p g a b -> p g (a b)"), in_=gbraw[:, :, 0:16], func=AF.Sigmoid), r=["gbraw"], w=["beta"])
        ggf = gg[:].rearrange("p g a b -> p g (a b)")
        S.dve(lambda e: e.tensor_tensor(out=ggf, in0=gbraw[:, :, 16:32], in1=dtb[:].unsqueeze(1).to_broadcast([64, NGR, 16]), op=ALU.add),
              r=["gbraw", "dtb"], w=["gg"])
        S.act(lambda e: e.activation(out=ggf, in_=ggf, func=AF.Exp), r=["gg"], w=["gg"])
        S.act(lambda e: e.activation(out=ggf, in_=ggf, func=AF.Ln, bias=1.0, scale=1.0), r=["gg"], w=["gg"])
        S.dve(lambda e: e.tensor_tensor(out=ggf, in0=ggf, in1=nea[:].unsqueeze(1).to_broadcast([64, NGR, 16]), op=ALU.mult),
              r=["gg", "nea"], w=["gg"])
        if stopat <= 2:
            return
        S.dve(lambda e: e.memset(gg[0:PAD, 0, :, :], 0.0), w=["gg"])
        S.dve(lambda e: e.memset(beta[0:PAD, 0, :, :], 0.0), w=["beta"])
        if stopat <= 3:
            return
        for g0 in range(0, NGR, 32):
            ng = min(32, NGR - g0)
            bk = mmbank()
            pv = pb[bk][0:64, :].rearrange("p (g a b) -> p g a b", a=2, b=8)
            fns = []
            for gi in range(ng):
                gr = g0 + gi
                fns.append(lambda e, gi=gi, gr=gr: e.matmul(pv[:, gi, 0, :], lhsT=masks[:, 0, :], rhs=gg[:, gr, 0, :], start=True, stop=True))
                fns.append(lambda e, gi=gi, gr=gr: e.matmul(pv[:, gi, 1, :], lhsT=masks[:, 1, :], rhs=gg[:, gr, 1, :], start=True, stop=True))
            S.pe(fns, r=["gg", "masks"], w=[("pb", bk)])
            S.act(lambda e: e.activation(out=gc[:, g0:g0 + ng], in_=pv[:, 0:ng], func=AF.Copy), r=[("pb", bk)], w=["gc"])
            if stopat <= 4:
                continue
            bk2 = mmbank()
            pv2 = pb[bk2][:, :].rearrange("p (g a b) -> p g a b", a=2, b=8)
            fns = [(lambda e, gi=gi: e.matmul(pv2[:, gi].rearrange("p a b -> p (a b)"), lhsT=ones32[:, :],
                                               rhs=gg[:, g0 + gi].rearrange("p a b -> p (a b)"), start=True, stop=True)) for gi in range(ng)]
            S.pe(fns, r=["gg", "ones32"], w=[("pb", bk2)])
            if stopat <= 5:
                continue
            S.act(lambda e: e.activation(out=egl[:, g0:g0 + ng], in_=pv2[:, 0:ng], func=AF.Exp), r=[("pb", bk2)], w=["egl"])
            if stopat <= 6:
                continue
            S.act(lambda e: e.activation(out=kds[:, g0:g0 + ng], in_=pv2[0:64, 0:ng], func=AF.Copy), r=[("pb", bk2)], w=["kds"])
            S.dve(lambda e: e.tensor_tensor(out=kds[:, g0:g0 + ng].rearrange("p g a b -> p (g a b)"), in0=kds[:, g0:g0 + ng].rearrange("p g a b -> p (g a b)"),
                                            in1=gc[:, g0:g0 + ng].rearrange("p g a b -> p (g a b)"), op=ALU.subtract),
                  r=["gc", "kds"], w=["kds"])
        if stopat <= 7:
            return
        S.act(lambda e: e.activation(out=kds[:], in_=kds[:], func=AF.Exp), r=["kds"], w=["kds"])
        S.act(lambda e: e.activation(out=egc[:], in_=gc[:], func=AF.Exp), r=["gc"], w=["egc"])
        S.dve(lambda e: e.tensor_scalar(out=negc[:], in0=egc[:], scalar1=-1.0, scalar2=None, op0=ALU.mult), r=["egc"], w=["negc"])


    aoff_after_g1 = aoff["p"]
    aoff["p"] = 0
    qT = av([128, LE], BF16)
    kT = av([128, LE], BF16)
    vz = av([128, LE], BF16)
    vtok = av([64, NGR, 256], BF16)
    ob = av([64, NGR, 256], BF16)
    Rr = av([128, NGR, 2, 64], BF16)
    At = av([128, NGR, 2, 64], BF16)
    PSET = []
    for si_ in range(2):
        PSET.append(dict(rhsD=av([64, 4, 2, 64]), DTi=av([64, 4, 2, 64]), DTs=av([64, 4, 2, 64]), X4=av([64, 8, 64], BF16), XT=av([64, 8, 64], BF16),
                         Pa=[av([64, 8, 64], BF16) for _ in range(2)], PaT=[av([64, 8, 64], BF16) for _ in range(2)], R32=av([64, 8, 64]), Rb=av([64, 8, 64], BF16)))
    PSET[0].update(banks=(pb[0], pb[1], pb[3]), bkeys=(("pb", 0), ("pb", 1), ("pb", 3)), ptx=ptb[:, 512:1024], ptxk="ptx0")
    PSET[1].update(banks=(pb[4], pb[5], pb[6]), bkeys=(("pb", 4), ("pb", 5), ("pb", 6)), ptx=ptb[:, 0:512], ptxk="ptx1")
    S32 = [av([128, 256]) for _ in range(2)]
    Sbf = [av([128, 256], BF16) for _ in range(2)]
    xb = [av([128, 256], BF16) for _ in range(2)]
    vn = [av([128, 256], BF16) for _ in range(2)]
    kd = [av([128, 128], BF16) for _ in range(2)]
    t1 = [av([64, 256]) for _ in range(2)]
    ssq = av([64, NGR + 3])
    rstd = av([64, NGR + 3])
    junk2 = [av([64, 256], BF16) for _ in range(2)]
    yb = [av([128, 256], BF16) for _ in range(2)]

    def phase_g2(s, heads=range(8)):
        S.dve(lambda e: e.tensor_copy(out=negm4[:], in_=masks[:, 4:6, :].unsqueeze(1).to_broadcast([64, 4, 2, 64])), r=["masks"], w=["negm4"])
        S.dve(lambda e: e.tensor_copy(out=strict4[:], in_=masks[:, 2:4, :].unsqueeze(1).to_broadcast([64, 4, 2, 64])), r=["masks"], w=["strict4"])
        S.dve(lambda e: e.memset(Rr[64:128].rearrange("p a b c -> p (a b c)"), 0.0), w=["Rr"])
        S.dve(lambda e: e.memset(At[64:128].rearrange("p a b c -> p (a b c)"), 0.0), w=["At"])
        for d_ in range(2):
            S.dve(lambda e, d_=d_: e.memset(xb[d_][64:128, :], 0.0), w=[("xb", d_)])
            S.dve(lambda e, d_=d_: e.memset(vn[d_][64:128, :], 0.0), w=[("vn", d_)])
            S.dve(lambda e, d_=d_: e.memset(kd[d_][64:128, :], 0.0), w=[("kd", d_)])
        for h in heads:
            S.load(lambda e: e.dma_start(out=qT[:, :], in_=qk_scr[h, :, :]), r=[("qk_scr", h)], w=["qT"])
            S.load(lambda e: e.dma_start(out=kT[:, :], in_=qk_scr[8 + h, :, :]), r=[("qk_scr", 8 + h)], w=["kT"])
            for half in range(2):
                S.load(lambda e: e.dma_start(out=vz[:, :], in_=v_scr[2 * h + half, :, :]), r=[("v_scr", 2 * h + half)], w=["vz"])
                for g0 in range(0, NGR, 8):
                    ng = min(8, NGR - g0)
                    S.pe([(lambda e, gi=gi: e.transpose(out=ptb[0:64, gi * 128:(gi + 1) * 128], in_=vz[:, (g0 + gi) * 64:(g0 + gi + 1) * 64], identity=identb[:, :]))
                          for gi in range(ng)], r=["vz", "identb"], w=["ptb"])
                    pv = ptb[0:64, :].rearrange("p (g n) -> p g n", n=128)
                    S.act(lambda e: e.activation(out=vtok[:, g0:g0 + ng, half * 128:(half + 1) * 128], in_=pv[:, 0:ng, :], func=AF.Copy),
                          r=["ptb"], w=["vtok"])
            S.marks.append(("g2_h%d_load" % h, dict(S.cnt)))
            def prep_gen(c0, si):
                B = PSET[si]
                rhsD_, DTi_, DTs_, X4_, XT_, Pa_, PaT_, R32_, Rb_ = B["rhsD"], B["DTi"], B["DTs"], B["X4"], B["XT"], B["Pa"], B["PaT"], B["R32"], B["Rb"]
                dif_ = rhsD_
                pA, pB_, pC = B["banks"]
                kA, kB, kC = B["bkeys"]
                px = B["ptx"]
                kx = B["ptxk"]
                sfx = "_%d" % si
                nck = min(4, NGR - c0)
                nu = nck * 2
                W = nck * 128
                for d in range(2):
                    S.dve(lambda e, d=d: e.tensor_tensor(out=rhsD_[:, 0:nck, d, :], in0=gg[:, c0:c0 + nck, d, h:h + 1].to_broadcast([64, nck, 64]),
                                                         in1=masks[:, d:d + 1, :].to_broadcast([64, nck, 64]), op=ALU.mult),
                          r=["gg", "masks"], w=["rhsD" + sfx])
                yield
                S.pe([lambda e: e.matmul(pB_[0:64, 0:W], lhsT=ones32[:, 0:64], rhs=rhsD_[:, 0:nck].rearrange("p a b c -> p (a b c)"), start=True, stop=False),
                      lambda e: e.matmul(pB_[0:64, 0:W], lhsT=ident[0:64, 0:64], rhs=negm4[:, 0:nck].rearrange("p a b c -> p (a b c)"), start=False, stop=True)],
                     r=["rhsD" + sfx, "ones32", "ident", "negm4"], w=[kB])
                pK = pA[0:64, :].rearrange("p (t g n) -> p t g n", t=2, g=4)
                fns = []
                for ci in range(nck):
                    cs = slice((c0 + ci) * 64, (c0 + ci + 1) * 64)
                    fns.append(lambda e, ci=ci, cs=cs: e.matmul(pK[:, 0, ci, :], lhsT=kT[:, cs], rhs=kT[:, cs], start=True, stop=True))
                    fns.append(lambda e, ci=ci, cs=cs: e.matmul(pK[:, 1, ci, :], lhsT=kT[:, cs], rhs=qT[:, cs], start=True, stop=True))
                S.pe(fns, r=["kT", "qT"], w=[kA])
                yield
                S.act(lambda e: e.activation(out=dif_[:, 0:nck].rearrange("p a b c -> p (a b c)"), in_=pB_[0:64, 0:W], func=AF.Copy), r=[kB], w=["rhsD" + sfx])
                yield
                S.dve(lambda e: e.tensor_tensor(out=dif_[:, 0:nck].rearrange("p a b c -> p (a b) c"), in0=dif_[:, 0:nck].rearrange("p a b c -> p (a b) c"),
                                                in1=gc[:, c0:c0 + nck].rearrange("p g a b -> p (g a) b")[:, :, h:h + 1].to_broadcast([64, nu, 64]), op=ALU.subtract),
                      r=["rhsD" + sfx, "gc"], w=["rhsD" + sfx])
                yield
                S.act(lambda e: e.activation(out=DTi_[:, 0:nck].rearrange("p a b c -> p (a b c)"), in_=dif_[:, 0:nck].rearrange("p a b c -> p (a b c)"), func=AF.Exp),
                      r=["rhsD" + sfx], w=["DTi" + sfx])
                yield
                S.dve(lambda e: e.tensor_tensor(out=DTs_[:, 0:nck].rearrange("p a b c -> p (a b c)"), in0=DTi_[:, 0:nck].rearrange("p a b c -> p (a b c)"),
                                                in1=strict4[:, 0:nck].rearrange("p a b c -> p (a b c)"), op=ALU.mult), r=["DTi" + sfx, "strict4"], w=["DTs" + sfx])
                S.dve(lambda e: e.tensor_tensor(out=DTs_[:, 0:nck].rearrange("p a b c -> p (a b) c"), in0=DTs_[:, 0:nck].rearrange("p a b c -> p (a b) c"),
                                                in1=beta[:, c0:c0 + nck].rearrange("p g a b -> p (g a) b")[:, :, h:h + 1].to_broadcast([64, nu, 64]), op=ALU.mult),
                      r=["DTs" + sfx, "beta"], w=["DTs" + sfx])
                X44 = X4_[:, :, :].rearrange("p (g d) n -> p g d n", d=2)
                for d in range(2):
                    S.dve(lambda e, d=d: e.tensor_tensor(out=X44[:, 0:nck, d, :], in0=pK[:, 0, 0:nck, :], in1=DTs_[:, 0:nck, d, :], op=ALU.mult),
                          r=[kA, "DTs" + sfx], w=["X4" + sfx])
                yield
                S.pe([(lambda e, u=u: e.transpose(out=px[0:64, u * 64:(u + 1) * 64], in_=X4_[:, u, :], identity=identb[0:64, 0:64])) for u in range(nu)],
                     r=["X4" + sfx, "identb"], w=[kx, "ptb"])
                for d in range(2):
                    S.dve(lambda e, d=d: e.tensor_tensor(out=At[0:64, c0:c0 + nck, d, :], in0=pK[:, 1, 0:nck, :], in1=DTi_[:, 0:nck, d, :], op=ALU.mult),
                          r=[kA, "DTi" + sfx], w=["At"])
                S.dve(lambda e: e.tensor_tensor(out=R32_[:, 0:nu, :], in0=ident[0:64, 0:64].unsqueeze(1).to_broadcast([64, nu, 64]), in1=X4_[:, 0:nu, :], op=ALU.subtract),
                      r=["ident", "X4" + sfx], w=["R32" + sfx])
                S.pool(lambda e: e.tensor_copy(out=Rb_[:, 0:nu, :], in_=R32_[:, 0:nu, :]), r=["R32" + sfx], w=["Rb" + sfx])
                yield
                S.act(lambda e: e.activation(out=XT_[:, 0:nu, :].rearrange("p a b -> p (a b)"), in_=px[0:64, 0:nu * 64], func=AF.Copy), r=[kx], w=["XT" + sfx])
                yield
                P, PT, Pk, PTk = X4_, XT_, "X4" + sfx, "XT" + sfx
                for lvl in range(5):
                    nb = lvl % 2
                    last = (lvl == 4)
                    if not last:
                        S.pe([(lambda e, u=u, P=P, PT=PT: e.matmul(pA[0:64, u * 64:(u + 1) * 64], lhsT=PT[:, u, :], rhs=P[:, u, :], start=True, stop=True)) for u in range(nu)],
                             r=[Pk, PTk], w=[kA])
                    S.pe([(lambda e, u=u, P=P, PT=PT: e.matmul(pB_[0:64, u * 64:(u + 1) * 64], lhsT=P[:, u, :], rhs=PT[:, u, :], start=True, stop=True)) for u in range(nu)],
                         r=[Pk, PTk], w=[kB])
                    yield
                    if not last:
                        S.act(lambda e, nb=nb: e.activation(out=Pa_[nb][:, 0:nu, :].rearrange("p a b -> p (a b)"), in_=pA[0:64, 0:nu * 64], func=AF.Copy),
                              r=[kA], w=[("Pa" + sfx, nb)])
                    S.dve(lambda e, nb=nb: e.tensor_copy(out=PaT_[nb][:, 0:nu, :].rearrange("p a b -> p (a b)"), in_=pB_[0:64, 0:nu * 64]),
                          r=[kB], w=[("PaT" + sfx, nb)])
                    yield
                    S.pe([(lambda e, u=u, nb=nb: e.matmul(pC[0:64, u * 64:(u + 1) * 64], lhsT=PaT_[nb][:, u, :], rhs=Rb_[:, u, :], start=True, stop=True)) for u in range(nu)],
                         r=[("PaT" + sfx, nb), "Rb" + sfx], w=[kC])
                    yield
                    if not last:
                        S.dve(lambda e: e.tensor_tensor(out=R32_[:, 0:nu, :].rearrange("p a b -> p (a b)"), in0=pC[0:64, 0:nu * 64],
                                                        in1=R32_[:, 0:nu, :].rearrange("p a b -> p (a b)"), op=ALU.add), r=[kC, "R32" + sfx], w=["R32" + sfx])
                        S.pool(lambda e: e.tensor_copy(out=Rb_[:, 0:nu, :], in_=R32_[:, 0:nu, :]), r=["R32" + sfx], w=["Rb" + sfx])
                    else:
                        S.dve(lambda e: e.tensor_tensor(out=Rr[0:64, c0:c0 + nck].rearrange("p a b c -> p (a b c)"), in0=pC[0:64, 0:nu * 64],
                                                        in1=R32_[:, 0:nu, :].rearrange("p a b -> p (a b)"), op=ALU.add), r=[kC, "R32" + sfx], w=["Rr"])
                    yield
                    P, PT, Pk, PTk = Pa_[nb], PaT_[nb], ("Pa" + sfx, nb), ("PaT" + sfx, nb)

            glist = list(range(0, NGR, 4))
            for gi0 in range(0, len(glist), 2):
                active = [prep_gen(glist[gi0], 0)]
                if gi0 + 1 < len(glist):
                    active.append(prep_gen(glist[gi0 + 1], 1))
                while active:
                    for g_ in list(active):
                        try:
                            next(g_)
                        except StopIteration:
                            active.remove(g_)
            S.barrier()
            S.marks.append(("g2_h%d_prep" % h, dict(S.cnt)))
            for d in range(2):
                S.dve(lambda e, d=d: e.memset(S32[d][:, :], 0.0), w=[("S32", d)])
                S.dve(lambda e, d=d: e.memset(Sbf[d][:, :], 0.0), w=[("Sbf", d)])
            for t in range(NGR):
                for d in range(2):
                    c = t if d == 0 else NGR - 1 - t
                    cs = slice(c * 64, (c + 1) * 64)
                    pS, pO = pb[4 + 2 * d], pb[5 + 2 * d]
                    kS_, kO_ = ("pb", 4 + 2 * d), ("pb", 5 + 2 * d)
                    S.pe(lambda e: e.matmul(pS[0:64, 0:256], lhsT=kT[:, cs], rhs=Sbf[d][:, :], start=True, stop=True), r=["kT", ("Sbf", d)], w=[(kS_, 0)])
                    S.pe(lambda e: e.matmul(pO[0:64, 0:256], lhsT=qT[:, cs], rhs=Sbf[d][:, :], start=True, stop=True), r=["qT", ("Sbf", d)], w=[(kO_, 0)])
                    S.pe(lambda e: e.transpose(out=ptb[0:64, d * 128:(d + 1) * 128], in_=kT[:, cs], identity=identb[:, :]), r=["kT", "identb"], w=[("ptk", d)])
                    S.dve(lambda e: e.scalar_tensor_tensor(out=xb[d][0:64, :], in0=pS[0:64, 0:256], scalar=negc[:, c, d, h:h + 1], in1=vtok[:, c, :],
                                                           op0=ALU.mult, op1=ALU.add), r=[(kS_, 0), "negc", "vtok"], w=[("xb", d)])
                    S.act(lambda e: e.activation(out=kd[d][0:64, :], in_=ptb[0:64, d * 128:(d + 1) * 128], func=AF.Copy, scale=kds[:, c, d, h:h + 1]),
                          r=[("ptk", d), "kds"], w=[("kd", d)])
                    S.act(lambda e: e.activation(out=t1[d][:, :], in_=pO[0:64, 0:256], func=AF.Copy, scale=egc[:, c, d, h:h + 1]),
                          r=[(kO_, 0), "egc"], w=[("t1", d)])
                    S.pe(lambda e: e.matmul(pS[0:64, 256:512], lhsT=Rr[:, c, d, :], rhs=xb[d][:, :], start=True, stop=True), r=["Rr", ("xb", d)], w=[(kS_, 1)])
                    S.act(lambda e: e.activation(out=vn[d][0:64, :], in_=pS[0:64, 256:512], func=AF.Copy, scale=beta[:, c, d, h:h + 1]),
                          r=[(kS_, 1), "beta"], w=[("vn", d)])
                    S.pe(lambda e: e.matmul(pO[0:64, 256:512], lhsT=At[:, c, d, :], rhs=vn[d][:, :], start=True, stop=True), r=["At", ("vn", d)], w=[(kO_, 1)])
                    S.pe(lambda e: e.matmul(pS[:, 0:256], lhsT=kd[d][:, :], rhs=vn[d][:, :], start=True, stop=True), r=[("kd", d), ("vn", d)], w=[(kS_, 0)])
                    if t < 32 or (t == 32 and d == 0):
                        S.dve(lambda e: e.tensor_tensor(out=ob[:, c, :], in0=pO[0:64, 256:512], in1=t1[d][:, :], op=ALU.add), r=[(kO_, 1), ("t1", d)], w=[("ob", c)])
                    else:
                        S.dve(lambda e: e.tensor_tensor(out=t1[d][:, :], in0=pO[0:64, 256:512], in1=t1[d][:, :], op=ALU.add), r=[(kO_, 1), ("t1", d)], w=[("t1", d)])
                        S.dve(lambda e: e.tensor_tensor(out=ob[:, c, :], in0=ob[:, c, :], in1=t1[d][:, :], op=ALU.add), r=[("ob", c), ("t1", d)], w=[("ob", c)])
                    S.dve(lambda e: e.scalar_tensor_tensor(out=Sbf[d][:, :], in0=S32[d][:, :], scalar=egl[:, c, d, h:h + 1], in1=pS[:, 0:256],
                                                           op0=ALU.mult, op1=ALU.add), r=[("S32", d), "egl", (kS_, 0)], w=[("Sbf", d)])
                    S.dve(lambda e: e.scalar_tensor_tensor(out=S32[d][:, :], in0=S32[d][:, :], scalar=egl[:, c, d, h:h + 1], in1=pS[:, 0:256],
                                                           op0=ALU.mult, op1=ALU.add), r=[("S32", d), "egl", (kS_, 0)], w=[("S32", d)])
            S.marks.append(("g2_h%d_scan" % h, dict(S.cnt)))
            S.marks.append(("g2_h%d_scan" % h, dict(S.cnt)))
            for c in range(NGR):
                S.act(lambda e, c=c: e.activation(out=junk2[c % 2][:, :], in_=ob[:, c, :], func=AF.Square, accum_out=ssq[:, c:c + 1]), r=[("ob", c)], w=[("junk2", c % 2), ("ssq", c)])
            S.act(lambda e: e.activation(out=rstd[:, 0:NGR], in_=ssq[:, 0:NGR], func=AF.Sqrt, bias=epsb[0:64, 0:1], scale=1.0 / 256), r=[("ssq", c_) for c_ in range(NGR)] + ["epsb"], w=["rstd"])
            S.dve(lambda e: e.reciprocal(out=rstd[:, 0:NGR], in_=rstd[:, 0:NGR]), r=["rstd"], w=["rstd"])
            for c in range(NGR):
                S.dve(lambda e, c=c: e.scalar_tensor_tensor(out=ob[:, c, :], in0=ob[:, c, :], scalar=rstd[:, c:c + 1], in1=gon[:, :], op0=ALU.mult, op1=ALU.mult),
                      r=[("ob", c), "rstd", "gon"], w=[("ob", c)])
            for half in range(2):
                S.load(lambda e: e.dma_start(out=vz[:, :], in_=z_scr[2 * h + half, :, :]), r=[("z_scr", 2 * h + half)], w=["vz"])
                for gi_, c0 in enumerate(range(0, NGR, 4)):
                    nck = min(4, NGR - c0)
                    ybb = yb[gi_ % 2]
                    yk = ("yb", gi_ % 2)
                    S.pe([(lambda e, ci=ci: e.transpose(out=ptb[:, 256 + ci * 64:256 + (ci + 1) * 64], in_=ob[:, c0 + ci, half * 128:(half + 1) * 128], identity=identb[0:64, 0:64]))
                          for ci in range(nck)], r=[("ob", c0 + ci) for ci in range(nck)] + ["identb"], w=["ptb2"])
                    S.dve(lambda e: e.tensor_tensor(out=ybb[:, 0:nck * 64], in0=ptb[:, 256:256 + nck * 64], in1=vz[:, c0 * 64:(c0 + nck) * 64], op=ALU.mult),
                          r=["ptb2", "vz"], w=[yk])
                    S.store(lambda e: e.dma_start(out=y_scr[2 * h + half, :, c0 * 64:(c0 + nck) * 64], in_=ybb[:, 0:nck * 64]), r=[yk], w=[("y_scr", 2 * h + half)])
            S.barrier()

    aoff_g2 = aoff["p"]
    aoff["p"] = aoff_after_p0
    wo = av([128, 16, D], BF16)
    wo_st = [av([128, D]) for _ in range(2)]
    ytile = [av([128, 16, 128], BF16) for _ in range(2)]
    h1t = [av([128, D]) for _ in range(2)]

    def load_wout(w_ap):
        wv = w_ap.rearrange("(c p) n -> p c n", p=128)
        for c in range(16):
            b = c % 2
            S.load(lambda e, c=c, b=b: e.dma_start(out=wo_st[b][:, :], in_=wv[:, c, :]), w=[("wo_st", b)])
            S.pool(lambda e, c=c, b=b: e.tensor_copy(out=wo[:, c, :], in_=wo_st[b][:, :]), r=[("wo_st", b)], w=["wo"])

    def phase_outproj(s, layer):
        load_wout(g_w_out if layer == 0 else m_w_out)
        yv = y_scr.rearrange("c p t -> p c t")
        for tt, (t0, n) in enumerate(TOKTILES):
            if layer == 1 and tt == 0:
                continue
            b = tt % 2
            S.load(lambda e: e.dma_start(out=ytile[b][:, :, 0:n], in_=yv[:, :, t0:t0 + n]), r=[("y_scr", c) for c in range(16)], w=[("ytile", b)])
            if layer == 0:
                load_x_tile(s, tt)
            else:
                S.load(lambda e: e.dma_start(out=xt[b][0:n, :], in_=h1_scr[t0:t0 + n, :]), r=["h1_scr"], w=[("xt", b)])
            for hf in range(2):
                bk = mmbank()
                S.pe([(lambda e, c=c: e.matmul(pb[bk][0:n, 0:512], lhsT=ytile[b][:, c, 0:n], rhs=wo[:, c, hf * 512:(hf + 1) * 512], start=(c == 0), stop=(c == 15)))
                      for c in range(16)], r=[("ytile", b), "wo"], w=[("pb", bk)])
                S.dve(lambda e: e.tensor_tensor(out=h1t[b][0:n, hf * 512:(hf + 1) * 512], in0=pb[bk][0:n, 0:512], in1=xt[b][0:n, hf * 512:(hf + 1) * 512], op=ALU.add),
                      r=[("pb", bk), ("xt", b)], w=[("h1t", b)])
            if layer == 0:
                S.store(lambda e: e.dma_start(out=h1_scr[t0:t0 + n, :], in_=h1t[b][0:n, :]), r=[("h1t", b)], w=["h1_scr"])
                norm_transpose(tt, h1t[b][0:n, :], ("h1t", b), n, t0, 1)
            else:
                r0 = t0 - 64
                S.store(lambda e: e.dma_start(out=out[s, r0:r0 + n, :], in_=h1t[b][0:n, :]), r=[("h1t", b)], w=["out"])
        S.barrier()


    aoff["p"] = aoff_after_p0
    wM = av([128, 8, 832], BF16)
    wMst = [av([128, 8, 128]) for _ in range(2)]
    wMz = [av([128, 8, 128], BF16) for _ in range(2)]
    raw = av([128, 7, 512])
    sqt = av([128, 7, 512], BF16)
    rs1 = [av([128, 512]) for _ in range(2)]
    o1 = [av([128, 7, 512], BF16) for _ in range(2)]
    kpg = av([64, 512], BF16)
    tmpa = av([64, 512])
    tmpb = av([64, 512])
    zo = [av([128, 512], BF16) for _ in range(2)]
    ropeb1 = av([64, 2, 512])
    gq = sb("gq", [128, 4])
    gkv = sb("gkv", [128, 2])
    gqa = sb("gqa", [128, 1])
    gqb = sb("gqb", [64, 1])
    gka = sb("gka", [128, 1])
    gkb = sb("gkb", [64, 1])
    rotb = sb("rotb", [64, 64], BF16)
    rot_st = sb("rot_st", [64, 64])
    nshift = sb("nshift", [128, 1])
    m_w_in_v = m_w_in.rearrange("(c p) n -> p c n", p=128)

    def setup_mla():
        S.load(lambda e: e.dma_start(out=gq[:], in_=m_qn.rearrange("(c p) -> p c", p=128), allow_slow_non_contiguous=True), w=["gq"])
        S.load(lambda e: e.dma_start(out=gkv[:], in_=m_kvn.rearrange("(c p) -> p c", p=128), allow_slow_non_contiguous=True), w=["gkv"])
        S.load(lambda e: e.dma_start(out=gqa[:], in_=m_qg[0:128].rearrange("(p c) -> p c", c=1)), w=["gqa"])
        S.load(lambda e: e.dma_start(out=gqb[:], in_=m_qg[128:192].rearrange("(p c) -> p c", c=1)), w=["gqb"])
        S.load(lambda e: e.dma_start(out=gka[:], in_=m_kg[0:128].rearrange("(p c) -> p c", c=1)), w=["gka"])
        S.load(lambda e: e.dma_start(out=gkb[:], in_=m_kg[128:192].rearrange("(p c) -> p c", c=1)), w=["gkb"])
        S.load(lambda e: e.dma_start(out=rot_st[:], in_=c_rot[:, :]), w=["rot_st"])
        S.dve(lambda e: e.tensor_copy(out=rotb[:], in_=rot_st[:]), r=["rot_st"], w=["rotb"])
        S.dve(lambda e: e.memset(nshift[:], -8.0), w=["nshift"])

    def rstd_from_psum(pbank, npart, w, div, dst):
        S.act(lambda e: e.activation(out=dst[0:npart, 0:w], in_=pbank[0:npart, 0:w], func=AF.Sqrt, bias=epsb[0:npart, 0:1], scale=1.0 / div),
              r=[("pb", 3), "epsb"], w=[("rs", id(dst))])
        S.dve(lambda e: e.reciprocal(out=dst[0:npart, 0:w], in_=dst[0:npart, 0:w]), r=[("rs", id(dst))], w=[("rs", id(dst))])

    def phase_m1(s):
        for f in range(7):
            b = f % 2
            nc_ = 128 if f < 6 else 64
            S.load(lambda e, f=f, b=b, nc_=nc_: e.dma_start(out=wMst[b][:, :, 0:nc_], in_=m_w_in_v[:, :, f * 128:f * 128 + nc_]), w=[("wMst", b)])
            S.pool(lambda e, f=f, b=b, nc_=nc_: e.tensor_copy(out=wM[:, :, f * 128:f * 128 + nc_], in_=wMst[b][:, :, 0:nc_]), r=[("wMst", b)], w=["wM"])
        for bi, (c0, w) in enumerate(COLBLKS):
            ob_ = o1[bi % 2]
            ok = ("o1", bi % 2)
            for f in range(7):
                nr = 128 if f < 6 else 64
                bk = mmbank()
                S.pe([(lambda e, c=c: e.matmul(pb[bk][0:nr, 0:w], lhsT=wM[:, c, f * 128:f * 128 + nr], rhs=hnT[:, c, c0:c0 + w], start=(c == 0), stop=(c == 7)))
                      for c in range(8)], r=["wM"] + gkeys("hnT", c0, w), w=[("pb", bk)])
                S.act(lambda e: e.activation(out=raw[0:nr, f, 0:w], in_=pb[bk][0:nr, 0:w], func=AF.Copy), r=[("pb", bk)], w=[("raw", f)])
                S.dve(lambda e: e.tensor_tensor(out=sqt[0:nr, f, 0:w], in0=raw[0:nr, f, 0:w], in1=raw[0:nr, f, 0:w], op=ALU.mult), r=[("raw", f)], w=[("sqt", f)])
            for (fl, div, gt, ri) in (([0, 1, 2, 3], 512.0, gq, 0), ([4, 5], 256.0, gkv, 1)):
                S.pe([(lambda e, i=i, f=f: e.matmul(pb[3][:, 0:w], lhsT=onesb[:, :], rhs=sqt[:, f, 0:w], start=(i == 0), stop=(i == len(fl) - 1)))
                      for i, f in enumerate(fl)], r=[("sqt", f) for f in fl] + ["onesb"], w=[("pb", 3)])
                rstd_from_psum(pb[3], 128, w, div, rs1[ri])
                for i, f in enumerate(fl):
                    S.dve(lambda e, i=i, f=f: e.scalar_tensor_tensor(out=ob_[:, f, 0:w], in0=raw[:, f, 0:w], scalar=gt[:, i:i + 1], in1=rs1[ri][:, 0:w],
                                                                     op0=ALU.mult, op1=ALU.mult), r=[("raw", f), ("rs", id(rs1[ri]))], w=[ok])
            S.dve(lambda e: e.tensor_scalar(out=kpg[:, 0:w], in0=raw[0:64, 6, 0:w], scalar1=gkb[:, 0:1], scalar2=None, op0=ALU.mult), r=[("raw", 6), "gkb"], w=["kpg"])
            S.pe(lambda e: e.matmul(pb[3][0:64, 0:w], lhsT=rotb[:, :], rhs=kpg[:, 0:w], start=True, stop=True), r=["kpg", "rotb"], w=[("pb", 3)])
            S.load(lambda e: e.dma_start(out=ropeb1[:, :, 0:w], in_=c_rope[:, :, c0:c0 + w]), w=["ropeb1"])
            S.dve(lambda e: e.tensor_tensor(out=tmpa[:, 0:w], in0=pb[3][0:64, 0:w], in1=ropeb1[:, 1, 0:w], op=ALU.mult), r=[("pb", 3), "ropeb1"], w=["tmpa"])
            S.dve(lambda e: e.tensor_tensor(out=tmpb[:, 0:w], in0=kpg[:, 0:w], in1=ropeb1[:, 0, 0:w], op=ALU.mult), r=["kpg", "ropeb1"], w=["tmpb"])
            S.dve(lambda e: e.tensor_tensor(out=ob_[0:64, 6, 0:w], in0=tmpa[:, 0:w], in1=tmpb[:, 0:w], op=ALU.add), r=["tmpa", "tmpb"], w=[ok])
            for f in range(6):
                S.store(lambda e, f=f: e.dma_start(out=qk_scr[f, :, c0:c0 + w], in_=ob_[:, f, 0:w]), r=[ok], w=[("qk_scr", f)])
            S.store(lambda e: e.dma_start(out=qk_scr[6, 0:64, c0:c0 + w], in_=ob_[0:64, 6, 0:w]), r=[ok], w=[("qk_scr", 6)])
            S.store(lambda e: e.dma_start(out=qk_scr[7, 0:64, c0:c0 + w], in_=sqt[0:64, 6, 0:w]), r=[("sqt", 6)], w=[("qk_scr", 7)])
        for hh in range(16):
            b = hh % 2
            S.load(lambda e, hh=hh, b=b: e.dma_start(out=wMst[b][:, :, :], in_=m_w_in_v[:, :, 832 + hh * 128:832 + (hh + 1) * 128]), w=[("wMst", b)])
            S.pool(lambda e, b=b: e.tensor_copy(out=wMz[b][:, :, :], in_=wMst[b][:, :, :]), r=[("wMst", b)], w=[("wMz", b)])
            for bi, (c0, w) in enumerate(COLBLKS):
                bk = mmbank()
                zb = zo[bi % 2]
                S.pe([(lambda e, c=c: e.matmul(pb[bk][:, 0:w], lhsT=wMz[b][:, c, :], rhs=hnT[:, c, c0:c0 + w], start=(c == 0), stop=(c == 7)))
                      for c in range(8)], r=[("wMz", b)] + gkeys("hnT", c0, w), w=[("pb", bk)])
                S.act(lambda e: e.activation(out=zb[:, 0:w], in_=pb[bk][:, 0:w], func=AF.Silu), r=[("pb", bk)], w=[("zo", bi % 2)])
                S.store(lambda e: e.dma_start(out=z_scr[hh, :, c0:c0 + w], in_=zb[:, 0:w]), r=[("zo", bi % 2)], w=[("z_scr", hh)])
        S.barrier()

    aoff["p"] = 0
    cqT = av([128, 4, LE], BF16)
    ckvT = av([128, 2, LE], BF16)
    krT = av([64, LE], BF16)
    sqk = av([128, LE], BF16)
    qTa = av([128, LE], BF16)
    qTb = av([128, LE], BF16)
    kTa = av([128, LE], BF16)
    kTb = av([128, LE], BF16)
    zq = [av([128, 512], BF16) for _ in range(2)]
    vaug = av([128, 33, 130], BF16)
    wq_st = av([128, 4, 192])
    wq = av([128, 4, 192], BF16)
    wkv_st = av([128, 2, 256])
    wkv = av([128, 2, 256], BF16)
    MSET = []
    for si_ in range(2):
        MSET.append(dict(ra=av([128, 512]), rb=av([64, 512]), sqa=av([128, 512], BF16), sqb2=av([128, 512], BF16), rsq=av([128, 512]),
                         qbg=av([64, 512], BF16), t2a=av([64, 512], BF16), t2b=av([64, 512], BF16), rope=av([64, 2, 512])))
        MSET[-1]["rsk"] = MSET[-1]["rsq"]
    MSET[0].update(banks=(pb[0], pb[3]), bkeys=(("pb", 0), ("pb", 3)))
    MSET[1].update(banks=(pb[1], pb[7]), bkeys=(("pb", 1), ("pb", 7)))
    pT = [av([128, 512], BF16) for _ in range(3)]
    rdn = av([1, 512])
    dacc = [[av([128, 512]) for _ in range(2)] for _ in range(2)]
    ones128 = av([128, 1])
    rbc = av([128, 512])
    ytb = [av([128, 512], BF16) for _ in range(2)]
    KT = [(64 + 128 * i, 128) for i in range(32)] + [(PAD, 16)]
    SCALE = 192.0 ** -0.5
    wuq_v = m_wuq.rearrange("(c p) n -> p c n", p=128)
    wukv_v = m_wukv.rearrange("(c p) n -> p c n", p=128)

    def phase_m2(s, heads=range(16)):
        for f in range(4):
            S.load(lambda e, f=f: e.dma_start(out=cqT[:, f, :], in_=qk_scr[f, :, :]), r=[("qk_scr", f)], w=["cqT"])
        for f in range(2):
            S.load(lambda e, f=f: e.dma_start(out=ckvT[:, f, :], in_=qk_scr[4 + f, :, :]), r=[("qk_scr", 4 + f)], w=["ckvT"])
        S.load(lambda e: e.dma_start(out=krT[:, :], in_=qk_scr[6, 0:64, :]), r=[("qk_scr", 6)], w=["krT"])
        S.load(lambda e: e.dma_start(out=sqk[0:64, :], in_=qk_scr[7, 0:64, :]), r=[("qk_scr", 7)], w=["sqk"])
        for t_, k_ in ((sqk, "sqk"), (qTb, "qTb"), (kTb, "kTb"), (MSET[0]["sqb2"], "sqb2_m0"), (MSET[1]["sqb2"], "sqb2_m1")):
            S.dve(lambda e, t_=t_: e.memset(t_[64:128, :], 0.0), w=[k_])
        S.dve(lambda e: e.memset(ones128[:, :], 1.0), w=["ones128"])
        for h in heads:
            S.marks.append(("m2_h%d_start" % h, dict(S.cnt)))
            S.load(lambda e: e.dma_start(out=wq_st[:], in_=wuq_v[:, :, h * 192:(h + 1) * 192]), w=["wq_st"])
            S.pool(lambda e: e.tensor_copy(out=wq[:], in_=wq_st[:]), r=["wq_st"], w=["wq"])
            S.load(lambda e: e.dma_start(out=wkv_st[:], in_=wukv_v[:, :, h * 256:(h + 1) * 256]), w=["wkv_st"])
            S.pool(lambda e: e.tensor_copy(out=wkv[:], in_=wkv_st[:]), r=["wkv_st"], w=["wkv"])
            def prol_gen(c0, w, si):
                Bf = MSET[si]
                ra_, rb_, sqa_, sqb2_, rsq_, rsk_, qbg_, t2a_, t2b_, rope_ = (Bf[k_] for k_ in ("ra", "rb", "sqa", "sqb2", "rsq", "rsk", "qbg", "t2a", "t2b", "rope"))
                pP, pQ = Bf["banks"]
                kP, kQ = Bf["bkeys"]
                x_ = "_m%d" % si
                cs = slice(c0, c0 + w)
                S.load(lambda e: e.dma_start(out=rope_[:, :, 0:w], in_=c_rope[:, :, cs]), w=["rope" + x_])
                S.pe([(lambda e, c=c: e.matmul(pP[:, 0:w], lhsT=wq[:, c, 0:128], rhs=cqT[:, c, cs], start=(c == 0), stop=(c == 3))) for c in range(4)],
                     r=["wq", "cqT"], w=[kP])
                yield
                S.act(lambda e: e.activation(out=ra_[:, 0:w], in_=pP[:, 0:w], func=AF.Copy), r=[kP], w=["ra" + x_])
                yield
                S.pe([(lambda e, c=c: e.matmul(pP[0:64, 0:w], lhsT=wq[:, c, 128:192], rhs=cqT[:, c, cs], start=(c == 0), stop=(c == 3))) for c in range(4)],
                     r=["wq", "cqT"], w=[kP])
                S.dve(lambda e: e.tensor_tensor(out=sqa_[:, 0:w], in0=ra_[:, 0:w], in1=ra_[:, 0:w], op=ALU.mult), r=["ra" + x_], w=["sqa" + x_])
                yield
                S.act(lambda e: e.activation(out=rb_[:, 0:w], in_=pP[0:64, 0:w], func=AF.Copy), r=[kP], w=["rb" + x_])
                yield
                S.dve(lambda e: e.tensor_tensor(out=sqb2_[0:64, 0:w], in0=rb_[:, 0:w], in1=rb_[:, 0:w], op=ALU.mult), r=["rb" + x_], w=["sqb2" + x_])
                yield
                S.pe([lambda e: e.matmul(pQ[:, 0:w], lhsT=onesb[:, :], rhs=sqa_[:, 0:w], start=True, stop=False),
                      lambda e: e.matmul(pQ[:, 0:w], lhsT=onesb[:, :], rhs=sqb2_[:, 0:w], start=False, stop=True)], r=["sqa" + x_, "sqb2" + x_, "onesb"], w=[kQ])
                S.pe([(lambda e, c=c: e.matmul(pP[:, 0:w], lhsT=wkv[:, c, 0:128], rhs=ckvT[:, c, cs], start=(c == 0), stop=(c == 1))) for c in range(2)],
                     r=["wkv", "ckvT"], w=[kP])
                yield
                S.act(lambda e: e.activation(out=rsq_[:, 0:w], in_=pQ[:, 0:w], func=AF.Sqrt, bias=epsb[:, 0:1], scale=1.0 / 192.0), r=[kQ, "epsb"], w=["rsq" + x_])
                yield
                S.dve(lambda e: e.reciprocal(out=rsq_[:, 0:w], in_=rsq_[:, 0:w]), r=["rsq" + x_], w=["rsq" + x_])
                yield
                S.dve(lambda e: e.scalar_tensor_tensor(out=qbg_[:, 0:w], in0=rb_[:, 0:w], scalar=gqb[:, 0:1], in1=rsq_[0:64, 0:w], op0=ALU.mult, op1=ALU.mult),
                      r=["rb" + x_, "rsq" + x_, "gqb"], w=["qbg" + x_])
                S.dve(lambda e: e.scalar_tensor_tensor(out=qTa[:, cs], in0=ra_[:, 0:w], scalar=gqa[:, 0:1], in1=rsq_[:, 0:w], op0=ALU.mult, op1=ALU.mult),
                      r=["ra" + x_, "rsq" + x_, "gqa"], w=["qTa"])
                yield
                S.pe(lambda e: e.matmul(pQ[0:64, 0:w], lhsT=rotb[:, :], rhs=qbg_[:, 0:w], start=True, stop=True), r=["qbg" + x_, "rotb"], w=[kQ])
                S.act(lambda e: e.activation(out=ra_[:, 0:w], in_=pP[:, 0:w], func=AF.Copy), r=[kP], w=["ra" + x_])
                S.dve(lambda e: e.tensor_tensor(out=t2b_[:, 0:w], in0=qbg_[:, 0:w], in1=rope_[:, 0, 0:w], op=ALU.mult), r=["qbg" + x_, "rope" + x_], w=["t2b" + x_])
                yield
                S.dve(lambda e: e.tensor_tensor(out=t2a_[:, 0:w], in0=pQ[0:64, 0:w], in1=rope_[:, 1, 0:w], op=ALU.mult), r=[kQ, "rope" + x_], w=["t2a" + x_])
                S.dve(lambda e: e.tensor_tensor(out=sqa_[:, 0:w], in0=ra_[:, 0:w], in1=ra_[:, 0:w], op=ALU.mult), r=["ra" + x_], w=["sqa" + x_])
                yield
                S.dve(lambda e: e.tensor_tensor(out=qTb[0:64, cs], in0=t2a_[:, 0:w], in1=t2b_[:, 0:w], op=ALU.add), r=["t2a" + x_, "t2b" + x_], w=["qTb"])
                S.pe([lambda e: e.matmul(pQ[:, 0:w], lhsT=onesb[:, :], rhs=sqa_[:, 0:w], start=True, stop=False),
                      lambda e: e.matmul(pQ[:, 0:w], lhsT=onesb[:, :], rhs=sqk[:, cs], start=False, stop=True)], r=["sqa" + x_, "sqk", "onesb"], w=[kQ])
                yield
                S.act(lambda e: e.activation(out=rsk_[:, 0:w], in_=pQ[:, 0:w], func=AF.Sqrt, bias=epsb[:, 0:1], scale=1.0 / 192.0), r=[kQ, "epsb"], w=["rsq" + x_])
                yield
                S.dve(lambda e: e.reciprocal(out=rsk_[:, 0:w], in_=rsk_[:, 0:w]), r=["rsq" + x_], w=["rsq" + x_])
                yield
                S.dve(lambda e: e.scalar_tensor_tensor(out=kTa[:, cs], in0=ra_[:, 0:w], scalar=gka[:, 0:1], in1=rsk_[:, 0:w], op0=ALU.mult, op1=ALU.mult),
                      r=["ra" + x_, "rsq" + x_, "gka"], w=["kTa"])
                S.dve(lambda e: e.tensor_tensor(out=kTb[0:64, cs], in0=krT[:, cs], in1=rsk_[0:64, 0:w], op=ALU.mult), r=["krT", "rsq" + x_], w=["kTb"])
                yield

            for bi0 in range(0, len(COLBLKS), 2):
                active = [prol_gen(COLBLKS[bi0][0], COLBLKS[bi0][1], 0)]
                if bi0 + 1 < len(COLBLKS):
                    active.append(prol_gen(COLBLKS[bi0 + 1][0], COLBLKS[bi0 + 1][1], 1))
                while active:
                    for g_ in list(active):
                        try:
                            next(g_)
                        except StopIteration:
                            active.remove(g_)
            S.marks.append(("m2_h%d_qk" % h, dict(S.cnt)))
            for kt, (k0, nk) in enumerate(KT):
                bk = mmbank()
                S.pe([(lambda e, c=c: e.matmul(pb[bk][0:nk, 0:128], lhsT=ckvT[:, c, k0:k0 + nk], rhs=wkv[:, c, 128:256], start=(c == 0), stop=(c == 1))) for c in range(2)],
                     r=["wkv", "ckvT"], w=[("pb", bk)])
                S.act(lambda e: e.activation(out=vaug[0:nk, kt, 0:128], in_=pb[bk][0:nk, 0:128], func=AF.Copy), r=[("pb", bk)], w=["vaug"])
            S.marks.append(("m2_h%d_v" % h, dict(S.cnt)))
            steps = [(qi, kt) for qi in range(8) for kt in range(33)]
            SB = [0, 1, 3]

            def emit_scores(st):
                qi, kt = steps[st]
                k0, nk = KT[kt]
                q0 = 64 + 512 * qi
                bk = SB[st % 3]
                S.pe([lambda e: e.matmul(pb[bk][0:nk, 0:512], lhsT=kTa[:, k0:k0 + nk], rhs=qTa[:, q0:q0 + 512], start=True, stop=False),
                      lambda e: e.matmul(pb[bk][0:nk, 0:512], lhsT=kTb[:, k0:k0 + nk], rhs=qTb[:, q0:q0 + 512], start=False, stop=True)],
                     r=["kTa", "kTb", "qTa", "qTb"], w=[("pb", bk)])

            emit_scores(0)
            emit_scores(1)
            for st, (qi, kt) in enumerate(steps):
                k0, nk = KT[kt]
                q0 = 64 + 512 * qi
                bk = SB[st % 3]
                ab = qi % 2
                acc_o = pb[4 + 2 * ab]
                acc_d = pb[5]
                if st + 2 < len(steps):
                    emit_scores(st + 2)
                S.act(lambda e: e.activation(out=pT[st % 3][0:nk, :], in_=pb[bk][0:nk, 0:512], func=AF.Exp, bias=nshift[0:nk, 0:1], scale=SCALE),
                      r=[("pb", bk), "nshift"], w=[("pT", st % 3)])
                S.pe(lambda e: e.matmul(acc_o[:, 0:512], lhsT=vaug[0:nk, kt, 0:128], rhs=pT[st % 3][0:nk, :], start=(kt == 0), stop=(kt == 32)),
                     r=[("pT", st % 3), "vaug"], w=[("acc", ab)])
                par = kt % 2
                if kt < 2:
                    S.dve(lambda e: e.tensor_copy(out=dacc[ab][par][:, :], in_=pT[st % 3][:, :]), r=[("pT", st % 3)], w=[("dacc", ab, par)])
                else:
                    S.dve(lambda e: e.tensor_tensor(out=dacc[ab][par][0:nk, :], in0=dacc[ab][par][0:nk, :], in1=pT[st % 3][0:nk, :], op=ALU.add),
                          r=[("pT", st % 3), ("dacc", ab, par)], w=[("dacc", ab, par)])
                if kt == 0:
                    S.load(lambda e: e.dma_start(out=zq[ab][:, :], in_=z_scr[h, :, q0:q0 + 512]), r=[("z_scr", h)], w=[("zq", ab)])
                if kt == 32:
                    yb_ = ytb[qi % 2]
                    S.dve(lambda e: e.tensor_tensor(out=dacc[ab][0][:, :], in0=dacc[ab][0][:, :], in1=dacc[ab][1][:, :], op=ALU.add),
                          r=[("dacc", ab, 0), ("dacc", ab, 1)], w=[("dacc", ab, 0)])
                    S.pe(lambda e: e.matmul(acc_d[0:1, 0:512], lhsT=ones128[:, 0:1], rhs=dacc[ab][0][:, :], start=True, stop=True),
                         r=[("dacc", ab, 0), "ones128"], w=["accd"])
                    S.dve(lambda e: e.reciprocal(out=rdn[0:1, :], in_=acc_d[0:1, 0:512]), r=["accd"], w=["rdn", "rbc"])
                    S.pe(lambda e: e.matmul(pb[7][:, 0:512], lhsT=ones32[0:1, :], rhs=rdn[0:1, :], start=True, stop=True), r=["rdn", "ones32"], w=[("pb", 7)])
                    S.act(lambda e: e.activation(out=rbc[:, :], in_=pb[7][:, 0:512], func=AF.Copy), r=[("pb", 7)], w=["rbc", "rdn"])
                    S.dve(lambda e: e.tensor_tensor(out=rbc[:, :], in0=acc_o[:, 0:512], in1=rbc[:, :], op=ALU.mult), r=[("acc", ab), "rbc"], w=["rbc"])
                    S.dve(lambda e: e.tensor_tensor(out=yb_[:, :], in0=rbc[:, :], in1=zq[ab][:, :], op=ALU.mult), r=["rbc", ("zq", ab)], w=[("ytb", qi % 2)])
                    S.store(lambda e: e.dma_start(out=y_scr[h, :, q0:q0 + 512], in_=yb_[:, :]), r=[("ytb", qi % 2)], w=[("y_scr", h)])
            S.barrier()

    def dump(name, t, shape, dt, keys):
        d = nc.dram_tensor("dbg_" + name, list(shape), dt, kind="ExternalOutput").ap()
        S.store(lambda e: e.dma_start(out=d, in_=t), r=keys)

    setup()
    setup_mla()
    for s in range(NSEQ):
        S.marks.append(("start%d" % s, dict(S.cnt)))
        phase_p0(s)
        S.marks.append(("p0", dict(S.cnt)))
        phase_g1(s)
        S.marks.append(("g1", dict(S.cnt)))
        S.barrier()
        if debug == "g2":
            phase_g2(s, heads=[0])
            break
        if debug not in ("m2", "m1"):
            phase_g2(s)
        S.marks.append(("g2", dict(S.cnt)))
        phase_outproj(s, 0)
        S.marks.append(("op0", dict(S.cnt)))
        if debug == "l0":
            break
        phase_m1(s)
        S.marks.append(("m1", dict(S.cnt)))
        if debug == "m1":
            break
        if debug == "m2":
            phase_m2(s, heads=[0])
            break
        phase_m2(s)
        S.marks.append(("m2", dict(S.cnt)))
        phase_outproj(s, 1)
        S.marks.append(("op1", dict(S.cnt)))
    print("arena max", aoff.get("max"), "ninst", S.ninst, S.cnt, "sbuf left", nc.sbuf_bytes_remaining)
    nc._marks = S.marks
    S.finish()
    es.close()
    return nc


def _consts():
    ident = np.eye(128, dtype=np.float32)
    i = np.arange(64)
    U = (i[:, None] <= i[None, :]).astype(np.float32)
    Lo = (i[:, None] >= i[None, :]).astype(np.float32)
    Us = (i[:, None] < i[None, :]).astype(np.float32)
    Ls = (i[:, None] > i[None, :]).astype(np.float32)
    NEG = -30000.0
    masks = np.stack([U, Lo, Us, Ls, (1 - U) * NEG, (1 - Lo) * NEG], axis=1).astype(np.float32)
    pos = np.arange(LE, dtype=np.float64) - PAD
    inv = 10000.0 ** (-np.arange(0, 64, 2, dtype=np.float64) / 64)
    ang = pos[None, :] * inv[:, None]
    cos = np.concatenate([np.cos(ang), np.cos(ang)], 0)
    sin = np.concatenate([np.sin(ang), np.sin(ang)], 0)
    rope = np.stack([cos, sin], 1).astype(np.float32)
    rot = np.zeros((64, 64), np.float32)
    for m in range(32):
        rot[m + 32, m] = -1.0
        rot[m, m + 32] = 1.0
    return dict(c_ident=ident, c_masks=masks, c_rope=rope, c_rot=rot)


_NC_CACHE = {}


def _in_maps(inputs):
    allx = np.concatenate([np.asarray(inputs["x_prompt"]), np.asarray(inputs["x_sample"])], 0)
    seqs = [[0, 1], [2, 3], [4, 5], [6, 7], [8, 8], [9, 9], [10, 10], [11, 11]]
    common = dict(
        meta=np.asarray(inputs["meta_tokens"]), ln_g=np.asarray(inputs["ln_g"]),
        g_w_in=np.asarray(inputs["gdn_w_in"])[0], g_conv=np.asarray(inputs["gdn_conv_w"])[0],
        g_alog=np.asarray(inputs["gdn_a_log"])[0].reshape(16), g_dtb=np.asarray(inputs["gdn_dt_bias"])[0].reshape(16),
        g_on=np.asarray(inputs["gdn_o_norm_g"])[0], g_w_out=np.asarray(inputs["gdn_w_out"])[0],
        m_w_in=np.asarray(inputs["mla_w_in"])[0], m_qn=np.asarray(inputs["mla_q_norm_g"])[0],
        m_kvn=np.asarray(inputs["mla_kv_norm_g"])[0], m_wuq=np.asarray(inputs["mla_w_uq"])[0],
        m_wukv=np.asarray(inputs["mla_w_ukv"])[0], m_qg=np.asarray(inputs["mla_qk_q_g"])[0],
        m_kg=np.asarray(inputs["mla_qk_k_g"])[0], m_w_out=np.asarray(inputs["mla_w_out"])[0],
    )
    common = {k: np.ascontiguousarray(v, dtype=np.float32) for k, v in common.items()}
    common.update(_consts())
    maps = []
    for c in range(8):
        m = dict(common)
        m["xs"] = np.ascontiguousarray(allx[seqs[c]])
        maps.append(m)
    return maps, seqs


def kernel(**inputs):
    if "nc" not in _NC_CACHE:
        _NC_CACHE["nc"] = build()
    nc = _NC_CACHE["nc"]
    maps, seqs = _in_maps(inputs)
    res = run_bass_kernel_spmd(nc, maps, core_ids=list(range(8)))
    full = np.zeros((12, LX, D), np.float32)
    for c in range(8):
        o = res.results[c]["out"]
        full[seqs[c][0]] = o[0]
        if seqs[c][1] != seqs[c][0]:
            full[seqs[c][1]] = o[1]
    return full[:4], full[4:]
```

```python
import numpy as np
import ml_dtypes
import concourse.bass as bass
import concourse.mybir as mybir
from concourse.bass_utils import run_bass_kernel_spmd

F32 = mybir.dt.float32
BF16 = mybir.dt.bfloat16
ALU = mybir.AluOpType
AF = mybir.ActivationFunctionType

D = 1024
LX = 4096
NMETA = 16
PAD = 48
LE = 4160
NGR = 65
NSEQ = 2
EPS = 1e-6
COLBLKS = [(0, 64)] + [(64 + 512 * i, 512) for i in range(8)]
TOKTILES = [(0, 64)] + [(64 + 128 * i, 128) for i in range(32)]
GDN_IN = 6176
import os
SCANSTOP = int(os.environ.get('SCANSTOP', '9'))
DMA_K = 6
SAME_ENG_SYNC = True


def gkeys(name, c0, n):
    return [(name, g) for g in range(c0 // 64, (c0 + n + 63) // 64)]


class Sched:
    def __init__(self, nc, es):
        self.nc = nc
        self.eng = {"pe": nc.tensor, "dve": nc.vector, "act": nc.scalar, "pool": nc.gpsimd, "sp": nc.sync}
        self.semh = {}
        for e in self.eng:
            self.semh[(e,)] = es.enter_context(nc.semaphore("s_" + e))
        for q in ("sp", "pool", "act"):
            for s in range(DMA_K):
                self.semh[(q, "d", s)] = es.enter_context(nc.semaphore(f"d_{q}{s}"))
        self.cnt = {e: 0 for e in self.eng}
        self.dman = {q: 0 for q in ("sp", "pool", "act")}
        self.seen = {e: {} for e in self.eng}
        self.lastw = {}
        self.readers = {}
        self.ninst = 0
        self.marks = []

    def _wait(self, e, semk, val):
        if val <= 0 or self.seen[e].get(semk, 0) >= val:
            return
        self.eng[e].wait_ge(self.semh[semk], val)
        self.seen[e][semk] = val

    def op(self, e, fn, reads=(), writes=(), dma=False):
        deps = {}
        for k in reads:
            t = self.lastw.get(k)
            if t is not None:
                deps[t[0]] = max(deps.get(t[0], 0), t[1])
        for k in writes:
            t = self.lastw.get(k)
            if t is not None:
                deps[t[0]] = max(deps.get(t[0], 0), t[1])
            for sk, v in self.readers.get(k, {}).items():
                deps[sk] = max(deps.get(sk, 0), v)
        for sk, v in deps.items():
            if sk == (e,) and (e == "pe" or not SAME_ENG_SYNC) and not dma:
                continue
            self._wait(e, sk, v)
        if dma:
            n = self.dman[e]
            slot = n % DMA_K
            sk = (e, "d", slot)
            self._wait(e, sk, 16 * (n // DMA_K))
            self.dman[e] = n + 1
            inst = fn(self.eng[e])
            inst.then_inc(self.semh[sk], 16)
            tok = (sk, 16 * (n // DMA_K + 1))
        else:
            fns = fn if isinstance(fn, (list, tuple)) else [fn]
            inst = None
            for f in fns:
                inst = f(self.eng[e])
                self.ninst += 1
            self.cnt[e] += 1
            inst.then_inc(self.semh[(e,)], 1)
            tok = ((e,), self.cnt[e])
        for k in reads:
            r = self.readers.setdefault(k, {})
            r[tok[0]] = max(r.get(tok[0], 0), tok[1])
        for k in writes:
            self.lastw[k] = tok
            self.readers[k] = {}
        return tok

    def pe(self, fn, r=(), w=()):
        return self.op("pe", fn, r, w)

    def dve(self, fn, r=(), w=()):
        return self.op("dve", fn, r, w)

    def act(self, fn, r=(), w=()):
        return self.op("act", fn, r, w)

    def pool(self, fn, r=(), w=()):
        return self.op("pool", fn, r, w)

    def load(self, fn, r=(), w=()):
        return self.op("sp", fn, r, w, dma=True)

    def store(self, fn, r=(), w=()):
        return self.op("pool", fn, r, w, dma=True)

    def barrier(self):
        for e in self.eng:
            for e2 in self.eng:
                if e2 != e:
                    self._wait(e, (e2,), self.cnt[e2])
            for q in self.dman:
                n = self.dman[q]
                for s in range(DMA_K):
                    if n > s:
                        last = ((n - 1 - s) // DMA_K) * DMA_K + s
                        self._wait(e, (q, "d", s), 16 * (last // DMA_K + 1))
        self.lastw = {}
        self.readers = {}

    def finish(self):
        self.barrier()


def build(debug=None):
    from contextlib import ExitStack
    nc = bass.Bass("TRN2", target_bir_lowering=False)
    es = ExitStack()

    def din(name, shape, dt=F32):
        return nc.dram_tensor(name, list(shape), dt, kind="ExternalInput").ap()

    xs = din("xs", [NSEQ, LX, D])
    meta = din("meta", [NMETA, D])
    ln_g = din("ln_g", [2, D])
    g_w_in = din("g_w_in", [D, GDN_IN])
    g_conv = din("g_conv", [5, 4096])
    g_alog = din("g_alog", [16])
    g_dtb = din("g_dtb", [16])
    g_on = din("g_on", [256])
    g_w_out = din("g_w_out", [2048, D])
    m_w_in = din("m_w_in", [D, 2880])
    m_qn = din("m_qn", [512])
    m_kvn = din("m_kvn", [256])
    m_wuq = din("m_wuq", [512, 3072])
    m_wukv = din("m_wukv", [256, 4096])
    m_qg = din("m_qg", [192])
    m_kg = din("m_kg", [192])
    m_w_out = din("m_w_out", [2048, D])
    c_ident = din("c_ident", [128, 128])
    c_masks = din("c_masks", [64, 6, 64])
    c_rope = din("c_rope", [64, 2, LE])
    c_rot = din("c_rot", [64, 64])
    out = nc.dram_tensor("out", [NSEQ, LX, D], F32, kind="ExternalOutput").ap()

    skind = "ExternalOutput" if debug else "Internal"

    def dscr(name, shape, dt):
        return nc.dram_tensor(name, list(shape), dt, kind=skind).ap()

    qk_scr = dscr("qk_scr", [16, 128, LE], BF16)
    v_scr = dscr("v_scr", [16, 128, LE], BF16)
    z_scr = dscr("z_scr", [16, 128, LE], BF16)
    y_scr = dscr("y_scr", [16, 128, LE], BF16)
    h1_scr = dscr("h1_scr", [LE, D], F32)

    S = Sched(nc, es)

    def sb(name, shape, dt=F32):
        return es.enter_context(nc.sbuf_tensor(name, list(shape), dt))

    def ps(name, shape, dt=F32):
        return es.enter_context(nc.psum_tensor(name, list(shape), dt))

    block = es.enter_context(nc.Block())

    ident = sb("ident", [128, 128])
    identb = sb("identb", [128, 128], BF16)
    onesb = sb("onesb", [128, 128], BF16)
    ones32 = sb("ones32", [64, 128])
    masks = sb("masks", [64, 6, 64])
    lng = sb("lng", [128, 2, 8])
    convw = sb("convw", [128, 5, 32])
    alog = sb("alog", [64, 16])
    dtb = sb("dtb", [64, 16])
    nea = sb("nea", [64, 16])
    gon = sb("gon", [64, 256])
    epsb = sb("epsb", [128, 1])

    pb = [ps(f"pb{i}", [128, 512]) for i in range(8) if i != 2]
    pb.insert(2, None)
    ptb = ps("ptb", [128, 1024], BF16)

    def setup():
        S.load(lambda e: e.dma_start(out=ident[:], in_=c_ident[:, :]), w=["ident"])
        S.load(lambda e: e.dma_start(out=masks[:], in_=c_masks[:, :, :]), w=["masks"])
        S.load(lambda e: e.dma_start(out=lng[:], in_=ln_g.rearrange("l (c p) -> p l c", p=128), allow_slow_non_contiguous=True), w=["lng"])
        for j in range(5):
            S.load(lambda e, j=j: e.dma_start(out=convw[:, j, :], in_=g_conv[j].rearrange("(c p) -> p c", p=128), allow_slow_non_contiguous=True), w=["convw"])
        S.load(lambda e: e.dma_start(out=alog[:], in_=g_alog.partition_broadcast(64)), w=["alog"])
        S.load(lambda e: e.dma_start(out=dtb[:], in_=g_dtb.partition_broadcast(64)), w=["dtb"])
        S.load(lambda e: e.dma_start(out=gon[:], in_=g_on.partition_broadcast(64)), w=["gon"])
        S.dve(lambda e: e.tensor_copy(out=identb[:], in_=ident[:]), r=["ident"], w=["identb"])
        S.dve(lambda e: e.memset(onesb[:], 1.0), w=["onesb"])
        S.dve(lambda e: e.memset(ones32[:], 1.0), w=["ones32"])
        S.dve(lambda e: e.memset(epsb[:], EPS), w=["epsb"])
        S.act(lambda e: e.activation(out=nea[:], in_=alog[:], func=AF.Exp), r=["alog"], w=["nea"])
        S.dve(lambda e: e.tensor_scalar(out=nea[:], in0=nea[:], scalar1=-1.0, scalar2=None, op0=ALU.mult), r=["nea"], w=["nea"])

    beta = sb("beta", [64, NGR, 2, 8])
    gg = sb("gg", [64, NGR, 2, 8])
    gc = sb("gc", [64, NGR, 2, 8])
    negc = sb("negc", [64, NGR, 2, 8])
    egc = sb("egc", [64, NGR, 2, 8])
    kds = sb("kds", [64, NGR, 2, 8])
    egl = sb("egl", [128, NGR, 2, 8])
    negm4 = sb("negm4", [64, 4, 2, 64])
    strict4 = sb("strict4", [64, 4, 2, 64])
    ARENA_BYTES = 171500
    arena = sb("arena", [128, ARENA_BYTES // 4])
    aoff = {"p": 0}

    def av(shape, dt=F32):
        n = 1
        for d_ in shape[1:]:
            n *= d_
        nb = (n * (2 if dt == BF16 else 4) + 3) // 4 * 4
        o = aoff["p"]
        aoff["p"] = o + nb
        aoff["max"] = max(aoff.get("max", 0), o + nb)
        assert aoff["p"] <= ARENA_BYTES, (aoff["p"], shape)
        v = arena[0:shape[0], o // 4:(o + nb) // 4]
        if dt == BF16:
            v = v.bitcast(BF16)
            if n % 2:
                v = v[:, 0:n]
        if len(shape) > 2:
            names = "abcd"[:len(shape) - 1]
            pat = "p (" + " ".join(names) + ") -> p " + " ".join(names)
            v = v.rearrange(pat, **{names[i]: shape[1 + i] for i in range(len(names))})
        return v

    def sbA(name, shape, dt=F32):
        return av(shape, dt)

    hnT = sbA("hnT", [128, 8, LE], BF16)
    xt = [sbA(f"xt{i}", [128, D]) for i in range(2)]
    xn = [sbA(f"xn{i}", [128, D], BF16) for i in range(2)]
    junk = sbA("junk", [128, D], BF16)
    ssb = [sbA(f"ss{i}", [128, 4]) for i in range(2)]

    aoff_after_p0 = aoff["p"]

    def norm_transpose(tt, src_ap, src_key, n, t0, layer):
        b = tt % 2
        ss = ssb[b]
        S.act(lambda e: e.activation(out=junk[0:n, :], in_=src_ap, func=AF.Square, accum_out=ss[0:n, 0:1]),
              r=[src_key], w=["junk", ("ss", b)])
        S.act(lambda e: e.activation(out=ss[0:n, 1:2], in_=ss[0:n, 0:1], func=AF.Sqrt, bias=epsb[0:n, 0:1], scale=1.0 / D),
              r=[("ss", b), "epsb"], w=[("ss", b)])
        S.dve(lambda e: e.reciprocal(out=ss[0:n, 2:3], in_=ss[0:n, 1:2]), r=[("ss", b)], w=[("ss", b)])
        S.act(lambda e: e.activation(out=xn[b][0:n, :], in_=src_ap, func=AF.Copy, scale=ss[0:n, 2:3]),
              r=[src_key, ("ss", b)], w=[("xn", b)])
        S.pe([(lambda e, c=c: e.transpose(out=ptb[:, c * 128:c * 128 + n], in_=xn[b][0:n, c * 128:(c + 1) * 128], identity=identb[0:n, 0:n]))
              for c in range(8)], r=[("xn", b), "identb"], w=["ptb"])
        pv = ptb[:, :].rearrange("p (c t) -> p c t", c=8)[:, :, 0:n]
        S.dve(lambda e: e.tensor_tensor(out=hnT[:, :, t0:t0 + n], in0=pv,
                                        in1=lng[:, layer, :].unsqueeze(2).to_broadcast([128, 8, n]), op=ALU.mult),
              r=["ptb", "lng"], w=gkeys("hnT", t0, n))

    def load_x_tile(s, tt):
        t0, n = TOKTILES[tt]
        b = tt % 2
        if tt == 0:
            S.dve(lambda e: e.memset(xt[b][0:64, :], 0.0), w=[("xt", b)])
            S.load(lambda e: e.dma_start(out=xt[b][PAD:64, :], in_=meta[:, :]), w=[("xt", b)])
        else:
            r0 = t0 - 64
            S.load(lambda e: e.dma_start(out=xt[b][0:n, :], in_=xs[s, r0:r0 + n, :]), w=[("xt", b)])

    def phase_p0(s):
        for tt, (t0, n) in enumerate(TOKTILES):
            load_x_tile(s, tt)
            norm_transpose(tt, xt[tt % 2][0:n, :], ("xt", tt % 2), n, t0, 0)

    wst = [sbA(f"wst{i}", [128, 8, 128]) for i in range(2)]
    wbf = [sbA(f"wbf{i}", [128, 8, 128], BF16) for i in range(2)]
    pre = [sbA(f"pre{i}", [128, LE + 4], BF16) for i in range(2)]
    acc = sbA("acc", [128, LE])
    sqb = sbA("sqb", [128, LE], BF16)
    obf = [sbA(f"obf{i}", [128, LE], BF16) for i in range(2)]
    rtmp = [sbA(f"rtmp{i}", [128, 512]) for i in range(2)]
    dg = [sbA(f"dg{i}", [128, 5, 128], BF16) for i in range(2)]
    gbraw = sbA("gbraw", [64, NGR, 32])
    wba_st = sbA("wba_st", [128, 8, 32])
    wba = sbA("wba", [128, 8, 32], BF16)

    w_in_v = g_w_in.rearrange("(c p) n -> p c n", p=128)
    mmrot = [0]

    def mmbank():
        mmrot[0] ^= 1
        return mmrot[0]

    def load_w(wv_ap, idx, ncol=128):
        b = idx % 2
        S.load(lambda e: e.dma_start(out=wst[b][:, :, 0:ncol], in_=wv_ap), w=[("wst", b)])
        S.pool(lambda e: e.tensor_copy(out=wbf[b][:, :, 0:ncol], in_=wst[b][:, :, 0:ncol]), r=[("wst", b)], w=[("wbf", b)])
        return wbf[b]

    def phase_g1(s):
        for b in range(2):
            S.dve(lambda e, b=b: e.memset(pre[b][:, 0:2], 0.0), w=[("pre", b)])
            S.dve(lambda e, b=b: e.memset(pre[b][:, LE + 2:LE + 4], 0.0), w=[("pre", b)])
        flist = range(48)
        if debug == "g1a":
            flist = [0, 16, 32]
        if debug == "g1b":
            flist = []
        for f in flist:
            wt = load_w(w_in_v[:, :, f * 128:(f + 1) * 128], f)
            wk = ("wbf", f % 2)
            pb_ = f % 2
            for (c0, w) in COLBLKS:
                bk = mmbank()
                S.pe([(lambda e, c=c: e.matmul(pb[bk][:, 0:w], lhsT=wt[:, c, :], rhs=hnT[:, c, c0:c0 + w], start=(c == 0), stop=(c == 7)))
                      for c in range(8)], r=[wk] + gkeys("hnT", c0, w), w=[("pb", bk)])
                if f < 32:
                    S.act(lambda e: e.activation(out=pre[pb_][:, 2 + c0:2 + c0 + w], in_=pb[bk][:, 0:w], func=AF.Copy),
                          r=[("pb", bk)], w=[("pre", pb_)])
                else:
                    ob = obf[f % 2]
                    S.act(lambda e: e.activation(out=ob[:, c0:c0 + w], in_=pb[bk][:, 0:w], func=AF.Silu),
                          r=[("pb", bk)], w=[("obf", f % 2)])
            if f >= 32:
                S.store(lambda e: e.dma_start(out=z_scr[f - 32, :, :], in_=obf[f % 2][:, :]), r=[("obf", f % 2)], w=[("z_scr", f - 32)])
                continue
            pr = pre[pb_]
            dgb = dg[f % 2]
            for j in range(5):
                S.pool(lambda e, j=j: e.tensor_scalar(out=dgb[:, j, :], in0=identb[:, :], scalar1=convw[:, j, f:f + 1], scalar2=None, op0=ALU.mult),
                       r=["identb", "convw"], w=[("dg", f % 2)])
            ob = obf[f % 2]
            ok = ("obf", f % 2)
            qscale = (128.0 ** -0.5) if f < 8 else 1.0

            def conv_gen(bi, c0, w, si):
                pC_, pN_ = (pb[3], pb[4]) if si == 0 else (pb[5], pb[6])
                kC_, kN_ = (("pb", 3), ("pb", 4)) if si == 0 else (("pb", 5), ("pb", 6))
                rt = rtmp[si]
                S.pe([(lambda e, j=j: e.matmul(pC_[:, 0:w], lhsT=dgb[:, j, :], rhs=pr[:, c0 + j:c0 + j + w], start=(j == 0), stop=(j == 4))) for j in range(5)],
                     r=[("pre", pb_), ("dg", f % 2)], w=[kC_])
                yield
                if f >= 16:
                    S.act(lambda e: e.activation(out=ob[:, c0:c0 + w], in_=pC_[:, 0:w], func=AF.Silu), r=[kC_], w=[ok])
                    return
                S.act(lambda e: e.activation(out=acc[:, c0:c0 + w], in_=pC_[:, 0:w], func=AF.Silu), r=[kC_], w=[("acc", bi)])
                yield
                S.dve(lambda e: e.tensor_tensor(out=sqb[:, c0:c0 + w], in0=acc[:, c0:c0 + w], in1=acc[:, c0:c0 + w], op=ALU.mult), r=[("acc", bi)], w=[("sqb", bi)])
                yield
                S.pe(lambda e: e.matmul(pN_[:, 0:w], lhsT=onesb[:, :], rhs=sqb[:, c0:c0 + w], start=True, stop=True), r=[("sqb", bi), "onesb"], w=[kN_])
                yield
                S.act(lambda e: e.activation(out=rt[:, 0:w], in_=pN_[:, 0:w], func=AF.Sqrt, bias=epsb[:, 0:1], scale=1.0), r=[kN_, "epsb"], w=[("rtmp", si)])
                yield
                S.dve(lambda e: e.reciprocal(out=rt[:, 0:w], in_=rt[:, 0:w]), r=[("rtmp", si)], w=[("rtmp", si)])
                yield
                S.dve(lambda e: e.scalar_tensor_tensor(out=ob[:, c0:c0 + w], in0=acc[:, c0:c0 + w], scalar=qscale, in1=rt[:, 0:w],
                                                       op0=ALU.mult, op1=ALU.mult), r=[("acc", bi), ("rtmp", si)], w=[ok])
                yield

            for bi0 in range(0, len(COLBLKS), 2):
                active = [conv_gen(bi0, COLBLKS[bi0][0], COLBLKS[bi0][1], 0)]
                if bi0 + 1 < len(COLBLKS):
                    active.append(conv_gen(bi0 + 1, COLBLKS[bi0 + 1][0], COLBLKS[bi0 + 1][1], 1))
                while active:
                    for g_ in list(active):
                        try:
                            next(g_)
                        except StopIteration:
                            active.remove(g_)
            S.dve(lambda e: e.memset(ob[:, 0:PAD], 0.0), w=[ok])
            if f >= 16:
                S.store(lambda e: e.dma_start(out=v_scr[f - 16, :, :], in_=ob[:, :]), r=[ok], w=[("v_scr", f - 16)])
            else:
                S.store(lambda e: e.dma_start(out=qk_scr[f, :, :], in_=ob[:, :]), r=[ok], w=[("qk_scr", f)])
        if debug == "g1a":
            return
        S.load(lambda e: e.dma_start(out=wba_st[:], in_=w_in_v[:, :, 6144:6176]), w=["wba_st"])
        S.pool(lambda e: e.tensor_copy(out=wba[:], in_=wba_st[:]), r=["wba_st"], w=["wba"])
        for g0 in range(0, NGR, 16):
            ng = min(16, NGR - g0)
            bk = mmbank()
            pv = pb[bk][0:64, :].rearrange("p (g n) -> p g n", n=32)
            for gi in range(ng):
                gr = g0 + gi
                S.pe([(lambda e, c=c: e.matmul(pv[:, gi, :], lhsT=hnT[:, c, gr * 64:(gr + 1) * 64], rhs=wba[:, c, :], start=(c == 0), stop=(c == 7)))
                      for c in range(8)], r=["wba", ("hnT", gr)], w=[("pb", bk)])
            S.act(lambda e: e.activation(out=gbraw[:, g0:g0 + ng, :], in_=pv[:, 0:ng, :], func=AF.Copy), r=[("pb", bk)], w=["gbraw"])
        import os
        stopat = int(os.environ.get("STOPAT", "99"))
        if stopat <= 1:
            return
        S.act(lambda e: e.activation(out=beta[:].rearrange("p g a b -> p g (a b)"), in_=gbraw[:, :, 0:16], func=AF.Sigmoid), r=["gbraw"], w=["beta"])
        ggf = gg[:].rearrange("p g a b -> p g (a b)")
        S.dve(lambda e: e.tensor_tensor(out=ggf, in0=gbraw[:, :, 16:32], in1=dtb[:].unsqueeze(1).to_broadcast([64, NGR, 16]), op=ALU.add),
              r=["gbraw", "dtb"], w=["gg"])
        S.act(lambda e: e.activation(out=ggf, in_=ggf, func=AF.Exp), r=["gg"], w=["gg"])
        S.act(lambda e: e.activation(out=ggf, in_=ggf, func=AF.Ln, bias=1.0, scale=1.0), r=["gg"], w=["gg"])
        S.dve(lambda e: e.tensor_tensor(out=ggf, in0=ggf, in1=nea[:].unsqueeze(1).to_broadcast([64, NGR, 16]), op=ALU.mult),
              r=["gg", "nea"], w=["gg"])
        if stopat <= 2:
            return
        S.dve(lambda e: e.memset(gg[0:PAD, 0, :, :], 0.0), w=["gg"])
        S.dve(lambda e: e.memset(beta[0:PAD, 0, :, :], 0.0), w=["beta"])
        if stopat <= 3:
            return
        for g0 in range(0, NGR, 32):
            ng = min(32, NGR - g0)
            bk = mmbank()
            pv = pb[bk][0:64, :].rearrange("p (g a b) -> p g a b", a=2, b=8)
            fns = []
            for gi in range(ng):
                gr = g0 + gi
                fns.append(lambda e, gi=gi, gr=gr: e.matmul(pv[:, gi, 0, :], lhsT=masks[:, 0, :], rhs=gg[:, gr, 0, :], start=True, stop=True))
                fns.append(lambda e, gi=gi, gr=gr: e.matmul(pv[:, gi, 1, :], lhsT=masks[:, 1, :], rhs=gg[:, gr, 1, :], start=True, stop=True))
            S.pe(fns, r=["gg", "masks"], w=[("pb", bk)])
            S.act(lambda e: e.activation(out=gc[:, g0:g0 + ng], in_=pv[:, 0:ng], func=AF.Copy), r=[("pb", bk)], w=["gc"])
            if stopat <= 4:
                continue
            bk2 = mmbank()
            pv2 = pb[bk2][:, :].rearrange("p (g a b) -> p g a b", a=2, b=8)
            fns = [(lambda e, gi=gi: e.matmul(pv2[:, gi].rearrange("p a b -> p (a b)"), lhsT=ones32[:, :],
                                               rhs=gg[:, g0 + gi].rearrange("p a b -> p (a b)"), start=True, stop=True)) for gi in range(ng)]
            S.pe(fns, r=["gg", "ones32"], w=[("pb", bk2)])
            if stopat <= 5:
                continue
            S.act(lambda e: e.activation(out=egl[:, g0:g0 + ng], in_=pv2[:, 0:ng], func=AF.Exp), r=[("pb", bk2)], w=["egl"])
            if stopat <= 6:
                continue
            S.act(lambda e: e.activation(out=kds[:, g0:g0 + ng], in_=pv2[0:64, 0:ng], func=AF.Copy), r=[("pb", bk2)], w=["kds"])
            S.dve(lambda e: e.tensor_tensor(out=kds[:, g0:g0 + ng].rearrange("p g a b -> p (g a b)"), in0=kds[:, g0:g0 + ng].rearrange("p g a b -> p (g a b)"),
                                            in1=gc[:, g0:g0 + ng].rearrange("p g a b -> p (g a b)"), op=ALU.subtract),
                  r=["gc", "kds"], w=["kds"])
        if stopat <= 7:
            return
        S.act(lambda e: e.activation(out=kds[:], in_=kds[:], func=AF.Exp), r=["kds"], w=["kds"])
        S.act(lambda e: e.activation(out=egc[:], in_=gc[:], func=AF.Exp), r=["gc"], w=["egc"])
        S.dve(lambda e: e.tensor_scalar(out=negc[:], in0=egc[:], scalar1=-1.0, scalar2=None, op0=ALU.mult), r=["egc"], w=["negc"])


    aoff_after_g1 = aoff["p"]
    aoff["p"] = 0
    qT = av([128, LE], BF16)
    kT = av([128, LE], BF16)
    vz = av([128, LE], BF16)
    vtok = av([64, NGR, 256], BF16)
    ob = av([64, NGR, 256], BF16)
    Rr = av([128, NGR, 2, 64], BF16)
    At = av([128, NGR, 2, 64], BF16)
    PSET = []
    for si_ in range(2):
        PSET.append(dict(rhsD=av([64, 4, 2, 64]), DTi=av([64, 4, 2, 64]), DTs=av([64, 4, 2, 64]), X4=av([64, 8, 64], BF16), XT=av([64, 8, 64], BF16),
                         Pa=[av([64, 8, 64], BF16) for _ in range(2)], PaT=[av([64, 8, 64], BF16) for _ in range(2)], R32=av([64, 8, 64]), Rb=av([64, 8, 64], BF16)))
    PSET[0].update(banks=(pb[0], pb[1], pb[3]), bkeys=(("pb", 0), ("pb", 1), ("pb", 3)), ptx=ptb[:, 512:1024], ptxk="ptx0")
    PSET[1].update(banks=(pb[4], pb[5], pb[6]), bkeys=(("pb", 4), ("pb", 5), ("pb", 6)), ptx=ptb[:, 0:512], ptxk="ptx1")
    S32 = [av([128, 256]) for _ in range(2)]
    Sbf = [av([128, 256], BF16) for _ in range(2)]
    xb = [av([128, 256], BF16) for _ in range(2)]
    vn = [av([128, 256], BF16) for _ in range(2)]
    kd = [av([128, 128], BF16) for _ in range(2)]
    t1 = [av([64, 256]) for _ in range(2)]
    ssq = av([64, NGR + 3])
    rstd = av([64, NGR + 3])
    junk2 = [av([64, 256], BF16) for _ in range(2)]
    yb = [av([128, 256], BF16) for _ in range(2)]

    def phase_g2(s, heads=range(8)):
        S.dve(lambda e: e.tensor_copy(out=negm4[:], in_=masks[:, 4:6, :].unsqueeze(1).to_broadcast([64, 4, 2, 64])), r=["masks"], w=["negm4"])
        S.dve(lambda e: e.tensor_copy(out=strict4[:], in_=masks[:, 2:4, :].unsqueeze(1).to_broadcast([64, 4, 2, 64])), r=["masks"], w=["strict4"])
        S.dve(lambda e: e.memset(Rr[64:128].rearrange("p a b c -> p (a b c)"), 0.0), w=["Rr"])
        S.dve(lambda e: e.memset(At[64:128].rearrange("p a b c -> p (a b c)"), 0.0), w=["At"])
        for d_ in range(2):
            S.dve(lambda e, d_=d_: e.memset(xb[d_][64:128, :], 0.0), w=[("xb", d_)])
            S.dve(lambda e, d_=d_: e.memset(vn[d_][64:128, :], 0.0), w=[("vn", d_)])
            S.dve(lambda e, d_=d_: e.memset(kd[d_][64:128, :], 0.0), w=[("kd", d_)])
        for h in heads:
            S.load(lambda e: e.dma_start(out=qT[:, :], in_=qk_scr[h, :, :]), r=[("qk_scr", h)], w=["qT"])
            S.load(lambda e: e.dma_start(out=kT[:, :], in_=qk_scr[8 + h, :, :]), r=[("qk_scr", 8 + h)], w=["kT"])
            for half in range(2):
                S.load(lambda e: e.dma_start(out=vz[:, :], in_=v_scr[2 * h + half, :, :]), r=[("v_scr", 2 * h + half)], w=["vz"])
                for g0 in range(0, NGR, 8):
                    ng = min(8, NGR - g0)
                    S.pe([(lambda e, gi=gi: e.transpose(out=ptb[0:64, gi * 128:(gi + 1) * 128], in_=vz[:, (g0 + gi) * 64:(g0 + gi + 1) * 64], identity=identb[:, :]))
                          for gi in range(ng)], r=["vz", "identb"], w=["ptb"])
                    pv = ptb[0:64, :].rearrange("p (g n) -> p g n", n=128)
                    S.act(lambda e: e.activation(out=vtok[:, g0:g0 + ng, half * 128:(half + 1) * 128], in_=pv[:, 0:ng, :], func=AF.Copy),
                          r=["ptb"], w=["vtok"])
            S.marks.append(("g2_h%d_load" % h, dict(S.cnt)))
            def prep_gen(c0, si):
                B = PSET[si]
                rhsD_, DTi_, DTs_, X4_, XT_, Pa_, PaT_, R32_, Rb_ = B["rhsD"], B["DTi"], B["DTs"], B["X4"], B["XT"], B["Pa"], B["PaT"], B["R32"], B["Rb"]
                dif_ = rhsD_
                pA, pB_, pC = B["banks"]
                kA, kB, kC = B["bkeys"]
                px = B["ptx"]
                kx = B["ptxk"]
                sfx = "_%d" % si
                nck = min(4, NGR - c0)
                nu = nck * 2
                W = nck * 128
                for d in range(2):
                    S.dve(lambda e, d=d: e.tensor_tensor(out=rhsD_[:, 0:nck, d, :], in0=gg[:, c0:c0 + nck, d, h:h + 1].to_broadcast([64, nck, 64]),
                                                         in1=masks[:, d:d + 1, :].to_broadcast([64, nck, 64]), op=ALU.mult),
                          r=["gg", "masks"], w=["rhsD" + sfx])
                yield
                S.pe([lambda e: e.matmul(pB_[0:64, 0:W], lhsT=ones32[:, 0:64], rhs=rhsD_[:, 0:nck].rearrange("p a b c -> p (a b c)"), start=True, stop=False),
                      lambda e: e.matmul(pB_[0:64, 0:W], lhsT=ident[0:64, 0:64], rhs=negm4[:, 0:nck].rearrange("p a b c -> p (a b c)"), start=False, stop=True)],
                     r=["rhsD" + sfx, "ones32", "ident", "negm4"], w=[kB])
                pK = pA[0:64, :].rearrange("p (t g n) -> p t g n", t=2, g=4)
                fns = []
                for ci in range(nck):
                    cs = slice((c0 + ci) * 64, (c0 + ci + 1) * 64)
                    fns.append(lambda e, ci=ci, cs=cs: e.matmul(pK[:, 0, ci, :], lhsT=kT[:, cs], rhs=kT[:, cs], start=True, stop=True))
                    fns.append(lambda e, ci=ci, cs=cs: e.matmul(pK[:, 1, ci, :], lhsT=kT[:, cs], rhs=qT[:, cs], start=True, stop=True))
                S.pe(fns, r=["kT", "qT"], w=[kA])
                yield
                S.act(lambda e: e.activation(out=dif_[:, 0:nck].rearrange("p a b c -> p (a b c)"), in_=pB_[0:64, 0:W], func=AF.Copy), r=[kB], w=["rhsD" + sfx])
                yield
                S.dve(lambda e: e.tensor_tensor(out=dif_[:, 0:nck].rearrange("p a b c -> p (a b) c"), in0=dif_[:, 0:nck].rearrange("p a b c -> p (a b) c"),
                                                in1=gc[:, c0:c0 + nck].rearrange("p g a b -> p (g a) b")[:, :, h:h + 1].to_broadcast([64, nu, 64]), op=ALU.subtract),
                      r=["rhsD" + sfx, "gc"], w=["rhsD" + sfx])
                yield
                S.act(lambda e: e.activation(out=DTi_[:, 0:nck].rearrange("p a b c -> p (a b c)"), in_=dif_[:, 0:nck].rearrange("p a b c -> p (a b c)"), func=AF.Exp),
                      r=["rhsD" + sfx], w=["DTi" + sfx])
                yield
                S.dve(lambda e: e.tensor_tensor(out=DTs_[:, 0:nck].rearrange("p a b c -> p (a b c)"), in0=DTi_[:, 0:nck].rearrange("p a b c -> p (a b c)"),
                                                in1=strict4[:, 0:nck].rearrange("p a b c -> p (a b c)"), op=ALU.mult), r=["DTi" + sfx, "strict4"], w=["DTs" + sfx])
                S.dve(lambda e: e.tensor_tensor(out=DTs_[:, 0:nck].rearrange("p a b c -> p (a b) c"), in0=DTs_[:, 0:nck].rearrange("p a b c -> p (a b) c"),
                                                in1=beta[:, c0:c0 + nck].rearrange("p g a b -> p (g a) b")[:, :, h:h + 1].to_broadcast([64, nu, 64]), op=ALU.mult),
                      r=["DTs" + sfx, "beta"], w=["DTs" + sfx])
                X44 = X4_[:, :, :].rearrange("p (g d) n -> p g d n", d=2)
                for d in range(2):
                    S.dve(lambda e, d=d: e.tensor_tensor(out=X44[:, 0:nck, d, :], in0=pK[:, 0, 0:nck, :], in1=DTs_[:, 0:nck, d, :], op=ALU.mult),
                          r=[kA, "DTs" + sfx], w=["X4" + sfx])
                yield
                S.pe([(lambda e, u=u: e.transpose(out=px[0:64, u * 64:(u + 1) * 64], in_=X4_[:, u, :], identity=identb[0:64, 0:64])) for u in range(nu)],
                     r=["X4" + sfx, "identb"], w=[kx, "ptb"])
                for d in range(2):
                    S.dve(lambda e, d=d: e.tensor_tensor(out=At[0:64, c0:c0 + nck, d, :], in0=pK[:, 1, 0:nck, :], in1=DTi_[:, 0:nck, d, :], op=ALU.mult),
                          r=[kA, "DTi" + sfx], w=["At"])
                S.dve(lambda e: e.tensor_tensor(out=R32_[:, 0:nu, :], in0=ident[0:64, 0:64].unsqueeze(1).to_broadcast([64, nu, 64]), in1=X4_[:, 0:nu, :], op=ALU.subtract),
                      r=["ident", "X4" + sfx], w=["R32" + sfx])
                S.pool(lambda e: e.tensor_copy(out=Rb_[:, 0:nu, :], in_=R32_[:, 0:nu, :]), r=["R32" + sfx], w=["Rb" + sfx])
                yield
                S.act(lambda e: e.activation(out=XT_[:, 0:nu, :].rearrange("p a b -> p (a b)"), in_=px[0:64, 0:nu * 64], func=AF.Copy), r=[kx], w=["XT" + sfx])
                yield
                P, PT, Pk, PTk = X4_, XT_, "X4" + sfx, "XT" + sfx
                for lvl in range(5):
                    nb = lvl % 2
                    last = (lvl == 4)
                    if not last:
                        S.pe([(lambda e, u=u, P=P, PT=PT: e.matmul(pA[0:64, u * 64:(u + 1) * 64], lhsT=PT[:, u, :], rhs=P[:, u, :], start=True, stop=True)) for u in range(nu)],
                             r=[Pk, PTk], w=[kA])
                    S.pe([(lambda e, u=u, P=P, PT=PT: e.matmul(pB_[0:64, u * 64:(u + 1) * 64], lhsT=P[:, u, :], rhs=PT[:, u, :], start=True, stop=True)) for u in range(nu)],
                         r=[Pk, PTk], w=[kB])
                    yield
                    if not last:
                        S.act(lambda e, nb=nb: e.activation(out=Pa_[nb][:, 0:nu, :].rearrange("p a b -> p (a b)"), in_=pA[0:64, 0:nu * 64], func=AF.Copy),
                              r=[kA], w=[("Pa" + sfx, nb)])
                    S.dve(lambda e, nb=nb: e.tensor_copy(out=PaT_[nb][:, 0:nu, :].rearrange("p a b -> p (a b)"), in_=pB_[0:64, 0:nu * 64]),
                          r=[kB], w=[("PaT" + sfx, nb)])
                    yield
                    S.pe([(lambda e, u=u, nb=nb: e.matmul(pC[0:64, u * 64:(u + 1) * 64], lhsT=PaT_[nb][:, u, :], rhs=Rb_[:, u, :], start=True, stop=True)) for u in range(nu)],
                         r=[("PaT" + sfx, nb), "Rb" + sfx], w=[kC])
                    yield
                    if not last:
                        S.dve(lambda e: e.tensor_tensor(out=R32_[:, 0:nu, :].rearrange("p a b -> p (a b)"), in0=pC[0:64, 0:nu * 64],
                                                        in1=R32_[:, 0:nu, :].rearrange("p a b -> p (a b)"), op=ALU.add), r=[kC, "R32" + sfx], w=["R32" + sfx])
                        S.pool(lambda e: e.tensor_copy(out=Rb_[:, 0:nu, :], in_=R32_[:, 0:nu, :]), r=["R32" + sfx], w=["Rb" + sfx])
                    else:
                        S.dve(lambda e: e.tensor_tensor(out=Rr[0:64, c0:c0 + nck].rearrange("p a b c -> p (a b c)"), in0=pC[0:64, 0:nu * 64],
                                                        in1=R32_[:, 0:nu, :].rearrange("p a b -> p (a b)"), op=ALU.add), r=[kC, "R32" + sfx], w=["Rr"])
                    yield
                    P, PT, Pk, PTk = Pa_[nb], PaT_[nb], ("Pa" + sfx, nb), ("PaT" + sfx, nb)

            glist = list(range(0, NGR, 4))
            for gi0 in range(0, len(glist), 2):
                active = [prep_gen(glist[gi0], 0)]
                if gi0 + 1 < len(glist):
                    active.append(prep_gen(glist[gi0 + 1], 1))
                while active:
                    for g_ in list(active):
                        try:
                            next(g_)
                        except StopIteration:
                            active.remove(g_)
            S.barrier()
            S.marks.append(("g2_h%d_prep" % h, dict(S.cnt)))
            for d in range(2):
                S.dve(lambda e, d=d: e.memset(S32[d][:, :], 0.0), w=[("S32", d)])
                S.dve(lambda e, d=d: e.memset(Sbf[d][:, :], 0.0), w=[("Sbf", d)])
            for t in range(NGR):
                for d in range(2):
                    c = t if d == 0 else NGR - 1 - t
                    cs = slice(c * 64, (c + 1) * 64)
                    pS, pO = pb[4 + 2 * d], pb[5 + 2 * d]
                    kS_, kO_ = ("pb", 4 + 2 * d), ("pb", 5 + 2 * d)
                    S.pe(lambda e: e.matmul(pS[0:64, 0:256], lhsT=kT[:, cs], rhs=Sbf[d][:, :], start=True, stop=True), r=["kT", ("Sbf", d)], w=[(kS_, 0)])
                    S.pe(lambda e: e.matmul(pO[0:64, 0:256], lhsT=qT[:, cs], rhs=Sbf[d][:, :], start=True, stop=True), r=["qT", ("Sbf", d)], w=[(kO_, 0)])
                    S.pe(lambda e: e.transpose(out=ptb[0:64, d * 128:(d + 1) * 128], in_=kT[:, cs], identity=identb[:, :]), r=["kT", "identb"], w=[("ptk", d)])
                    S.dve(lambda e: e.scalar_tensor_tensor(out=xb[d][0:64, :], in0=pS[0:64, 0:256], scalar=negc[:, c, d, h:h + 1], in1=vtok[:, c, :],
                                                           op0=ALU.mult, op1=ALU.add), r=[(kS_, 0), "negc", "vtok"], w=[("xb", d)])
                    S.act(lambda e: e.activation(out=kd[d][0:64, :], in_=ptb[0:64, d * 128:(d + 1) * 128], func=AF.Copy, scale=kds[:, c, d, h:h + 1]),
                          r=[("ptk", d), "kds"], w=[("kd", d)])
                    S.act(lambda e: e.activation(out=t1[d][:, :], in_=pO[0:64, 0:256], func=AF.Copy, scale=egc[:, c, d, h:h + 1]),
                          r=[(kO_, 0), "egc"], w=[("t1", d)])
                    S.pe(lambda e: e.matmul(pS[0:64, 256:512], lhsT=Rr[:, c, d, :], rhs=xb[d][:, :], start=True, stop=True), r=["Rr", ("xb", d)], w=[(kS_, 1)])
                    S.act(lambda e: e.activation(out=vn[d][0:64, :], in_=pS[0:64, 256:512], func=AF.Copy, scale=beta[:, c, d, h:h + 1]),
                          r=[(kS_, 1), "beta"], w=[("vn", d)])
                    S.pe(lambda e: e.matmul(pO[0:64, 256:512], lhsT=At[:, c, d, :], rhs=vn[d][:, :], start=True, stop=True), r=["At", ("vn", d)], w=[(kO_, 1)])
                    S.pe(lambda e: e.matmul(pS[:, 0:256], lhsT=kd[d][:, :], rhs=vn[d][:, :], start=True, stop=True), r=[("kd", d), ("vn", d)], w=[(kS_, 0)])
                    if t < 32 or (t == 32 and d == 0):
                        S.dve(lambda e: e.tensor_tensor(out=ob[:, c, :], in0=pO[0:64, 256:512], in1=t1[d][:, :], op=ALU.add), r=[(kO_, 1), ("t1", d)], w=[("ob", c)])
                    else:
                        S.dve(lambda e: e.tensor_tensor(out=t1[d][:, :], in0=pO[0:64, 256:512], in1=t1[d][:, :], op=ALU.add), r=[(kO_, 1), ("t1", d)], w=[("t1", d)])
                        S.dve(lambda e: e.tensor_tensor(out=ob[:, c, :], in0=ob[:, c, :], in1=t1[d][:, :], op=ALU.add), r=[("ob", c), ("t1", d)], w=[("ob", c)])
                    S.dve(lambda e: e.scalar_tensor_tensor(out=Sbf[d][:, :], in0=S32[d][:, :], scalar=egl[:, c, d, h:h + 1], in1=pS[:, 0:256],
                                                           op0=ALU.mult, op1=ALU.add), r=[("S32", d), "egl", (kS_, 0)], w=[("Sbf", d)])
                    S.dve(lambda e: e.scalar_tensor_tensor(out=S32[d][:, :], in0=S32[d][:, :], scalar=egl[:, c, d, h:h + 1], in1=pS[:, 0:256],
                                                           op0=ALU.mult, op1=ALU.add), r=[("S32", d), "egl", (kS_, 0)], w=[("S32", d)])
            S.marks.append(("g2_h%d_scan" % h, dict(S.cnt)))
            S.marks.append(("g2_h%d_scan" % h, dict(S.cnt)))
            for c in range(NGR):
                S.act(lambda e, c=c: e.activation(out=junk2[c % 2][:, :], in_=ob[:, c, :], func=AF.Square, accum_out=ssq[:, c:c + 1]), r=[("ob", c)], w=[("junk2", c % 2), ("ssq", c)])
            S.act(lambda e: e.activation(out=rstd[:, 0:NGR], in_=ssq[:, 0:NGR], func=AF.Sqrt, bias=epsb[0:64, 0:1], scale=1.0 / 256), r=[("ssq", c_) for c_ in range(NGR)] + ["epsb"], w=["rstd"])
            S.dve(lambda e: e.reciprocal(out=rstd[:, 0:NGR], in_=rstd[:, 0:NGR]), r=["rstd"], w=["rstd"])
            for c in range(NGR):
                S.dve(lambda e, c=c: e.scalar_tensor_tensor(out=ob[:, c, :], in0=ob[:, c, :], scalar=rstd[:, c:c + 1], in1=gon[:, :], op0=ALU.mult, op1=ALU.mult),
                      r=[("ob", c), "rstd", "gon"], w=[("ob", c)])
            for half in range(2):
                S.load(lambda e: e.dma_start(out=vz[:, :], in_=z_scr[2 * h + half, :, :]), r=[("z_scr", 2 * h + half)], w=["vz"])
                for gi_, c0 in enumerate(range(0, NGR, 4)):
                    nck = min(4, NGR - c0)
                    ybb = yb[gi_ % 2]
                    yk = ("yb", gi_ % 2)
                    S.pe([(lambda e, ci=ci: e.transpose(out=ptb[:, 256 + ci * 64:256 + (ci + 1) * 64], in_=ob[:, c0 + ci, half * 128:(half + 1) * 128], identity=identb[0:64, 0:64]))
                          for ci in range(nck)], r=[("ob", c0 + ci) for ci in range(nck)] + ["identb"], w=["ptb2"])
                    S.dve(lambda e: e.tensor_tensor(out=ybb[:, 0:nck * 64], in0=ptb[:, 256:256 + nck * 64], in1=vz[:, c0 * 64:(c0 + nck) * 64], op=ALU.mult),
                          r=["ptb2", "vz"], w=[yk])
                    S.store(lambda e: e.dma_start(out=y_scr[2 * h + half, :, c0 * 64:(c0 + nck) * 64], in_=ybb[:, 0:nck * 64]), r=[yk], w=[("y_scr", 2 * h + half)])
            S.barrier()

    aoff_g2 = aoff["p"]
    aoff["p"] = aoff_after_p0
    wo = av([128, 16, D], BF16)
    wo_st = [av([128, D]) for _ in range(2)]
    ytile = [av([128, 16, 128], BF16) for _ in range(2)]
    h1t = [av([128, D]) for _ in range(2)]

    def load_wout(w_ap):
        wv = w_ap.rearrange("(c p) n -> p c n", p=128)
        for c in range(16):
            b = c % 2
            S.load(lambda e, c=c, b=b: e.dma_start(out=wo_st[b][:, :], in_=wv[:, c, :]), w=[("wo_st", b)])
            S.pool(lambda e, c=c, b=b: e.tensor_copy(out=wo[:, c, :], in_=wo_st[b][:, :]), r=[("wo_st", b)], w=["wo"])

    def phase_outproj(s, layer):
        load_wout(g_w_out if layer == 0 else m_w_out)
        yv = y_scr.rearrange("c p t -> p c t")
        for tt, (t0, n) in enumerate(TOKTILES):
            if layer == 1 and tt == 0:
                continue
            b = tt % 2
            S.load(lambda e: e.dma_start(out=ytile[b][:, :, 0:n], in_=yv[:, :, t0:t0 + n]), r=[("y_scr", c) for c in range(16)], w=[("ytile", b)])
            if layer == 0:
                load_x_tile(s, tt)
            else:
                S.load(lambda e: e.dma_start(out=xt[b][0:n, :], in_=h1_scr[t0:t0 + n, :]), r=["h1_scr"], w=[("xt", b)])
            for hf in range(2):
                bk = mmbank()
                S.pe([(lambda e, c=c: e.matmul(pb[bk][0:n, 0:512], lhsT=ytile[b][:, c, 0:n], rhs=wo[:, c, hf * 512:(hf + 1) * 512], start=(c == 0), stop=(c == 15)))
                      for c in range(16)], r=[("ytile", b), "wo"], w=[("pb", bk)])
                S.dve(lambda e: e.tensor_tensor(out=h1t[b][0:n, hf * 512:(hf + 1) * 512], in0=pb[bk][0:n, 0:512], in1=xt[b][0:n, hf * 512:(hf + 1) * 512], op=ALU.add),
                      r=[("pb", bk), ("xt", b)], w=[("h1t", b)])
            if layer == 0:
                S.store(lambda e: e.dma_start(out=h1_scr[t0:t0 + n, :], in_=h1t[b][0:n, :]), r=[("h1t", b)], w=["h1_scr"])
                norm_transpose(tt, h1t[b][0:n, :], ("h1t", b), n, t0, 1)
            else:
                r0 = t0 - 64
                S.store(lambda e: e.dma_start(out=out[s, r0:r0 + n, :], in_=h1t[b][0:n, :]), r=[("h1t", b)], w=["out"])
        S.barrier()


    aoff["p"] = aoff_after_p0
    wM = av([128, 8, 832], BF16)
    wMst = [av([128, 8, 128]) for _ in range(2)]
    wMz = [av([128, 8, 128], BF16) for _ in range(2)]
    raw = av([128, 7, 512])
    sqt = av([128, 7, 512], BF16)
    rs1 = [av([128, 512]) for _ in range(2)]
    o1 = [av([128, 7, 512], BF16) for _ in range(2)]
    kpg = av([64, 512], BF16)
    tmpa = av([64, 512])
    tmpb = av([64, 512])
    zo = [av([128, 512], BF16) for _ in range(2)]
    ropeb1 = av([64, 2, 512])
    gq = sb("gq", [128, 4])
    gkv = sb("gkv", [128, 2])
    gqa = sb("gqa", [128, 1])
    gqb = sb("gqb", [64, 1])
    gka = sb("gka", [128, 1])
    gkb = sb("gkb", [64, 1])
    rotb = sb("rotb", [64, 64], BF16)
    rot_st = sb("rot_st", [64, 64])
    nshift = sb("nshift", [128, 1])
    m_w_in_v = m_w_in.rearrange("(c p) n -> p c n", p=128)

    def setup_mla():
        S.load(lambda e: e.dma_start(out=gq[:], in_=m_qn.rearrange("(c p) -> p c", p=128), allow_slow_non_contiguous=True), w=["gq"])
        S.load(lambda e: e.dma_start(out=gkv[:], in_=m_kvn.rearrange("(c p) -> p c", p=128), allow_slow_non_contiguous=True), w=["gkv"])
        S.load(lambda e: e.dma_start(out=gqa[:], in_=m_qg[0:128].rearrange("(p c) -> p c", c=1)), w=["gqa"])
        S.load(lambda e: e.dma_start(out=gqb[:], in_=m_qg[128:192].rearrange("(p c) -> p c", c=1)), w=["gqb"])
        S.load(lambda e: e.dma_start(out=gka[:], in_=m_kg[0:128].rearrange("(p c) -> p c", c=1)), w=["gka"])
        S.load(lambda e: e.dma_start(out=gkb[:], in_=m_kg[128:192].rearrange("(p c) -> p c", c=1)), w=["gkb"])
        S.load(lambda e: e.dma_start(out=rot_st[:], in_=c_rot[:, :]), w=["rot_st"])
        S.dve(lambda e: e.tensor_copy(out=rotb[:], in_=rot_st[:]), r=["rot_st"], w=["rotb"])
        S.dve(lambda e: e.memset(nshift[:], -8.0), w=["nshift"])

    def rstd_from_psum(pbank, npart, w, div, dst):
        S.act(lambda e: e.activation(out=dst[0:npart, 0:w], in_=pbank[0:npart, 0:w], func=AF.Sqrt, bias=epsb[0:npart, 0:1], scale=1.0 / div),
              r=[("pb", 3), "epsb"], w=[("rs", id(dst))])
        S.dve(lambda e: e.reciprocal(out=dst[0:npart, 0:w], in_=dst[0:npart, 0:w]), r=[("rs", id(dst))], w=[("rs", id(dst))])

    def phase_m1(s):
        for f in range(7):
            b = f % 2
            nc_ = 128 if f < 6 else 64
            S.load(lambda e, f=f, b=b, nc_=nc_: e.dma_start(out=wMst[b][:, :, 0:nc_], in_=m_w_in_v[:, :, f * 128:f * 128 + nc_]), w=[("wMst", b)])
            S.pool(lambda e, f=f, b=b, nc_=nc_: e.tensor_copy(out=wM[:, :, f * 128:f * 128 + nc_], in_=wMst[b][:, :, 0:nc_]), r=[("wMst", b)], w=["wM"])
        for bi, (c0, w) in enumerate(COLBLKS):
            ob_ = o1[bi % 2]
            ok = ("o1", bi % 2)
            for f in range(7):
                nr = 128 if f < 6 else 64
                bk = mmbank()
                S.pe([(lambda e, c=c: e.matmul(pb[bk][0:nr, 0:w], lhsT=wM[:, c, f * 128:f * 128 + nr], rhs=hnT[:, c, c0:c0 + w], start=(c == 0), stop=(c == 7)))
                      for c in range(8)], r=["wM"] + gkeys("hnT", c0, w), w=[("pb", bk)])
                S.act(lambda e: e.activation(out=raw[0:nr, f, 0:w], in_=pb[bk][0:nr, 0:w], func=AF.Copy), r=[("pb", bk)], w=[("raw", f)])
                S.dve(lambda e: e.tensor_tensor(out=sqt[0:nr, f, 0:w], in0=raw[0:nr, f, 0:w], in1=raw[0:nr, f, 0:w], op=ALU.mult), r=[("raw", f)], w=[("sqt", f)])
            for (fl, div, gt, ri) in (([0, 1, 2, 3], 512.0, gq, 0), ([4, 5], 256.0, gkv, 1)):
                S.pe([(lambda e, i=i, f=f: e.matmul(pb[3][:, 0:w], lhsT=onesb[:, :], rhs=sqt[:, f, 0:w], start=(i == 0), stop=(i == len(fl) - 1)))
                      for i, f in enumerate(fl)], r=[("sqt", f) for f in fl] + ["onesb"], w=[("pb", 3)])
                rstd_from_psum(pb[3], 128, w, div, rs1[ri])
                for i, f in enumerate(fl):
                    S.dve(lambda e, i=i, f=f: e.scalar_tensor_tensor(out=ob_[:, f, 0:w], in0=raw[:, f, 0:w], scalar=gt[:, i:i + 1], in1=rs1[ri][:, 0:w],
                                                                     op0=ALU.mult, op1=ALU.mult), r=[("raw", f), ("rs", id(rs1[ri]))], w=[ok])
            S.dve(lambda e: e.tensor_scalar(out=kpg[:, 0:w], in0=raw[0:64, 6, 0:w], scalar1=gkb[:, 0:1], scalar2=None, op0=ALU.mult), r=[("raw", 6), "gkb"], w=["kpg"])
            S.pe(lambda e: e.matmul(pb[3][0:64, 0:w], lhsT=rotb[:, :], rhs=kpg[:, 0:w], start=True, stop=True), r=["kpg", "rotb"], w=[("pb", 3)])
            S.load(lambda e: e.dma_start(out=ropeb1[:, :, 0:w], in_=c_rope[:, :, c0:c0 + w]), w=["ropeb1"])
            S.dve(lambda e: e.tensor_tensor(out=tmpa[:, 0:w], in0=pb[3][0:64, 0:w], in1=ropeb1[:, 1, 0:w], op=ALU.mult), r=[("pb", 3), "ropeb1"], w=["tmpa"])
            S.dve(lambda e: e.tensor_tensor(out=tmpb[:, 0:w], in0=kpg[:, 0:w], in1=ropeb1[:, 0, 0:w], op=ALU.mult), r=["kpg", "ropeb1"], w=["tmpb"])
            S.dve(lambda e: e.tensor_tensor(out=ob_[0:64, 6, 0:w], in0=tmpa[:, 0:w], in1=tmpb[:, 0:w], op=ALU.add), r=["tmpa", "tmpb"], w=[ok])
            for f in range(6):
                S.store(lambda e, f=f: e.dma_start(out=qk_scr[f, :, c0:c0 + w], in_=ob_[:, f, 0:w]), r=[ok], w=[("qk_scr", f)])
            S.store(lambda e: e.dma_start(out=qk_scr[6, 0:64, c0:c0 + w], in_=ob_[0:64, 6, 0:w]), r=[ok], w=[("qk_scr", 6)])
            S.store(lambda e: e.dma_start(out=qk_scr[7, 0:64, c0:c0 + w], in_=sqt[0:64, 6, 0:w]), r=[("sqt", 6)], w=[("qk_scr", 7)])
        for hh in range(16):
            b = hh % 2
            S.load(lambda e, hh=hh, b=b: e.dma_start(out=wMst[b][:, :, :], in_=m_w_in_v[:, :, 832 + hh * 128:832 + (hh + 1) * 128]), w=[("wMst", b)])
            S.pool(lambda e, b=b: e.tensor_copy(out=wMz[b][:, :, :], in_=wMst[b][:, :, :]), r=[("wMst", b)], w=[("wMz", b)])
            for bi, (c0, w) in enumerate(COLBLKS):
                bk = mmbank()
                zb = zo[bi % 2]
                S.pe([(lambda e, c=c: e.matmul(pb[bk][:, 0:w], lhsT=wMz[b][:, c, :], rhs=hnT[:, c, c0:c0 + w], start=(c == 0), stop=(c == 7)))
                      for c in range(8)], r=[("wMz", b)] + gkeys("hnT", c0, w), w=[("pb", bk)])
                S.act(lambda e: e.activation(out=zb[:, 0:w], in_=pb[bk][:, 0:w], func=AF.Silu), r=[("pb", bk)], w=[("zo", bi % 2)])
                S.store(lambda e: e.dma_start(out=z_scr[hh, :, c0:c0 + w], in_=zb[:, 0:w]), r=[("zo", bi % 2)], w=[("z_scr", hh)])
        S.barrier()

    aoff["p"] = 0
    cqT = av([128, 4, LE], BF16)
    ckvT = av([128, 2, LE], BF16)
    krT = av([64, LE], BF16)
    sqk = av([128, LE], BF16)
    qTa = av([128, LE], BF16)
    qTb = av([128, LE], BF16)
    kTa = av([128, LE], BF16)
    kTb = av([128, LE], BF16)
    zq = [av([128, 512], BF16) for _ in range(2)]
    vaug = av([128, 33, 130], BF16)
    wq_st = av([128, 4, 192])
    wq = av([128, 4, 192], BF16)
    wkv_st = av([128, 2, 256])
    wkv = av([128, 2, 256], BF16)
    MSET = []
    for si_ in range(2):
        MSET.append(dict(ra=av([128, 512]), rb=av([64, 512]), sqa=av([128, 512], BF16), sqb2=av([128, 512], BF16), rsq=av([128, 512]),
                         qbg=av([64, 512], BF16), t2a=av([64, 512], BF16), t2b=av([64, 512], BF16), rope=av([64, 2, 512])))
        MSET[-1]["rsk"] = MSET[-1]["rsq"]
    MSET[0].update(banks=(pb[0], pb[3]), bkeys=(("pb", 0), ("pb", 3)))
    MSET[1].update(banks=(pb[1], pb[7]), bkeys=(("pb", 1), ("pb", 7)))
    pT = [av([128, 512], BF16) for _ in range(3)]
    rdn = av([1, 512])
    dacc = [[av([128, 512]) for _ in range(2)] for _ in range(2)]
    ones128 = av([128, 1])
    rbc = av([128, 512])
    ytb = [av([128, 512], BF16) for _ in range(2)]
    KT = [(64 + 128 * i, 128) for i in range(32)] + [(PAD, 16)]
    SCALE = 192.0 ** -0.5
    wuq_v = m_wuq.rearrange("(c p) n -> p c n", p=128)
    wukv_v = m_wukv.rearrange("(c p) n -> p c n", p=128)

    def phase_m2(s, heads=range(16)):
        for f in range(4):
            S.load(lambda e, f=f: e.dma_start(out=cqT[:, f, :], in_=qk_scr[f, :, :]), r=[("qk_scr", f)], w=["cqT"])
        for f in range(2):
            S.load(lambda e, f=f: e.dma_start(out=ckvT[:, f, :], in_=qk_scr[4 + f, :, :]), r=[("qk_scr", 4 + f)], w=["ckvT"])
        S.load(lambda e: e.dma_start(out=krT[:, :], in_=qk_scr[6, 0:64, :]), r=[("qk_scr", 6)], w=["krT"])
        S.load(lambda e: e.dma_start(out=sqk[0:64, :], in_=qk_scr[7, 0:64, :]), r=[("qk_scr", 7)], w=["sqk"])
        for t_, k_ in ((sqk, "sqk"), (qTb, "qTb"), (kTb, "kTb"), (MSET[0]["sqb2"], "sqb2_m0"), (MSET[1]["sqb2"], "sqb2_m1")):
            S.dve(lambda e, t_=t_: e.memset(t_[64:128, :], 0.0), w=[k_])
        S.dve(lambda e: e.memset(ones128[:, :], 1.0), w=["ones128"])
        for h in heads:
            S.marks.append(("m2_h%d_start" % h, dict(S.cnt)))
            S.load(lambda e: e.dma_start(out=wq_st[:], in_=wuq_v[:, :, h * 192:(h + 1) * 192]), w=["wq_st"])
            S.pool(lambda e: e.tensor_copy(out=wq[:], in_=wq_st[:]), r=["wq_st"], w=["wq"])
            S.load(lambda e: e.dma_start(out=wkv_st[:], in_=wukv_v[:, :, h * 256:(h + 1) * 256]), w=["wkv_st"])
            S.pool(lambda e: e.tensor_copy(out=wkv[:], in_=wkv_st[:]), r=["wkv_st"], w=["wkv"])
            def prol_gen(c0, w, si):
                Bf = MSET[si]
                ra_, rb_, sqa_, sqb2_, rsq_, rsk_, qbg_, t2a_, t2b_, rope_ = (Bf[k_] for k_ in ("ra", "rb", "sqa", "sqb2", "rsq", "rsk", "qbg", "t2a", "t2b", "rope"))
                pP, pQ = Bf["banks"]
                kP, kQ = Bf["bkeys"]
                x_ = "_m%d" % si
                cs = slice(c0, c0 + w)
                S.load(lambda e: e.dma_start(out=rope_[:, :, 0:w], in_=c_rope[:, :, cs]), w=["rope" + x_])
                S.pe([(lambda e, c=c: e.matmul(pP[:, 0:w], lhsT=wq[:, c, 0:128], rhs=cqT[:, c, cs], start=(c == 0), stop=(c == 3))) for c in range(4)],
                     r=["wq", "cqT"], w=[kP])
                yield
                S.act(lambda e: e.activation(out=ra_[:, 0:w], in_=pP[:, 0:w], func=AF.Copy), r=[kP], w=["ra" + x_])
                yield
                S.pe([(lambda e, c=c: e.matmul(pP[0:64, 0:w], lhsT=wq[:, c, 128:192], rhs=cqT[:, c, cs], start=(c == 0), stop=(c == 3))) for c in range(4)],
                     r=["wq", "cqT"], w=[kP])
                S.dve(lambda e: e.tensor_tensor(out=sqa_[:, 0:w], in0=ra_[:, 0:w], in1=ra_[:, 0:w], op=ALU.mult), r=["ra" + x_], w=["sqa" + x_])
                yield
                S.act(lambda e: e.activation(out=rb_[:, 0:w], in_=pP[0:64, 0:w], func=AF.Copy), r=[kP], w=["rb" + x_])
                yield
                S.dve(lambda e: e.tensor_tensor(out=sqb2_[0:64, 0:w], in0=rb_[:, 0:w], in1=rb_[:, 0:w], op=ALU.mult), r=["rb" + x_], w=["sqb2" + x_])
                yield
                S.pe([lambda e: e.matmul(pQ[:, 0:w], lhsT=onesb[:, :], rhs=sqa_[:, 0:w], start=True, stop=False),
                      lambda e: e.matmul(pQ[:, 0:w], lhsT=onesb[:, :], rhs=sqb2_[:, 0:w], start=False, stop=True)], r=["sqa" + x_, "sqb2" + x_, "onesb"], w=[kQ])
                S.pe([(lambda e, c=c: e.matmul(pP[:, 0:w], lhsT=wkv[:, c, 0:128], rhs=ckvT[:, c, cs], start=(c == 0), stop=(c == 1))) for c in range(2)],
                     r=["wkv", "ckvT"], w=[kP])
                yield
                S.act(lambda e: e.activation(out=rsq_[:, 0:w], in_=pQ[:, 0:w], func=AF.Sqrt, bias=epsb[:, 0:1], scale=1.0 / 192.0), r=[kQ, "epsb"], w=["rsq" + x_])
                yield
                S.dve(lambda e: e.reciprocal(out=rsq_[:, 0:w], in_=rsq_[:, 0:w]), r=["rsq" + x_], w=["rsq" + x_])
                yield
                S.dve(lambda e: e.scalar_tensor_tensor(out=qbg_[:, 0:w], in0=rb_[:, 0:w], scalar=gqb[:, 0:1], in1=rsq_[0:64, 0:w], op0=ALU.mult, op1=ALU.mult),
                      r=["rb" + x_, "rsq" + x_, "gqb"], w=["qbg" + x_])
                S.dve(lambda e: e.scalar_tensor_tensor(out=qTa[:, cs], in0=ra_[:, 0:w], scalar=gqa[:, 0:1], in1=rsq_[:, 0:w], op0=ALU.mult, op1=ALU.mult),
                      r=["ra" + x_, "rsq" + x_, "gqa"], w=["qTa"])
                yield
                S.pe(lambda e: e.matmul(pQ[0:64, 0:w], lhsT=rotb[:, :], rhs=qbg_[:, 0:w], start=True, stop=True), r=["qbg" + x_, "rotb"], w=[kQ])
                S.act(lambda e: e.activation(out=ra_[:, 0:w], in_=pP[:, 0:w], func=AF.Copy), r=[kP], w=["ra" + x_])
                S.dve(lambda e: e.tensor_tensor(out=t2b_[:, 0:w], in0=qbg_[:, 0:w], in1=rope_[:, 0, 0:w], op=ALU.mult), r=["qbg" + x_, "rope" + x_], w=["t2b" + x_])
                yield
                S.dve(lambda e: e.tensor_tensor(out=t2a_[:, 0:w], in0=pQ[0:64, 0:w], in1=rope_[:, 1, 0:w], op=ALU.mult), r=[kQ, "rope" + x_], w=["t2a" + x_])
                S.dve(lambda e: e.tensor_tensor(out=sqa_[:, 0:w], in0=ra_[:, 0:w], in1=ra_[:, 0:w], op=ALU.mult), r=["ra" + x_], w=["sqa" + x_])
                yield
                S.dve(lambda e: e.tensor_tensor(out=qTb[0:64, cs], in0=t2a_[:, 0:w], in1=t2b_[:, 0:w], op=ALU.add), r=["t2a" + x_, "t2b" + x_], w=["qTb"])
                S.pe([lambda e: e.matmul(pQ[:, 0:w], lhsT=onesb[:, :], rhs=sqa_[:, 0:w], start=True, stop=False),
                      lambda e: e.matmul(pQ[:, 0:w], lhsT=onesb[:, :], rhs=sqk[:, cs], start=False, stop=True)], r=["sqa" + x_, "sqk", "onesb"], w=[kQ])
                yield
                S.act(lambda e: e.activation(out=rsk_[:, 0:w], in_=pQ[:, 0:w], func=AF.Sqrt, bias=epsb[:, 0:1], scale=1.0 / 192.0), r=[kQ, "epsb"], w=["rsq" + x_])
                yield
                S.dve(lambda e: e.reciprocal(out=rsk_[:, 0:w], in_=rsk_[:, 0:w]), r=["rsq" + x_], w=["rsq" + x_])
                yield
                S.dve(lambda e: e.scalar_tensor_tensor(out=kTa[:, cs], in0=ra_[:, 0:w], scalar=gka[:, 0:1], in1=rsk_[:, 0:w], op0=ALU.mult, op1=ALU.mult),
                      r=["ra" + x_, "rsq" + x_, "gka"], w=["kTa"])
                S.dve(lambda e: e.tensor_tensor(out=kTb[0:64, cs], in0=krT[:, cs], in1=rsk_[0:64, 0:w], op=ALU.mult), r=["krT", "rsq" + x_], w=["kTb"])
                yield

            for bi0 in range(0, len(COLBLKS), 2):
                active = [prol_gen(COLBLKS[bi0][0], COLBLKS[bi0][1], 0)]
                if bi0 + 1 < len(COLBLKS):
                    active.append(prol_gen(COLBLKS[bi0 + 1][0], COLBLKS[bi0 + 1][1], 1))
                while active:
                    for g_ in list(active):
                        try:
                            next(g_)
                        except StopIteration:
                            active.remove(g_)
            S.marks.append(("m2_h%d_qk" % h, dict(S.cnt)))
            for kt, (k0, nk) in enumerate(KT):
                bk = mmbank()
                S.pe([(lambda e, c=c: e.matmul(pb[bk][0:nk, 0:128], lhsT=ckvT[:, c, k0:k0 + nk], rhs=wkv[:, c, 128:256], start=(c == 0), stop=(c == 1))) for c in range(2)],
                     r=["wkv", "ckvT"], w=[("pb", bk)])
                S.act(lambda e: e.activation(out=vaug[0:nk, kt, 0:128], in_=pb[bk][0:nk, 0:128], func=AF.Copy), r=[("pb", bk)], w=["vaug"])
            S.marks.append(("m2_h%d_v" % h, dict(S.cnt)))
            steps = [(qi, kt) for qi in range(8) for kt in range(33)]
            SB = [0, 1, 3]

            def emit_scores(st):
                qi, kt = steps[st]
                k0, nk = KT[kt]
                q0 = 64 + 512 * qi
                bk = SB[st % 3]
                S.pe([lambda e: e.matmul(pb[bk][0:nk, 0:512], lhsT=kTa[:, k0:k0 + nk], rhs=qTa[:, q0:q0 + 512], start=True, stop=False),
                      lambda e: e.matmul(pb[bk][0:nk, 0:512], lhsT=kTb[:, k0:k0 + nk], rhs=qTb[:, q0:q0 + 512], start=False, stop=True)],
                     r=["kTa", "kTb", "qTa", "qTb"], w=[("pb", bk)])

            emit_scores(0)
            emit_scores(1)
            for st, (qi, kt) in enumerate(steps):
                k0, nk = KT[kt]
                q0 = 64 + 512 * qi
                bk = SB[st % 3]
                ab = qi % 2
                acc_o = pb[4 + 2 * ab]
                acc_d = pb[5]
                if st + 2 < len(steps):
                    emit_scores(st + 2)
                S.act(lambda e: e.activation(out=pT[st % 3][0:nk, :], in_=pb[bk][0:nk, 0:512], func=AF.Exp, bias=nshift[0:nk, 0:1], scale=SCALE),
                      r=[("pb", bk), "nshift"], w=[("pT", st % 3)])
                S.pe(lambda e: e.matmul(acc_o[:, 0:512], lhsT=vaug[0:nk, kt, 0:128], rhs=pT[st % 3][0:nk, :], start=(kt == 0), stop=(kt == 32)),
                     r=[("pT", st % 3), "vaug"], w=[("acc", ab)])
                par = kt % 2
                if kt < 2:
                    S.dve(lambda e: e.tensor_copy(out=dacc[ab][par][:, :], in_=pT[st % 3][:, :]), r=[("pT", st % 3)], w=[("dacc", ab, par)])
                else:
                    S.dve(lambda e: e.tensor_tensor(out=dacc[ab][par][0:nk, :], in0=dacc[ab][par][0:nk, :], in1=pT[st % 3][0:nk, :], op=ALU.add),
                          r=[("pT", st % 3), ("dacc", ab, par)], w=[("dacc", ab, par)])
                if kt == 0:
                    S.load(lambda e: e.dma_start(out=zq[ab][:, :], in_=z_scr[h, :, q0:q0 + 512]), r=[("z_scr", h)], w=[("zq", ab)])
                if kt == 32:
                    yb_ = ytb[qi % 2]
                    S.dve(lambda e: e.tensor_tensor(out=dacc[ab][0][:, :], in0=dacc[ab][0][:, :], in1=dacc[ab][1][:, :], op=ALU.add),
                          r=[("dacc", ab, 0), ("dacc", ab, 1)], w=[("dacc", ab, 0)])
                    S.pe(lambda e: e.matmul(acc_d[0:1, 0:512], lhsT=ones128[:, 0:1], rhs=dacc[ab][0][:, :], start=True, stop=True),
                         r=[("dacc", ab, 0), "ones128"], w=["accd"])
                    S.dve(lambda e: e.reciprocal(out=rdn[0:1, :], in_=acc_d[0:1, 0:512]), r=["accd"], w=["rdn", "rbc"])
                    S.pe(lambda e: e.matmul(pb[7][:, 0:512], lhsT=ones32[0:1, :], rhs=rdn[0:1, :], start=True, stop=True), r=["rdn", "ones32"], w=[("pb", 7)])
                    S.act(lambda e: e.activation(out=rbc[:, :], in_=pb[7][:, 0:512], func=AF.Copy), r=[("pb", 7)], w=["rbc", "rdn"])
                    S.dve(lambda e: e.tensor_tensor(out=rbc[:, :], in0=acc_o[:, 0:512], in1=rbc[:, :], op=ALU.mult), r=[("acc", ab), "rbc"], w=["rbc"])
                    S.dve(lambda e: e.tensor_tensor(out=yb_[:, :], in0=rbc[:, :], in1=zq[ab][:, :], op=ALU.mult), r=["rbc", ("zq", ab)], w=[("ytb", qi % 2)])
                    S.store(lambda e: e.dma_start(out=y_scr[h, :, q0:q0 + 512], in_=yb_[:, :]), r=[("ytb", qi % 2)], w=[("y_scr", h)])
            S.barrier()

    def dump(name, t, shape, dt, keys):
        d = nc.dram_tensor("dbg_" + name, list(shape), dt, kind="ExternalOutput").ap()
        S.store(lambda e: e.dma_start(out=d, in_=t), r=keys)

    setup()
    setup_mla()
    for s in range(NSEQ):
        S.marks.append(("start%d" % s, dict(S.cnt)))
        phase_p0(s)
        S.marks.append(("p0", dict(S.cnt)))
        phase_g1(s)
        S.marks.append(("g1", dict(S.cnt)))
        S.barrier()
        if debug == "g2":
            phase_g2(s, heads=[0])
            break
        if debug not in ("m2", "m1"):
            phase_g2(s)
        S.marks.append(("g2", dict(S.cnt)))
        phase_outproj(s, 0)
        S.marks.append(("op0", dict(S.cnt)))
        if debug == "l0":
            break
        phase_m1(s)
        S.marks.append(("m1", dict(S.cnt)))
        if debug == "m1":
            break
        if debug == "m2":
            phase_m2(s, heads=[0])
            break
        phase_m2(s)
        S.marks.append(("m2", dict(S.cnt)))
        phase_outproj(s, 1)
        S.marks.append(("op1", dict(S.cnt)))
    print("arena max", aoff.get("max"), "ninst", S.ninst, S.cnt, "sbuf left", nc.sbuf_bytes_remaining)
    nc._marks = S.marks
    S.finish()
    es.close()
    return nc


def _consts():
    ident = np.eye(128, dtype=np.float32)
    i = np.arange(64)
    U = (i[:, None] <= i[None, :]).astype(np.float32)
    Lo = (i[:, None] >= i[None, :]).astype(np.float32)
    Us = (i[:, None] < i[None, :]).astype(np.float32)
    Ls = (i[:, None] > i[None, :]).astype(np.float32)
    NEG = -30000.0
    masks = np.stack([U, Lo, Us, Ls, (1 - U) * NEG, (1 - Lo) * NEG], axis=1).astype(np.float32)
    pos = np.arange(LE, dtype=np.float64) - PAD
    inv = 10000.0 ** (-np.arange(0, 64, 2, dtype=np.float64) / 64)
    ang = pos[None, :] * inv[:, None]
    cos = np.concatenate([np.cos(ang), np.cos(ang)], 0)
    sin = np.concatenate([np.sin(ang), np.sin(ang)], 0)
    rope = np.stack([cos, sin], 1).astype(np.float32)
    rot = np.zeros((64, 64), np.float32)
    for m in range(32):
        rot[m + 32, m] = -1.0
        rot[m, m + 32] = 1.0
    return dict(c_ident=ident, c_masks=masks, c_rope=rope, c_rot=rot)


_NC_CACHE = {}


def _in_maps(inputs):
    allx = np.concatenate([np.asarray(inputs["x_prompt"]), np.asarray(inputs["x_sample"])], 0)
    seqs = [[0, 1], [2, 3], [4, 5], [6, 7], [8, 8], [9, 9], [10, 10], [11, 11]]
    common = dict(
        meta=np.asarray(inputs["meta_tokens"]), ln_g=np.asarray(inputs["ln_g"]),
        g_w_in=np.asarray(inputs["gdn_w_in"])[0], g_conv=np.asarray(inputs["gdn_conv_w"])[0],
        g_alog=np.asarray(inputs["gdn_a_log"])[0].reshape(16), g_dtb=np.asarray(inputs["gdn_dt_bias"])[0].reshape(16),
        g_on=np.asarray(inputs["gdn_o_norm_g"])[0], g_w_out=np.asarray(inputs["gdn_w_out"])[0],
        m_w_in=np.asarray(inputs["mla_w_in"])[0], m_qn=np.asarray(inputs["mla_q_norm_g"])[0],
        m_kvn=np.asarray(inputs["mla_kv_norm_g"])[0], m_wuq=np.asarray(inputs["mla_w_uq"])[0],
        m_wukv=np.asarray(inputs["mla_w_ukv"])[0], m_qg=np.asarray(inputs["mla_qk_q_g"])[0],
        m_kg=np.asarray(inputs["mla_qk_k_g"])[0], m_w_out=np.asarray(inputs["mla_w_out"])[0],
    )
    common = {k: np.ascontiguousarray(v, dtype=np.float32) for k, v in common.items()}
    common.update(_consts())
    maps = []
    for c in range(8):
        m = dict(common)
        m["xs"] = np.ascontiguousarray(allx[seqs[c]])
        maps.append(m)
    return maps, seqs


def kernel(**inputs):
    if "nc" not in _NC_CACHE:
        _NC_CACHE["nc"] = build()
    nc = _NC_CACHE["nc"]
    maps, seqs = _in_maps(inputs)
    res = run_bass_kernel_spmd(nc, maps, core_ids=list(range(8)))
    full = np.zeros((12, LX, D), np.float32)
    for c in range(8):
        o = res.results[c]["out"]
        full[seqs[c][0]] = o[0]
        if seqs[c][1] != seqs[c][0]:
            full[seqs[c][1]] = o[1]
    return full[:4], full[4:]
```

```python
import numpy as np
import ml_dtypes
import concourse.bass as bass
import concourse.mybir as mybir
from concourse.bass_utils import run_bass_kernel_spmd

F32 = mybir.dt.float32
BF16 = mybir.dt.bfloat16
ALU = mybir.AluOpType
AF = mybir.ActivationFunctionType

D = 1024
LX = 4096
NMETA = 16
PAD = 48
LE = 4160
NGR = 65
NSEQ = 2
EPS = 1e-6
COLBLKS = [(0, 64)] + [(64 + 512 * i, 512) for i in range(8)]
TOKTILES = [(0, 64)] + [(64 + 128 * i, 128) for i in range(32)]
GDN_IN = 6176
import os
SCANSTOP = int(os.environ.get('SCANSTOP', '9'))
DMA_K = 6
SAME_ENG_SYNC = True


def gkeys(name, c0, n):
    return [(name, g) for g in range(c0 // 64, (c0 + n + 63) // 64)]


class Sched:
    def __init__(self, nc, es):
        self.nc = nc
        self.eng = {"pe": nc.tensor, "dve": nc.vector, "act": nc.scalar, "pool": nc.gpsimd, "sp": nc.sync}
        self.semh = {}
        for e in self.eng:
            self.semh[(e,)] = es.enter_context(nc.semaphore("s_" + e))
        for q in ("sp", "pool", "act"):
            for s in range(DMA_K):
                self.semh[(q, "d", s)] = es.enter_context(nc.semaphore(f"d_{q}{s}"))
        self.cnt = {e: 0 for e in self.eng}
        self.dman = {q: 0 for q in ("sp", "pool", "act")}
        self.seen = {e: {} for e in self.eng}
        self.lastw = {}
        self.readers = {}
        self.ninst = 0
        self.marks = []

    def _wait(self, e, semk, val):
        if val <= 0 or self.seen[e].get(semk, 0) >= val:
            return
        self.eng[e].wait_ge(self.semh[semk], val)
        self.seen[e][semk] = val

    @staticmethod
    def canon(k):
        if isinstance(k, tuple) and k and isinstance(k[0], tuple) and k[0] and k[0][0] == "pb":
            return k[0]
        if k in ("ptb2", "ptx0", "ptx1"):
            return "ptb"
        if isinstance(k, tuple) and k and k[0] in ("ptk", "ptq"):
            return "ptb"
        if isinstance(k, tuple) and k and k[0] == "acc":
            return ("pb", 4 + 2 * k[1])
        if k == "accd":
            return ("pb", 5)
        return k

    def op(self, e, fn, reads=(), writes=(), dma=False):
        reads = [self.canon(k) for k in reads]
        writes = [self.canon(k) for k in writes]
        deps = {}
        for k in reads:
            t = self.lastw.get(k)
            if t is not None:
                deps[t[0]] = max(deps.get(t[0], 0), t[1])
        for k in writes:
            t = self.lastw.get(k)
            if t is not None:
                deps[t[0]] = max(deps.get(t[0], 0), t[1])
            for sk, v in self.readers.get(k, {}).items():
                deps[sk] = max(deps.get(sk, 0), v)
        for sk, v in deps.items():
            if sk == (e,) and (e == "pe" or not SAME_ENG_SYNC) and not dma:
                continue
            self._wait(e, sk, v)
        if dma:
            n = self.dman[e]
            slot = n % DMA_K
            sk = (e, "d", slot)
            self._wait(e, sk, 16 * (n // DMA_K))
            self.dman[e] = n + 1
            inst = fn(self.eng[e])
            inst.then_inc(self.semh[sk], 16)
            tok = (sk, 16 * (n // DMA_K + 1))
        else:
            fns = fn if isinstance(fn, (list, tuple)) else [fn]
            inst = None
            for f in fns:
                inst = f(self.eng[e])
                self.ninst += 1
            self.cnt[e] += 1
            inst.then_inc(self.semh[(e,)], 1)
            tok = ((e,), self.cnt[e])
        for k in reads:
            r = self.readers.setdefault(k, {})
            r[tok[0]] = max(r.get(tok[0], 0), tok[1])
        for k in writes:
            self.lastw[k] = tok
            self.readers[k] = {}
        return tok

    def pe(self, fn, r=(), w=()):
        return self.op("pe", fn, r, w)

    def dve(self, fn, r=(), w=()):
        return self.op("dve", fn, r, w)

    def act(self, fn, r=(), w=()):
        return self.op("act", fn, r, w)

    def pool(self, fn, r=(), w=()):
        return self.op("pool", fn, r, w)

    def load(self, fn, r=(), w=()):
        return self.op("sp", fn, r, w, dma=True)

    def store(self, fn, r=(), w=()):
        return self.op("pool", fn, r, w, dma=True)

    def barrier(self):
        for e in self.eng:
            for e2 in self.eng:
                if e2 != e:
                    self._wait(e, (e2,), self.cnt[e2])
            for q in self.dman:
                n = self.dman[q]
                for s in range(DMA_K):
                    if n > s:
                        last = ((n - 1 - s) // DMA_K) * DMA_K + s
                        self._wait(e, (q, "d", s), 16 * (last // DMA_K + 1))
        self.lastw = {}
        self.readers = {}

    def finish(self):
        self.barrier()


def build(debug=None):
    from contextlib import ExitStack
    nc = bass.Bass("TRN2", target_bir_lowering=False)
    es = ExitStack()

    def din(name, shape, dt=F32):
        return nc.dram_tensor(name, list(shape), dt, kind="ExternalInput").ap()

    xs = din("xs", [NSEQ, LX, D])
    meta = din("meta", [NMETA, D])
    ln_g = din("ln_g", [2, D])
    g_w_in = din("g_w_in", [D, GDN_IN])
    g_conv = din("g_conv", [5, 4096])
    g_alog = din("g_alog", [16])
    g_dtb = din("g_dtb", [16])
    g_on = din("g_on", [256])
    g_w_out = din("g_w_out", [2048, D])
    m_w_in = din("m_w_in", [D, 2880])
    m_qn = din("m_qn", [512])
    m_kvn = din("m_kvn", [256])
    m_wuq = din("m_wuq", [512, 3072])
    m_wukv = din("m_wukv", [256, 4096])
    m_qg = din("m_qg", [192])
    m_kg = din("m_kg", [192])
    m_w_out = din("m_w_out", [2048, D])
    c_ident = din("c_ident", [128, 128])
    c_masks = din("c_masks", [64, 6, 64])
    c_rope = din("c_rope", [64, 2, LE])
    c_rot = din("c_rot", [64, 64])
    out = nc.dram_tensor("out", [NSEQ, LX, D], F32, kind="ExternalOutput").ap()

    skind = "ExternalOutput" if debug else "Internal"

    def dscr(name, shape, dt):
        return nc.dram_tensor(name, list(shape), dt, kind=skind).ap()

    qk_scr = dscr("qk_scr", [16, 128, LE], BF16)
    v_scr = dscr("v_scr", [16, 128, LE], BF16)
    z_scr = dscr("z_scr", [16, 128, LE], BF16)
    y_scr = dscr("y_scr", [16, 128, LE], BF16)
    h1_scr = dscr("h1_scr", [LE, D], F32)

    S = Sched(nc, es)

    def sb(name, shape, dt=F32):
        return es.enter_context(nc.sbuf_tensor(name, list(shape), dt))

    def ps(name, shape, dt=F32):
        return es.enter_context(nc.psum_tensor(name, list(shape), dt))

    block = es.enter_context(nc.Block())

    ident = sb("ident", [128, 128])
    identb = sb("identb", [128, 128], BF16)
    onesb = sb("onesb", [128, 128], BF16)
    ones32 = sb("ones32", [64, 128])
    masks = sb("masks", [64, 6, 64])
    lng = sb("lng", [128, 2, 8])
    convw = sb("convw", [128, 5, 32])
    alog = sb("alog", [64, 16])
    dtb = sb("dtb", [64, 16])
    nea = sb("nea", [64, 16])
    gon = sb("gon", [64, 256])
    epsb = sb("epsb", [128, 1])

    pb = [ps(f"pb{i}", [128, 512]) for i in range(8) if i != 2]
    pb.insert(2, None)
    ptb = ps("ptb", [128, 1024], BF16)

    def setup():
        S.load(lambda e: e.dma_start(out=ident[:], in_=c_ident[:, :]), w=["ident"])
        S.load(lambda e: e.dma_start(out=masks[:], in_=c_masks[:, :, :]), w=["masks"])
        S.load(lambda e: e.dma_start(out=lng[:], in_=ln_g.rearrange("l (c p) -> p l c", p=128), allow_slow_non_contiguous=True), w=["lng"])
        for j in range(5):
            S.load(lambda e, j=j: e.dma_start(out=convw[:, j, :], in_=g_conv[j].rearrange("(c p) -> p c", p=128), allow_slow_non_contiguous=True), w=["convw"])
        S.load(lambda e: e.dma_start(out=alog[:], in_=g_alog.partition_broadcast(64)), w=["alog"])
        S.load(lambda e: e.dma_start(out=dtb[:], in_=g_dtb.partition_broadcast(64)), w=["dtb"])
        S.load(lambda e: e.dma_start(out=gon[:], in_=g_on.partition_broadcast(64)), w=["gon"])
        S.dve(lambda e: e.tensor_copy(out=identb[:], in_=ident[:]), r=["ident"], w=["identb"])
        S.dve(lambda e: e.memset(onesb[:], 1.0), w=["onesb"])
        S.dve(lambda e: e.memset(ones32[:], 1.0), w=["ones32"])
        S.dve(lambda e: e.memset(epsb[:], EPS), w=["epsb"])
        S.act(lambda e: e.activation(out=nea[:], in_=alog[:], func=AF.Exp), r=["alog"], w=["nea"])
        S.dve(lambda e: e.tensor_scalar(out=nea[:], in0=nea[:], scalar1=-1.0, scalar2=None, op0=ALU.mult), r=["nea"], w=["nea"])

    beta = sb("beta", [64, NGR, 2, 8])
    gg = sb("gg", [64, NGR, 2, 8])
    gc = sb("gc", [64, NGR, 2, 8])
    negc = sb("negc", [64, NGR, 2, 8])
    egc = sb("egc", [64, NGR, 2, 8])
    kds = sb("kds", [64, NGR, 2, 8])
    egl = sb("egl", [128, NGR, 2, 8])
    negm4 = sb("negm4", [64, 4, 2, 64])
    strict4 = sb("strict4", [64, 4, 2, 64])
    ARENA_BYTES = 171500
    arena = sb("arena", [128, ARENA_BYTES // 4])
    aoff = {"p": 0}

    def av(shape, dt=F32):
        n = 1
        for d_ in shape[1:]:
            n *= d_
        nb = (n * (2 if dt == BF16 else 4) + 3) // 4 * 4
        o = aoff["p"]
        aoff["p"] = o + nb
        aoff["max"] = max(aoff.get("max", 0), o + nb)
        assert aoff["p"] <= ARENA_BYTES, (aoff["p"], shape)
        v = arena[0:shape[0], o // 4:(o + nb) // 4]
        if dt == BF16:
            v = v.bitcast(BF16)
            if n % 2:
                v = v[:, 0:n]
        if len(shape) > 2:
            names = "abcd"[:len(shape) - 1]
            pat = "p (" + " ".join(names) + ") -> p " + " ".join(names)
            v = v.rearrange(pat, **{names[i]: shape[1 + i] for i in range(len(names))})
        return v

    def sbA(name, shape, dt=F32):
        return av(shape, dt)

    hnT = sbA("hnT", [128, 8, LE], BF16)
    xt = [sbA(f"xt{i}", [128, D]) for i in range(2)]
    xn = [sbA(f"xn{i}", [128, D], BF16) for i in range(2)]
    junk = sbA("junk", [128, D], BF16)
    ssb = [sbA(f"ss{i}", [128, 4]) for i in range(2)]

    aoff_after_p0 = aoff["p"]

    def norm_transpose(tt, src_ap, src_key, n, t0, layer):
        b = tt % 2
        ss = ssb[b]
        S.act(lambda e: e.activation(out=junk[0:n, :], in_=src_ap, func=AF.Square, accum_out=ss[0:n, 0:1]),
              r=[src_key], w=["junk", ("ss", b)])
        S.act(lambda e: e.activation(out=ss[0:n, 1:2], in_=ss[0:n, 0:1], func=AF.Sqrt, bias=epsb[0:n, 0:1], scale=1.0 / D),
              r=[("ss", b), "epsb"], w=[("ss", b)])
        S.dve(lambda e: e.reciprocal(out=ss[0:n, 2:3], in_=ss[0:n, 1:2]), r=[("ss", b)], w=[("ss", b)])
        S.act(lambda e: e.activation(out=xn[b][0:n, :], in_=src_ap, func=AF.Copy, scale=ss[0:n, 2:3]),
              r=[src_key, ("ss", b)], w=[("xn", b)])
        S.pe([(lambda e, c=c: e.transpose(out=ptb[:, c * 128:c * 128 + n], in_=xn[b][0:n, c * 128:(c + 1) * 128], identity=identb[0:n, 0:n]))
              for c in range(8)], r=[("xn", b), "identb"], w=["ptb"])
        pv = ptb[:, :].rearrange("p (c t) -> p c t", c=8)[:, :, 0:n]
        S.dve(lambda e: e.tensor_tensor(out=hnT[:, :, t0:t0 + n], in0=pv,
                                        in1=lng[:, layer, :].unsqueeze(2).to_broadcast([128, 8, n]), op=ALU.mult),
              r=["ptb", "lng"], w=gkeys("hnT", t0, n))

    def load_x_tile(s, tt):
        t0, n = TOKTILES[tt]
        b = tt % 2
        if tt == 0:
            S.dve(lambda e: e.memset(xt[b][0:64, :], 0.0), w=[("xt", b)])
            S.load(lambda e: e.dma_start(out=xt[b][PAD:64, :], in_=meta[:, :]), w=[("xt", b)])
        else:
            r0 = t0 - 64
            S.load(lambda e: e.dma_start(out=xt[b][0:n, :], in_=xs[s, r0:r0 + n, :]), w=[("xt", b)])

    def phase_p0(s):
        for tt, (t0, n) in enumerate(TOKTILES):
            load_x_tile(s, tt)
            norm_transpose(tt, xt[tt % 2][0:n, :], ("xt", tt % 2), n, t0, 0)

    wst = [sbA(f"wst{i}", [128, 8, 128]) for i in range(2)]
    wbf = [sbA(f"wbf{i}", [128, 8, 128], BF16) for i in range(2)]
    pre = [sbA(f"pre{i}", [128, LE + 4], BF16) for i in range(2)]
    acc = sbA("acc", [128, LE])
    sqb = sbA("sqb", [128, LE], BF16)
    obf = [sbA(f"obf{i}", [128, LE], BF16) for i in range(2)]
    rtmp = [sbA(f"rtmp{i}", [128, 512]) for i in range(2)]
    dg = [sbA(f"dg{i}", [128, 5, 128], BF16) for i in range(2)]
    gbraw = sbA("gbraw", [64, NGR, 32])
    wba_st = sbA("wba_st", [128, 8, 32])
    wba = sbA("wba", [128, 8, 32], BF16)

    w_in_v = g_w_in.rearrange("(c p) n -> p c n", p=128)
    mmrot = [0]

    def mmbank():
        mmrot[0] ^= 1
        return mmrot[0]

    def load_w(wv_ap, idx, ncol=128):
        b = idx % 2
        S.load(lambda e: e.dma_start(out=wst[b][:, :, 0:ncol], in_=wv_ap), w=[("wst", b)])
        S.pool(lambda e: e.tensor_copy(out=wbf[b][:, :, 0:ncol], in_=wst[b][:, :, 0:ncol]), r=[("wst", b)], w=[("wbf", b)])
        return wbf[b]

    def phase_g1(s):
        for b in range(2):
            S.dve(lambda e, b=b: e.memset(pre[b][:, 0:2], 0.0), w=[("pre", b)])
            S.dve(lambda e, b=b: e.memset(pre[b][:, LE + 2:LE + 4], 0.0), w=[("pre", b)])
        flist = range(48)
        if debug == "g1a":
            flist = [0, 16, 32]
        if debug == "g1b":
            flist = []
        for f in flist:
            wt = load_w(w_in_v[:, :, f * 128:(f + 1) * 128], f)
            wk = ("wbf", f % 2)
            pb_ = f % 2
            for (c0, w) in COLBLKS:
                bk = mmbank()
                S.pe([(lambda e, c=c: e.matmul(pb[bk][:, 0:w], lhsT=wt[:, c, :], rhs=hnT[:, c, c0:c0 + w], start=(c == 0), stop=(c == 7)))
                      for c in range(8)], r=[wk] + gkeys("hnT", c0, w), w=[("pb", bk)])
                if f < 32:
                    S.act(lambda e: e.activation(out=pre[pb_][:, 2 + c0:2 + c0 + w], in_=pb[bk][:, 0:w], func=AF.Copy),
                          r=[("pb", bk)], w=[("pre", pb_)])
                else:
                    ob = obf[f % 2]
                    S.act(lambda e: e.activation(out=ob[:, c0:c0 + w], in_=pb[bk][:, 0:w], func=AF.Silu),
                          r=[("pb", bk)], w=[("obf", f % 2)])
            if f >= 32:
                S.store(lambda e: e.dma_start(out=z_scr[f - 32, :, :], in_=obf[f % 2][:, :]), r=[("obf", f % 2)], w=[("z_scr", f - 32)])
                continue
            pr = pre[pb_]
            dgb = dg[f % 2]
            for j in range(5):
                S.pool(lambda e, j=j: e.tensor_scalar(out=dgb[:, j, :], in0=identb[:, :], scalar1=convw[:, j, f:f + 1], scalar2=None, op0=ALU.mult),
                       r=["identb", "convw"], w=[("dg", f % 2)])
            ob = obf[f % 2]
            ok = ("obf", f % 2)
            qscale = (128.0 ** -0.5) if f < 8 else 1.0

            def conv_gen(bi, c0, w, si):
                pC_, pN_ = (pb[3], pb[4]) if si == 0 else (pb[5], pb[6])
                kC_, kN_ = (("pb", 3), ("pb", 4)) if si == 0 else (("pb", 5), ("pb", 6))
                rt = rtmp[si]
                S.pe([(lambda e, j=j: e.matmul(pC_[:, 0:w], lhsT=dgb[:, j, :], rhs=pr[:, c0 + j:c0 + j + w], start=(j == 0), stop=(j == 4))) for j in range(5)],
                     r=[("pre", pb_), ("dg", f % 2)], w=[kC_])
                yield
                if f >= 16:
                    S.act(lambda e: e.activation(out=ob[:, c0:c0 + w], in_=pC_[:, 0:w], func=AF.Silu), r=[kC_], w=[ok])
                    return
                S.act(lambda e: e.activation(out=acc[:, c0:c0 + w], in_=pC_[:, 0:w], func=AF.Silu), r=[kC_], w=[("acc", bi)])
                yield
                S.dve(lambda e: e.tensor_tensor(out=sqb[:, c0:c0 + w], in0=acc[:, c0:c0 + w], in1=acc[:, c0:c0 + w], op=ALU.mult), r=[("acc", bi)], w=[("sqb", bi)])
                yield
                S.pe(lambda e: e.matmul(pN_[:, 0:w], lhsT=onesb[:, :], rhs=sqb[:, c0:c0 + w], start=True, stop=True), r=[("sqb", bi), "onesb"], w=[kN_])
                yield
                S.act(lambda e: e.activation(out=rt[:, 0:w], in_=pN_[:, 0:w], func=AF.Sqrt, bias=epsb[:, 0:1], scale=1.0), r=[kN_, "epsb"], w=[("rtmp", si)])
                yield
                S.dve(lambda e: e.reciprocal(out=rt[:, 0:w], in_=rt[:, 0:w]), r=[("rtmp", si)], w=[("rtmp", si)])
                yield
                S.dve(lambda e: e.scalar_tensor_tensor(out=ob[:, c0:c0 + w], in0=acc[:, c0:c0 + w], scalar=qscale, in1=rt[:, 0:w],
                                                       op0=ALU.mult, op1=ALU.mult), r=[("acc", bi), ("rtmp", si)], w=[ok])
                yield

            for bi0 in range(0, len(COLBLKS), 2):
                active = [conv_gen(bi0, COLBLKS[bi0][0], COLBLKS[bi0][1], 0)]
                if bi0 + 1 < len(COLBLKS):
                    active.append(conv_gen(bi0 + 1, COLBLKS[bi0 + 1][0], COLBLKS[bi0 + 1][1], 1))
                while active:
                    for g_ in list(active):
                        try:
                            next(g_)
                        except StopIteration:
                            active.remove(g_)
            S.dve(lambda e: e.memset(ob[:, 0:PAD], 0.0), w=[ok])
            if f >= 16:
                S.store(lambda e: e.dma_start(out=v_scr[f - 16, :, :], in_=ob[:, :]), r=[ok], w=[("v_scr", f - 16)])
            else:
                S.store(lambda e: e.dma_start(out=qk_scr[f, :, :], in_=ob[:, :]), r=[ok], w=[("qk_scr", f)])
        if debug == "g1a":
            return
        S.load(lambda e: e.dma_start(out=wba_st[:], in_=w_in_v[:, :, 6144:6176]), w=["wba_st"])
        S.pool(lambda e: e.tensor_copy(out=wba[:], in_=wba_st[:]), r=["wba_st"], w=["wba"])
        for g0 in range(0, NGR, 16):
            ng = min(16, NGR - g0)
            bk = mmbank()
            pv = pb[bk][0:64, :].rearrange("p (g n) -> p g n", n=32)
            for gi in range(ng):
                gr = g0 + gi
                S.pe([(lambda e, c=c: e.matmul(pv[:, gi, :], lhsT=hnT[:, c, gr * 64:(gr + 1) * 64], rhs=wba[:, c, :], start=(c == 0), stop=(c == 7)))
                      for c in range(8)], r=["wba", ("hnT", gr)], w=[("pb", bk)])
            S.act(lambda e: e.activation(out=gbraw[:, g0:g0 + ng, :], in_=pv[:, 0:ng, :], func=AF.Copy), r=[("pb", bk)], w=["gbraw"])
        import os
        stopat = int(os.environ.get("STOPAT", "99"))
        if stopat <= 1:
            return
        S.act(lambda e: e.activation(out=beta[:].rearrange("p g a b -> p g (a b)"), in_=gbraw[:, :, 0:16], func=AF.Sigmoid), r=["gbraw"], w=["beta"])
        ggf = gg[:].rearrange("p g a b -> p g (a b)")
        S.dve(lambda e: e.tensor_tensor(out=ggf, in0=gbraw[:, :, 16:32], in1=dtb[:].unsqueeze(1).to_broadcast([64, NGR, 16]), op=ALU.add),
              r=["gbraw", "dtb"], w=["gg"])
        S.act(lambda e: e.activation(out=ggf, in_=ggf, func=AF.Exp), r=["gg"], w=["gg"])
        S.act(lambda e: e.activation(out=ggf, in_=ggf, func=AF.Ln, bias=1.0, scale=1.0), r=["gg"], w=["gg"])
        S.dve(lambda e: e.tensor_tensor(out=ggf, in0=ggf, in1=nea[:].unsqueeze(1).to_broadcast([64, NGR, 16]), op=ALU.mult),
              r=["gg", "nea"], w=["gg"])
        if stopat <= 2:
            return
        S.dve(lambda e: e.memset(gg[0:PAD, 0, :, :], 0.0), w=["gg"])
        S.dve(lambda e: e.memset(beta[0:PAD, 0, :, :], 0.0), w=["beta"])
        if stopat <= 3:
            return
        for g0 in range(0, NGR, 32):
            ng = min(32, NGR - g0)
            bk = mmbank()
            pv = pb[bk][0:64, :].rearrange("p (g a b) -> p g a b", a=2, b=8)
            fns = []
            for gi in range(ng):
                gr = g0 + gi
                fns.append(lambda e, gi=gi, gr=gr: e.matmul(pv[:, gi, 0, :], lhsT=masks[:, 0, :], rhs=gg[:, gr, 0, :], start=True, stop=True))
                fns.append(lambda e, gi=gi, gr=gr: e.matmul(pv[:, gi, 1, :], lhsT=masks[:, 1, :], rhs=gg[:, gr, 1, :], start=True, stop=True))
            S.pe(fns, r=["gg", "masks"], w=[("pb", bk)])
            S.act(lambda e: e.activation(out=gc[:, g0:g0 + ng], in_=pv[:, 0:ng], func=AF.Copy), r=[("pb", bk)], w=["gc"])
            if stopat <= 4:
                continue
            bk2 = mmbank()
            pv2 = pb[bk2][:, :].rearrange("p (g a b) -> p g a b", a=2, b=8)
            fns = [(lambda e, gi=gi: e.matmul(pv2[:, gi].rearrange("p a b -> p (a b)"), lhsT=ones32[:, :],
                                               rhs=gg[:, g0 + gi].rearrange("p a b -> p (a b)"), start=True, stop=True)) for gi in range(ng)]
            S.pe(fns, r=["gg", "ones32"], w=[("pb", bk2)])
            if stopat <= 5:
                continue
            S.act(lambda e: e.activation(out=egl[:, g0:g0 + ng], in_=pv2[:, 0:ng], func=AF.Exp), r=[("pb", bk2)], w=["egl"])
            if stopat <= 6:
                continue
            S.act(lambda e: e.activation(out=kds[:, g0:g0 + ng], in_=pv2[0:64, 0:ng], func=AF.Copy), r=[("pb", bk2)], w=["kds"])
            S.dve(lambda e: e.tensor_tensor(out=kds[:, g0:g0 + ng].rearrange("p g a b -> p (g a b)"), in0=kds[:, g0:g0 + ng].rearrange("p g a b -> p (g a b)"),
                                            in1=gc[:, g0:g0 + ng].rearrange("p g a b -> p (g a b)"), op=ALU.subtract),
                  r=["gc", "kds"], w=["kds"])
        if stopat <= 7:
            return
        S.act(lambda e: e.activation(out=kds[:], in_=kds[:], func=AF.Exp), r=["kds"], w=["kds"])
        S.act(lambda e: e.activation(out=egc[:], in_=gc[:], func=AF.Exp), r=["gc"], w=["egc"])
        S.dve(lambda e: e.tensor_scalar(out=negc[:], in0=egc[:], scalar1=-1.0, scalar2=None, op0=ALU.mult), r=["egc"], w=["negc"])


    aoff_after_g1 = aoff["p"]
    aoff["p"] = 0
    qT = av([128, LE], BF16)
    kT = av([128, LE], BF16)
    vz = av([128, LE], BF16)
    vtok = av([64, NGR, 256], BF16)
    ob = av([64, NGR, 256], BF16)
    Rr = av([128, NGR, 2, 64], BF16)
    At = av([128, NGR, 2, 64], BF16)
    PSET = []
    for si_ in range(2):
        PSET.append(dict(rhsD=av([64, 4, 2, 64]), DTi=av([64, 4, 2, 64]), DTs=av([64, 4, 2, 64]), X4=av([64, 8, 64], BF16), XT=av([64, 8, 64], BF16),
                         Pa=[av([64, 8, 64], BF16) for _ in range(2)], PaT=[av([64, 8, 64], BF16) for _ in range(2)], R32=av([64, 8, 64]), Rb=av([64, 8, 64], BF16)))
    PSET[0].update(banks=(pb[0], pb[1], pb[3]), bkeys=(("pb", 0), ("pb", 1), ("pb", 3)), ptx=ptb[:, 512:1024], ptxk="ptx0")
    PSET[1].update(banks=(pb[4], pb[5], pb[6]), bkeys=(("pb", 4), ("pb", 5), ("pb", 6)), ptx=ptb[:, 0:512], ptxk="ptx1")
    S32 = [av([128, 256]) for _ in range(2)]
    Sbf = [av([128, 256], BF16) for _ in range(2)]
    xb = [av([128, 256], BF16) for _ in range(2)]
    vn = [av([128, 256], BF16) for _ in range(2)]
    kd = [av([128, 128], BF16) for _ in range(2)]
    t1 = [av([64, 256]) for _ in range(2)]
    ssq = av([64, NGR + 3])
    rstd = av([64, NGR + 3])
    junk2 = [av([64, 256], BF16) for _ in range(2)]
    yb = [av([128, 256], BF16) for _ in range(2)]

    def phase_g2(s, heads=range(8)):
        S.dve(lambda e: e.tensor_copy(out=negm4[:], in_=masks[:, 4:6, :].unsqueeze(1).to_broadcast([64, 4, 2, 64])), r=["masks"], w=["negm4"])
        S.dve(lambda e: e.tensor_copy(out=strict4[:], in_=masks[:, 2:4, :].unsqueeze(1).to_broadcast([64, 4, 2, 64])), r=["masks"], w=["strict4"])
        S.dve(lambda e: e.memset(Rr[64:128].rearrange("p a b c -> p (a b c)"), 0.0), w=["Rr"])
        S.dve(lambda e: e.memset(At[64:128].rearrange("p a b c -> p (a b c)"), 0.0), w=["At"])
        for d_ in range(2):
            S.dve(lambda e, d_=d_: e.memset(xb[d_][64:128, :], 0.0), w=[("xb", d_)])
            S.dve(lambda e, d_=d_: e.memset(vn[d_][64:128, :], 0.0), w=[("vn", d_)])
            S.dve(lambda e, d_=d_: e.memset(kd[d_][64:128, :], 0.0), w=[("kd", d_)])
        for h in heads:
            S.load(lambda e: e.dma_start(out=qT[:, :], in_=qk_scr[h, :, :]), r=[("qk_scr", h)], w=["qT"])
            S.load(lambda e: e.dma_start(out=kT[:, :], in_=qk_scr[8 + h, :, :]), r=[("qk_scr", 8 + h)], w=["kT"])
            for half in range(2):
                S.load(lambda e: e.dma_start(out=vz[:, :], in_=v_scr[2 * h + half, :, :]), r=[("v_scr", 2 * h + half)], w=["vz"])
                for g0 in range(0, NGR, 8):
                    ng = min(8, NGR - g0)
                    S.pe([(lambda e, gi=gi: e.transpose(out=ptb[0:64, gi * 128:(gi + 1) * 128], in_=vz[:, (g0 + gi) * 64:(g0 + gi + 1) * 64], identity=identb[:, :]))
                          for gi in range(ng)], r=["vz", "identb"], w=["ptb"])
                    pv = ptb[0:64, :].rearrange("p (g n) -> p g n", n=128)
                    S.act(lambda e: e.activation(out=vtok[:, g0:g0 + ng, half * 128:(half + 1) * 128], in_=pv[:, 0:ng, :], func=AF.Copy),
                          r=["ptb"], w=["vtok"])
            S.marks.append(("g2_h%d_load" % h, dict(S.cnt)))
            def prep_gen(c0, si):
                B = PSET[si]
                rhsD_, DTi_, DTs_, X4_, XT_, Pa_, PaT_, R32_, Rb_ = B["rhsD"], B["DTi"], B["DTs"], B["X4"], B["XT"], B["Pa"], B["PaT"], B["R32"], B["Rb"]
                dif_ = rhsD_
                pA, pB_, pC = B["banks"]
                kA, kB, kC = B["bkeys"]
                px = B["ptx"]
                kx = B["ptxk"]
                sfx = "_%d" % si
                nck = min(4, NGR - c0)
                nu = nck * 2
                W = nck * 128
                for d in range(2):
                    S.dve(lambda e, d=d: e.tensor_tensor(out=rhsD_[:, 0:nck, d, :], in0=gg[:, c0:c0 + nck, d, h:h + 1].to_broadcast([64, nck, 64]),
                                                         in1=masks[:, d:d + 1, :].to_broadcast([64, nck, 64]), op=ALU.mult),
                          r=["gg", "masks"], w=["rhsD" + sfx])
                yield
                S.pe([lambda e: e.matmul(pB_[0:64, 0:W], lhsT=ones32[:, 0:64], rhs=rhsD_[:, 0:nck].rearrange("p a b c -> p (a b c)"), start=True, stop=False),
                      lambda e: e.matmul(pB_[0:64, 0:W], lhsT=ident[0:64, 0:64], rhs=negm4[:, 0:nck].rearrange("p a b c -> p (a b c)"), start=False, stop=True)],
                     r=["rhsD" + sfx, "ones32", "ident", "negm4"], w=[kB])
                pK = pA[0:64, :].rearrange("p (t g n) -> p t g n", t=2, g=4)
                fns = []
                for ci in range(nck):
                    cs = slice((c0 + ci) * 64, (c0 + ci + 1) * 64)
                    fns.append(lambda e, ci=ci, cs=cs: e.matmul(pK[:, 0, ci, :], lhsT=kT[:, cs], rhs=kT[:, cs], start=True, stop=True))
                    fns.append(lambda e, ci=ci, cs=cs: e.matmul(pK[:, 1, ci, :], lhsT=kT[:, cs], rhs=qT[:, cs], start=True, stop=True))
                S.pe(fns, r=["kT", "qT"], w=[kA])
                yield
                S.act(lambda e: e.activation(out=dif_[:, 0:nck].rearrange("p a b c -> p (a b c)"), in_=pB_[0:64, 0:W], func=AF.Copy), r=[kB], w=["rhsD" + sfx])
                yield
                S.dve(lambda e: e.tensor_tensor(out=dif_[:, 0:nck].rearrange("p a b c -> p (a b) c"), in0=dif_[:, 0:nck].rearrange("p a b c -> p (a b) c"),
                                                in1=gc[:, c0:c0 + nck].rearrange("p g a b -> p (g a) b")[:, :, h:h + 1].to_broadcast([64, nu, 64]), op=ALU.subtract),
                      r=["rhsD" + sfx, "gc"], w=["rhsD" + sfx])
                yield
                S.act(lambda e: e.activation(out=DTi_[:, 0:nck].rearrange("p a b c -> p (a b c)"), in_=dif_[:, 0:nck].rearrange("p a b c -> p (a b c)"), func=AF.Exp),
                      r=["rhsD" + sfx], w=["DTi" + sfx])
                yield
                S.dve(lambda e: e.tensor_tensor(out=DTs_[:, 0:nck].rearrange("p a b c -> p (a b c)"), in0=DTi_[:, 0:nck].rearrange("p a b c -> p (a b c)"),
                                                in1=strict4[:, 0:nck].rearrange("p a b c -> p (a b c)"), op=ALU.mult), r=["DTi" + sfx, "strict4"], w=["DTs" + sfx])
                S.dve(lambda e: e.tensor_tensor(out=DTs_[:, 0:nck].rearrange("p a b c -> p (a b) c"), in0=DTs_[:, 0:nck].rearrange("p a b c -> p (a b) c"),
                                                in1=beta[:, c0:c0 + nck].rearrange("p g a b -> p (g a) b")[:, :, h:h + 1].to_broadcast([64, nu, 64]), op=ALU.mult),
                      r=["DTs" + sfx, "beta"], w=["DTs" + sfx])
                X44 = X4_[:, :, :].rearrange("p (g d) n -> p g d n", d=2)
                for d in range(2):
                    S.dve(lambda e, d=d: e.tensor_tensor(out=X44[:, 0:nck, d, :], in0=pK[:, 0, 0:nck, :], in1=DTs_[:, 0:nck, d, :], op=ALU.mult),
                          r=[kA, "DTs" + sfx], w=["X4" + sfx])
                yield
                S.pe([(lambda e, u=u: e.transpose(out=px[0:64, u * 64:(u + 1) * 64], in_=X4_[:, u, :], identity=identb[0:64, 0:64])) for u in range(nu)],
                     r=["X4" + sfx, "identb"], w=[kx, "ptb"])
                for d in range(2):
                    S.dve(lambda e, d=d: e.tensor_tensor(out=At[0:64, c0:c0 + nck, d, :], in0=pK[:, 1, 0:nck, :], in1=DTi_[:, 0:nck, d, :], op=ALU.mult),
                          r=[kA, "DTi" + sfx], w=["At"])
                S.dve(lambda e: e.tensor_tensor(out=R32_[:, 0:nu, :], in0=ident[0:64, 0:64].unsqueeze(1).to_broadcast([64, nu, 64]), in1=X4_[:, 0:nu, :], op=ALU.subtract),
                      r=["ident", "X4" + sfx], w=["R32" + sfx])
                S.pool(lambda e: e.tensor_copy(out=Rb_[:, 0:nu, :], in_=R32_[:, 0:nu, :]), r=["R32" + sfx], w=["Rb" + sfx])
                yield
                S.act(lambda e: e.activation(out=XT_[:, 0:nu, :].rearrange("p a b -> p (a b)"), in_=px[0:64, 0:nu * 64], func=AF.Copy), r=[kx], w=["XT" + sfx])
                yield
                P, PT, Pk, PTk = X4_, XT_, "X4" + sfx, "XT" + sfx
                for lvl in range(5):
                    nb = lvl % 2
                    last = (lvl == 4)
                    if not last:
                        S.pe([(lambda e, u=u, P=P, PT=PT: e.matmul(pA[0:64, u * 64:(u + 1) * 64], lhsT=PT[:, u, :], rhs=P[:, u, :], start=True, stop=True)) for u in range(nu)],
                             r=[Pk, PTk], w=[kA])
                    S.pe([(lambda e, u=u, P=P, PT=PT: e.matmul(pB_[0:64, u * 64:(u + 1) * 64], lhsT=P[:, u, :], rhs=PT[:, u, :], start=True, stop=True)) for u in range(nu)],
                         r=[Pk, PTk], w=[kB])
                    yield
                    if not last:
                        S.act(lambda e, nb=nb: e.activation(out=Pa_[nb][:, 0:nu, :].rearrange("p a b -> p (a b)"), in_=pA[0:64, 0:nu * 64], func=AF.Copy),
                              r=[kA], w=[("Pa" + sfx, nb)])
                    S.dve(lambda e, nb=nb: e.tensor_copy(out=PaT_[nb][:, 0:nu, :].rearrange("p a b -> p (a b)"), in_=pB_[0:64, 0:nu * 64]),
                          r=[kB], w=[("PaT" + sfx, nb)])
                    yield
                    S.pe([(lambda e, u=u, nb=nb: e.matmul(pC[0:64, u * 64:(u + 1) * 64], lhsT=PaT_[nb][:, u, :], rhs=Rb_[:, u, :], start=True, stop=True)) for u in range(nu)],
                         r=[("PaT" + sfx, nb), "Rb" + sfx], w=[kC])
                    yield
                    if not last:
                        S.dve(lambda e: e.tensor_tensor(out=R32_[:, 0:nu, :].rearrange("p a b -> p (a b)"), in0=pC[0:64, 0:nu * 64],
                                                        in1=R32_[:, 0:nu, :].rearrange("p a b -> p (a b)"), op=ALU.add), r=[kC, "R32" + sfx], w=["R32" + sfx])
                        S.pool(lambda e: e.tensor_copy(out=Rb_[:, 0:nu, :], in_=R32_[:, 0:nu, :]), r=["R32" + sfx], w=["Rb" + sfx])
                    else:
                        S.dve(lambda e: e.tensor_tensor(out=Rr[0:64, c0:c0 + nck].rearrange("p a b c -> p (a b c)"), in0=pC[0:64, 0:nu * 64],
                                                        in1=R32_[:, 0:nu, :].rearrange("p a b -> p (a b)"), op=ALU.add), r=[kC, "R32" + sfx], w=["Rr"])
                    yield
                    P, PT, Pk, PTk = Pa_[nb], PaT_[nb], ("Pa" + sfx, nb), ("PaT" + sfx, nb)

            glist = list(range(0, NGR, 4))
            for gi0 in range(0, len(glist), 2):
                active = [prep_gen(glist[gi0], 0)]
                if gi0 + 1 < len(glist):
                    active.append(prep_gen(glist[gi0 + 1], 1))
                while active:
                    for g_ in list(active):
                        try:
                            next(g_)
                        except StopIteration:
                            active.remove(g_)
            S.barrier()
            S.marks.append(("g2_h%d_prep" % h, dict(S.cnt)))
            for d in range(2):
                S.dve(lambda e, d=d: e.memset(S32[d][:, :], 0.0), w=[("S32", d)])
                S.dve(lambda e, d=d: e.memset(Sbf[d][:, :], 0.0), w=[("Sbf", d)])
            for t in range(NGR):
                cc = [t, NGR - 1 - t]
                pSs = [pb[4], pb[6]]
                pOs = [pb[5], pb[7]]
                kSs = [("pb", 4), ("pb", 6)]
                kOs = [("pb", 5), ("pb", 7)]
                css = [slice(cc[d] * 64, (cc[d] + 1) * 64) for d in range(2)]
                for d in range(2):
                    S.pe(lambda e: e.matmul(pSs[d][0:64, 0:256], lhsT=kT[:, css[d]], rhs=Sbf[d][:, :], start=True, stop=True), r=["kT", ("Sbf", d)], w=[kSs[d]])
                for d in range(2):
                    S.pe(lambda e: e.transpose(out=ptb[0:64, d * 128:(d + 1) * 128], in_=kT[:, css[d]], identity=identb[:, :]), r=["kT", "identb"], w=["ptb"])
                for d in range(2):
                    S.pe(lambda e: e.matmul(pOs[d][0:64, 0:256], lhsT=qT[:, css[d]], rhs=Sbf[d][:, :], start=True, stop=True), r=["qT", ("Sbf", d)], w=[kOs[d]])
                for d in range(2):
                    c = cc[d]
                    S.dve(lambda e: e.scalar_tensor_tensor(out=xb[d][0:64, :], in0=pSs[d][0:64, 0:256], scalar=negc[:, c, d, h:h + 1], in1=vtok[:, c, :],
                                                           op0=ALU.mult, op1=ALU.add), r=[kSs[d], "negc", "vtok"], w=[("xb", d)])
                for d in range(2):
                    c = cc[d]
                    S.act(lambda e: e.activation(out=kd[d][0:64, :], in_=ptb[0:64, d * 128:(d + 1) * 128], func=AF.Copy, scale=kds[:, c, d, h:h + 1]),
                          r=["ptb", "kds"], w=[("kd", d)])
                for d in range(2):
                    c = cc[d]
                    S.pe(lambda e: e.matmul(pSs[d][0:64, 256:512], lhsT=Rr[:, c, d, :], rhs=xb[d][:, :], start=True, stop=True), r=["Rr", ("xb", d)], w=[kSs[d]])
                for d in range(2):
                    c = cc[d]
                    S.act(lambda e: e.activation(out=t1[d][:, :], in_=pOs[d][0:64, 0:256], func=AF.Copy, scale=egc[:, c, d, h:h + 1]),
                          r=[kOs[d], "egc"], w=[("t1", d)])
                for d in range(2):
                    c = cc[d]
                    S.act(lambda e: e.activation(out=vn[d][0:64, :], in_=pSs[d][0:64, 256:512], func=AF.Copy, scale=beta[:, c, d, h:h + 1]),
                          r=[kSs[d], "beta"], w=[("vn", d)])
                for d in range(2):
                    c = cc[d]
                    S.pe(lambda e: e.matmul(pSs[d][:, 0:256], lhsT=kd[d][:, :], rhs=vn[d][:, :], start=True, stop=True), r=[("kd", d), ("vn", d)], w=[kSs[d]])
                for d in range(2):
                    c = cc[d]
                    S.pe(lambda e: e.matmul(pOs[d][0:64, 256:512], lhsT=At[:, c, d, :], rhs=vn[d][:, :], start=True, stop=True), r=["At", ("vn", d)], w=[kOs[d]])
                for d in range(2):
                    c = cc[d]
                    S.dve(lambda e: e.scalar_tensor_tensor(out=Sbf[d][:, :], in0=S32[d][:, :], scalar=egl[:, c, d, h:h + 1], in1=pSs[d][:, 0:256],
                                                           op0=ALU.mult, op1=ALU.add), r=[("S32", d), "egl", kSs[d]], w=[("Sbf", d)])
                for d in range(2):
                    c = cc[d]
                    if t < 32 or (t == 32 and d == 0):
                        S.dve(lambda e: e.tensor_tensor(out=ob[:, c, :], in0=pOs[d][0:64, 256:512], in1=t1[d][:, :], op=ALU.add), r=[kOs[d], ("t1", d)], w=[("ob", c)])
                    else:
                        S.dve(lambda e: e.tensor_tensor(out=t1[d][:, :], in0=pOs[d][0:64, 256:512], in1=t1[d][:, :], op=ALU.add), r=[kOs[d], ("t1", d)], w=[("t1", d)])
                        S.dve(lambda e: e.tensor_tensor(out=ob[:, c, :], in0=ob[:, c, :], in1=t1[d][:, :], op=ALU.add), r=[("ob", c), ("t1", d)], w=[("ob", c)])
                for d in range(2):
                    c = cc[d]
                    S.dve(lambda e: e.scalar_tensor_tensor(out=S32[d][:, :], in0=S32[d][:, :], scalar=egl[:, c, d, h:h + 1], in1=pSs[d][:, 0:256],
                                                           op0=ALU.mult, op1=ALU.add), r=[("S32", d), "egl", kSs[d]], w=[("S32", d)])
            S.marks.append(("g2_h%d_scan" % h, dict(S.cnt)))
            S.marks.append(("g2_h%d_scan" % h, dict(S.cnt)))
            for c in range(NGR):
                S.act(lambda e, c=c: e.activation(out=junk2[c % 2][:, :], in_=ob[:, c, :], func=AF.Square, accum_out=ssq[:, c:c + 1]), r=[("ob", c)], w=[("junk2", c % 2), ("ssq", c)])
            S.act(lambda e: e.activation(out=rstd[:, 0:NGR], in_=ssq[:, 0:NGR], func=AF.Sqrt, bias=epsb[0:64, 0:1], scale=1.0 / 256), r=[("ssq", c_) for c_ in range(NGR)] + ["epsb"], w=["rstd"])
            S.dve(lambda e: e.reciprocal(out=rstd[:, 0:NGR], in_=rstd[:, 0:NGR]), r=["rstd"], w=["rstd"])
            for c in range(NGR):
                S.dve(lambda e, c=c: e.scalar_tensor_tensor(out=ob[:, c, :], in0=ob[:, c, :], scalar=rstd[:, c:c + 1], in1=gon[:, :], op0=ALU.mult, op1=ALU.mult),
                      r=[("ob", c), "rstd", "gon"], w=[("ob", c)])
            for half in range(2):
                S.load(lambda e: e.dma_start(out=vz[:, :], in_=z_scr[2 * h + half, :, :]), r=[("z_scr", 2 * h + half)], w=["vz"])
                for gi_, c0 in enumerate(range(0, NGR, 4)):
                    nck = min(4, NGR - c0)
                    ybb = yb[gi_ % 2]
                    yk = ("yb", gi_ % 2)
                    S.pe([(lambda e, ci=ci: e.transpose(out=ptb[:, 256 + ci * 64:256 + (ci + 1) * 64], in_=ob[:, c0 + ci, half * 128:(half + 1) * 128], identity=identb[0:64, 0:64]))
                          for ci in range(nck)], r=[("ob", c0 + ci) for ci in range(nck)] + ["identb"], w=["ptb2"])
                    S.dve(lambda e: e.tensor_tensor(out=ybb[:, 0:nck * 64], in0=ptb[:, 256:256 + nck * 64], in1=vz[:, c0 * 64:(c0 + nck) * 64], op=ALU.mult),
                          r=["ptb2", "vz"], w=[yk])
                    S.store(lambda e: e.dma_start(out=y_scr[2 * h + half, :, c0 * 64:(c0 + nck) * 64], in_=ybb[:, 0:nck * 64]), r=[yk], w=[("y_scr", 2 * h + half)])
            S.barrier()

    aoff_g2 = aoff["p"]
    aoff["p"] = aoff_after_p0
    wo = av([128, 16, D], BF16)
    wo_st = [av([128, D]) for _ in range(2)]
    ytile = [av([128, 16, 128], BF16) for _ in range(2)]
    h1t = [av([128, D]) for _ in range(2)]

    def load_wout(w_ap):
        wv = w_ap.rearrange("(c p) n -> p c n", p=128)
        for c in range(16):
            b = c % 2
            S.load(lambda e, c=c, b=b: e.dma_start(out=wo_st[b][:, :], in_=wv[:, c, :]), w=[("wo_st", b)])
            S.pool(lambda e, c=c, b=b: e.tensor_copy(out=wo[:, c, :], in_=wo_st[b][:, :]), r=[("wo_st", b)], w=["wo"])

    def phase_outproj(s, layer):
        load_wout(g_w_out if layer == 0 else m_w_out)
        yv = y_scr.rearrange("c p t -> p c t")
        for tt, (t0, n) in enumerate(TOKTILES):
            if layer == 1 and tt == 0:
                continue
            b = tt % 2
            S.load(lambda e: e.dma_start(out=ytile[b][:, :, 0:n], in_=yv[:, :, t0:t0 + n]), r=[("y_scr", c) for c in range(16)], w=[("ytile", b)])
            if layer == 0:
                load_x_tile(s, tt)
            else:
                S.load(lambda e: e.dma_start(out=xt[b][0:n, :], in_=h1_scr[t0:t0 + n, :]), r=["h1_scr"], w=[("xt", b)])
            for hf in range(2):
                bk = mmbank()
                S.pe([(lambda e, c=c: e.matmul(pb[bk][0:n, 0:512], lhsT=ytile[b][:, c, 0:n], rhs=wo[:, c, hf * 512:(hf + 1) * 512], start=(c == 0), stop=(c == 15)))
                      for c in range(16)], r=[("ytile", b), "wo"], w=[("pb", bk)])
                S.dve(lambda e: e.tensor_tensor(out=h1t[b][0:n, hf * 512:(hf + 1) * 512], in0=pb[bk][0:n, 0:512], in1=xt[b][0:n, hf * 512:(hf + 1) * 512], op=ALU.add),
                      r=[("pb", bk), ("xt", b)], w=[("h1t", b)])
            if layer == 0:
                S.store(lambda e: e.dma_start(out=h1_scr[t0:t0 + n, :], in_=h1t[b][0:n, :]), r=[("h1t", b)], w=["h1_scr"])
                norm_transpose(tt, h1t[b][0:n, :], ("h1t", b), n, t0, 1)
            else:
                r0 = t0 - 64
                S.store(lambda e: e.dma_start(out=out[s, r0:r0 + n, :], in_=h1t[b][0:n, :]), r=[("h1t", b)], w=["out"])
        S.barrier()


    aoff["p"] = aoff_after_p0
    wM = av([128, 8, 832], BF16)
    wMst = [av([128, 8, 128]) for _ in range(2)]
    wMz = [av([128, 8, 128], BF16) for _ in range(2)]
    raw = av([128, 7, 512])
    sqt = av([128, 7, 512], BF16)
    rs1 = [av([128, 512]) for _ in range(2)]
    o1 = [av([128, 7, 512], BF16) for _ in range(2)]
    kpg = av([64, 512], BF16)
    tmpa = av([64, 512])
    tmpb = av([64, 512])
    zo = [av([128, 512], BF16) for _ in range(2)]
    ropeb1 = av([64, 2, 512])
    gq = sb("gq", [128, 4])
    gkv = sb("gkv", [128, 2])
    gqa = sb("gqa", [128, 1])
    gqb = sb("gqb", [64, 1])
    gka = sb("gka", [128, 1])
    gkb = sb("gkb", [64, 1])
    rotb = sb("rotb", [64, 64], BF16)
    rot_st = sb("rot_st", [64, 64])
    nshift = sb("nshift", [128, 1])
    m_w_in_v = m_w_in.rearrange("(c p) n -> p c n", p=128)

    def setup_mla():
        S.load(lambda e: e.dma_start(out=gq[:], in_=m_qn.rearrange("(c p) -> p c", p=128), allow_slow_non_contiguous=True), w=["gq"])
        S.load(lambda e: e.dma_start(out=gkv[:], in_=m_kvn.rearrange("(c p) -> p c", p=128), allow_slow_non_contiguous=True), w=["gkv"])
        S.load(lambda e: e.dma_start(out=gqa[:], in_=m_qg[0:128].rearrange("(p c) -> p c", c=1)), w=["gqa"])
        S.load(lambda e: e.dma_start(out=gqb[:], in_=m_qg[128:192].rearrange("(p c) -> p c", c=1)), w=["gqb"])
        S.load(lambda e: e.dma_start(out=gka[:], in_=m_kg[0:128].rearrange("(p c) -> p c", c=1)), w=["gka"])
        S.load(lambda e: e.dma_start(out=gkb[:], in_=m_kg[128:192].rearrange("(p c) -> p c", c=1)), w=["gkb"])
        S.load(lambda e: e.dma_start(out=rot_st[:], in_=c_rot[:, :]), w=["rot_st"])
        S.dve(lambda e: e.tensor_copy(out=rotb[:], in_=rot_st[:]), r=["rot_st"], w=["rotb"])
        S.dve(lambda e: e.memset(nshift[:], -8.0), w=["nshift"])

    def rstd_from_psum(pbank, npart, w, div, dst):
        S.act(lambda e: e.activation(out=dst[0:npart, 0:w], in_=pbank[0:npart, 0:w], func=AF.Sqrt, bias=epsb[0:npart, 0:1], scale=1.0 / div),
              r=[("pb", 3), "epsb"], w=[("rs", id(dst))])
        S.dve(lambda e: e.reciprocal(out=dst[0:npart, 0:w], in_=dst[0:npart, 0:w]), r=[("rs", id(dst))], w=[("rs", id(dst))])

    def phase_m1(s):
        for f in range(7):
            b = f % 2
            nc_ = 128 if f < 6 else 64
            S.load(lambda e, f=f, b=b, nc_=nc_: e.dma_start(out=wMst[b][:, :, 0:nc_], in_=m_w_in_v[:, :, f * 128:f * 128 + nc_]), w=[("wMst", b)])
            S.pool(lambda e, f=f, b=b, nc_=nc_: e.tensor_copy(out=wM[:, :, f * 128:f * 128 + nc_], in_=wMst[b][:, :, 0:nc_]), r=[("wMst", b)], w=["wM"])
        for bi, (c0, w) in enumerate(COLBLKS):
            ob_ = o1[bi % 2]
            ok = ("o1", bi % 2)
            for f in range(7):
                nr = 128 if f < 6 else 64
                bk = mmbank()
                S.pe([(lambda e, c=c: e.matmul(pb[bk][0:nr, 0:w], lhsT=wM[:, c, f * 128:f * 128 + nr], rhs=hnT[:, c, c0:c0 + w], start=(c == 0), stop=(c == 7)))
                      for c in range(8)], r=["wM"] + gkeys("hnT", c0, w), w=[("pb", bk)])
                S.act(lambda e: e.activation(out=raw[0:nr, f, 0:w], in_=pb[bk][0:nr, 0:w], func=AF.Copy), r=[("pb", bk)], w=[("raw", f)])
                S.dve(lambda e: e.tensor_tensor(out=sqt[0:nr, f, 0:w], in0=raw[0:nr, f, 0:w], in1=raw[0:nr, f, 0:w], op=ALU.mult), r=[("raw", f)], w=[("sqt", f)])
            for (fl, div, gt, ri) in (([0, 1, 2, 3], 512.0, gq, 0), ([4, 5], 256.0, gkv, 1)):
                S.pe([(lambda e, i=i, f=f: e.matmul(pb[3][:, 0:w], lhsT=onesb[:, :], rhs=sqt[:, f, 0:w], start=(i == 0), stop=(i == len(fl) - 1)))
                      for i, f in enumerate(fl)], r=[("sqt", f) for f in fl] + ["onesb"], w=[("pb", 3)])
                rstd_from_psum(pb[3], 128, w, div, rs1[ri])
                for i, f in enumerate(fl):
                    S.dve(lambda e, i=i, f=f: e.scalar_tensor_tensor(out=ob_[:, f, 0:w], in0=raw[:, f, 0:w], scalar=gt[:, i:i + 1], in1=rs1[ri][:, 0:w],
                                                                     op0=ALU.mult, op1=ALU.mult), r=[("raw", f), ("rs", id(rs1[ri]))], w=[ok])
            S.dve(lambda e: e.tensor_scalar(out=kpg[:, 0:w], in0=raw[0:64, 6, 0:w], scalar1=gkb[:, 0:1], scalar2=None, op0=ALU.mult), r=[("raw", 6), "gkb"], w=["kpg"])
            S.pe(lambda e: e.matmul(pb[3][0:64, 0:w], lhsT=rotb[:, :], rhs=kpg[:, 0:w], start=True, stop=True), r=["kpg", "rotb"], w=[("pb", 3)])
            S.load(lambda e: e.dma_start(out=ropeb1[:, :, 0:w], in_=c_rope[:, :, c0:c0 + w]), w=["ropeb1"])
            S.dve(lambda e: e.tensor_tensor(out=tmpa[:, 0:w], in0=pb[3][0:64, 0:w], in1=ropeb1[:, 1, 0:w], op=ALU.mult), r=[("pb", 3), "ropeb1"], w=["tmpa"])
            S.dve(lambda e: e.tensor_tensor(out=tmpb[:, 0:w], in0=kpg[:, 0:w], in1=ropeb1[:, 0, 0:w], op=ALU.mult), r=["kpg", "ropeb1"], w=["tmpb"])
            S.dve(lambda e: e.tensor_tensor(out=ob_[0:64, 6, 0:w], in0=tmpa[:, 0:w], in1=tmpb[:, 0:w], op=ALU.add), r=["tmpa", "tmpb"], w=[ok])
            for f in range(6):
                S.store(lambda e, f=f: e.dma_start(out=qk_scr[f, :, c0:c0 + w], in_=ob_[:, f, 0:w]), r=[ok], w=[("qk_scr", f)])
            S.store(lambda e: e.dma_start(out=qk_scr[6, 0:64, c0:c0 + w], in_=ob_[0:64, 6, 0:w]), r=[ok], w=[("qk_scr", 6)])
            S.store(lambda e: e.dma_start(out=qk_scr[7, 0:64, c0:c0 + w], in_=sqt[0:64, 6, 0:w]), r=[("sqt", 6)], w=[("qk_scr", 7)])
        for hh in range(16):
            b = hh % 2
            S.load(lambda e, hh=hh, b=b: e.dma_start(out=wMst[b][:, :, :], in_=m_w_in_v[:, :, 832 + hh * 128:832 + (hh + 1) * 128]), w=[("wMst", b)])
            S.pool(lambda e, b=b: e.tensor_copy(out=wMz[b][:, :, :], in_=wMst[b][:, :, :]), r=[("wMst", b)], w=[("wMz", b)])
            for bi, (c0, w) in enumerate(COLBLKS):
                bk = mmbank()
                zb = zo[bi % 2]
                S.pe([(lambda e, c=c: e.matmul(pb[bk][:, 0:w], lhsT=wMz[b][:, c, :], rhs=hnT[:, c, c0:c0 + w], start=(c == 0), stop=(c == 7)))
                      for c in range(8)], r=[("wMz", b)] + gkeys("hnT", c0, w), w=[("pb", bk)])
                S.act(lambda e: e.activation(out=zb[:, 0:w], in_=pb[bk][:, 0:w], func=AF.Silu), r=[("pb", bk)], w=[("zo", bi % 2)])
                S.store(lambda e: e.dma_start(out=z_scr[hh, :, c0:c0 + w], in_=zb[:, 0:w]), r=[("zo", bi % 2)], w=[("z_scr", hh)])
        S.barrier()

    aoff["p"] = 0
    cqT = av([128, 4, LE], BF16)
    ckvT = av([128, 2, LE], BF16)
    krT = av([64, LE], BF16)
    sqk = av([128, LE], BF16)
    qTa = av([128, LE], BF16)
    qTb = av([128, LE], BF16)
    kTa = av([128, LE], BF16)
    kTb = av([128, LE], BF16)
    zq = [av([128, 512], BF16) for _ in range(2)]
    vaug = av([128, 33, 130], BF16)
    wq_st = av([128, 4, 192])
    wq = av([128, 4, 192], BF16)
    wkv_st = av([128, 2, 256])
    wkv = av([128, 2, 256], BF16)
    MSET = []
    for si_ in range(2):
        MSET.append(dict(ra=av([128, 512]), rb=av([64, 512]), sqa=av([128, 512], BF16), sqb2=av([128, 512], BF16), rsq=av([128, 512]),
                         qbg=av([64, 512], BF16), t2a=av([64, 512], BF16), t2b=av([64, 512], BF16), rope=av([64, 2, 512])))
        MSET[-1]["rsk"] = MSET[-1]["rsq"]
    MSET[0].update(banks=(pb[0], pb[3]), bkeys=(("pb", 0), ("pb", 3)))
    MSET[1].update(banks=(pb[1], pb[7]), bkeys=(("pb", 1), ("pb", 7)))
    pT = [av([128, 512], BF16) for _ in range(3)]
    rdn = av([1, 512])
    dacc = [[av([128, 512]) for _ in range(2)] for _ in range(2)]
    ones128 = av([128, 1])
    rbc = av([128, 512])
    ytb = [av([128, 512], BF16) for _ in range(2)]
    KT = [(64 + 128 * i, 128) for i in range(32)] + [(PAD, 16)]
    SCALE = 192.0 ** -0.5
    wuq_v = m_wuq.rearrange("(c p) n -> p c n", p=128)
    wukv_v = m_wukv.rearrange("(c p) n -> p c n", p=128)

    def phase_m2(s, heads=range(16)):
        for f in range(4):
            S.load(lambda e, f=f: e.dma_start(out=cqT[:, f, :], in_=qk_scr[f, :, :]), r=[("qk_scr", f)], w=["cqT"])
        for f in range(2):
            S.load(lambda e, f=f: e.dma_start(out=ckvT[:, f, :], in_=qk_scr[4 + f, :, :]), r=[("qk_scr", 4 + f)], w=["ckvT"])
        S.load(lambda e: e.dma_start(out=krT[:, :], in_=qk_scr[6, 0:64, :]), r=[("qk_scr", 6)], w=["krT"])
        S.load(lambda e: e.dma_start(out=sqk[0:64, :], in_=qk_scr[7, 0:64, :]), r=[("qk_scr", 7)], w=["sqk"])
        for t_, k_ in ((sqk, "sqk"), (qTb, "qTb"), (kTb, "kTb"), (MSET[0]["sqb2"], "sqb2_m0"), (MSET[1]["sqb2"], "sqb2_m1")):
            S.dve(lambda e, t_=t_: e.memset(t_[64:128, :], 0.0), w=[k_])
        S.dve(lambda e: e.memset(ones128[:, :], 1.0), w=["ones128"])
        for h in heads:
            S.marks.append(("m2_h%d_start" % h, dict(S.cnt)))
            S.load(lambda e: e.dma_start(out=wq_st[:], in_=wuq_v[:, :, h * 192:(h + 1) * 192]), w=["wq_st"])
            S.pool(lambda e: e.tensor_copy(out=wq[:], in_=wq_st[:]), r=["wq_st"], w=["wq"])
            S.load(lambda e: e.dma_start(out=wkv_st[:], in_=wukv_v[:, :, h * 256:(h + 1) * 256]), w=["wkv_st"])
            S.pool(lambda e: e.tensor_copy(out=wkv[:], in_=wkv_st[:]), r=["wkv_st"], w=["wkv"])
            def prol_gen(c0, w, si):
                Bf = MSET[si]
                ra_, rb_, sqa_, sqb2_, rsq_, rsk_, qbg_, t2a_, t2b_, rope_ = (Bf[k_] for k_ in ("ra", "rb", "sqa", "sqb2", "rsq", "rsk", "qbg", "t2a", "t2b", "rope"))
                pP, pQ = Bf["banks"]
                kP, kQ = Bf["bkeys"]
                x_ = "_m%d" % si
                cs = slice(c0, c0 + w)
                S.load(lambda e: e.dma_start(out=rope_[:, :, 0:w], in_=c_rope[:, :, cs]), w=["rope" + x_])
                S.pe([(lambda e, c=c: e.matmul(pP[:, 0:w], lhsT=wq[:, c, 0:128], rhs=cqT[:, c, cs], start=(c == 0), stop=(c == 3))) for c in range(4)],
                     r=["wq", "cqT"], w=[kP])
                yield
                S.act(lambda e: e.activation(out=ra_[:, 0:w], in_=pP[:, 0:w], func=AF.Copy), r=[kP], w=["ra" + x_])
                yield
                S.pe([(lambda e, c=c: e.matmul(pP[0:64, 0:w], lhsT=wq[:, c, 128:192], rhs=cqT[:, c, cs], start=(c == 0), stop=(c == 3))) for c in range(4)],
                     r=["wq", "cqT"], w=[kP])
                S.dve(lambda e: e.tensor_tensor(out=sqa_[:, 0:w], in0=ra_[:, 0:w], in1=ra_[:, 0:w], op=ALU.mult), r=["ra" + x_], w=["sqa" + x_])
                yield
                S.act(lambda e: e.activation(out=rb_[:, 0:w], in_=pP[0:64, 0:w], func=AF.Copy), r=[kP], w=["rb" + x_])
                yield
                S.dve(lambda e: e.tensor_tensor(out=sqb2_[0:64, 0:w], in0=rb_[:, 0:w], in1=rb_[:, 0:w], op=ALU.mult), r=["rb" + x_], w=["sqb2" + x_])
                yield
                S.pe([lambda e: e.matmul(pQ[:, 0:w], lhsT=onesb[:, :], rhs=sqa_[:, 0:w], start=True, stop=False),
                      lambda e: e.matmul(pQ[:, 0:w], lhsT=onesb[:, :], rhs=sqb2_[:, 0:w], start=False, stop=True)], r=["sqa" + x_, "sqb2" + x_, "onesb"], w=[kQ])
                S.pe([(lambda e, c=c: e.matmul(pP[:, 0:w], lhsT=wkv[:, c, 0:128], rhs=ckvT[:, c, cs], start=(c == 0), stop=(c == 1))) for c in range(2)],
                     r=["wkv", "ckvT"], w=[kP])
                yield
                S.act(lambda e: e.activation(out=rsq_[:, 0:w], in_=pQ[:, 0:w], func=AF.Sqrt, bias=epsb[:, 0:1], scale=1.0 / 192.0), r=[kQ, "epsb"], w=["rsq" + x_])
                yield
                S.dve(lambda e: e.reciprocal(out=rsq_[:, 0:w], in_=rsq_[:, 0:w]), r=["rsq" + x_], w=["rsq" + x_])
                yield
                S.dve(lambda e: e.scalar_tensor_tensor(out=qbg_[:, 0:w], in0=rb_[:, 0:w], scalar=gqb[:, 0:1], in1=rsq_[0:64, 0:w], op0=ALU.mult, op1=ALU.mult),
                      r=["rb" + x_, "rsq" + x_, "gqb"], w=["qbg" + x_])
                S.dve(lambda e: e.scalar_tensor_tensor(out=qTa[:, cs], in0=ra_[:, 0:w], scalar=gqa[:, 0:1], in1=rsq_[:, 0:w], op0=ALU.mult, op1=ALU.mult),
                      r=["ra" + x_, "rsq" + x_, "gqa"], w=["qTa"])
                yield
                S.pe(lambda e: e.matmul(pQ[0:64, 0:w], lhsT=rotb[:, :], rhs=qbg_[:, 0:w], start=True, stop=True), r=["qbg" + x_, "rotb"], w=[kQ])
                S.act(lambda e: e.activation(out=ra_[:, 0:w], in_=pP[:, 0:w], func=AF.Copy), r=[kP], w=["ra" + x_])
                S.dve(lambda e: e.tensor_tensor(out=t2b_[:, 0:w], in0=qbg_[:, 0:w], in1=rope_[:, 0, 0:w], op=ALU.mult), r=["qbg" + x_, "rope" + x_], w=["t2b" + x_])
                yield
                S.dve(lambda e: e.tensor_tensor(out=t2a_[:, 0:w], in0=pQ[0:64, 0:w], in1=rope_[:, 1, 0:w], op=ALU.mult), r=[kQ, "rope" + x_], w=["t2a" + x_])
                S.dve(lambda e: e.tensor_tensor(out=sqa_[:, 0:w], in0=ra_[:, 0:w], in1=ra_[:, 0:w], op=ALU.mult), r=["ra" + x_], w=["sqa" + x_])
                yield
                S.dve(lambda e: e.tensor_tensor(out=qTb[0:64, cs], in0=t2a_[:, 0:w], in1=t2b_[:, 0:w], op=ALU.add), r=["t2a" + x_, "t2b" + x_], w=["qTb"])
                S.pe([lambda e: e.matmul(pQ[:, 0:w], lhsT=onesb[:, :], rhs=sqa_[:, 0:w], start=True, stop=False),
                      lambda e: e.matmul(pQ[:, 0:w], lhsT=onesb[:, :], rhs=sqk[:, cs], start=False, stop=True)], r=["sqa" + x_, "sqk", "onesb"], w=[kQ])
                yield
                S.act(lambda e: e.activation(out=rsk_[:, 0:w], in_=pQ[:, 0:w], func=AF.Sqrt, bias=epsb[:, 0:1], scale=1.0 / 192.0), r=[kQ, "epsb"], w=["rsq" + x_])
                yield
                S.dve(lambda e: e.reciprocal(out=rsk_[:, 0:w], in_=rsk_[:, 0:w]), r=["rsq" + x_], w=["rsq" + x_])
                yield
                S.dve(lambda e: e.scalar_tensor_tensor(out=kTa[:, cs], in0=ra_[:, 0:w], scalar=gka[:, 0:1], in1=rsk_[:, 0:w], op0=ALU.mult, op1=ALU.mult),
                      r=["ra" + x_, "rsq" + x_, "gka"], w=["kTa"])
                S.dve(lambda e: e.tensor_tensor(out=kTb[0:64, cs], in0=krT[:, cs], in1=rsk_[0:64, 0:w], op=ALU.mult), r=["krT", "rsq" + x_], w=["kTb"])
                yield

            for bi0 in range(0, len(COLBLKS), 2):
                active = [prol_gen(COLBLKS[bi0][0], COLBLKS[bi0][1], 0)]
                if bi0 + 1 < len(COLBLKS):
                    active.append(prol_gen(COLBLKS[bi0 + 1][0], COLBLKS[bi0 + 1][1], 1))
                while active:
                    for g_ in list(active):
                        try:
                            next(g_)
                        except StopIteration:
                            active.remove(g_)
            S.marks.append(("m2_h%d_qk" % h, dict(S.cnt)))
            for kt, (k0, nk) in enumerate(KT):
                bk = mmbank()
                S.pe([(lambda e, c=c: e.matmul(pb[bk][0:nk, 0:128], lhsT=ckvT[:, c, k0:k0 + nk], rhs=wkv[:, c, 128:256], start=(c == 0), stop=(c == 1))) for c in range(2)],
                     r=["wkv", "ckvT"], w=[("pb", bk)])
                S.act(lambda e: e.activation(out=vaug[0:nk, kt, 0:128], in_=pb[bk][0:nk, 0:128], func=AF.Copy), r=[("pb", bk)], w=["vaug"])
            S.marks.append(("m2_h%d_v" % h, dict(S.cnt)))
            steps = [(qi, kt) for qi in range(8) for kt in range(33)]
            SB = [0, 1, 3]

            def emit_scores(st):
                qi, kt = steps[st]
                k0, nk = KT[kt]
                q0 = 64 + 512 * qi
                bk = SB[st % 3]
                S.pe([lambda e: e.matmul(pb[bk][0:nk, 0:512], lhsT=kTa[:, k0:k0 + nk], rhs=qTa[:, q0:q0 + 512], start=True, stop=False),
                      lambda e: e.matmul(pb[bk][0:nk, 0:512], lhsT=kTb[:, k0:k0 + nk], rhs=qTb[:, q0:q0 + 512], start=False, stop=True)],
                     r=["kTa", "kTb", "qTa", "qTb"], w=[("pb", bk)])

            emit_scores(0)
            emit_scores(1)
            for st, (qi, kt) in enumerate(steps):
                k0, nk = KT[kt]
                q0 = 64 + 512 * qi
                bk = SB[st % 3]
                ab = qi % 2
                acc_o = pb[4 + 2 * ab]
                acc_d = pb[5]
                if st + 2 < len(steps):
                    emit_scores(st + 2)
                S.act(lambda e: e.activation(out=pT[st % 3][0:nk, :], in_=pb[bk][0:nk, 0:512], func=AF.Exp, bias=nshift[0:nk, 0:1], scale=SCALE),
                      r=[("pb", bk), "nshift"], w=[("pT", st % 3)])
                S.pe(lambda e: e.matmul(acc_o[:, 0:512], lhsT=vaug[0:nk, kt, 0:128], rhs=pT[st % 3][0:nk, :], start=(kt == 0), stop=(kt == 32)),
                     r=[("pT", st % 3), "vaug"], w=[("acc", ab)])
                par = kt % 2
                if kt < 2:
                    S.dve(lambda e: e.tensor_copy(out=dacc[ab][par][:, :], in_=pT[st % 3][:, :]), r=[("pT", st % 3)], w=[("dacc", ab, par)])
                else:
                    S.dve(lambda e: e.tensor_tensor(out=dacc[ab][par][0:nk, :], in0=dacc[ab][par][0:nk, :], in1=pT[st % 3][0:nk, :], op=ALU.add),
                          r=[("pT", st % 3), ("dacc", ab, par)], w=[("dacc", ab, par)])
                if kt == 0:
                    S.load(lambda e: e.dma_start(out=zq[ab][:, :], in_=z_scr[h, :, q0:q0 + 512]), r=[("z_scr", h)], w=[("zq", ab)])
                if kt == 32:
                    yb_ = ytb[qi % 2]
                    S.dve(lambda e: e.tensor_tensor(out=dacc[ab][0][:, :], in0=dacc[ab][0][:, :], in1=dacc[ab][1][:, :], op=ALU.add),
                          r=[("dacc", ab, 0), ("dacc", ab, 1)], w=[("dacc", ab, 0)])
                    S.pe(lambda e: e.matmul(acc_d[0:1, 0:512], lhsT=ones128[:, 0:1], rhs=dacc[ab][0][:, :], start=True, stop=True),
                         r=[("dacc", ab, 0), "ones128"], w=["accd"])
                    S.dve(lambda e: e.reciprocal(out=rdn[0:1, :], in_=acc_d[0:1, 0:512]), r=["accd"], w=["rdn", "rbc"])
                    S.pe(lambda e: e.matmul(pb[7][:, 0:512], lhsT=ones32[0:1, :], rhs=rdn[0:1, :], start=True, stop=True), r=["rdn", "ones32"], w=[("pb", 7)])
                    S.act(lambda e: e.activation(out=rbc[:, :], in_=pb[7][:, 0:512], func=AF.Copy), r=[("pb", 7)], w=["rbc", "rdn"])
                    S.dve(lambda e: e.tensor_tensor(out=rbc[:, :], in0=acc_o[:, 0:512], in1=rbc[:, :], op=ALU.mult), r=[("acc", ab), "rbc"], w=["rbc"])
                    S.dve(lambda e: e.tensor_tensor(out=yb_[:, :], in0=rbc[:, :], in1=zq[ab][:, :], op=ALU.mult), r=["rbc", ("zq", ab)], w=[("ytb", qi % 2)])
                    S.store(lambda e: e.dma_start(out=y_scr[h, :, q0:q0 + 512], in_=yb_[:, :]), r=[("ytb", qi % 2)], w=[("y_scr", h)])
            S.barrier()

    def dump(name, t, shape, dt, keys):
        d = nc.dram_tensor("dbg_" + name, list(shape), dt, kind="ExternalOutput").ap()
        S.store(lambda e: e.dma_start(out=d, in_=t), r=keys)

    setup()
    setup_mla()
    for s in range(NSEQ):
        S.marks.append(("start%d" % s, dict(S.cnt)))
        phase_p0(s)
        S.marks.append(("p0", dict(S.cnt)))
        phase_g1(s)
        S.marks.append(("g1", dict(S.cnt)))
        S.barrier()
        if debug == "g2":
            phase_g2(s, heads=[0])
            break
        if debug not in ("m2", "m1"):
            phase_g2(s)
        S.marks.append(("g2", dict(S.cnt)))
        phase_outproj(s, 0)
        S.marks.append(("op0", dict(S.cnt)))
        if debug == "l0":
            break
        phase_m1(s)
        S.marks.append(("m1", dict(S.cnt)))
        if debug == "m1":
            break
        if debug == "m2":
            phase_m2(s, heads=[0])
            break
        phase_m2(s)
        S.marks.append(("m2", dict(S.cnt)))
        phase_outproj(s, 1)
        S.marks.append(("op1", dict(S.cnt)))
    print("arena max", aoff.get("max"), "ninst", S.ninst, S.cnt, "sbuf left", nc.sbuf_bytes_remaining)
    nc._marks = S.marks
    S.finish()
    es.close()
    return nc


def _consts():
    ident = np.eye(128, dtype=np.float32)
    i = np.arange(64)
    U = (i[:, None] <= i[None, :]).astype(np.float32)
    Lo = (i[:, None] >= i[None, :]).astype(np.float32)
    Us = (i[:, None] < i[None, :]).astype(np.float32)
    Ls = (i[:, None] > i[None, :]).astype(np.float32)
    NEG = -30000.0
    masks = np.stack([U, Lo, Us, Ls, (1 - U) * NEG, (1 - Lo) * NEG], axis=1).astype(np.float32)
    pos = np.arange(LE, dtype=np.float64) - PAD
    inv = 10000.0 ** (-np.arange(0, 64, 2, dtype=np.float64) / 64)
    ang = pos[None, :] * inv[:, None]
    cos = np.concatenate([np.cos(ang), np.cos(ang)], 0)
    sin = np.concatenate([np.sin(ang), np.sin(ang)], 0)
    rope = np.stack([cos, sin], 1).astype(np.float32)
    rot = np.zeros((64, 64), np.float32)
    for m in range(32):
        rot[m + 32, m] = -1.0
        rot[m, m + 32] = 1.0
    return dict(c_ident=ident, c_masks=masks, c_rope=rope, c_rot=rot)


_NC_CACHE = {}


def _in_maps(inputs):
    allx = np.concatenate([np.asarray(inputs["x_prompt"]), np.asarray(inputs["x_sample"])], 0)
    seqs = [[0, 1], [2, 3], [4, 5], [6, 7], [8, 8], [9, 9], [10, 10], [11, 11]]
    common = dict(
        meta=np.asarray(inputs["meta_tokens"]), ln_g=np.asarray(inputs["ln_g"]),
        g_w_in=np.asarray(inputs["gdn_w_in"])[0], g_conv=np.asarray(inputs["gdn_conv_w"])[0],
        g_alog=np.asarray(inputs["gdn_a_log"])[0].reshape(16), g_dtb=np.asarray(inputs["gdn_dt_bias"])[0].reshape(16),
        g_on=np.asarray(inputs["gdn_o_norm_g"])[0], g_w_out=np.asarray(inputs["gdn_w_out"])[0],
        m_w_in=np.asarray(inputs["mla_w_in"])[0], m_qn=np.asarray(inputs["mla_q_norm_g"])[0],
        m_kvn=np.asarray(inputs["mla_kv_norm_g"])[0], m_wuq=np.asarray(inputs["mla_w_uq"])[0],
        m_wukv=np.asarray(inputs["mla_w_ukv"])[0], m_qg=np.asarray(inputs["mla_qk_q_g"])[0],
        m_kg=np.asarray(inputs["mla_qk_k_g"])[0], m_w_out=np.asarray(inputs["mla_w_out"])[0],
    )
    common = {k: np.ascontiguousarray(v, dtype=np.float32) for k, v in common.items()}
    common.update(_consts())
    maps = []
    for c in range(8):
        m = dict(common)
        m["xs"] = np.ascontiguousarray(allx[seqs[c]])
        maps.append(m)
    return maps, seqs


def kernel(**inputs):
    if "nc" not in _NC_CACHE:
        _NC_CACHE["nc"] = build()
    nc = _NC_CACHE["nc"]
    maps, seqs = _in_maps(inputs)
    res = run_bass_kernel_spmd(nc, maps, core_ids=list(range(8)))
    full = np.zeros((12, LX, D), np.float32)
    for c in range(8):
        o = res.results[c]["out"]
        full[seqs[c][0]] = o[0]
        if seqs[c][1] != seqs[c][0]:
            full[seqs[c][1]] = o[1]
    return full[:4], full[4:]
```

```python
import numpy as np
import ml_dtypes
import concourse.bass as bass
import concourse.mybir as mybir
from concourse.bass_utils import run_bass_kernel_spmd

F32 = mybir.dt.float32
BF16 = mybir.dt.bfloat16
ALU = mybir.AluOpType
AF = mybir.ActivationFunctionType

D = 1024
LX = 4096
NMETA = 16
PAD = 48
LE = 4160
NGR = 65
NSEQ = 2
EPS = 1e-6
COLBLKS = [(0, 64)] + [(64 + 512 * i, 512) for i in range(8)]
TOKTILES = [(0, 64)] + [(64 + 128 * i, 128) for i in range(32)]
GDN_IN = 6176
import os
SCANSTOP = int(os.environ.get('SCANSTOP', '9'))
DMA_K = 6
SAME_ENG_SYNC = True


def gkeys(name, c0, n):
    return [(name, g) for g in range(c0 // 64, (c0 + n + 63) // 64)]


class Sched:
    def __init__(self, nc, es):
        self.nc = nc
        self.eng = {"pe": nc.tensor, "dve": nc.vector, "act": nc.scalar, "pool": nc.gpsimd, "sp": nc.sync}
        self.semh = {}
        for e in self.eng:
            self.semh[(e,)] = es.enter_context(nc.semaphore("s_" + e))
        for q in ("sp", "pool", "act"):
            for s in range(DMA_K):
                self.semh[(q, "d", s)] = es.enter_context(nc.semaphore(f"d_{q}{s}"))
        self.cnt = {e: 0 for e in self.eng}
        self.dman = {q: 0 for q in ("sp", "pool", "act")}
        self.seen = {e: {} for e in self.eng}
        self.lastw = {}
        self.readers = {}
        self.ninst = 0
        self.marks = []

    def _wait(self, e, semk, val):
        if val <= 0 or self.seen[e].get(semk, 0) >= val:
            return
        self.eng[e].wait_ge(self.semh[semk], val)
        self.seen[e][semk] = val

    @staticmethod
    def canon(k):
        if isinstance(k, tuple) and k and isinstance(k[0], tuple) and k[0] and k[0][0] == "pb":
            return k[0]
        if k in ("ptb2", "ptx0", "ptx1"):
            return "ptb"
        if isinstance(k, tuple) and k and k[0] in ("ptk", "ptq"):
            return "ptb"
        if isinstance(k, tuple) and k and k[0] == "acc":
            return ("pb", 4 + 2 * k[1])
        if k == "accd":
            return ("pb", 5)
        return k

    def op(self, e, fn, reads=(), writes=(), dma=False):
        reads = [self.canon(k) for k in reads]
        writes = [self.canon(k) for k in writes]
        deps = {}
        for k in reads:
            t = self.lastw.get(k)
            if t is not None:
                deps[t[0]] = max(deps.get(t[0], 0), t[1])
        for k in writes:
            t = self.lastw.get(k)
            if t is not None:
                deps[t[0]] = max(deps.get(t[0], 0), t[1])
            for sk, v in self.readers.get(k, {}).items():
                deps[sk] = max(deps.get(sk, 0), v)
        for sk, v in deps.items():
            if sk == (e,) and (e == "pe" or not SAME_ENG_SYNC) and not dma:
                continue
            self._wait(e, sk, v)
        if dma:
            n = self.dman[e]
            slot = n % DMA_K
            sk = (e, "d", slot)
            self._wait(e, sk, 16 * (n // DMA_K))
            self.dman[e] = n + 1
            inst = fn(self.eng[e])
            inst.then_inc(self.semh[sk], 16)
            tok = (sk, 16 * (n // DMA_K + 1))
        else:
            fns = fn if isinstance(fn, (list, tuple)) else [fn]
            inst = None
            for f in fns:
                inst = f(self.eng[e])
                self.ninst += 1
            self.cnt[e] += 1
            inst.then_inc(self.semh[(e,)], 1)
            tok = ((e,), self.cnt[e])
        for k in reads:
            r = self.readers.setdefault(k, {})
            r[tok[0]] = max(r.get(tok[0], 0), tok[1])
        for k in writes:
            self.lastw[k] = tok
            self.readers[k] = {}
        return tok

    def pe(self, fn, r=(), w=()):
        return self.op("pe", fn, r, w)

    def dve(self, fn, r=(), w=()):
        return self.op("dve", fn, r, w)

    def act(self, fn, r=(), w=()):
        return self.op("act", fn, r, w)

    def pool(self, fn, r=(), w=()):
        return self.op("pool", fn, r, w)

    def load(self, fn, r=(), w=()):
        return self.op("sp", fn, r, w, dma=True)

    def store(self, fn, r=(), w=()):
        return self.op("pool", fn, r, w, dma=True)

    def barrier(self):
        for e in self.eng:
            for e2 in self.eng:
                if e2 != e:
                    self._wait(e, (e2,), self.cnt[e2])
            for q in self.dman:
                n = self.dman[q]
                for s in range(DMA_K):
                    if n > s:
                        last = ((n - 1 - s) // DMA_K) * DMA_K + s
                        self._wait(e, (q, "d", s), 16 * (last // DMA_K + 1))
        self.lastw = {}
        self.readers = {}

    def finish(self):
        self.barrier()


def build(debug=None):
    from contextlib import ExitStack
    nc = bass.Bass("TRN2", target_bir_lowering=False)
    es = ExitStack()

    def din(name, shape, dt=F32):
        return nc.dram_tensor(name, list(shape), dt, kind="ExternalInput").ap()

    xs = din("xs", [NSEQ, LX, D])
    meta = din("meta", [NMETA, D])
    ln_g = din("ln_g", [2, D])
    g_w_in = din("g_w_in", [D, GDN_IN])
    g_conv = din("g_conv", [5, 4096])
    g_alog = din("g_alog", [16])
    g_dtb = din("g_dtb", [16])
    g_on = din("g_on", [256])
    g_w_out = din("g_w_out", [2048, D])
    m_w_in = din("m_w_in", [D, 2880])
    m_qn = din("m_qn", [512])
    m_kvn = din("m_kvn", [256])
    m_wuq = din("m_wuq", [512, 3072])
    m_wukv = din("m_wukv", [256, 4096])
    m_qg = din("m_qg", [192])
    m_kg = din("m_kg", [192])
    m_w_out = din("m_w_out", [2048, D])
    c_ident = din("c_ident", [128, 128])
    c_masks = din("c_masks", [64, 6, 64])
    c_rope = din("c_rope", [64, 2, LE])
    c_rot = din("c_rot", [64, 64])
    out = nc.dram_tensor("out", [NSEQ, LX, D], F32, kind="ExternalOutput").ap()

    skind = "ExternalOutput" if debug else "Internal"

    def dscr(name, shape, dt):
        return nc.dram_tensor(name, list(shape), dt, kind=skind).ap()

    qk_scr = dscr("qk_scr", [16, 128, LE], BF16)
    v_scr = dscr("v_scr", [16, 128, LE], BF16)
    z_scr = dscr("z_scr", [16, 128, LE], BF16)
    y_scr = dscr("y_scr", [16, 128, LE], BF16)
    h1_scr = dscr("h1_scr", [LE, D], F32)

    S = Sched(nc, es)

    def sb(name, shape, dt=F32):
        return es.enter_context(nc.sbuf_tensor(name, list(shape), dt))

    def ps(name, shape, dt=F32):
        return es.enter_context(nc.psum_tensor(name, list(shape), dt))

    block = es.enter_context(nc.Block())

    ident = sb("ident", [128, 128])
    identb = sb("identb", [128, 128], BF16)
    onesb = sb("onesb", [128, 128], BF16)
    ones32 = sb("ones32", [64, 128])
    masks = sb("masks", [64, 6, 64])
    lng = sb("lng", [128, 2, 8])
    convw = sb("convw", [128, 5, 32])
    alog = sb("alog", [64, 16])
    dtb = sb("dtb", [64, 16])
    nea = sb("nea", [64, 16])
    gon = sb("gon", [64, 256])
    epsb = sb("epsb", [128, 1])

    pb = [ps(f"pb{i}", [128, 512]) for i in range(8) if i != 2]
    pb.insert(2, None)
    ptb = ps("ptb", [128, 1024], BF16)

    def setup():
        S.load(lambda e: e.dma_start(out=ident[:], in_=c_ident[:, :]), w=["ident"])
        S.load(lambda e: e.dma_start(out=masks[:], in_=c_masks[:, :, :]), w=["masks"])
        S.load(lambda e: e.dma_start(out=lng[:], in_=ln_g.rearrange("l (c p) -> p l c", p=128), allow_slow_non_contiguous=True), w=["lng"])
        for j in range(5):
            S.load(lambda e, j=j: e.dma_start(out=convw[:, j, :], in_=g_conv[j].rearrange("(c p) -> p c", p=128), allow_slow_non_contiguous=True), w=["convw"])
        S.load(lambda e: e.dma_start(out=alog[:], in_=g_alog.partition_broadcast(64)), w=["alog"])
        S.load(lambda e: e.dma_start(out=dtb[:], in_=g_dtb.partition_broadcast(64)), w=["dtb"])
        S.load(lambda e: e.dma_start(out=gon[:], in_=g_on.partition_broadcast(64)), w=["gon"])
        S.dve(lambda e: e.tensor_copy(out=identb[:], in_=ident[:]), r=["ident"], w=["identb"])
        S.dve(lambda e: e.memset(onesb[:], 1.0), w=["onesb"])
        S.dve(lambda e: e.memset(ones32[:], 1.0), w=["ones32"])
        S.dve(lambda e: e.memset(epsb[:], EPS), w=["epsb"])
        S.act(lambda e: e.activation(out=nea[:], in_=alog[:], func=AF.Exp), r=["alog"], w=["nea"])
        S.dve(lambda e: e.tensor_scalar(out=nea[:], in0=nea[:], scalar1=-1.0, scalar2=None, op0=ALU.mult), r=["nea"], w=["nea"])

    beta = sb("beta", [64, NGR, 2, 8])
    gg = sb("gg", [64, NGR, 2, 8])
    gc = sb("gc", [64, NGR, 2, 8])
    negc = sb("negc", [64, NGR, 2, 8])
    egc = sb("egc", [64, NGR, 2, 8])
    kds = sb("kds", [64, NGR, 2, 8])
    egl = sb("egl", [128, NGR, 2, 8])
    negm4 = sb("negm4", [64, 4, 2, 64])
    strict4 = sb("strict4", [64, 4, 2, 64])
    ARENA_BYTES = 171500
    arena = sb("arena", [128, ARENA_BYTES // 4])
    aoff = {"p": 0}

    def av(shape, dt=F32):
        n = 1
        for d_ in shape[1:]:
            n *= d_
        nb = (n * (2 if dt == BF16 else 4) + 3) // 4 * 4
        o = aoff["p"]
        aoff["p"] = o + nb
        aoff["max"] = max(aoff.get("max", 0), o + nb)
        assert aoff["p"] <= ARENA_BYTES, (aoff["p"], shape)
        v = arena[0:shape[0], o // 4:(o + nb) // 4]
        if dt == BF16:
            v = v.bitcast(BF16)
            if n % 2:
                v = v[:, 0:n]
        if len(shape) > 2:
            names = "abcd"[:len(shape) - 1]
            pat = "p (" + " ".join(names) + ") -> p " + " ".join(names)
            v = v.rearrange(pat, **{names[i]: shape[1 + i] for i in range(len(names))})
        return v

    def sbA(name, shape, dt=F32):
        return av(shape, dt)

    hnT = sbA("hnT", [128, 8, LE], BF16)
    xt = [sbA(f"xt{i}", [128, D]) for i in range(2)]
    xn = [sbA(f"xn{i}", [128, D], BF16) for i in range(2)]
    junk = sbA("junk", [128, D], BF16)
    ssb = [sbA(f"ss{i}", [128, 4]) for i in range(2)]

    aoff_after_p0 = aoff["p"]

    def norm_transpose(tt, src_ap, src_key, n, t0, layer):
        b = tt % 2
        ss = ssb[b]
        S.act(lambda e: e.activation(out=junk[0:n, :], in_=src_ap, func=AF.Square, accum_out=ss[0:n, 0:1]),
              r=[src_key], w=["junk", ("ss", b)])
        S.act(lambda e: e.activation(out=ss[0:n, 1:2], in_=ss[0:n, 0:1], func=AF.Sqrt, bias=epsb[0:n, 0:1], scale=1.0 / D),
              r=[("ss", b), "epsb"], w=[("ss", b)])
        S.dve(lambda e: e.reciprocal(out=ss[0:n, 2:3], in_=ss[0:n, 1:2]), r=[("ss", b)], w=[("ss", b)])
        S.act(lambda e: e.activation(out=xn[b][0:n, :], in_=src_ap, func=AF.Copy, scale=ss[0:n, 2:3]),
              r=[src_key, ("ss", b)], w=[("xn", b)])
        S.pe([(lambda e, c=c: e.transpose(out=ptb[:, c * 128:c * 128 + n], in_=xn[b][0:n, c * 128:(c + 1) * 128], identity=identb[0:n, 0:n]))
              for c in range(8)], r=[("xn", b), "identb"], w=["ptb"])
        pv = ptb[:, :].rearrange("p (c t) -> p c t", c=8)[:, :, 0:n]
        S.dve(lambda e: e.tensor_tensor(out=hnT[:, :, t0:t0 + n], in0=pv,
                                        in1=lng[:, layer, :].unsqueeze(2).to_broadcast([128, 8, n]), op=ALU.mult),
              r=["ptb", "lng"], w=gkeys("hnT", t0, n))

    def load_x_tile(s, tt):
        t0, n = TOKTILES[tt]
        b = tt % 2
        if tt == 0:
            S.dve(lambda e: e.memset(xt[b][0:64, :], 0.0), w=[("xt", b)])
            S.load(lambda e: e.dma_start(out=xt[b][PAD:64, :], in_=meta[:, :]), w=[("xt", b)])
        else:
            r0 = t0 - 64
            S.load(lambda e: e.dma_start(out=xt[b][0:n, :], in_=xs[s, r0:r0 + n, :]), w=[("xt", b)])

    def phase_p0(s):
        for tt, (t0, n) in enumerate(TOKTILES):
            load_x_tile(s, tt)
            norm_transpose(tt, xt[tt % 2][0:n, :], ("xt", tt % 2), n, t0, 0)

    wst = [sbA(f"wst{i}", [128, 8, 128]) for i in range(2)]
    wbf = [sbA(f"wbf{i}", [128, 8, 128], BF16) for i in range(2)]
    pre = [sbA(f"pre{i}", [128, LE + 4], BF16) for i in range(2)]
    acc = sbA("acc", [128, LE])
    sqb = sbA("sqb", [128, LE], BF16)
    obf = [sbA(f"obf{i}", [128, LE], BF16) for i in range(2)]
    rtmp = [sbA(f"rtmp{i}", [128, 512]) for i in range(2)]
    dg = [sbA(f"dg{i}", [128, 5, 128], BF16) for i in range(2)]
    gbraw = sbA("gbraw", [64, NGR, 32])
    wba_st = sbA("wba_st", [128, 8, 32])
    wba = sbA("wba", [128, 8, 32], BF16)

    w_in_v = g_w_in.rearrange("(c p) n -> p c n", p=128)
    mmrot = [0]

    def mmbank():
        mmrot[0] ^= 1
        return mmrot[0]

    def load_w(wv_ap, idx, ncol=128):
        b = idx % 2
        S.load(lambda e: e.dma_start(out=wst[b][:, :, 0:ncol], in_=wv_ap), w=[("wst", b)])
        S.pool(lambda e: e.tensor_copy(out=wbf[b][:, :, 0:ncol], in_=wst[b][:, :, 0:ncol]), r=[("wst", b)], w=[("wbf", b)])
        return wbf[b]

    def phase_g1(s):
        for b in range(2):
            S.dve(lambda e, b=b: e.memset(pre[b][:, 0:2], 0.0), w=[("pre", b)])
            S.dve(lambda e, b=b: e.memset(pre[b][:, LE + 2:LE + 4], 0.0), w=[("pre", b)])
        flist = range(48)
        if debug == "g1a":
            flist = [0, 16, 32]
        if debug == "g1b":
            flist = []
        for f in flist:
            wt = load_w(w_in_v[:, :, f * 128:(f + 1) * 128], f)
            wk = ("wbf", f % 2)
            pb_ = f % 2
            for (c0, w) in COLBLKS:
                bk = mmbank()
                S.pe([(lambda e, c=c: e.matmul(pb[bk][:, 0:w], lhsT=wt[:, c, :], rhs=hnT[:, c, c0:c0 + w], start=(c == 0), stop=(c == 7)))
                      for c in range(8)], r=[wk] + gkeys("hnT", c0, w), w=[("pb", bk)])
                if f < 32:
                    S.act(lambda e: e.activation(out=pre[pb_][:, 2 + c0:2 + c0 + w], in_=pb[bk][:, 0:w], func=AF.Copy),
                          r=[("pb", bk)], w=[("pre", pb_)])
                else:
                    ob = obf[f % 2]
                    S.act(lambda e: e.activation(out=ob[:, c0:c0 + w], in_=pb[bk][:, 0:w], func=AF.Silu),
                          r=[("pb", bk)], w=[("obf", f % 2)])
            if f >= 32:
                S.store(lambda e: e.dma_start(out=z_scr[f - 32, :, :], in_=obf[f % 2][:, :]), r=[("obf", f % 2)], w=[("z_scr", f - 32)])
                continue
            pr = pre[pb_]
            dgb = dg[f % 2]
            for j in range(5):
                S.pool(lambda e, j=j: e.tensor_scalar(out=dgb[:, j, :], in0=identb[:, :], scalar1=convw[:, j, f:f + 1], scalar2=None, op0=ALU.mult),
                       r=["identb", "convw"], w=[("dg", f % 2)])
            ob = obf[f % 2]
            ok = ("obf", f % 2)
            qscale = (128.0 ** -0.5) if f < 8 else 1.0

            def conv_gen(bi, c0, w, si):
                pC_, pN_ = (pb[3], pb[4]) if si == 0 else (pb[5], pb[6])
                kC_, kN_ = (("pb", 3), ("pb", 4)) if si == 0 else (("pb", 5), ("pb", 6))
                rt = rtmp[si]
                S.pe([(lambda e, j=j: e.matmul(pC_[:, 0:w], lhsT=dgb[:, j, :], rhs=pr[:, c0 + j:c0 + j + w], start=(j == 0), stop=(j == 4))) for j in range(5)],
                     r=[("pre", pb_), ("dg", f % 2)], w=[kC_])
                yield
                if f >= 16:
                    S.act(lambda e: e.activation(out=ob[:, c0:c0 + w], in_=pC_[:, 0:w], func=AF.Silu), r=[kC_], w=[ok])
                    return
                S.act(lambda e: e.activation(out=acc[:, c0:c0 + w], in_=pC_[:, 0:w], func=AF.Silu), r=[kC_], w=[("acc", bi)])
                yield
                S.dve(lambda e: e.tensor_tensor(out=sqb[:, c0:c0 + w], in0=acc[:, c0:c0 + w], in1=acc[:, c0:c0 + w], op=ALU.mult), r=[("acc", bi)], w=[("sqb", bi)])
                yield
                S.pe(lambda e: e.matmul(pN_[:, 0:w], lhsT=onesb[:, :], rhs=sqb[:, c0:c0 + w], start=True, stop=True), r=[("sqb", bi), "onesb"], w=[kN_])
                yield
                S.act(lambda e: e.activation(out=rt[:, 0:w], in_=pN_[:, 0:w], func=AF.Sqrt, bias=epsb[:, 0:1], scale=1.0), r=[kN_, "epsb"], w=[("rtmp", si)])
                yield
                S.dve(lambda e: e.reciprocal(out=rt[:, 0:w], in_=rt[:, 0:w]), r=[("rtmp", si)], w=[("rtmp", si)])
                yield
                S.dve(lambda e: e.scalar_tensor_tensor(out=ob[:, c0:c0 + w], in0=acc[:, c0:c0 + w], scalar=qscale, in1=rt[:, 0:w],
                                                       op0=ALU.mult, op1=ALU.mult), r=[("acc", bi), ("rtmp", si)], w=[ok])
                yield

            for bi0 in range(0, len(COLBLKS), 2):
                active = [conv_gen(bi0, COLBLKS[bi0][0], COLBLKS[bi0][1], 0)]
                if bi0 + 1 < len(COLBLKS):
                    active.append(conv_gen(bi0 + 1, COLBLKS[bi0 + 1][0], COLBLKS[bi0 + 1][1], 1))
                while active:
                    for g_ in list(active):
                        try:
                            next(g_)
                        except StopIteration:
                            active.remove(g_)
            S.dve(lambda e: e.memset(ob[:, 0:PAD], 0.0), w=[ok])
            if f >= 16:
                S.store(lambda e: e.dma_start(out=v_scr[f - 16, :, :], in_=ob[:, :]), r=[ok], w=[("v_scr", f - 16)])
            else:
                S.store(lambda e: e.dma_start(out=qk_scr[f, :, :], in_=ob[:, :]), r=[ok], w=[("qk_scr", f)])
        if debug == "g1a":
            return
        S.load(lambda e: e.dma_start(out=wba_st[:], in_=w_in_v[:, :, 6144:6176]), w=["wba_st"])
        S.pool(lambda e: e.tensor_copy(out=wba[:], in_=wba_st[:]), r=["wba_st"], w=["wba"])
        for g0 in range(0, NGR, 16):
            ng = min(16, NGR - g0)
            bk = mmbank()
            pv = pb[bk][0:64, :].rearrange("p (g n) -> p g n", n=32)
            for gi in range(ng):
                gr = g0 + gi
                S.pe([(lambda e, c=c: e.matmul(pv[:, gi, :], lhsT=hnT[:, c, gr * 64:(gr + 1) * 64], rhs=wba[:, c, :], start=(c == 0), stop=(c == 7)))
                      for c in range(8)], r=["wba", ("hnT", gr)], w=[("pb", bk)])
            S.act(lambda e: e.activation(out=gbraw[:, g0:g0 + ng, :], in_=pv[:, 0:ng, :], func=AF.Copy), r=[("pb", bk)], w=["gbraw"])
        import os
        stopat = int(os.environ.get("STOPAT", "99"))
        if stopat <= 1:
            return
        S.act(lambda e: e.activation(out=beta[:].rearrange("p g a b -> p g (a b)"), in_=gbraw[:, :, 0:16], func=AF.Sigmoid), r=["gbraw"], w=["beta"])
        ggf = gg[:].rearrange("p g a b -> p g (a b)")
        S.dve(lambda e: e.tensor_tensor(out=ggf, in0=gbraw[:, :, 16:32], in1=dtb[:].unsqueeze(1).to_broadcast([64, NGR, 16]), op=ALU.add),
              r=["gbraw", "dtb"], w=["gg"])
        S.act(lambda e: e.activation(out=ggf, in_=ggf, func=AF.Exp), r=["gg"], w=["gg"])
        S.act(lambda e: e.activation(out=ggf, in_=ggf, func=AF.Ln, bias=1.0, scale=1.0), r=["gg"], w=["gg"])
        S.dve(lambda e: e.tensor_tensor(out=ggf, in0=ggf, in1=nea[:].unsqueeze(1).to_broadcast([64, NGR, 16]), op=ALU.mult),
              r=["gg", "nea"], w=["gg"])
        if stopat <= 2:
            return
        S.dve(lambda e: e.memset(gg[0:PAD, 0, :, :], 0.0), w=["gg"])
        S.dve(lambda e: e.memset(beta[0:PAD, 0, :, :], 0.0), w=["beta"])
        if stopat <= 3:
            return
        for g0 in range(0, NGR, 32):
            ng = min(32, NGR - g0)
            bk = mmbank()
            pv = pb[bk][0:64, :].rearrange("p (g a b) -> p g a b", a=2, b=8)
            fns = []
            for gi in range(ng):
                gr = g0 + gi
                fns.append(lambda e, gi=gi, gr=gr: e.matmul(pv[:, gi, 0, :], lhsT=masks[:, 0, :], rhs=gg[:, gr, 0, :], start=True, stop=True))
                fns.append(lambda e, gi=gi, gr=gr: e.matmul(pv[:, gi, 1, :], lhsT=masks[:, 1, :], rhs=gg[:, gr, 1, :], start=True, stop=True))
            S.pe(fns, r=["gg", "masks"], w=[("pb", bk)])
            S.act(lambda e: e.activation(out=gc[:, g0:g0 + ng], in_=pv[:, 0:ng], func=AF.Copy), r=[("pb", bk)], w=["gc"])
            if stopat <= 4:
                continue
            bk2 = mmbank()
            pv2 = pb[bk2][:, :].rearrange("p (g a b) -> p g a b", a=2, b=8)
            fns = [(lambda e, gi=gi: e.matmul(pv2[:, gi].rearrange("p a b -> p (a b)"), lhsT=ones32[:, :],
                                               rhs=gg[:, g0 + gi].rearrange("p a b -> p (a b)"), start=True, stop=True)) for gi in range(ng)]
            S.pe(fns, r=["gg", "ones32"], w=[("pb", bk2)])
            if stopat <= 5:
                continue
            S.act(lambda e: e.activation(out=egl[:, g0:g0 + ng], in_=pv2[:, 0:ng], func=AF.Exp), r=[("pb", bk2)], w=["egl"])
            if stopat <= 6:
                continue
            S.act(lambda e: e.activation(out=kds[:, g0:g0 + ng], in_=pv2[0:64, 0:ng], func=AF.Copy), r=[("pb", bk2)], w=["kds"])
            S.dve(lambda e: e.tensor_tensor(out=kds[:, g0:g0 + ng].rearrange("p g a b -> p (g a b)"), in0=kds[:, g0:g0 + ng].rearrange("p g a b -> p (g a b)"),
                                            in1=gc[:, g0:g0 + ng].rearrange("p g a b -> p (g a b)"), op=ALU.subtract),
                  r=["gc", "kds"], w=["kds"])
        if stopat <= 7:
            return
        S.act(lambda e: e.activation(out=kds[:], in_=kds[:], func=AF.Exp), r=["kds"], w=["kds"])
        S.act(lambda e: e.activation(out=egc[:], in_=gc[:], func=AF.Exp), r=["gc"], w=["egc"])
        S.dve(lambda e: e.tensor_scalar(out=negc[:], in0=egc[:], scalar1=-1.0, scalar2=None, op0=ALU.mult), r=["egc"], w=["negc"])


    aoff_after_g1 = aoff["p"]
    aoff["p"] = 0
    qT = av([128, LE], BF16)
    kT = av([128, LE], BF16)
    vz = av([128, LE], BF16)
    vtok = av([64, NGR, 256], BF16)
    ob = av([64, NGR, 256], BF16)
    Rr = av([128, NGR, 2, 64], BF16)
    At = av([128, NGR, 2, 64], BF16)
    PSET = []
    for si_ in range(2):
        PSET.append(dict(rhsD=av([64, 4, 2, 64]), DTi=av([64, 4, 2, 64]), DTs=av([64, 4, 2, 64]), X4=av([64, 8, 64], BF16), XT=av([64, 8, 64], BF16),
                         Pa=[av([64, 8, 64], BF16) for _ in range(2)], PaT=[av([64, 8, 64], BF16) for _ in range(2)], R32=av([64, 8, 64]), Rb=av([64, 8, 64], BF16)))
    PSET[0].update(banks=(pb[0], pb[1], pb[3]), bkeys=(("pb", 0), ("pb", 1), ("pb", 3)), ptx=ptb[:, 512:1024], ptxk="ptx0")
    PSET[1].update(banks=(pb[4], pb[5], pb[6]), bkeys=(("pb", 4), ("pb", 5), ("pb", 6)), ptx=ptb[:, 0:512], ptxk="ptx1")
    S32 = [av([128, 256]) for _ in range(2)]
    Sbf = [av([128, 256], BF16) for _ in range(2)]
    xb = [av([128, 256], BF16) for _ in range(2)]
    vn = [av([128, 256], BF16) for _ in range(2)]
    kd = [av([128, 128], BF16) for _ in range(2)]
    t1 = [av([64, 256]) for _ in range(2)]
    ssq = av([64, NGR + 3])
    rstd = av([64, NGR + 3])
    junk2 = [av([64, 256], BF16) for _ in range(2)]
    yb = [av([128, 256], BF16) for _ in range(2)]

    def phase_g2(s, heads=range(8)):
        S.dve(lambda e: e.tensor_copy(out=negm4[:], in_=masks[:, 4:6, :].unsqueeze(1).to_broadcast([64, 4, 2, 64])), r=["masks"], w=["negm4"])
        S.dve(lambda e: e.tensor_copy(out=strict4[:], in_=masks[:, 2:4, :].unsqueeze(1).to_broadcast([64, 4, 2, 64])), r=["masks"], w=["strict4"])
        S.dve(lambda e: e.memset(Rr[64:128].rearrange("p a b c -> p (a b c)"), 0.0), w=["Rr"])
        S.dve(lambda e: e.memset(At[64:128].rearrange("p a b c -> p (a b c)"), 0.0), w=["At"])
        for d_ in range(2):
            S.dve(lambda e, d_=d_: e.memset(xb[d_][64:128, :], 0.0), w=[("xb", d_)])
            S.dve(lambda e, d_=d_: e.memset(vn[d_][64:128, :], 0.0), w=[("vn", d_)])
            S.dve(lambda e, d_=d_: e.memset(kd[d_][64:128, :], 0.0), w=[("kd", d_)])
        for h in heads:
            S.load(lambda e: e.dma_start(out=qT[:, :], in_=qk_scr[h, :, :]), r=[("qk_scr", h)], w=["qT"])
            S.load(lambda e: e.dma_start(out=kT[:, :], in_=qk_scr[8 + h, :, :]), r=[("qk_scr", 8 + h)], w=["kT"])
            for half in range(2):
                S.load(lambda e: e.dma_start(out=vz[:, :], in_=v_scr[2 * h + half, :, :]), r=[("v_scr", 2 * h + half)], w=["vz"])
                for g0 in range(0, NGR, 8):
                    ng = min(8, NGR - g0)
                    S.pe([(lambda e, gi=gi: e.transpose(out=ptb[0:64, gi * 128:(gi + 1) * 128], in_=vz[:, (g0 + gi) * 64:(g0 + gi + 1) * 64], identity=identb[:, :]))
                          for gi in range(ng)], r=["vz", "identb"], w=["ptb"])
                    pv = ptb[0:64, :].rearrange("p (g n) -> p g n", n=128)
                    S.act(lambda e: e.activation(out=vtok[:, g0:g0 + ng, half * 128:(half + 1) * 128], in_=pv[:, 0:ng, :], func=AF.Copy),
                          r=["ptb"], w=["vtok"])
            S.marks.append(("g2_h%d_load" % h, dict(S.cnt)))
            def prep_gen(c0, si):
                B = PSET[si]
                rhsD_, DTi_, DTs_, X4_, XT_, Pa_, PaT_, R32_, Rb_ = B["rhsD"], B["DTi"], B["DTs"], B["X4"], B["XT"], B["Pa"], B["PaT"], B["R32"], B["Rb"]
                dif_ = rhsD_
                pA, pB_, pC = B["banks"]
                kA, kB, kC = B["bkeys"]
                px = B["ptx"]
                kx = B["ptxk"]
                sfx = "_%d" % si
                nck = min(4, NGR - c0)
                nu = nck * 2
                W = nck * 128
                for d in range(2):
                    S.dve(lambda e, d=d: e.tensor_tensor(out=rhsD_[:, 0:nck, d, :], in0=gg[:, c0:c0 + nck, d, h:h + 1].to_broadcast([64, nck, 64]),
                                                         in1=masks[:, d:d + 1, :].to_broadcast([64, nck, 64]), op=ALU.mult),
                          r=["gg", "masks"], w=["rhsD" + sfx])
                yield
                S.pe([lambda e: e.matmul(pB_[0:64, 0:W], lhsT=ones32[:, 0:64], rhs=rhsD_[:, 0:nck].rearrange("p a b c -> p (a b c)"), start=True, stop=False),
                      lambda e: e.matmul(pB_[0:64, 0:W], lhsT=ident[0:64, 0:64], rhs=negm4[:, 0:nck].rearrange("p a b c -> p (a b c)"), start=False, stop=True)],
                     r=["rhsD" + sfx, "ones32", "ident", "negm4"], w=[kB])
                pK = pA[0:64, :].rearrange("p (t g n) -> p t g n", t=2, g=4)
                fns = []
                for ci in range(nck):
                    cs = slice((c0 + ci) * 64, (c0 + ci + 1) * 64)
                    fns.append(lambda e, ci=ci, cs=cs: e.matmul(pK[:, 0, ci, :], lhsT=kT[:, cs], rhs=kT[:, cs], start=True, stop=True))
                    fns.append(lambda e, ci=ci, cs=cs: e.matmul(pK[:, 1, ci, :], lhsT=kT[:, cs], rhs=qT[:, cs], start=True, stop=True))
                S.pe(fns, r=["kT", "qT"], w=[kA])
                yield
                S.act(lambda e: e.activation(out=dif_[:, 0:nck].rearrange("p a b c -> p (a b c)"), in_=pB_[0:64, 0:W], func=AF.Copy), r=[kB], w=["rhsD" + sfx])
                yield
                S.dve(lambda e: e.tensor_tensor(out=dif_[:, 0:nck].rearrange("p a b c -> p (a b) c"), in0=dif_[:, 0:nck].rearrange("p a b c -> p (a b) c"),
                                                in1=gc[:, c0:c0 + nck].rearrange("p g a b -> p (g a) b")[:, :, h:h + 1].to_broadcast([64, nu, 64]), op=ALU.subtract),
                      r=["rhsD" + sfx, "gc"], w=["rhsD" + sfx])
                yield
                S.act(lambda e: e.activation(out=DTi_[:, 0:nck].rearrange("p a b c -> p (a b c)"), in_=dif_[:, 0:nck].rearrange("p a b c -> p (a b c)"), func=AF.Exp),
                      r=["rhsD" + sfx], w=["DTi" + sfx])
                yield
                S.dve(lambda e: e.tensor_tensor(out=DTs_[:, 0:nck].rearrange("p a b c -> p (a b c)"), in0=DTi_[:, 0:nck].rearrange("p a b c -> p (a b c)"),
                                                in1=strict4[:, 0:nck].rearrange("p a b c -> p (a b c)"), op=ALU.mult), r=["DTi" + sfx, "strict4"], w=["DTs" + sfx])
                S.dve(lambda e: e.tensor_tensor(out=DTs_[:, 0:nck].rearrange("p a b c -> p (a b) c"), in0=DTs_[:, 0:nck].rearrange("p a b c -> p (a b) c"),
                                                in1=beta[:, c0:c0 + nck].rearrange("p g a b -> p (g a) b")[:, :, h:h + 1].to_broadcast([64, nu, 64]), op=ALU.mult),
                      r=["DTs" + sfx, "beta"], w=["DTs" + sfx])
                X44 = X4_[:, :, :].rearrange("p (g d) n -> p g d n", d=2)
                for d in range(2):
                    S.dve(lambda e, d=d: e.tensor_tensor(out=X44[:, 0:nck, d, :], in0=pK[:, 0, 0:nck, :], in1=DTs_[:, 0:nck, d, :], op=ALU.mult),
                          r=[kA, "DTs" + sfx], w=["X4" + sfx])
                yield
                S.pe([(lambda e, u=u: e.transpose(out=px[0:64, u * 64:(u + 1) * 64], in_=X4_[:, u, :], identity=identb[0:64, 0:64])) for u in range(nu)],
                     r=["X4" + sfx, "identb"], w=[kx, "ptb"])
                for d in range(2):
                    S.dve(lambda e, d=d: e.tensor_tensor(out=At[0:64, c0:c0 + nck, d, :], in0=pK[:, 1, 0:nck, :], in1=DTi_[:, 0:nck, d, :], op=ALU.mult),
                          r=[kA, "DTi" + sfx], w=["At"])
                S.dve(lambda e: e.tensor_tensor(out=R32_[:, 0:nu, :], in0=ident[0:64, 0:64].unsqueeze(1).to_broadcast([64, nu, 64]), in1=X4_[:, 0:nu, :], op=ALU.subtract),
                      r=["ident", "X4" + sfx], w=["R32" + sfx])
                S.pool(lambda e: e.tensor_copy(out=Rb_[:, 0:nu, :], in_=R32_[:, 0:nu, :]), r=["R32" + sfx], w=["Rb" + sfx])
                yield
                S.act(lambda e: e.activation(out=XT_[:, 0:nu, :].rearrange("p a b -> p (a b)"), in_=px[0:64, 0:nu * 64], func=AF.Copy), r=[kx], w=["XT" + sfx])
                yield
                P, PT, Pk, PTk = X4_, XT_, "X4" + sfx, "XT" + sfx
                for lvl in range(5):
                    nb = lvl % 2
                    last = (lvl == 4)
                    if not last:
                        S.pe([(lambda e, u=u, P=P, PT=PT: e.matmul(pA[0:64, u * 64:(u + 1) * 64], lhsT=PT[:, u, :], rhs=P[:, u, :], start=True, stop=True)) for u in range(nu)],
                             r=[Pk, PTk], w=[kA])
                    S.pe([(lambda e, u=u, P=P, PT=PT: e.matmul(pB_[0:64, u * 64:(u + 1) * 64], lhsT=P[:, u, :], rhs=PT[:, u, :], start=True, stop=True)) for u in range(nu)],
                         r=[Pk, PTk], w=[kB])
                    yield
                    if not last:
                        S.act(lambda e, nb=nb: e.activation(out=Pa_[nb][:, 0:nu, :].rearrange("p a b -> p (a b)"), in_=pA[0:64, 0:nu * 64], func=AF.Copy),
                              r=[kA], w=[("Pa" + sfx, nb)])
                    S.dve(lambda e, nb=nb: e.tensor_copy(out=PaT_[nb][:, 0:nu, :].rearrange("p a b -> p (a b)"), in_=pB_[0:64, 0:nu * 64]),
                          r=[kB], w=[("PaT" + sfx, nb)])
                    yield
                    S.pe([(lambda e, u=u, nb=nb: e.matmul(pC[0:64, u * 64:(u + 1) * 64], lhsT=PaT_[nb][:, u, :], rhs=Rb_[:, u, :], start=True, stop=True)) for u in range(nu)],
                         r=[("PaT" + sfx, nb), "Rb" + sfx], w=[kC])
                    yield
                    if not last:
                        S.dve(lambda e: e.tensor_tensor(out=R32_[:, 0:nu, :].rearrange("p a b -> p (a b)"), in0=pC[0:64, 0:nu * 64],
                                                        in1=R32_[:, 0:nu, :].rearrange("p a b -> p (a b)"), op=ALU.add), r=[kC, "R32" + sfx], w=["R32" + sfx])
                        S.pool(lambda e: e.tensor_copy(out=Rb_[:, 0:nu, :], in_=R32_[:, 0:nu, :]), r=["R32" + sfx], w=["Rb" + sfx])
                    else:
                        S.dve(lambda e: e.tensor_tensor(out=Rr[0:64, c0:c0 + nck].rearrange("p a b c -> p (a b c)"), in0=pC[0:64, 0:nu * 64],
                                                        in1=R32_[:, 0:nu, :].rearrange("p a b -> p (a b)"), op=ALU.add), r=[kC, "R32" + sfx], w=["Rr"])
                    yield
                    P, PT, Pk, PTk = Pa_[nb], PaT_[nb], ("Pa" + sfx, nb), ("PaT" + sfx, nb)

            glist = list(range(0, NGR, 4))
            for gi0 in range(0, len(glist), 2):
                active = [prep_gen(glist[gi0], 0)]
                if gi0 + 1 < len(glist):
                    active.append(prep_gen(glist[gi0 + 1], 1))
                while active:
                    for g_ in list(active):
                        try:
                            next(g_)
                        except StopIteration:
                            active.remove(g_)
            S.marks.append(("g2_h%d_prep" % h, dict(S.cnt)))
            for d in range(2):
                S.dve(lambda e, d=d: e.memset(S32[d][:, :], 0.0), w=[("S32", d)])
                S.dve(lambda e, d=d: e.memset(Sbf[d][:, :], 0.0), w=[("Sbf", d)])
            for t in range(NGR):
                cc = [t, NGR - 1 - t]
                pSs = [pb[4], pb[6]]
                pOs = [pb[5], pb[7]]
                kSs = [("pb", 4), ("pb", 6)]
                kOs = [("pb", 5), ("pb", 7)]
                css = [slice(cc[d] * 64, (cc[d] + 1) * 64) for d in range(2)]
                for d in range(2):
                    S.pe(lambda e: e.matmul(pSs[d][0:64, 0:256], lhsT=kT[:, css[d]], rhs=Sbf[d][:, :], start=True, stop=True), r=["kT", ("Sbf", d)], w=[kSs[d]])
                for d in range(2):
                    S.pe(lambda e: e.transpose(out=ptb[0:64, d * 128:(d + 1) * 128], in_=kT[:, css[d]], identity=identb[:, :]), r=["kT", "identb"], w=["ptb"])
                for d in range(2):
                    S.pe(lambda e: e.matmul(pOs[d][0:64, 0:256], lhsT=qT[:, css[d]], rhs=Sbf[d][:, :], start=True, stop=True), r=["qT", ("Sbf", d)], w=[kOs[d]])
                for d in range(2):
                    c = cc[d]
                    S.dve(lambda e: e.scalar_tensor_tensor(out=xb[d][0:64, :], in0=pSs[d][0:64, 0:256], scalar=negc[:, c, d, h:h + 1], in1=vtok[:, c, :],
                                                           op0=ALU.mult, op1=ALU.add), r=[kSs[d], "negc", "vtok"], w=[("xb", d)])
                for d in range(2):
                    c = cc[d]
                    S.act(lambda e: e.activation(out=kd[d][0:64, :], in_=ptb[0:64, d * 128:(d + 1) * 128], func=AF.Copy, scale=kds[:, c, d, h:h + 1]),
                          r=["ptb", "kds"], w=[("kd", d)])
                for d in range(2):
                    c = cc[d]
                    S.pe(lambda e: e.matmul(pSs[d][0:64, 256:512], lhsT=Rr[:, c, d, :], rhs=xb[d][:, :], start=True, stop=True), r=["Rr", ("xb", d)], w=[kSs[d]])
                for d in range(2):
                    c = cc[d]
                    S.act(lambda e: e.activation(out=t1[d][:, :], in_=pOs[d][0:64, 0:256], func=AF.Copy, scale=egc[:, c, d, h:h + 1]),
                          r=[kOs[d], "egc"], w=[("t1", d)])
                for d in range(2):
                    c = cc[d]
                    S.act(lambda e: e.activation(out=vn[d][0:64, :], in_=pSs[d][0:64, 256:512], func=AF.Copy, scale=beta[:, c, d, h:h + 1]),
                          r=[kSs[d], "beta"], w=[("vn", d)])
                for d in range(2):
                    c = cc[d]
                    S.pe(lambda e: e.matmul(pSs[d][:, 0:256], lhsT=kd[d][:, :], rhs=vn[d][:, :], start=True, stop=True), r=[("kd", d), ("vn", d)], w=[kSs[d]])
                for d in range(2):
                    c = cc[d]
                    S.pe(lambda e: e.matmul(pOs[d][0:64, 256:512], lhsT=At[:, c, d, :], rhs=vn[d][:, :], start=True, stop=True), r=["At", ("vn", d)], w=[kOs[d]])
                for d in range(2):
                    c = cc[d]
                    S.dve(lambda e: e.scalar_tensor_tensor(out=Sbf[d][:, :], in0=S32[d][:, :], scalar=egl[:, c, d, h:h + 1], in1=pSs[d][:, 0:256],
                                                           op0=ALU.mult, op1=ALU.add), r=[("S32", d), "egl", kSs[d]], w=[("Sbf", d)])
                for d in range(2):
                    c = cc[d]
                    if t < 32 or (t == 32 and d == 0):
                        S.dve(lambda e: e.tensor_tensor(out=ob[:, c, :], in0=pOs[d][0:64, 256:512], in1=t1[d][:, :], op=ALU.add), r=[kOs[d], ("t1", d)], w=[("ob", c)])
                    else:
                        S.dve(lambda e: e.tensor_tensor(out=t1[d][:, :], in0=pOs[d][0:64, 256:512], in1=t1[d][:, :], op=ALU.add), r=[kOs[d], ("t1", d)], w=[("t1", d)])
                        S.dve(lambda e: e.tensor_tensor(out=ob[:, c, :], in0=ob[:, c, :], in1=t1[d][:, :], op=ALU.add), r=[("ob", c), ("t1", d)], w=[("ob", c)])
                for d in range(2):
                    c = cc[d]
                    S.dve(lambda e: e.scalar_tensor_tensor(out=S32[d][:, :], in0=S32[d][:, :], scalar=egl[:, c, d, h:h + 1], in1=pSs[d][:, 0:256],
                                                           op0=ALU.mult, op1=ALU.add), r=[("S32", d), "egl", kSs[d]], w=[("S32", d)])
            S.marks.append(("g2_h%d_scan" % h, dict(S.cnt)))
            S.marks.append(("g2_h%d_scan" % h, dict(S.cnt)))
            for c in range(NGR):
                S.act(lambda e, c=c: e.activation(out=junk2[c % 2][:, :], in_=ob[:, c, :], func=AF.Square, accum_out=ssq[:, c:c + 1]), r=[("ob", c)], w=[("junk2", c % 2), ("ssq", c)])
            S.act(lambda e: e.activation(out=rstd[:, 0:NGR], in_=ssq[:, 0:NGR], func=AF.Sqrt, bias=epsb[0:64, 0:1], scale=1.0 / 256), r=[("ssq", c_) for c_ in range(NGR)] + ["epsb"], w=["rstd"])
            S.dve(lambda e: e.reciprocal(out=rstd[:, 0:NGR], in_=rstd[:, 0:NGR]), r=["rstd"], w=["rstd"])
            for c in range(NGR):
                S.dve(lambda e, c=c: e.scalar_tensor_tensor(out=ob[:, c, :], in0=ob[:, c, :], scalar=rstd[:, c:c + 1], in1=gon[:, :], op0=ALU.mult, op1=ALU.mult),
                      r=[("ob", c), "rstd", "gon"], w=[("ob", c)])
            for half in range(2):
                S.load(lambda e: e.dma_start(out=vz[:, :], in_=z_scr[2 * h + half, :, :]), r=[("z_scr", 2 * h + half)], w=["vz"])
                for gi_, c0 in enumerate(range(0, NGR, 4)):
                    nck = min(4, NGR - c0)
                    ybb = yb[gi_ % 2]
                    yk = ("yb", gi_ % 2)
                    S.pe([(lambda e, ci=ci: e.transpose(out=ptb[:, 256 + ci * 64:256 + (ci + 1) * 64], in_=ob[:, c0 + ci, half * 128:(half + 1) * 128], identity=identb[0:64, 0:64]))
                          for ci in range(nck)], r=[("ob", c0 + ci) for ci in range(nck)] + ["identb"], w=["ptb2"])
                    S.dve(lambda e: e.tensor_tensor(out=ybb[:, 0:nck * 64], in0=ptb[:, 256:256 + nck * 64], in1=vz[:, c0 * 64:(c0 + nck) * 64], op=ALU.mult),
                          r=["ptb2", "vz"], w=[yk])
                    S.store(lambda e: e.dma_start(out=y_scr[2 * h + half, :, c0 * 64:(c0 + nck) * 64], in_=ybb[:, 0:nck * 64]), r=[yk], w=[("y_scr", 2 * h + half)])
        S.barrier()

    aoff_g2 = aoff["p"]
    aoff["p"] = aoff_after_p0
    wo = av([128, 16, D], BF16)
    wo_st = [av([128, D]) for _ in range(2)]
    ytile = [av([128, 16, 128], BF16) for _ in range(2)]
    h1t = [av([128, D]) for _ in range(2)]

    def load_wout(w_ap):
        wv = w_ap.rearrange("(c p) n -> p c n", p=128)
        for c in range(16):
            b = c % 2
            S.load(lambda e, c=c, b=b: e.dma_start(out=wo_st[b][:, :], in_=wv[:, c, :]), w=[("wo_st", b)])
            S.pool(lambda e, c=c, b=b: e.tensor_copy(out=wo[:, c, :], in_=wo_st[b][:, :]), r=[("wo_st", b)], w=["wo"])

    def phase_outproj(s, layer):
        load_wout(g_w_out if layer == 0 else m_w_out)
        yv = y_scr.rearrange("c p t -> p c t")
        for tt, (t0, n) in enumerate(TOKTILES):
            if layer == 1 and tt == 0:
                continue
            b = tt % 2
            S.load(lambda e: e.dma_start(out=ytile[b][:, :, 0:n], in_=yv[:, :, t0:t0 + n]), r=[("y_scr", c) for c in range(16)], w=[("ytile", b)])
            if layer == 0:
                load_x_tile(s, tt)
            else:
                S.load(lambda e: e.dma_start(out=xt[b][0:n, :], in_=h1_scr[t0:t0 + n, :]), r=["h1_scr"], w=[("xt", b)])
            for hf in range(2):
                bk = mmbank()
                S.pe([(lambda e, c=c: e.matmul(pb[bk][0:n, 0:512], lhsT=ytile[b][:, c, 0:n], rhs=wo[:, c, hf * 512:(hf + 1) * 512], start=(c == 0), stop=(c == 15)))
                      for c in range(16)], r=[("ytile", b), "wo"], w=[("pb", bk)])
                S.dve(lambda e: e.tensor_tensor(out=h1t[b][0:n, hf * 512:(hf + 1) * 512], in0=pb[bk][0:n, 0:512], in1=xt[b][0:n, hf * 512:(hf + 1) * 512], op=ALU.add),
                      r=[("pb", bk), ("xt", b)], w=[("h1t", b)])
            if layer == 0:
                S.store(lambda e: e.dma_start(out=h1_scr[t0:t0 + n, :], in_=h1t[b][0:n, :]), r=[("h1t", b)], w=["h1_scr"])
                norm_transpose(tt, h1t[b][0:n, :], ("h1t", b), n, t0, 1)
            else:
                r0 = t0 - 64
                S.store(lambda e: e.dma_start(out=out[s, r0:r0 + n, :], in_=h1t[b][0:n, :]), r=[("h1t", b)], w=["out"])
        S.barrier()


    aoff["p"] = aoff_after_p0
    wM = av([128, 8, 832], BF16)
    wMst = [av([128, 8, 128]) for _ in range(2)]
    wMz = [av([128, 8, 128], BF16) for _ in range(2)]
    raw = av([128, 7, 512])
    sqt = av([128, 7, 512], BF16)
    rs1 = [av([128, 512]) for _ in range(2)]
    o1 = [av([128, 7, 512], BF16) for _ in range(2)]
    kpg = av([64, 512], BF16)
    tmpa = av([64, 512])
    tmpb = av([64, 512])
    zo = [av([128, 512], BF16) for _ in range(2)]
    ropeb1 = av([64, 2, 512])
    gq = sb("gq", [128, 4])
    gkv = sb("gkv", [128, 2])
    gqa = sb("gqa", [128, 1])
    gqb = sb("gqb", [64, 1])
    gka = sb("gka", [128, 1])
    gkb = sb("gkb", [64, 1])
    rotb = sb("rotb", [64, 64], BF16)
    rot_st = sb("rot_st", [64, 64])
    nshift = sb("nshift", [128, 1])
    m_w_in_v = m_w_in.rearrange("(c p) n -> p c n", p=128)

    def setup_mla():
        S.load(lambda e: e.dma_start(out=gq[:], in_=m_qn.rearrange("(c p) -> p c", p=128), allow_slow_non_contiguous=True), w=["gq"])
        S.load(lambda e: e.dma_start(out=gkv[:], in_=m_kvn.rearrange("(c p) -> p c", p=128), allow_slow_non_contiguous=True), w=["gkv"])
        S.load(lambda e: e.dma_start(out=gqa[:], in_=m_qg[0:128].rearrange("(p c) -> p c", c=1)), w=["gqa"])
        S.load(lambda e: e.dma_start(out=gqb[:], in_=m_qg[128:192].rearrange("(p c) -> p c", c=1)), w=["gqb"])
        S.load(lambda e: e.dma_start(out=gka[:], in_=m_kg[0:128].rearrange("(p c) -> p c", c=1)), w=["gka"])
        S.load(lambda e: e.dma_start(out=gkb[:], in_=m_kg[128:192].rearrange("(p c) -> p c", c=1)), w=["gkb"])
        S.load(lambda e: e.dma_start(out=rot_st[:], in_=c_rot[:, :]), w=["rot_st"])
        S.dve(lambda e: e.tensor_copy(out=rotb[:], in_=rot_st[:]), r=["rot_st"], w=["rotb"])
        S.dve(lambda e: e.memset(nshift[:], -8.0), w=["nshift"])

    def rstd_from_psum(pbank, npart, w, div, dst):
        S.act(lambda e: e.activation(out=dst[0:npart, 0:w], in_=pbank[0:npart, 0:w], func=AF.Sqrt, bias=epsb[0:npart, 0:1], scale=1.0 / div),
              r=[("pb", 3), "epsb"], w=[("rs", id(dst))])
        S.dve(lambda e: e.reciprocal(out=dst[0:npart, 0:w], in_=dst[0:npart, 0:w]), r=[("rs", id(dst))], w=[("rs", id(dst))])

    def phase_m1(s):
        for f in range(7):
            b = f % 2
            nc_ = 128 if f < 6 else 64
            S.load(lambda e, f=f, b=b, nc_=nc_: e.dma_start(out=wMst[b][:, :, 0:nc_], in_=m_w_in_v[:, :, f * 128:f * 128 + nc_]), w=[("wMst", b)])
            S.pool(lambda e, f=f, b=b, nc_=nc_: e.tensor_copy(out=wM[:, :, f * 128:f * 128 + nc_], in_=wMst[b][:, :, 0:nc_]), r=[("wMst", b)], w=["wM"])
        for bi, (c0, w) in enumerate(COLBLKS):
            ob_ = o1[bi % 2]
            ok = ("o1", bi % 2)
            for f in range(7):
                nr = 128 if f < 6 else 64
                bk = mmbank()
                S.pe([(lambda e, c=c: e.matmul(pb[bk][0:nr, 0:w], lhsT=wM[:, c, f * 128:f * 128 + nr], rhs=hnT[:, c, c0:c0 + w], start=(c == 0), stop=(c == 7)))
                      for c in range(8)], r=["wM"] + gkeys("hnT", c0, w), w=[("pb", bk)])
                S.act(lambda e: e.activation(out=raw[0:nr, f, 0:w], in_=pb[bk][0:nr, 0:w], func=AF.Copy), r=[("pb", bk)], w=[("raw", f)])
                S.dve(lambda e: e.tensor_tensor(out=sqt[0:nr, f, 0:w], in0=raw[0:nr, f, 0:w], in1=raw[0:nr, f, 0:w], op=ALU.mult), r=[("raw", f)], w=[("sqt", f)])
            for (fl, div, gt, ri) in (([0, 1, 2, 3], 512.0, gq, 0), ([4, 5], 256.0, gkv, 1)):
                S.pe([(lambda e, i=i, f=f: e.matmul(pb[3][:, 0:w], lhsT=onesb[:, :], rhs=sqt[:, f, 0:w], start=(i == 0), stop=(i == len(fl) - 1)))
                      for i, f in enumerate(fl)], r=[("sqt", f) for f in fl] + ["onesb"], w=[("pb", 3)])
                rstd_from_psum(pb[3], 128, w, div, rs1[ri])
                for i, f in enumerate(fl):
                    S.dve(lambda e, i=i, f=f: e.scalar_tensor_tensor(out=ob_[:, f, 0:w], in0=raw[:, f, 0:w], scalar=gt[:, i:i + 1], in1=rs1[ri][:, 0:w],
                                                                     op0=ALU.mult, op1=ALU.mult), r=[("raw", f), ("rs", id(rs1[ri]))], w=[ok])
            S.dve(lambda e: e.tensor_scalar(out=kpg[:, 0:w], in0=raw[0:64, 6, 0:w], scalar1=gkb[:, 0:1], scalar2=None, op0=ALU.mult), r=[("raw", 6), "gkb"], w=["kpg"])
            S.pe(lambda e: e.matmul(pb[3][0:64, 0:w], lhsT=rotb[:, :], rhs=kpg[:, 0:w], start=True, stop=True), r=["kpg", "rotb"], w=[("pb", 3)])
            S.load(lambda e: e.dma_start(out=ropeb1[:, :, 0:w], in_=c_rope[:, :, c0:c0 + w]), w=["ropeb1"])
            S.dve(lambda e: e.tensor_tensor(out=tmpa[:, 0:w], in0=pb[3][0:64, 0:w], in1=ropeb1[:, 1, 0:w], op=ALU.mult), r=[("pb", 3), "ropeb1"], w=["tmpa"])
            S.dve(lambda e: e.tensor_tensor(out=tmpb[:, 0:w], in0=kpg[:, 0:w], in1=ropeb1[:, 0, 0:w], op=ALU.mult), r=["kpg", "ropeb1"], w=["tmpb"])
            S.dve(lambda e: e.tensor_tensor(out=ob_[0:64, 6, 0:w], in0=tmpa[:, 0:w], in1=tmpb[:, 0:w], op=ALU.add), r=["tmpa", "tmpb"], w=[ok])
            for f in range(6):
                S.store(lambda e, f=f: e.dma_start(out=qk_scr[f, :, c0:c0 + w], in_=ob_[:, f, 0:w]), r=[ok], w=[("qk_scr", f)])
            S.store(lambda e: e.dma_start(out=qk_scr[6, 0:64, c0:c0 + w], in_=ob_[0:64, 6, 0:w]), r=[ok], w=[("qk_scr", 6)])
            S.store(lambda e: e.dma_start(out=qk_scr[7, 0:64, c0:c0 + w], in_=sqt[0:64, 6, 0:w]), r=[("sqt", 6)], w=[("qk_scr", 7)])
        for hh in range(16):
            b = hh % 2
            S.load(lambda e, hh=hh, b=b: e.dma_start(out=wMst[b][:, :, :], in_=m_w_in_v[:, :, 832 + hh * 128:832 + (hh + 1) * 128]), w=[("wMst", b)])
            S.pool(lambda e, b=b: e.tensor_copy(out=wMz[b][:, :, :], in_=wMst[b][:, :, :]), r=[("wMst", b)], w=[("wMz", b)])
            for bi, (c0, w) in enumerate(COLBLKS):
                bk = mmbank()
                zb = zo[bi % 2]
                S.pe([(lambda e, c=c: e.matmul(pb[bk][:, 0:w], lhsT=wMz[b][:, c, :], rhs=hnT[:, c, c0:c0 + w], start=(c == 0), stop=(c == 7)))
                      for c in range(8)], r=[("wMz", b)] + gkeys("hnT", c0, w), w=[("pb", bk)])
                S.act(lambda e: e.activation(out=zb[:, 0:w], in_=pb[bk][:, 0:w], func=AF.Silu), r=[("pb", bk)], w=[("zo", bi % 2)])
                S.store(lambda e: e.dma_start(out=z_scr[hh, :, c0:c0 + w], in_=zb[:, 0:w]), r=[("zo", bi % 2)], w=[("z_scr", hh)])
        S.barrier()

    aoff["p"] = 0
    cqT = av([128, 4, LE], BF16)
    ckvT = av([128, 2, LE], BF16)
    krT = av([64, LE], BF16)
    sqk = av([128, LE], BF16)
    qTa = av([128, LE], BF16)
    qTb = av([128, LE], BF16)
    kTa = av([128, LE], BF16)
    kTb = av([128, LE], BF16)
    zq = [av([128, 512], BF16) for _ in range(2)]
    vaug = av([128, 33, 130], BF16)
    wq_st = av([128, 4, 192])
    wq = av([128, 4, 192], BF16)
    wkv_st = av([128, 2, 256])
    wkv = av([128, 2, 256], BF16)
    MSET = []
    for si_ in range(2):
        MSET.append(dict(ra=av([128, 512]), rb=av([64, 512]), sqa=av([128, 512], BF16), sqb2=av([128, 512], BF16), rsq=av([128, 512]),
                         qbg=av([64, 512], BF16), t2a=av([64, 512], BF16), t2b=av([64, 512], BF16), rope=av([64, 2, 512])))
        MSET[-1]["rsk"] = MSET[-1]["rsq"]
    MSET[0].update(banks=(pb[0], pb[3]), bkeys=(("pb", 0), ("pb", 3)))
    MSET[1].update(banks=(pb[1], pb[7]), bkeys=(("pb", 1), ("pb", 7)))
    pT = [av([128, 512], BF16) for _ in range(3)]
    rdn = av([1, 512])
    dacc = [[av([128, 512]) for _ in range(2)] for _ in range(2)]
    ones128 = av([128, 1])
    rbc = av([128, 512])
    ytb = [av([128, 512], BF16) for _ in range(2)]
    KT = [(64 + 128 * i, 128) for i in range(32)] + [(PAD, 16)]
    SCALE = 192.0 ** -0.5
    wuq_v = m_wuq.rearrange("(c p) n -> p c n", p=128)
    wukv_v = m_wukv.rearrange("(c p) n -> p c n", p=128)

    def phase_m2(s, heads=range(16)):
        for f in range(4):
            S.load(lambda e, f=f: e.dma_start(out=cqT[:, f, :], in_=qk_scr[f, :, :]), r=[("qk_scr", f)], w=["cqT"])
        for f in range(2):
            S.load(lambda e, f=f: e.dma_start(out=ckvT[:, f, :], in_=qk_scr[4 + f, :, :]), r=[("qk_scr", 4 + f)], w=["ckvT"])
        S.load(lambda e: e.dma_start(out=krT[:, :], in_=qk_scr[6, 0:64, :]), r=[("qk_scr", 6)], w=["krT"])
        S.load(lambda e: e.dma_start(out=sqk[0:64, :], in_=qk_scr[7, 0:64, :]), r=[("qk_scr", 7)], w=["sqk"])
        for t_, k_ in ((sqk, "sqk"), (qTb, "qTb"), (kTb, "kTb"), (MSET[0]["sqb2"], "sqb2_m0"), (MSET[1]["sqb2"], "sqb2_m1")):
            S.dve(lambda e, t_=t_: e.memset(t_[64:128, :], 0.0), w=[k_])
        S.dve(lambda e: e.memset(ones128[:, :], 1.0), w=["ones128"])
        for h in heads:
            S.marks.append(("m2_h%d_start" % h, dict(S.cnt)))
            S.load(lambda e: e.dma_start(out=wq_st[:], in_=wuq_v[:, :, h * 192:(h + 1) * 192]), w=["wq_st"])
            S.pool(lambda e: e.tensor_copy(out=wq[:], in_=wq_st[:]), r=["wq_st"], w=["wq"])
            S.load(lambda e: e.dma_start(out=wkv_st[:], in_=wukv_v[:, :, h * 256:(h + 1) * 256]), w=["wkv_st"])
            S.pool(lambda e: e.tensor_copy(out=wkv[:], in_=wkv_st[:]), r=["wkv_st"], w=["wkv"])
            def prol_gen(c0, w, si):
                Bf = MSET[si]
                ra_, rb_, sqa_, sqb2_, rsq_, rsk_, qbg_, t2a_, t2b_, rope_ = (Bf[k_] for k_ in ("ra", "rb", "sqa", "sqb2", "rsq", "rsk", "qbg", "t2a", "t2b", "rope"))
                pP, pQ = Bf["banks"]
                kP, kQ = Bf["bkeys"]
                x_ = "_m%d" % si
                cs = slice(c0, c0 + w)
                S.load(lambda e: e.dma_start(out=rope_[:, :, 0:w], in_=c_rope[:, :, cs]), w=["rope" + x_])
                S.pe([(lambda e, c=c: e.matmul(pP[:, 0:w], lhsT=wq[:, c, 0:128], rhs=cqT[:, c, cs], start=(c == 0), stop=(c == 3))) for c in range(4)],
                     r=["wq", "cqT"], w=[kP])
                yield
                S.act(lambda e: e.activation(out=ra_[:, 0:w], in_=pP[:, 0:w], func=AF.Copy), r=[kP], w=["ra" + x_])
                yield
                S.pe([(lambda e, c=c: e.matmul(pP[0:64, 0:w], lhsT=wq[:, c, 128:192], rhs=cqT[:, c, cs], start=(c == 0), stop=(c == 3))) for c in range(4)],
                     r=["wq", "cqT"], w=[kP])
                S.dve(lambda e: e.tensor_tensor(out=sqa_[:, 0:w], in0=ra_[:, 0:w], in1=ra_[:, 0:w], op=ALU.mult), r=["ra" + x_], w=["sqa" + x_])
                yield
                S.act(lambda e: e.activation(out=rb_[:, 0:w], in_=pP[0:64, 0:w], func=AF.Copy), r=[kP], w=["rb" + x_])
                yield
                S.dve(lambda e: e.tensor_tensor(out=sqb2_[0:64, 0:w], in0=rb_[:, 0:w], in1=rb_[:, 0:w], op=ALU.mult), r=["rb" + x_], w=["sqb2" + x_])
                yield
                S.pe([lambda e: e.matmul(pQ[:, 0:w], lhsT=onesb[:, :], rhs=sqa_[:, 0:w], start=True, stop=False),
                      lambda e: e.matmul(pQ[:, 0:w], lhsT=onesb[:, :], rhs=sqb2_[:, 0:w], start=False, stop=True)], r=["sqa" + x_, "sqb2" + x_, "onesb"], w=[kQ])
                S.pe([(lambda e, c=c: e.matmul(pP[:, 0:w], lhsT=wkv[:, c, 0:128], rhs=ckvT[:, c, cs], start=(c == 0), stop=(c == 1))) for c in range(2)],
                     r=["wkv", "ckvT"], w=[kP])
                yield
                S.act(lambda e: e.activation(out=rsq_[:, 0:w], in_=pQ[:, 0:w], func=AF.Sqrt, bias=epsb[:, 0:1], scale=1.0 / 192.0), r=[kQ, "epsb"], w=["rsq" + x_])
                yield
                S.dve(lambda e: e.reciprocal(out=rsq_[:, 0:w], in_=rsq_[:, 0:w]), r=["rsq" + x_], w=["rsq" + x_])
                yield
                S.dve(lambda e: e.scalar_tensor_tensor(out=qbg_[:, 0:w], in0=rb_[:, 0:w], scalar=gqb[:, 0:1], in1=rsq_[0:64, 0:w], op0=ALU.mult, op1=ALU.mult),
                      r=["rb" + x_, "rsq" + x_, "gqb"], w=["qbg" + x_])
                S.dve(lambda e: e.scalar_tensor_tensor(out=qTa[:, cs], in0=ra_[:, 0:w], scalar=gqa[:, 0:1], in1=rsq_[:, 0:w], op0=ALU.mult, op1=ALU.mult),
                      r=["ra" + x_, "rsq" + x_, "gqa"], w=["qTa"])
                yield
                S.pe(lambda e: e.matmul(pQ[0:64, 0:w], lhsT=rotb[:, :], rhs=qbg_[:, 0:w], start=True, stop=True), r=["qbg" + x_, "rotb"], w=[kQ])
                S.act(lambda e: e.activation(out=ra_[:, 0:w], in_=pP[:, 0:w], func=AF.Copy), r=[kP], w=["ra" + x_])
                S.dve(lambda e: e.tensor_tensor(out=t2b_[:, 0:w], in0=qbg_[:, 0:w], in1=rope_[:, 0, 0:w], op=ALU.mult), r=["qbg" + x_, "rope" + x_], w=["t2b" + x_])
                yield
                S.dve(lambda e: e.tensor_tensor(out=t2a_[:, 0:w], in0=pQ[0:64, 0:w], in1=rope_[:, 1, 0:w], op=ALU.mult), r=[kQ, "rope" + x_], w=["t2a" + x_])
                S.dve(lambda e: e.tensor_tensor(out=sqa_[:, 0:w], in0=ra_[:, 0:w], in1=ra_[:, 0:w], op=ALU.mult), r=["ra" + x_], w=["sqa" + x_])
                yield
                S.dve(lambda e: e.tensor_tensor(out=qTb[0:64, cs], in0=t2a_[:, 0:w], in1=t2b_[:, 0:w], op=ALU.add), r=["t2a" + x_, "t2b" + x_], w=["qTb"])
                S.pe([lambda e: e.matmul(pQ[:, 0:w], lhsT=onesb[:, :], rhs=sqa_[:, 0:w], start=True, stop=False),
                      lambda e: e.matmul(pQ[:, 0:w], lhsT=onesb[:, :], rhs=sqk[:, cs], start=False, stop=True)], r=["sqa" + x_, "sqk", "onesb"], w=[kQ])
                yield
                S.act(lambda e: e.activation(out=rsk_[:, 0:w], in_=pQ[:, 0:w], func=AF.Sqrt, bias=epsb[:, 0:1], scale=1.0 / 192.0), r=[kQ, "epsb"], w=["rsq" + x_])
                yield
                S.dve(lambda e: e.reciprocal(out=rsk_[:, 0:w], in_=rsk_[:, 0:w]), r=["rsq" + x_], w=["rsq" + x_])
                yield
                S.dve(lambda e: e.scalar_tensor_tensor(out=kTa[:, cs], in0=ra_[:, 0:w], scalar=gka[:, 0:1], in1=rsk_[:, 0:w], op0=ALU.mult, op1=ALU.mult),
                      r=["ra" + x_, "rsq" + x_, "gka"], w=["kTa"])
                S.dve(lambda e: e.tensor_tensor(out=kTb[0:64, cs], in0=krT[:, cs], in1=rsk_[0:64, 0:w], op=ALU.mult), r=["krT", "rsq" + x_], w=["kTb"])
                yield

            for bi0 in range(0, len(COLBLKS), 2):
                active = [prol_gen(COLBLKS[bi0][0], COLBLKS[bi0][1], 0)]
                if bi0 + 1 < len(COLBLKS):
                    active.append(prol_gen(COLBLKS[bi0 + 1][0], COLBLKS[bi0 + 1][1], 1))
                while active:
                    for g_ in list(active):
                        try:
                            next(g_)
                        except StopIteration:
                            active.remove(g_)
            S.marks.append(("m2_h%d_qk" % h, dict(S.cnt)))
            for kt, (k0, nk) in enumerate(KT):
                bk = mmbank()
                S.pe([(lambda e, c=c: e.matmul(pb[bk][0:nk, 0:128], lhsT=ckvT[:, c, k0:k0 + nk], rhs=wkv[:, c, 128:256], start=(c == 0), stop=(c == 1))) for c in range(2)],
                     r=["wkv", "ckvT"], w=[("pb", bk)])
                S.act(lambda e: e.activation(out=vaug[0:nk, kt, 0:128], in_=pb[bk][0:nk, 0:128], func=AF.Copy), r=[("pb", bk)], w=["vaug"])
            S.marks.append(("m2_h%d_v" % h, dict(S.cnt)))
            steps = [(qi, kt) for qi in range(8) for kt in range(33)]
            SB = [0, 1, 3]

            def emit_scores(st):
                qi, kt = steps[st]
                k0, nk = KT[kt]
                q0 = 64 + 512 * qi
                bk = SB[st % 3]
                S.pe([lambda e: e.matmul(pb[bk][0:nk, 0:512], lhsT=kTa[:, k0:k0 + nk], rhs=qTa[:, q0:q0 + 512], start=True, stop=False),
                      lambda e: e.matmul(pb[bk][0:nk, 0:512], lhsT=kTb[:, k0:k0 + nk], rhs=qTb[:, q0:q0 + 512], start=False, stop=True)],
                     r=["kTa", "kTb", "qTa", "qTb"], w=[("pb", bk)])

            emit_scores(0)
            emit_scores(1)
            for st, (qi, kt) in enumerate(steps):
                k0, nk = KT[kt]
                q0 = 64 + 512 * qi
                bk = SB[st % 3]
                ab = qi % 2
                acc_o = pb[4 + 2 * ab]
                acc_d = pb[5]
                if st + 2 < len(steps):
                    emit_scores(st + 2)
                S.act(lambda e: e.activation(out=pT[st % 3][0:nk, :], in_=pb[bk][0:nk, 0:512], func=AF.Exp, bias=nshift[0:nk, 0:1], scale=SCALE),
                      r=[("pb", bk), "nshift"], w=[("pT", st % 3)])
                S.pe(lambda e: e.matmul(acc_o[:, 0:512], lhsT=vaug[0:nk, kt, 0:128], rhs=pT[st % 3][0:nk, :], start=(kt == 0), stop=(kt == 32)),
                     r=[("pT", st % 3), "vaug"], w=[("acc", ab)])
                par = kt % 2
                if kt < 2:
                    S.dve(lambda e: e.tensor_copy(out=dacc[ab][par][:, :], in_=pT[st % 3][:, :]), r=[("pT", st % 3)], w=[("dacc", ab, par)])
                else:
                    S.dve(lambda e: e.tensor_tensor(out=dacc[ab][par][0:nk, :], in0=dacc[ab][par][0:nk, :], in1=pT[st % 3][0:nk, :], op=ALU.add),
                          r=[("pT", st % 3), ("dacc", ab, par)], w=[("dacc", ab, par)])
                if kt == 0:
                    S.load(lambda e: e.dma_start(out=zq[ab][:, :], in_=z_scr[h, :, q0:q0 + 512]), r=[("z_scr", h)], w=[("zq", ab)])
                if kt == 32:
                    yb_ = ytb[qi % 2]
                    S.dve(lambda e: e.tensor_tensor(out=dacc[ab][0][:, :], in0=dacc[ab][0][:, :], in1=dacc[ab][1][:, :], op=ALU.add),
                          r=[("dacc", ab, 0), ("dacc", ab, 1)], w=[("dacc", ab, 0)])
                    S.pe(lambda e: e.matmul(acc_d[0:1, 0:512], lhsT=ones128[:, 0:1], rhs=dacc[ab][0][:, :], start=True, stop=True),
                         r=[("dacc", ab, 0), "ones128"], w=["accd"])
                    S.dve(lambda e: e.reciprocal(out=rdn[0:1, :], in_=acc_d[0:1, 0:512]), r=["accd"], w=["rdn", "rbc"])
                    S.pe(lambda e: e.matmul(pb[7][:, 0:512], lhsT=ones32[0:1, :], rhs=rdn[0:1, :], start=True, stop=True), r=["rdn", "ones32"], w=[("pb", 7)])
                    S.act(lambda e: e.activation(out=rbc[:, :], in_=pb[7][:, 0:512], func=AF.Copy), r=[("pb", 7)], w=["rbc", "rdn"])
                    S.dve(lambda e: e.tensor_tensor(out=rbc[:, :], in0=acc_o[:, 0:512], in1=rbc[:, :], op=ALU.mult), r=[("acc", ab), "rbc"], w=["rbc"])
                    S.dve(lambda e: e.tensor_tensor(out=yb_[:, :], in0=rbc[:, :], in1=zq[ab][:, :], op=ALU.mult), r=["rbc", ("zq", ab)], w=[("ytb", qi % 2)])
                    S.store(lambda e: e.dma_start(out=y_scr[h, :, q0:q0 + 512], in_=yb_[:, :]), r=[("ytb", qi % 2)], w=[("y_scr", h)])
            S.barrier()

    def dump(name, t, shape, dt, keys):
        d = nc.dram_tensor("dbg_" + name, list(shape), dt, kind="ExternalOutput").ap()
        S.store(lambda e: e.dma_start(out=d, in_=t), r=keys)

    setup()
    setup_mla()
    for s in range(NSEQ):
        S.marks.append(("start%d" % s, dict(S.cnt)))
        phase_p0(s)
        S.marks.append(("p0", dict(S.cnt)))
        phase_g1(s)
        S.marks.append(("g1", dict(S.cnt)))
        S.barrier()
        if debug == "g2":
            phase_g2(s, heads=[0])
            break
        if debug not in ("m2", "m1"):
            phase_g2(s)
        S.marks.append(("g2", dict(S.cnt)))
        phase_outproj(s, 0)
        S.marks.append(("op0", dict(S.cnt)))
        if debug == "l0":
            break
        phase_m1(s)
        S.marks.append(("m1", dict(S.cnt)))
        if debug == "m1":
            break
        if debug == "m2":
            phase_m2(s, heads=[0])
            break
        phase_m2(s)
        S.marks.append(("m2", dict(S.cnt)))
        phase_outproj(s, 1)
        S.marks.append(("op1", dict(S.cnt)))
    print("arena max", aoff.get("max"), "ninst", S.ninst, S.cnt, "sbuf left", nc.sbuf_bytes_remaining)
    nc._marks = S.marks
    S.finish()
    es.close()
    return nc


def _consts():
    ident = np.eye(128, dtype=np.float32)
    i = np.arange(64)
    U = (i[:, None] <= i[None, :]).astype(np.float32)
    Lo = (i[:, None] >= i[None, :]).astype(np.float32)
    Us = (i[:, None] < i[None, :]).astype(np.float32)
    Ls = (i[:, None] > i[None, :]).astype(np.float32)
    NEG = -30000.0
    masks = np.stack([U, Lo, Us, Ls, (1 - U) * NEG, (1 - Lo) * NEG], axis=1).astype(np.float32)
    pos = np.arange(LE, dtype=np.float64) - PAD
    inv = 10000.0 ** (-np.arange(0, 64, 2, dtype=np.float64) / 64)
    ang = pos[None, :] * inv[:, None]
    cos = np.concatenate([np.cos(ang), np.cos(ang)], 0)
    sin = np.concatenate([np.sin(ang), np.sin(ang)], 0)
    rope = np.stack([cos, sin], 1).astype(np.float32)
    rot = np.zeros((64, 64), np.float32)
    for m in range(32):
        rot[m + 32, m] = -1.0
        rot[m, m + 32] = 1.0
    return dict(c_ident=ident, c_masks=masks, c_rope=rope, c_rot=rot)


_NC_CACHE = {}


def _in_maps(inputs):
    allx = np.concatenate([np.asarray(inputs["x_prompt"]), np.asarray(inputs["x_sample"])], 0)
    seqs = [[0, 1], [2, 3], [4, 5], [6, 7], [8, 8], [9, 9], [10, 10], [11, 11]]
    common = dict(
        meta=np.asarray(inputs["meta_tokens"]), ln_g=np.asarray(inputs["ln_g"]),
        g_w_in=np.asarray(inputs["gdn_w_in"])[0], g_conv=np.asarray(inputs["gdn_conv_w"])[0],
        g_alog=np.asarray(inputs["gdn_a_log"])[0].reshape(16), g_dtb=np.asarray(inputs["gdn_dt_bias"])[0].reshape(16),
        g_on=np.asarray(inputs["gdn_o_norm_g"])[0], g_w_out=np.asarray(inputs["gdn_w_out"])[0],
        m_w_in=np.asarray(inputs["mla_w_in"])[0], m_qn=np.asarray(inputs["mla_q_norm_g"])[0],
        m_kvn=np.asarray(inputs["mla_kv_norm_g"])[0], m_wuq=np.asarray(inputs["mla_w_uq"])[0],
        m_wukv=np.asarray(inputs["mla_w_ukv"])[0], m_qg=np.asarray(inputs["mla_qk_q_g"])[0],
        m_kg=np.asarray(inputs["mla_qk_k_g"])[0], m_w_out=np.asarray(inputs["mla_w_out"])[0],
    )
    common = {k: np.ascontiguousarray(v, dtype=np.float32) for k, v in common.items()}
    common.update(_consts())
    maps = []
    for c in range(8):
        m = dict(common)
        m["xs"] = np.ascontiguousarray(allx[seqs[c]])
        maps.append(m)
    return maps, seqs


def kernel(**inputs):
    if "nc" not in _NC_CACHE:
        _NC_CACHE["nc"] = build()
    nc = _NC_CACHE["nc"]
    maps, seqs = _in_maps(inputs)
    res = run_bass_kernel_spmd(nc, maps, core_ids=list(range(8)))
    full = np.zeros((12, LX, D), np.float32)
    for c in range(8):
        o = res.results[c]["out"]
        full[seqs[c][0]] = o[0]
        if seqs[c][1] != seqs[c][0]:
            full[seqs[c][1]] = o[1]
    return full[:4], full[4:]
```
